# Optimizing a Trainium2 kernel written in Bass

```python
import math
import jax
import jax.numpy as jnp
from jax import lax
import numpy as np

D_MODEL = 1024
BATCH = 4
SEQ = 4096
DEPTH = 1
DEC_BATCH = 1
DEC_SEQ = 16384
PAST_LEN = 128

GRID_W = 64
N_HEADS = 8
HEAD_DIM = 64
D_ATTN = N_HEADS * HEAD_DIM
WIN_R_MAX = 8
WIN_C = 16
D_HYENA = 512
SHORT_K = 3
FILTER_EMB = 33
FILTER_HIDDEN = 64
FILTER_OUT_SCALE = 0.05
DECAY_TARGET = 1e-2
FAST_DECAY_PCT = 0.3
SLOW_DECAY_PCT = 1.5
D_FF = -(-8 * D_MODEL // (3 * 256)) * 256
D_IN = 3 * D_ATTN + 3 * D_HYENA + 2 * D_MODEL
EPS = 1e-6

kernel_name = 'hybrid_natten_hyena_encoder'


def rmsnorm(x, g):
    xf = x.astype(jnp.float32)
    inv = lax.rsqrt(jnp.mean(xf * xf, axis=-1, keepdims=True) + EPS)
    return (xf * inv).astype(x.dtype) * g


def neighbourhood_attention(q, k, v, rpb):
    b, L = q.shape[0], q.shape[1]
    rows = L // GRID_W
    kr = min(WIN_R_MAX, rows)
    q = q.reshape(b, rows, GRID_W, N_HEADS, HEAD_DIM)
    k = k.reshape(b, rows, GRID_W, N_HEADS, HEAD_DIM)
    v = v.reshape(b, rows, GRID_W, N_HEADS, HEAD_DIM)
    r = jnp.arange(rows)
    row_start = jnp.clip(r - kr // 2, 0, rows - kr)
    key_rows = row_start[:, None] + jnp.arange(kr)[None, :]
    k_blk = k[:, key_rows]
    v_blk = v[:, key_rows]
    c = jnp.arange(GRID_W)
    col_start = jnp.clip(c - WIN_C // 2, 0, GRID_W - WIN_C)
    col_in = (c[None, :] >= col_start[:, None]) & (c[None, :] < col_start[:, None] + WIN_C)
    dr = key_rows - r[:, None] + (WIN_R_MAX - 1)
    dc = jnp.clip(c[None, :] - c[:, None], -(WIN_C - 1), WIN_C - 1) + (WIN_C - 1)
    bias = rpb[:, dr[:, None, :, None], dc[None, :, None, :]]
    bias = jnp.transpose(bias, (1, 0, 2, 3, 4)).astype(jnp.float32)
    s = jnp.einsum('brqhd,brikhd->brhqik', q, k_blk, preferred_element_type=jnp.float32)
    s = s * (HEAD_DIM ** -0.5) + bias
    s = jnp.where(col_in[:, None, :], s, -jnp.inf)
    p = jax.nn.softmax(s.reshape(s.shape[:4] + (kr * GRID_W,)), axis=-1).reshape(s.shape)
    o = jnp.einsum('brhqik,brikhd->brqhd', p.astype(v.dtype), v_blk)
    return o.reshape(b, L, D_ATTN)


def short_conv(x, w, bias):
    L = x.shape[1]
    pad = SHORT_K // 2
    xp = jnp.pad(x, ((0, 0), (pad, SHORT_K - 1 - pad), (0, 0)))
    y = xp[:, 0:L] * w[0]
    for j in range(1, SHORT_K):
        y = y + xp[:, j:j + L] * w[j]
    return y + bias


def implicit_filters(L, w1, b1, w2, b2, w3, b3, w4, freq):
    f32 = jnp.float32
    t = jnp.linspace(0.0, 1.0, L, dtype=f32)[:, None]
    bands = (FILTER_EMB - 1) // 2
    omega = 2.0 * math.pi * jnp.arange(L, dtype=f32) / L
    fb = jnp.linspace(1e-4, bands - 1, bands, dtype=f32)
    ang = omega[:, None] * fb[None, :]
    z = jnp.concatenate([t, jnp.cos(ang), -jnp.sin(ang)], axis=-1)
    fr = freq.astype(f32)
    h = jnp.sin(fr * (z @ w1.astype(f32) + b1.astype(f32)))
    h = jnp.sin(fr * (h @ w2.astype(f32) + b2.astype(f32)))
    h = jnp.sin(fr * (h @ w3.astype(f32) + b3.astype(f32)))
    h = h @ w4.astype(f32)
    max_decay = math.log(DECAY_TARGET) / FAST_DECAY_PCT
    min_decay = math.log(DECAY_TARGET) / SLOW_DECAY_PCT
    deltas = jnp.abs(jnp.linspace(min_decay, max_decay, D_HYENA, dtype=f32))
    decay = jnp.exp(-t * deltas[None, :])
    return h[:, :D_HYENA] * decay, h[:, D_HYENA:] * decay


def bidirectional_fftconv(u, h_fwd, h_bwd, d_skip):
    L = u.shape[1]
    kf = jnp.pad(h_fwd, ((0, L), (0, 0)))
    kb = jnp.concatenate([h_bwd[:1], jnp.zeros((L, D_HYENA), jnp.float32), h_bwd[:0:-1]], axis=0)
    k_f = jnp.fft.rfft(kf + kb, n=2 * L, axis=0)
    uf = u.astype(jnp.float32)
    y = jnp.fft.irfft(jnp.fft.rfft(uf, n=2 * L, axis=1) * k_f[None], n=2 * L, axis=1)[:, :L]
    return (y + uf * d_skip.astype(jnp.float32)).astype(u.dtype)


def hybrid_layer(x, norm_mix, w_in, rpb, conv_w, conv_b, filt_w1, filt_b1, filt_w2, filt_b2,
                 filt_w3, filt_b3, filt_w4, filt_freq, hyena_d, w_br_attn, w_br_hyena, w_out,
                 norm_ffn, w_gate, w_up, w_down):
    b, L, _ = x.shape
    h = rmsnorm(x, norm_mix)
    z = h @ w_in
    q, k, v, hy, g_attn, g_hyena = jnp.split(
        z, [D_ATTN, 2 * D_ATTN, 3 * D_ATTN, 3 * D_ATTN + 3 * D_HYENA,
            3 * D_ATTN + 3 * D_HYENA + D_MODEL], axis=-1)
    heads = (b, L, N_HEADS, HEAD_DIM)
    y_attn = neighbourhood_attention(q.reshape(heads), k.reshape(heads), v.reshape(heads), rpb)
    hy = short_conv(hy, conv_w, conv_b)
    x0, x1, hv = jnp.split(hy, 3, axis=-1)
    h_fwd, h_bwd = implicit_filters(L, filt_w1, filt_b1, filt_w2, filt_b2, filt_w3, filt_b3,
                                    filt_w4, filt_freq)
    y_hyena = x0 * bidirectional_fftconv(x1 * hv, h_fwd, h_bwd, hyena_d)
    merged = (jax.nn.sigmoid(g_attn) * (y_attn @ w_br_attn)
              + jax.nn.sigmoid(g_hyena) * (y_hyena @ w_br_hyena))
    x = x + merged @ w_out
    h = rmsnorm(x, norm_ffn)
    x = x + (jax.nn.silu(h @ w_gate) * (h @ w_up)) @ w_down
    return x


def trunk(x, norm_mix, w_in, rpb, conv_w, conv_b, filt_w1, filt_b1, filt_w2, filt_b2,
          filt_w3, filt_b3, filt_w4, filt_freq, hyena_d, w_br_attn, w_br_hyena, w_out,
          norm_ffn, w_gate, w_up, w_down, norm_final):
    for l in range(DEPTH):
        x = hybrid_layer(x, norm_mix[l], w_in[l], rpb[l], conv_w[l], conv_b[l],
                         filt_w1[l], filt_b1[l], filt_w2[l], filt_b2[l], filt_w3[l], filt_b3[l],
                         filt_w4[l], filt_freq[l], hyena_d[l], w_br_attn[l], w_br_hyena[l],
                         w_out[l], norm_ffn[l], w_gate[l], w_up[l], w_down[l])
    return rmsnorm(x, norm_final)


def setup_inputs(seed: int = 0) -> dict:
    key = jax.random.key(seed)
    ks = jax.random.split(key, 24)

    def nrm(k, shape, scale):
        return jax.random.normal(k, shape, jnp.float32) * scale

    return {
        'x_prompt': nrm(ks[0], (BATCH, SEQ, D_MODEL), 1.0),
        'x_sample': nrm(ks[1], (DEC_BATCH, DEC_SEQ, D_MODEL), 1.0),
        'norm_mix': 1.0 + nrm(ks[2], (DEPTH, D_MODEL), 0.01),
        'w_in': nrm(ks[3], (DEPTH, D_MODEL, D_IN), D_MODEL ** -0.5),
        'rpb': nrm(ks[4], (DEPTH, N_HEADS, 2 * WIN_R_MAX - 1, 2 * WIN_C - 1), 0.02),
        'conv_w': nrm(ks[5], (DEPTH, SHORT_K, 3 * D_HYENA), SHORT_K ** -0.5),
        'conv_b': nrm(ks[6], (DEPTH, 3 * D_HYENA), 0.01),
        'filt_w1': nrm(ks[7], (DEPTH, FILTER_EMB, FILTER_HIDDEN), FILTER_EMB ** -0.5),
        'filt_b1': nrm(ks[8], (DEPTH, FILTER_HIDDEN), 0.1),
        'filt_w2': nrm(ks[9], (DEPTH, FILTER_HIDDEN, FILTER_HIDDEN), FILTER_HIDDEN ** -0.5),
        'filt_b2': nrm(ks[10], (DEPTH, FILTER_HIDDEN), 0.1),
        'filt_w3': nrm(ks[11], (DEPTH, FILTER_HIDDEN, FILTER_HIDDEN), FILTER_HIDDEN ** -0.5),
        'filt_b3': nrm(ks[12], (DEPTH, FILTER_HIDDEN), 0.1),
        'filt_w4': nrm(ks[13], (DEPTH, FILTER_HIDDEN, 2 * D_HYENA), FILTER_HIDDEN ** -0.5 * FILTER_OUT_SCALE),
        'filt_freq': 1.0 + nrm(ks[14], (DEPTH, FILTER_HIDDEN), 0.01),
        'hyena_d': nrm(ks[15], (DEPTH, D_HYENA), 1.0),
        'w_br_attn': nrm(ks[16], (DEPTH, D_ATTN, D_MODEL), D_ATTN ** -0.5),
        'w_br_hyena': nrm(ks[17], (DEPTH, D_HYENA, D_MODEL), D_HYENA ** -0.5),
        'w_out': nrm(ks[18], (DEPTH, D_MODEL, D_MODEL), D_MODEL ** -0.5),
        'norm_ffn': 1.0 + nrm(ks[19], (DEPTH, D_MODEL), 0.01),
        'w_gate': nrm(ks[20], (DEPTH, D_MODEL, D_FF), D_MODEL ** -0.5),
        'w_up': nrm(ks[21], (DEPTH, D_MODEL, D_FF), D_MODEL ** -0.5),
        'w_down': nrm(ks[22], (DEPTH, D_FF, D_MODEL), D_FF ** -0.5),
        'norm_final': 1.0 + nrm(ks[23], (D_MODEL,), 0.01),
    }


def reference(x_prompt, x_sample, norm_mix, w_in, rpb, conv_w, conv_b, filt_w1, filt_b1,
              filt_w2, filt_b2, filt_w3, filt_b3, filt_w4, filt_freq, hyena_d, w_br_attn,
              w_br_hyena, w_out, norm_ffn, w_gate, w_up, w_down, norm_final):
    y_prompt = trunk(x_prompt, norm_mix, w_in, rpb, conv_w, conv_b, filt_w1, filt_b1, filt_w2,
                     filt_b2, filt_w3, filt_b3, filt_w4, filt_freq, hyena_d, w_br_attn,
                     w_br_hyena, w_out, norm_ffn, w_gate, w_up, w_down, norm_final)
    y_sample = trunk(x_sample, norm_mix, w_in, rpb, conv_w, conv_b, filt_w1, filt_b1, filt_w2,
                     filt_b2, filt_w3, filt_b3, filt_w4, filt_freq, hyena_d, w_br_attn,
                     w_br_hyena, w_out, norm_ffn, w_gate, w_up, w_down, norm_final)
    return (y_prompt, y_sample)
```

```python
import math
import numpy as np
import ml_dtypes
from contextlib import ExitStack
import concourse.bass as bass
import concourse.mybir as mybir
from concourse.bass_utils import run_bass_kernel_spmd

F32 = mybir.dt.float32
BF16 = mybir.dt.bfloat16
AF = mybir.ActivationFunctionType
ALU = mybir.AluOpType
BF = ml_dtypes.bfloat16

D_MODEL = 1024
D_IN = 5120
D_FF = 2816
NFF = 22
EPS = 1e-6
NEG = -30000.0
PI = float(np.pi)


class Dep:
    __slots__ = ("w", "r")

    def __init__(self):
        self.w = {}
        self.r = {}


class DSem:
    def __init__(self, sem):
        self.sem = sem
        self.n = 0
        self.nobar = False


class Prog:
    ENGS = ["sync", "scalar", "vector", "gpsimd", "tensor"]

    def __init__(self, nc, es):
        self.nc = nc
        self.h = {"sync": nc.sync, "scalar": nc.scalar, "vector": nc.vector,
                  "gpsimd": nc.gpsimd, "tensor": nc.tensor}
        self.sem = {e: es.enter_context(nc.semaphore("s_" + e)) for e in self.ENGS}
        self.cnt = {e: 0 for e in self.ENGS}
        self.known = {e: {} for e in self.ENGS}
        self.dsems = []
        self.es = es
        self.rr = 0

    def dsem(self, name):
        s = DSem(self.es.enter_context(self.nc.semaphore(name)))
        self.dsems.append(s)
        return s

    def _need(self, eng, items):
        k = self.known[eng]
        for it in items:
            if it is None:
                continue
            sem, val, e2 = it
            if e2 == eng and eng == "tensor":
                continue
            if k.get(id(sem), 0) >= val:
                continue
            k[id(sem)] = val
            self.h[eng].wait_ge(sem, val)

    def _items(self, reads, writes):
        items = []
        for d in reads:
            items.extend(d.w.values())
        for d in writes:
            items.extend(d.w.values())
            items.extend(d.r.values())
        return items

    def _upd(self, tok, reads, writes):
        key = id(tok[0])
        for d in reads:
            d.r[key] = tok
        for d in writes:
            d.w[key] = tok
            d.r = {}

    def op(self, eng, fn, r=(), w=()):
        self._need(eng, self._items(r, w))
        ins = fn(self.h[eng])
        self.cnt[eng] += 1
        ins.then_inc(self.sem[eng], 1)
        tok = (self.sem[eng], self.cnt[eng], eng)
        self._upd(tok, r, w)
        return tok

    def dma(self, eng, ds, out, in_, r=(), w=()):
        self._need(eng, self._items(r, w))
        ins = self.h[eng].dma_start(out=out, in_=in_)
        ds.n += 16
        ins.then_inc(ds.sem, 16)
        tok = (ds.sem, ds.n, "dma")
        self._upd(tok, r, w)
        return tok

    def barrier(self):
        toks = [(self.sem[e], self.cnt[e], e) for e in self.ENGS if self.cnt[e] > 0]
        toks += [(d.sem, d.n, "dma") for d in self.dsems if d.n > 0 and not d.nobar]
        for e in self.ENGS:
            k = self.known[e]
            for sem, val, e2 in toks:
                if e2 == e:
                    continue
                if k.get(id(sem), 0) >= val:
                    continue
                k[id(sem)] = val
                self.h[e].wait_ge(sem, val)

    def alt(self, a="vector", b="gpsimd"):
        self.rr += 1
        return a if self.rr % 2 else b


_UID = [0]


def pipeline(n_iter, stages):
    ns = len(stages)
    for t in range(n_iter + ns - 1):
        for k, f in enumerate(stages):
            i = t - k
            if 0 <= i < n_iter:
                f(i)


def T(es, nc, name, shape, dt):
    _UID[0] += 1
    return es.enter_context(nc.sbuf_tensor(f"sb{_UID[0]}_{name}", shape, dt))


def PS(es, nc, name, shape, dt):
    _UID[0] += 1
    return es.enter_context(nc.psum_tensor(f"ps{_UID[0]}_{name}", shape, dt))


def rms_transpose(P, nc, es, xrows, nsub, hT, d_hT, grep, d_grep, ident, d_ident, pfx, dq):
    NB = 3
    xs = [T(es, nc, f"{pfx}xs{i}", [128, 1024], F32) for i in range(NB)]
    d_xs = [Dep() for _ in range(NB)]
    xn = [T(es, nc, f"{pfx}xn{i}", [128, 1024], BF16) for i in range(NB)]
    d_xn = [Dep() for _ in range(NB)]
    junk = T(es, nc, f"{pfx}junk", [128, 1024], BF16)
    d_junk = Dep()
    stt = [T(es, nc, f"{pfx}st{i}", [128, 4], F32) for i in range(NB)]
    d_st = [Dep() for _ in range(NB)]
    pt = [PS(es, nc, f"{pfx}pt{i}", [128, 8, 128], BF16) for i in range(2)]
    d_pt = [Dep() for _ in range(2)]
    for i in range(NB):
        P.op("gpsimd", lambda e: e.memset(stt[i][:], EPS), w=[d_st[i]])

    def S1(st):
        b = st % NB
        P.dma("sync", dq[b], xs[b][:], xrows[st * 128:(st + 1) * 128, :], w=[d_xs[b]])
        P.op("scalar", lambda e: e.activation(out=junk[:], in_=xs[b][:], func=AF.Square,
                                              accum_out=stt[b][:, 0:1]),
             r=[d_xs[b]], w=[d_junk, d_st[b]])
        P.op("scalar", lambda e: e.activation(out=stt[b][:, 1:2], in_=stt[b][:, 0:1], func=AF.Sqrt,
                                              scale=1.0 / 1024, bias=stt[b][:, 3:4]),
             r=[d_st[b]], w=[d_st[b]])

    def S2(st):
        b = st % NB
        P.op("vector", lambda e: e.reciprocal(out=stt[b][:, 2:3], in_=stt[b][:, 1:2]),
             r=[d_st[b]], w=[d_st[b]])
        if st % 2:
            P.op("scalar", lambda e: e.activation(out=xn[b][:], in_=xs[b][:], func=AF.Copy, scale=stt[b][:, 2:3]),
                 r=[d_st[b], d_xs[b]], w=[d_xn[b]])
        else:
            P.op("vector", lambda e: e.tensor_scalar(out=xn[b][:], in0=xs[b][:], scalar1=stt[b][:, 2:3],
                                                     scalar2=None, op0=ALU.mult),
                 r=[d_st[b], d_xs[b]], w=[d_xn[b]])

    def S3(st):
        b = st % NB
        p = st % 2
        for kc in range(8):
            P.op("tensor", lambda e: e.transpose(pt[p][:, kc, :], xn[b][:, kc * 128:(kc + 1) * 128], ident[:]),
                 r=[d_xn[b], d_ident], w=[d_pt[p]])
        P.op("vector", lambda e: e.tensor_tensor(out=hT[:, :, st * 128:(st + 1) * 128], in0=pt[p][:],
                                                 in1=grep[:], op=ALU.mult),
             r=[d_pt[p], d_grep], w=[d_hT])

    pipeline(nsub, [S1, S2, S3])


def make_grep(P, nc, es, gcol, name, dq):
    g = T(es, nc, name + "g", [128, 8], F32)
    d_g = Dep()
    grep = T(es, nc, name + "rep", [128, 8, 128], F32)
    d_grep = Dep()
    P.dma("sync", dq, g[:], gcol, w=[d_g])
    P.op("gpsimd", lambda e: e.memset(grep[:], 1.0), w=[d_grep])
    for kc in range(8):
        P.op("vector", lambda e: e.tensor_scalar(out=grep[:, kc, :], in0=grep[:, kc, :],
                                                 scalar1=g[:, kc:kc + 1], scalar2=None, op0=ALU.mult),
             r=[d_g], w=[d_grep])
    return grep, d_grep


def load_ident(P, nc, es, D, dq):
    ident = T(es, nc, "ident", [128, 128], BF16)
    d_ident = Dep()
    P.dma("sync", dq, ident[:], D["ident"], w=[d_ident])
    return ident, d_ident


def eps_init(P, stt_list):
    pass


def build_program(dev=False, phases="ABCDEFGHI"):
    nc = bass.Bass("TRN2", target_bir_lowering=False)
    D = {}

    def inp(name, shape, dt=F32):
        D[name] = nc.dram_tensor(name, list(shape), dt, kind="ExternalInput").ap()

    def scr(name, shape, dt=BF16):
        if dev:
            D[name] = nc.dram_tensor(name, list(shape), dt, kind="ExternalOutput").ap()
        else:
            D[name] = nc.dram_tensor(name, list(shape), dt).ap()

    inp("xext", [4608, 1024])
    inp("xctx", [4, 4224, 1024])
    inp("w_in", [1024, D_IN])
    inp("w_br_attn", [512, 1024])
    inp("w_br_hyena", [512, 1024])
    inp("w_out", [1024, 1024])
    inp("w_gate", [1024, D_FF])
    inp("w_up", [1024, D_FF])
    inp("w_down", [D_FF, 1024])
    inp("g_mix", [128, 8])
    inp("g_ffn", [128, 8])
    inp("g_final", [128, 1024])
    inp("conv_w", [128, 12, 3])
    inp("conv_b", [128, 12])
    inp("fw1", [33, 64])
    inp("fw2", [64, 64])
    inp("fw3", [64, 64])
    inp("fw4", [64, 1024])
    inp("fb", [128, 3])
    inp("ffr", [128, 1])
    inp("hyd", [128, 512])
    inp("ident", [128, 128], BF16)
    inp("Etab", [128, 64, 2, 128], BF16)
    inp("F2tab", [128, 3, 128], BF16)
    inp("ITtab", [128, 33, 3, 128], BF16)
    inp("ICtab", [66, 2, 64], BF16)
    inp("zs", [4, 128, 4096], BF16)
    inp("maskT", [4, 128, 8192], BF16)
    inp("tpos", [128, 4, 64])
    inp("vbias", [128, 4, 64])
    inp("e0", [128, 1])
    inp("negdelta", [128, 512])
    inp("abias", [3, 8, 128, 6, 256], BF16)

    scr("QT", [4, 128, 4096])
    scr("KT", [4, 128, 4608])
    scr("VA", [36, 128, 8 * 65])
    scr("X0T", [4, 128, 4096])
    scr("DU", [4, 64, 4, 64, 128])
    scr("AU", [4, 64, 66, 2, 512])
    scr("AG", [4, 64, 66, 2, 512])
    scr("CS", [66, 64, 2, 512])
    scr("YHT", [4, 128, 4096])
    scr("YAT", [4, 128, 4096])
    for nm, shp in (("WG", [16, 128, 8, 128]), ("WBA", [8, 128, 4, 128]), ("WBH", [8, 128, 4, 128]),
                    ("WO", [128, 8, 1024]), ("WFG", [22, 128, 8, 128]), ("WFU", [22, 128, 8, 128]),
                    ("WD", [2, 128, 22, 512])):
        D[nm] = nc.dram_tensor(nm, shp, BF16).ap()
    D["y"] = nc.dram_tensor("y", [4096, 1024], F32, kind="ExternalOutput").ap()

    with ExitStack() as es0:
        P = Prog(nc, es0)
        dq = [P.dsem(f"dq{i}") for i in range(64)]
        gq = [P.dsem(f"gq{i}") for i in range(16)]
        wsem = [P.dsem(f"wq{i}") for i in range(4)]
        if "A" in phases:
            phase_A(P, nc, D, dq, gq)
            P.barrier()
        if "B" in phases:
            phase_B(P, nc, D, dq, gq)
            P.barrier()
        if "I" in phases:
            phase_W(P, nc, D, wsem)
        if "C" in phases:
            phase_C(P, nc, D, dq, gq)
            P.barrier()
        if "D" in phases:
            phase_D(P, nc, D, dq, gq)
            P.barrier()
        if "F" in phases:
            phase_F(P, nc, D, dq, gq)
            P.barrier()
        if "G" in phases:
            phase_G(P, nc, D, dq, gq)
            P.barrier()
        if "H" in phases:
            phase_H(P, nc, D, dq, gq)
            P.barrier()
        if "I" in phases:
            for w in wsem:
                w.nobar = False
            P.barrier()
            phase_I(P, nc, D, dq, gq)
            P.barrier()
    return nc


def wchunk_src(w, col0, ncols, nk=8):
    return w.rearrange("(kc p) n -> p kc n", p=128)[:, :, col0:col0 + ncols]


def phase_A(P, nc, D, dq, gq):
    with ExitStack() as es:
        ident, d_ident = load_ident(P, nc, es, D, dq[0])
        grep, d_grep = make_grep(P, nc, es, D["g_mix"], "gm", dq[1])
        hT = T(es, nc, "A_hT", [128, 8, 4608], BF16)
        d_hT = Dep()
        with ExitStack() as es1:
            rms_transpose(P, nc, es1, D["xext"], 36, hT, d_hT, grep, d_grep, ident, d_ident, "A_", dq[40:43])
        P.barrier()
        wq = [T(es, nc, f"A_wq{i}", [128, 8, 128], BF16) for i in range(2)]
        d_wq = [Dep() for _ in range(2)]
        stg = [T(es, nc, f"A_stg{i}", [128, 4608], BF16) for i in range(2)]
        d_stg = [Dep() for _ in range(2)]
        ps = [PS(es, nc, f"A_ps{i}", [128, 512], F32) for i in range(4)]
        d_ps = [Dep() for _ in range(4)]
        pi = 0
        for ci in range(8):
            b = ci % 2
            isq = ci < 4
            P.dma("gpsimd", gq[4 + b], wq[b][:], wchunk_src(D["w_in"], ci * 128, 128), w=[d_wq[b]])
            t0, ntile = (256, 8) if isq else (0, 9)
            for tt in range(ntile):
                p = pi % 4
                pi += 1
                for kc in range(8):
                    P.op("tensor", lambda e: e.matmul(ps[p][:], wq[b][:, kc, :],
                                                      hT[:, kc, t0 + tt * 512:t0 + (tt + 1) * 512],
                                                      start=(kc == 0), stop=(kc == 7)),
                         r=[d_wq[b], d_hT], w=[d_ps[p]])
                if tt % 2 == 0:
                    P.op("scalar", lambda e: e.activation(out=stg[b][:, tt * 512:(tt + 1) * 512], in_=ps[p][:],
                                                          func=AF.Copy, scale=(0.125 if isq else 1.0)),
                         r=[d_ps[p]], w=[d_stg[b]])
                else:
                    P.op("vector", lambda e: e.tensor_scalar(out=stg[b][:, tt * 512:(tt + 1) * 512], in0=ps[p][:],
                                                             scalar1=(0.125 if isq else 1.0), scalar2=None,
                                                             op0=ALU.mult),
                         r=[d_ps[p]], w=[d_stg[b]])
            if isq:
                P.dma("sync", dq[6 + b], D["QT"][ci], stg[b][:, 0:4096], r=[d_stg[b]])
            else:
                P.dma("sync", dq[6 + b], D["KT"][ci - 4], stg[b][:, 0:4608], r=[d_stg[b]])
        wv = T(es, nc, "A_wv", [128, 8, 512], BF16)
        d_wv = Dep()
        P.dma("gpsimd", gq[8], wv[:], wchunk_src(D["w_in"], 1024, 512), w=[d_wv])
        vst = [T(es, nc, f"A_vst{i}", [128, 8, 65], BF16) for i in range(2)]
        d_vst = [Dep() for _ in range(2)]
        for i in range(2):
            P.op("gpsimd", lambda e: e.memset(vst[i][:], 1.0), w=[d_vst[i]])
        for st in range(36):
            p = pi % 4
            pi += 1
            b = st % 2
            for kc in range(8):
                P.op("tensor", lambda e: e.matmul(ps[p][:], hT[:, kc, st * 128:(st + 1) * 128], wv[:, kc, :],
                                                  start=(kc == 0), stop=(kc == 7)),
                     r=[d_wv, d_hT], w=[d_ps[p]])
            P.op("scalar" if st % 2 else "vector",
                 (lambda e: e.activation(out=vst[b][:, :, 0:64], in_=ps[p][:].rearrange("p (h d) -> p h d", d=64),
                                         func=AF.Copy)) if st % 2 else
                 (lambda e: e.tensor_copy(out=vst[b][:, :, 0:64], in_=ps[p][:].rearrange("p (h d) -> p h d", d=64))),
                 r=[d_ps[p]], w=[d_vst[b]])
            P.dma("sync", dq[9 + b], D["VA"][st], vst[b][:].rearrange("p h d -> p (h d)"), r=[d_vst[b]])


def phase_B(P, nc, D, dq, gq):
    with ExitStack() as es:
        ident, d_ident = load_ident(P, nc, es, D, dq[0])
        grep, d_grep = make_grep(P, nc, es, D["g_mix"], "gm", dq[1])
        cw = T(es, nc, "B_cw", [128, 12, 3], F32)
        cb = T(es, nc, "B_cb", [128, 12], F32)
        d_c = Dep()
        P.dma("sync", dq[2], cw[:], D["conv_w"], w=[d_c])
        P.dma("sync", dq[3], cb[:], D["conv_b"], w=[d_c])
        hT = T(es, nc, "B_hT", [128, 8, 4224], BF16)
        d_hT = Dep()
        hy = [T(es, nc, f"B_hy{i}", [128, 4224], F32) for i in range(2)]
        d_hy = [Dep() for _ in range(2)]
        cx = [T(es, nc, f"B_cx{i}", [128, 4096], BF16) for i in range(3)]
        d_cx = [Dep() for _ in range(3)]
        uT = [T(es, nc, f"B_uT{i}", [128, 4096], BF16) for i in range(2)]
        d_uT = [Dep() for _ in range(2)]
        du = T(es, nc, "B_du", [64, 64, 128], BF16)
        d_du = Dep()
        wq = [T(es, nc, f"B_wq{i}", [128, 8, 128], BF16) for i in range(2)]
        d_wq = [Dep() for _ in range(2)]
        ps = [PS(es, nc, f"B_ps{i}", [128, 512], F32) for i in range(4)]
        d_ps = [Dep() for _ in range(4)]
        ptr = [PS(es, nc, f"B_ptr{i}", [64, 8, 128], BF16) for i in range(2)]
        d_ptr = [Dep() for _ in range(2)]
        st8 = {"pi": 0, "wi": 0}
        for blk in range(4):
            with ExitStack() as es1:
                P.barrier()
                rms_transpose(P, nc, es1, D["xctx"][blk], 33, hT, d_hT, grep, d_grep, ident, d_ident,
                              f"B{blk}_", dq[40:43])
            P.barrier()
            items = []
            for j in range(4):
                items.append((1, 2048 + j * 128, 4 + j, j))
                items.append((2, 2560 + j * 128, 8 + j, j))
                if blk == 0:
                    items.append((0, 1536 + j * 128, j, j))
            base = st8["wi"]

            def S1(c, items=items, base=base):
                role, col0, cidx, j = items[c]
                b = (base + c) % 2
                k3 = (base + c) % 3
                P.dma("gpsimd", gq[b], wq[b][:], wchunk_src(D["w_in"], col0, 128), w=[d_wq[b]])
                for tt in range(9):
                    p = st8["pi"] % 4
                    st8["pi"] += 1
                    n = 512 if tt < 8 else 128
                    for kc in range(8):
                        P.op("tensor", lambda e: e.matmul(ps[p][:, 0:n], wq[b][:, kc, :],
                                                          hT[:, kc, tt * 512:tt * 512 + n],
                                                          start=(kc == 0), stop=(kc == 7)),
                             r=[d_wq[b], d_hT], w=[d_ps[p]])
                    P.op("scalar", lambda e: e.activation(out=hy[b][:, tt * 512:tt * 512 + n], in_=ps[p][:, 0:n],
                                                          func=AF.Copy),
                         r=[d_ps[p]], w=[d_hy[b]])
                    if tt < 8:
                        P.op("scalar", lambda e: e.activation(out=cx[k3][:, tt * 512:(tt + 1) * 512], in_=ps[p][:],
                                                              func=AF.Identity, scale=cw[:, cidx, 1:2],
                                                              bias=cb[:, cidx:cidx + 1]),
                             r=[d_ps[p], d_c], w=[d_cx[k3]])

            def S2(c, items=items, base=base):
                role, col0, cidx, j = items[c]
                b = (base + c) % 2
                k3 = (base + c) % 3
                cc = cx[k3]
                dc = d_cx[k3]
                h = hy[b]
                P.op("vector", lambda e: e.scalar_tensor_tensor(out=cc[:, 1:4096], in0=h[:, 0:4095],
                                                                scalar=cw[:, cidx, 0:1], in1=cc[:, 1:4096],
                                                                op0=ALU.mult, op1=ALU.add),
                     r=[d_hy[b], d_c], w=[dc])
                P.op("vector", lambda e: e.scalar_tensor_tensor(out=cc[:, 0:4095], in0=h[:, 1:4096],
                                                                scalar=cw[:, cidx, 2:3], in1=cc[:, 0:4095],
                                                                op0=ALU.mult, op1=ALU.add),
                     r=[d_hy[b], d_c], w=[dc])
                P.op("vector", lambda e: e.scalar_tensor_tensor(out=cc[:, 0:1], in0=h[:, 4096:4097],
                                                                scalar=cw[:, cidx, 0:1], in1=cc[:, 0:1],
                                                                op0=ALU.mult, op1=ALU.add),
                     r=[d_hy[b], d_c], w=[dc])
                P.op("vector", lambda e: e.scalar_tensor_tensor(out=cc[:, 4095:4096], in0=h[:, 4097:4098],
                                                                scalar=cw[:, cidx, 2:3], in1=cc[:, 4095:4096],
                                                                op0=ALU.mult, op1=ALU.add),
                     r=[d_hy[b], d_c], w=[dc])
                if role == 2:
                    kx = (base + c - 1) % 3
                    P.op("vector", lambda e: e.tensor_tensor(out=uT[j % 2][:], in0=cx[kx][:], in1=cc[:], op=ALU.mult),
                         r=[d_cx[kx], dc], w=[d_uT[j % 2]])

            def S3(c, items=items, base=base, blk=blk):
                role, col0, cidx, j = items[c]
                k3 = (base + c) % 3
                if role == 2:
                    uv = uT[j % 2][:].rearrange("p (a b) -> p a b", b=64)
                    for g8 in range(8):
                        q = g8 % 2
                        for tl in range(8):
                            t2 = g8 * 8 + tl
                            P.op("tensor", lambda e: e.transpose(ptr[q][:, tl, :], uv[:, :, t2], ident[:]),
                                 r=[d_uT[j % 2], d_ident], w=[d_ptr[q]])
                        P.op("scalar" if g8 % 2 else "vector",
                             (lambda e: e.activation(out=du[:, g8 * 8:(g8 + 1) * 8, :], in_=ptr[q][:], func=AF.Copy))
                             if g8 % 2 else
                             (lambda e: e.tensor_copy(out=du[:, g8 * 8:(g8 + 1) * 8, :], in_=ptr[q][:])),
                             r=[d_ptr[q]], w=[d_du])
                    P.dma("sync", dq[8], D["DU"][blk, :, j], du[:], r=[d_du])
                if role == 0:
                    P.dma("sync", dq[9], D["X0T"][j], cx[k3][:], r=[d_cx[k3]])

            pipeline(len(items), [S1, S2, S3])
            st8["wi"] += len(items)


def fft_stage1(P, nc, es, K, rhs_of, d_src, Et, d_Et, Ascr, dq, pfx, nps=4):
    ps = [PS(es, nc, f"{pfx}s1p{i}", [128, 512], F32) for i in range(nps)]
    d_ps = [Dep() for _ in range(nps)]
    ast = [T(es, nc, f"{pfx}ast{i}", [66, 2, 512], BF16) for i in range(3)]
    d_ast = [Dep() for _ in range(3)]
    for t2 in range(64):
        a = t2 % 3
        for ri in range(2):
            p = (t2 * 2 + ri) % nps
            P.op("tensor", lambda e: e.matmul(ps[p][0:66, :], Et[0:K, t2, ri, 0:66], rhs_of(t2), start=True, stop=True),
                 r=[d_Et, d_src], w=[d_ps[p]])
            if ri == 0:
                P.op("scalar", lambda e: e.activation(out=ast[a][:, ri, :], in_=ps[p][0:66, :], func=AF.Copy),
                     r=[d_ps[p]], w=[d_ast[a]])
            else:
                P.op("vector", lambda e: e.tensor_copy(out=ast[a][:, ri, :], in_=ps[p][0:66, :]),
                     r=[d_ps[p]], w=[d_ast[a]])
        P.dma("sync", dq[a], Ascr[t2], ast[a][:], r=[d_ast[a]])


def sin_group(P, pm, d_pm, tmp, d_tmp, t1, d_t1, fr, frb, d_f, out, d_out):
    P.op("scalar", lambda e: e.activation(out=tmp[:], in_=pm[:], func=AF.Identity, scale=fr, bias=frb),
         r=[d_pm, d_f], w=[d_tmp])
    P.op("vector", lambda e: e.tensor_scalar(out=t1[:], in0=tmp[:], scalar1=-1.0, scalar2=PI,
                                             op0=ALU.mult, op1=ALU.add),
         r=[d_tmp], w=[d_t1])
    P.op("vector", lambda e: e.tensor_tensor(out=tmp[:], in0=tmp[:], in1=t1[:], op=ALU.min),
         r=[d_t1], w=[d_tmp])
    P.op("vector", lambda e: e.scalar_tensor_tensor(out=tmp[:], in0=t1[:], scalar=-2 * PI, in1=tmp[:],
                                                    op0=ALU.add, op1=ALU.max),
         r=[d_t1], w=[d_tmp])
    P.op("scalar", lambda e: e.activation(out=out, in_=tmp[:], func=AF.Sin), r=[d_tmp], w=[d_out])


def phase_C(P, nc, D, dq, gq):
    with ExitStack() as es:
        Et = T(es, nc, "C_Et", [128, 64, 2, 128], BF16)
        d_Et = Dep()
        P.dma("sync", dq[0], Et[:], D["Etab"], w=[d_Et])
        W1 = T(es, nc, "C_W1", [128, 128], BF16)
        W2 = T(es, nc, "C_W2", [128, 128], BF16)
        W3 = T(es, nc, "C_W3", [128, 128], BF16)
        W4 = T(es, nc, "C_W4", [128, 512], BF16)
        d_w = Dep()
        P.op("gpsimd", lambda e: e.memset(W1[:], 0.0), w=[d_w])
        P.op("gpsimd", lambda e: e.memset(W2[:], 0.0), w=[d_w])
        for hf in range(2):
            P.dma("gpsimd", gq[0], W1[hf * 64:hf * 64 + 33, hf * 64:(hf + 1) * 64], D["fw1"], w=[d_w])
            P.dma("gpsimd", gq[1], W2[hf * 64:(hf + 1) * 64, hf * 64:(hf + 1) * 64], D["fw2"], w=[d_w])
            for h2 in range(2):
                P.dma("gpsimd", gq[2], W3[hf * 64:(hf + 1) * 64, h2 * 64:(h2 + 1) * 64], D["fw3"], w=[d_w])
            P.dma("gpsimd", gq[3], W4[hf * 64:(hf + 1) * 64, :], D["fw4"][:, hf * 512:(hf + 1) * 512], w=[d_w])
        fb = T(es, nc, "C_fb", [128, 3], F32)
        fr = T(es, nc, "C_fr", [128, 1], F32)
        frb = T(es, nc, "C_frb", [128, 3], F32)
        d_f = Dep()
        P.dma("sync", dq[5], fb[:], D["fb"], w=[d_f])
        P.dma("sync", dq[6], fr[:], D["ffr"], w=[d_f])
        P.op("vector", lambda e: e.tensor_scalar(out=frb[:], in0=fb[:], scalar1=fr[:, 0:1], scalar2=None,
                                                 op0=ALU.mult), r=[d_f], w=[d_f])
        tpos = T(es, nc, "C_tpos", [128, 4, 64], F32)
        vbias = T(es, nc, "C_vbias", [128, 4, 64], F32)
        e0 = T(es, nc, "C_e0", [128, 1], F32)
        negd = T(es, nc, "C_negd", [128, 512], F32)
        hyd = T(es, nc, "C_hyd", [128, 512], F32)
        d_t = Dep()
        P.dma("sync", dq[7], tpos[:], D["tpos"], w=[d_t])
        P.dma("sync", dq[8], vbias[:], D["vbias"], w=[d_t])
        P.dma("sync", dq[9], e0[:], D["e0"], w=[d_t])
        P.dma("sync", dq[10], negd[:], D["negdelta"], w=[d_t])
        P.dma("sync", dq[11], hyd[:], D["hyd"], w=[d_t])
        zs = T(es, nc, "C_zs", [128, 4096], BF16)
        d_z = Dep()
        mk = T(es, nc, "C_mk", [128, 8192], BF16)
        d_mk = Dep()
        hA = T(es, nc, "C_hA", [128, 4096], BF16)
        hB = T(es, nc, "C_hB", [128, 4096], BF16)
        d_hAg = [Dep() for _ in range(4)]
        d_hBg = [Dep() for _ in range(4)]
        h3 = T(es, nc, "C_h3", [128, 8192], BF16)
        d_h3g = [Dep() for _ in range(8)]
        tmp = [T(es, nc, f"C_tmp{i}", [128, 1024], F32) for i in range(2)]
        t1b = [T(es, nc, f"C_t1{i}", [128, 1024], F32) for i in range(2)]
        d_tmp = [Dep() for _ in range(2)]
        d_t1b = [Dep() for _ in range(2)]
        gi = 0
        gf = [T(es, nc, f"C_gf{i}", [128, 512], BF16) for i in range(3)]
        d_gf = [Dep() for _ in range(3)]
        ast = [T(es, nc, f"C_ast{i}", [66, 2, 512], BF16) for i in range(3)]
        d_ast = [Dep() for _ in range(3)]
        ps1 = [PS(es, nc, f"C_ps1{i}", [128, 512], F32) for i in range(2)]
        d_ps1 = [Dep() for _ in range(2)]
        dec = [T(es, nc, f"C_dec{i}", [128, 512], F32) for i in range(2)]
        d_dec = [Dep() for _ in range(2)]
        gt = T(es, nc, "C_gt", [128, 512], F32)
        d_gt = Dep()
        pm = [PS(es, nc, f"C_pm{i}", [128, 1024], F32) for i in range(2)]
        d_pm = [Dep() for _ in range(2)]
        p4 = [PS(es, nc, f"C_p4{i}", [128, 512], F32) for i in range(2)]
        d_p4 = [Dep() for _ in range(2)]
        for s in range(4):
            P.dma("sync", dq[12], zs[:], D["zs"][s], w=[d_z])
            P.dma("sync", dq[13], mk[:], D["maskT"][s], w=[d_mk])
            groups = []
            for g in range(4):
                groups.append((W1[:], zs[:, g * 1024:(g + 1) * 1024], [d_z], hA[:, g * 1024:(g + 1) * 1024],
                               d_hAg[g], 0, None))
            for g in range(4):
                groups.append((W2[:], hA[:, g * 1024:(g + 1) * 1024], [d_hAg[g]], hB[:, g * 1024:(g + 1) * 1024],
                               d_hBg[g], 1, None))
            for g in range(8):
                hf = g // 4
                groups.append((W3[hf * 64:(hf + 1) * 64, :],
                               hB[hf * 64:(hf + 1) * 64, (g % 4) * 1024:(g % 4 + 1) * 1024], [d_hBg[g % 4]],
                               h3[:, g * 1024:(g + 1) * 1024], d_h3g[g], 2, mk[:, g * 1024:(g + 1) * 1024]))

            def M1(i):
                Wm, src, dsrc, dst, ddst, li, mask = groups[i]
                b = i % 2
                for k in range(2):
                    P.op("tensor", lambda e: e.matmul(pm[b][:, k * 512:(k + 1) * 512], Wm, src[:, k * 512:(k + 1) * 512],
                                                      start=True, stop=True),
                         r=[d_w] + dsrc, w=[d_pm[b]])
                P.op("scalar", lambda e: e.activation(out=tmp[b][:], in_=pm[b][:], func=AF.Identity,
                                                      scale=fr[:, 0:1], bias=frb[:, li:li + 1]),
                     r=[d_pm[b], d_f], w=[d_tmp[b]])

            def M2(i):
                b = i % 2
                P.op("vector", lambda e: e.tensor_scalar(out=t1b[b][:], in0=tmp[b][:], scalar1=-1.0, scalar2=PI,
                                                         op0=ALU.mult, op1=ALU.add),
                     r=[d_tmp[b]], w=[d_t1b[b]])
                P.op("vector", lambda e: e.tensor_tensor(out=tmp[b][:], in0=tmp[b][:], in1=t1b[b][:], op=ALU.min),
                     r=[d_t1b[b]], w=[d_tmp[b]])
                P.op("vector", lambda e: e.scalar_tensor_tensor(out=t1b[b][:], in0=t1b[b][:], scalar=-2 * PI,
                                                                in1=tmp[b][:], op0=ALU.add, op1=ALU.max),
                     r=[d_tmp[b]], w=[d_t1b[b]])

            def M3(i):
                Wm, src, dsrc, dst, ddst, li, mask = groups[i]
                b = i % 2
                P.op("scalar", lambda e: e.activation(out=dst, in_=t1b[b][:], func=AF.Sin), r=[d_t1b[b]], w=[ddst])
                if mask is not None:
                    P.op("vector", lambda e: e.tensor_tensor(out=dst, in0=dst, in1=mask, op=ALU.mult),
                         r=[d_mk], w=[ddst])

            pipeline(16, [M1, M2, M3])
            h3v = h3[:].rearrange("p (a b) -> p a b", b=64)

            def L1(t2, s=s, h3v=h3v):
                b = t2 % 2
                P.op("tensor", lambda e: e.matmul(p4[b][:], h3v[:, :, t2], W4[:], start=True, stop=True),
                     r=d_h3g + [d_w], w=[d_p4[b]])
                P.op("scalar", lambda e: e.activation(out=dec[b][:], in_=negd[:], func=AF.Exp,
                                                      scale=tpos[:, s, t2:t2 + 1], bias=vbias[:, s, t2:t2 + 1]),
                     r=[d_t], w=[d_dec[b]])
                k = t2 % 3
                if s == 0 and t2 == 0:
                    P.op("vector", lambda e: e.tensor_tensor(out=gt[:], in0=p4[b][:], in1=dec[b][:], op=ALU.mult),
                         r=[d_p4[b], d_dec[b]], w=[d_gt])
                    P.op("vector", lambda e: e.scalar_tensor_tensor(out=gf[k][:], in0=hyd[:], scalar=e0[:, 0:1],
                                                                    in1=gt[:], op0=ALU.mult, op1=ALU.add),
                         r=[d_t, d_gt], w=[d_gf[k]])
                else:
                    P.op("vector", lambda e: e.tensor_tensor(out=gf[k][:], in0=p4[b][:], in1=dec[b][:], op=ALU.mult),
                         r=[d_p4[b], d_dec[b]], w=[d_gf[k]])

            def L2(t2, s=s):
                k = t2 % 3
                a = t2 % 3
                for ri in range(2):
                    p = (t2 * 2 + ri) % 2
                    P.op("tensor", lambda e: e.matmul(ps1[p][0:66, :], Et[:, t2, ri, 0:66], gf[k][:], start=True, stop=True),
                         r=[d_Et, d_gf[k]], w=[d_ps1[p]])
                    if ri == 0:
                        P.op("scalar", lambda e: e.activation(out=ast[a][:, ri, :], in_=ps1[p][0:66, :], func=AF.Copy),
                             r=[d_ps1[p]], w=[d_ast[a]])
                    else:
                        P.op("vector", lambda e: e.tensor_copy(out=ast[a][:, ri, :], in_=ps1[p][0:66, :]),
                             r=[d_ps1[p]], w=[d_ast[a]])
                P.dma("sync", dq[14 + a], D["AG"][s][t2], ast[a][:], r=[d_ast[a]])

            pipeline(64, [L1, L2])


def phase_D(P, nc, D, dq, gq):
    with ExitStack() as es:
        Et = T(es, nc, "D_Et", [64, 64, 2, 128], BF16)
        d_Et = Dep()
        P.dma("sync", dq[0], Et[:], D["Etab"][0:64], w=[d_Et])
        Du = [T(es, nc, f"D_Du{i}", [64, 4, 64, 128], BF16) for i in range(2)]
        d_Du = [Dep() for _ in range(2)]
        ps = [PS(es, nc, f"D_s1p{i}", [128, 512], F32) for i in range(6)]
        d_ps = [Dep() for _ in range(6)]
        ast = [T(es, nc, f"D_ast{i}", [66, 2, 512], BF16) for i in range(4)]
        d_ast = [Dep() for _ in range(4)]
        P.dma("sync", dq[1], Du[0][:], D["DU"][0], w=[d_Du[0]])
        it = 0
        for blk in range(4):
            b = blk % 2
            if blk + 1 < 4:
                P.dma("sync", dq[1 + (blk + 1) % 2], Du[(blk + 1) % 2][:], D["DU"][blk + 1], w=[d_Du[(blk + 1) % 2]])
            for t2 in range(64):
                a = it % 4
                for ri in range(2):
                    p = (it * 2 + ri) % 6
                    P.op("tensor", lambda e: e.matmul(ps[p][0:66, :], Et[:, t2, ri, 0:66], Du[b][:, :, t2, :],
                                                      start=True, stop=True),
                         r=[d_Et, d_Du[b]], w=[d_ps[p]])
                    if ri == 0:
                        P.op("scalar", lambda e: e.activation(out=ast[a][:, ri, :], in_=ps[p][0:66, :], func=AF.Copy),
                             r=[d_ps[p]], w=[d_ast[a]])
                    else:
                        P.op("vector", lambda e: e.tensor_copy(out=ast[a][:, ri, :], in_=ps[p][0:66, :]),
                             r=[d_ps[p]], w=[d_ast[a]])
                P.dma("sync", dq[3 + a], D["AU"][blk][t2], ast[a][:], r=[d_ast[a]])
                it += 1


KG = 3
NQ = 11


def stage2_mm(P, F2, d_F2, Bt, d_B, kk, out, d_out):
    P.op("tensor", lambda e: e.matmul(out[:, 0, :], F2[:, 0, :], Bt[:, kk, 0, :], start=True, stop=False),
         r=[d_F2, d_B], w=[d_out])
    P.op("tensor", lambda e: e.matmul(out[:, 0, :], F2[:, 2, :], Bt[:, kk, 1, :], start=False, stop=True),
         r=[d_F2, d_B], w=[d_out])
    P.op("tensor", lambda e: e.matmul(out[:, 1, :], F2[:, 1, :], Bt[:, kk, 0, :], start=True, stop=False),
         r=[d_F2, d_B], w=[d_out])
    P.op("tensor", lambda e: e.matmul(out[:, 1, :], F2[:, 0, :], Bt[:, kk, 1, :], start=False, stop=True),
         r=[d_F2, d_B], w=[d_out])


def phase_F(P, nc, D, dq, gq):
    with ExitStack() as es:
        ident, d_ident = load_ident(P, nc, es, D, dq[0])
        F2 = T(es, nc, "F_F2", [128, 3, 128], BF16)
        d_F2 = Dep()
        P.dma("sync", dq[1], F2[:], D["F2tab"], w=[d_F2])
        IT = [T(es, nc, f"F_IT{i}", [128, KG, 3, 128], BF16) for i in range(2)]
        d_IT = [Dep() for _ in range(2)]
        Bg = [[T(es, nc, f"F_Bg{i}_{s}", [128, KG, 2, 512], BF16) for s in range(4)] for i in range(2)]
        Bu = [[T(es, nc, f"F_Bu{i}_{s}", [128, KG, 2, 512], BF16) for s in range(4)] for i in range(2)]
        d_Bg = [[Dep() for s in range(4)] for i in range(2)]
        d_Bu = [[Dep() for s in range(4)] for i in range(2)]
        Gs = [T(es, nc, f"F_Gs{i}", [128, 3, 512], BF16) for i in range(2)]
        d_Gs = [Dep() for _ in range(2)]
        TA = [T(es, nc, f"F_TA{i}", [128, 2, 512], BF16) for i in range(2)]
        d_TA = [Dep() for _ in range(2)]
        TB = [T(es, nc, f"F_TB{i}", [128, 2, 512], BF16) for i in range(2)]
        d_TB = [Dep() for _ in range(2)]
        Yb = [T(es, nc, f"F_Yb{i}", [128, 2, 512], BF16) for i in range(2)]
        d_Yb = [Dep() for _ in range(2)]
        Cst = [T(es, nc, f"F_C{i}", [128, 2, 512], BF16) for i in range(2)]
        d_C = [Dep() for _ in range(2)]
        Gp = PS(es, nc, "F_Gp", [128, 2, 512], F32)
        d_Gp = Dep()
        Up = [PS(es, nc, f"F_Up{i}", [128, 2, 512], F32) for i in range(2)]
        d_Up = [Dep() for _ in range(2)]
        Yp = PS(es, nc, "F_Yp", [128, 2, 512], F32)
        d_Yp = Dep()
        def loads(q):
            qb = q % 2
            P.dma("sync", dq[2 + qb], IT[qb][:], D["ITtab"][:, q * KG:(q + 1) * KG], w=[d_IT[qb]])
            for s in range(4):
                for kp in range(2):
                    P.dma("sync", dq[24 + qb * 16 + s * 2 + kp], Bg[qb][s][kp * 64:(kp + 1) * 64],
                          D["AG"][s][:, kp * 33 + q * KG:kp * 33 + (q + 1) * KG], w=[d_Bg[qb][s]])
                    P.dma("sync", dq[32 + qb * 16 + s * 2 + kp], Bu[qb][s][kp * 64:(kp + 1) * 64],
                          D["AU"][s][:, kp * 33 + q * KG:kp * 33 + (q + 1) * KG], w=[d_Bu[qb][s]])

        def idx(i):
            q, r = divmod(i, KG * 4)
            kk, s = divmod(r, 4)
            return q, kk, s

        def S1(i):
            q, kk, s = idx(i)
            qb = q % 2
            p = i % 2
            if i == 0:
                loads(0)
            if kk == 1 and s == 0 and q + 1 < NQ:
                loads(q + 1)
            stage2_mm(P, F2, d_F2, Bg[qb][s], d_Bg[qb][s], kk, Gp, d_Gp)
            P.op("scalar", lambda e: e.activation(out=Gs[p][:, 0:2, :], in_=Gp[:], func=AF.Copy),
                 r=[d_Gp], w=[d_Gs[p]])
            P.op("scalar", lambda e: e.activation(out=Gs[p][:, 2, :], in_=Gp[:, 1, :], func=AF.Copy, scale=-1.0),
                 r=[d_Gp], w=[d_Gs[p]])
            stage2_mm(P, F2, d_F2, Bu[qb][s], d_Bu[qb][s], kk, Up[p], d_Up[p])

        def S2(i):
            p = i % 2
            P.op("vector", lambda e: e.tensor_tensor(out=TA[p][:], in0=Up[p][:],
                                                     in1=Gs[p][:, 0:1, :].to_broadcast([128, 2, 512]), op=ALU.mult),
                 r=[d_Up[p], d_Gs[p]], w=[d_TA[p]])
            P.op("vector", lambda e: e.tensor_tensor(out=TB[p][:, 0, :], in0=Up[p][:, 1, :], in1=Gs[p][:, 2, :],
                                                     op=ALU.mult),
                 r=[d_Up[p], d_Gs[p]], w=[d_TB[p]])
            P.op("vector", lambda e: e.tensor_tensor(out=TB[p][:, 1, :], in0=Up[p][:, 0, :], in1=Gs[p][:, 1, :],
                                                     op=ALU.mult),
                 r=[d_Up[p], d_Gs[p]], w=[d_TB[p]])

        def S3(i):
            q, kk, s = idx(i)
            qb = q % 2
            p = i % 2
            for ri in range(2):
                P.op("tensor", lambda e: e.matmul(Yp[:, ri, :], ident[:], TA[p][:, ri, :], start=(s == 0), stop=False),
                     r=[d_ident, d_TA[p]], w=[d_Yp])
                P.op("tensor", lambda e: e.matmul(Yp[:, ri, :], ident[:], TB[p][:, ri, :], start=False, stop=(s == 3)),
                     r=[d_ident, d_TB[p]], w=[d_Yp])
            if s != 3:
                return
            c = (q * KG + kk) % 2
            P.op("scalar", lambda e: e.activation(out=Yb[c][:], in_=Yp[:], func=AF.Copy), r=[d_Yp], w=[d_Yb[c]])
            ITq = IT[qb]
            P.op("tensor", lambda e: e.matmul(Gp[:, 0, :], ITq[:, kk, 0, :], Yb[c][:, 0, :], start=True, stop=False),
                 r=[d_IT[qb], d_Yb[c]], w=[d_Gp])
            P.op("tensor", lambda e: e.matmul(Gp[:, 0, :], ITq[:, kk, 2, :], Yb[c][:, 1, :], start=False, stop=True),
                 r=[d_IT[qb], d_Yb[c]], w=[d_Gp])
            P.op("tensor", lambda e: e.matmul(Gp[:, 1, :], ITq[:, kk, 1, :], Yb[c][:, 0, :], start=True, stop=False),
                 r=[d_IT[qb], d_Yb[c]], w=[d_Gp])
            P.op("tensor", lambda e: e.matmul(Gp[:, 1, :], ITq[:, kk, 0, :], Yb[c][:, 1, :], start=False, stop=True),
                 r=[d_IT[qb], d_Yb[c]], w=[d_Gp])
            P.op("vector", lambda e: e.tensor_copy(out=Cst[c][:], in_=Gp[:]), r=[d_Gp], w=[d_C[c]])
            kh = q * KG + kk
            for kp in range(2):
                P.dma("sync", dq[20 + 2 * c + kp], D["CS"][kp * 33 + kh], Cst[c][kp * 64:(kp + 1) * 64], r=[d_C[c]])

        pipeline(NQ * KG * 4, [S1, S2, S3])


def phase_G(P, nc, D, dq, gq):
    with ExitStack() as es:
        IC = T(es, nc, "G_IC", [66, 2, 64], BF16)
        d_IC = Dep()
        P.dma("sync", dq[0], IC[:], D["ICtab"], w=[d_IC])
        x0 = T(es, nc, "G_x0", [128, 4, 4096], BF16)
        d_x0 = Dep()
        P.dma("sync", dq[1], x0[:], D["X0T"].rearrange("j p t -> p j t"), w=[d_x0])
        yh = T(es, nc, "G_yh", [128, 4, 4096], BF16)
        d_yh = Dep()
        Cg = [T(es, nc, f"G_C{i}", [66, 8, 2, 512], BF16) for i in range(2)]
        d_Cg = [Dep() for _ in range(2)]
        ps = [PS(es, nc, f"G_ps{i}", [128, 8, 64], F32) for i in range(4)]
        d_ps = [Dep() for _ in range(4)]
        pi = 0
        for tg in range(8):
            b = tg % 2
            P.dma("sync", dq[2 + b], Cg[b][:], D["CS"][:, tg * 8:(tg + 1) * 8], w=[d_Cg[b]])
            for j in range(4):
                p = pi % 4
                pi += 1
                for tl in range(8):
                    P.op("tensor", lambda e: e.matmul(ps[p][:, tl, :], Cg[b][:, tl, 0, j * 128:(j + 1) * 128],
                                                      IC[:, 0, :], start=True, stop=False),
                         r=[d_Cg[b], d_IC], w=[d_ps[p]])
                    P.op("tensor", lambda e: e.matmul(ps[p][:, tl, :], Cg[b][:, tl, 1, j * 128:(j + 1) * 128],
                                                      IC[:, 1, :], start=False, stop=True),
                         r=[d_Cg[b], d_IC], w=[d_ps[p]])
                ov = yh[:, j, :].rearrange("p (a b) -> p b a", b=64)[:, tg * 8:(tg + 1) * 8, :]
                xv = x0[:, j, :].rearrange("p (a b) -> p b a", b=64)[:, tg * 8:(tg + 1) * 8, :]
                P.op("vector", lambda e: e.tensor_tensor(out=ov, in0=ps[p][:], in1=xv, op=ALU.mult),
                     r=[d_ps[p], d_x0], w=[d_yh])
        P.dma("sync", dq[4], D["YHT"].rearrange("j p t -> p j t"), yh[:], r=[d_yh])


def phase_H(P, nc, D, dq, gq):
    with ExitStack() as es:
        QT = T(es, nc, "H_QT", [128, 4, 4096], BF16)
        KT = T(es, nc, "H_KT", [128, 4, 4608], BF16)
        VA = T(es, nc, "H_VA", [128, 36, 8, 65], BF16)
        d_in = Dep()
        P.dma("sync", dq[1], QT[:], D["QT"].rearrange("j p t -> p j t"), w=[d_in])
        P.dma("sync", dq[2], KT[:], D["KT"].rearrange("j p t -> p j t"), w=[d_in])
        P.dma("sync", dq[3], VA[:].rearrange("p s h d -> p s (h d)"), D["VA"].rearrange("s p x -> p s x"), w=[d_in])
        bt = [T(es, nc, f"H_bt{i}", [128, 8, 6, 256], BF16) for i in range(2)]
        d_bt = [Dep() for _ in range(2)]
        EX = [T(es, nc, f"H_EX{i}", [128, 6, 256], BF16) for i in range(2)]
        d_EX = [Dep() for _ in range(2)]
        PT = [T(es, nc, f"H_PT{i}", [128, 6, 256], BF16) for i in range(2)]
        d_PT = [Dep() for _ in range(2)]
        onesr = T(es, nc, "H_ones", [128, 64], BF16)
        d_ones = Dep()
        P.op("gpsimd", lambda e: e.memset(onesr[:], 1.0), w=[d_ones])
        rdb = [T(es, nc, f"H_rdb{i}", [128, 256], BF16) for i in range(2)]
        d_rdb = [Dep() for _ in range(2)]
        rdf = [T(es, nc, f"H_rdf{i}", [128, 256], F32) for i in range(2)]
        d_rdf = [Dep() for _ in range(2)]
        Bs = [T(es, nc, f"H_Bs{i}", [64, 256], F32) for i in range(2)]
        d_Bs = [Dep() for _ in range(2)]
        yTs = [T(es, nc, f"H_yT{i}", [128, 4, 256], BF16) for i in range(2)]
        d_yTs = [Dep() for _ in range(2)]
        YATv = D["YAT"].rearrange("j p t -> p j t")
        psS = [PS(es, nc, f"H_pS{i}", [128, 512], F32) for i in range(4)]
        d_pS = [Dep() for _ in range(4)]
        psO = [PS(es, nc, f"H_pO{i}", [128, 256], F32) for i in range(2)]
        d_pO = [Dep() for _ in range(2)]
        psB = [PS(es, nc, f"H_pB{i}", [64, 256], F32) for i in range(2)]
        d_pB = [Dep() for _ in range(2)]
        EX3 = EX + [T(es, nc, "H_EX2", [128, 6, 256], BF16)]
        PT3 = PT + [T(es, nc, "H_PT2", [128, 6, 256], BF16)]
        d_EX3 = d_EX + [Dep()]
        d_PT3 = d_PT + [Dep()]
        state = {"cur_v": None, "cur_bb": 0, "si": 0}
        vb_of_g = {}

        def S1(i):
            g, h = divmod(i, 8)
            v = 0 if g == 0 else (2 if g == 15 else 1)
            if h == 0 and v != state["cur_v"]:
                bb = g % 2
                P.dma("sync", dq[4 + bb], bt[bb][:], D["abias"][v].rearrange("h p a q -> p h a q"), w=[d_bt[bb]])
                for hh in range(8):
                    P.op("scalar", lambda e: e.activation(out=bt[bb][:, hh], in_=bt[bb][:, hh], func=AF.Exp),
                         w=[d_bt[bb]])
                state["cur_v"] = v
                state["cur_bb"] = bb
            cbb = state["cur_bb"]
            ch, po = h // 2, (h % 2) * 64
            pb = i % 3
            for pp in range(3):
                sp = state["si"] % 4
                state["si"] += 1
                for e2 in range(2):
                    pr = 2 * pp + e2
                    k0 = (4 * g + 2 * pr) * 64
                    P.op("tensor", lambda e: e.matmul(psS[sp][:, e2 * 256:(e2 + 1) * 256],
                                                      KT[po:po + 64, ch, k0:k0 + 128],
                                                      QT[po:po + 64, ch, g * 256:(g + 1) * 256],
                                                      start=True, stop=True),
                         r=[d_in], w=[d_pS[sp]])
                P.op("scalar", lambda e: e.activation(out=EX3[pb][:, 2 * pp:2 * pp + 2, :],
                                                      in_=psS[sp][:].rearrange("p (a q) -> p a q", q=256),
                                                      func=AF.Exp),
                     r=[d_pS[sp]], w=[d_EX3[pb]])
                P.op("vector", lambda e: e.tensor_tensor(out=PT3[pb][:, 2 * pp:2 * pp + 2, :],
                                                         in0=EX3[pb][:, 2 * pp:2 * pp + 2, :],
                                                         in1=bt[cbb][:, h, 2 * pp:2 * pp + 2, :], op=ALU.mult),
                     r=[d_EX3[pb], d_bt[cbb]], w=[d_PT3[pb]])

        def S2(i):
            g, h = divmod(i, 8)
            pb = i % 3
            o = i % 2
            for pr in range(6):
                st = 2 * g + pr
                P.op("tensor", lambda e: e.matmul(psO[o][0:65, :], VA[:, st, h, :], PT3[pb][:, pr, :],
                                                  start=(pr == 0), stop=(pr == 5)),
                     r=[d_PT3[pb], d_in], w=[d_pO[o]])
            P.op("vector", lambda e: e.reciprocal(out=rdf[o][64:65, :], in_=psO[o][64:65, :]),
                 r=[d_pO[o]], w=[d_rdf[o]])
            P.op("vector", lambda e: e.tensor_copy(out=rdb[o][64:65, :], in_=rdf[o][64:65, :]),
                 r=[d_rdf[o]], w=[d_rdb[o]])

        def S3(i):
            g, h = divmod(i, 8)
            ch, po = h // 2, (h % 2) * 64
            o = i % 2
            yb = g % 2
            P.op("tensor", lambda e: e.matmul(psB[o][:], onesr[64:65, :], rdb[o][64:65, :], start=True, stop=True),
                 r=[d_ones, d_rdb[o]], w=[d_pB[o]])
            P.op("scalar", lambda e: e.activation(out=Bs[o][:], in_=psB[o][:], func=AF.Copy),
                 r=[d_pB[o]], w=[d_Bs[o]])
            P.op("vector", lambda e: e.tensor_tensor(out=yTs[yb][po:po + 64, ch, :], in0=psO[o][0:64, :],
                                                     in1=Bs[o][:], op=ALU.mult),
                 r=[d_pO[o], d_Bs[o]], w=[d_yTs[yb]])
            if h == 7:
                P.dma("sync", dq[6 + yb], YATv[:, :, g * 256:(g + 1) * 256], yTs[yb][:], r=[d_yTs[yb]])

        pipeline(128, [S1, S2, S3])


def phase_W(P, nc, D, wsem):
    for w in wsem:
        w.nobar = True
    k = [0]

    def cv(dst, src):
        P.dma("gpsimd", wsem[k[0] % 4], dst, src)
        k[0] += 1

    for m in range(16):
        cv(D["WG"][m], wchunk_src(D["w_in"], 3072 + m * 128, 128))
    for m in range(8):
        cv(D["WBA"][m], wchunk_src(D["w_br_attn"], m * 128, 128))
        cv(D["WBH"][m], wchunk_src(D["w_br_hyena"], m * 128, 128))
    for kc in range(8):
        cv(D["WO"][:, kc, :], D["w_out"][kc * 128:(kc + 1) * 128, :])
    for f in range(NFF):
        cv(D["WFG"][f], wchunk_src(D["w_gate"], f * 128, 128))
        cv(D["WFU"][f], wchunk_src(D["w_up"], f * 128, 128))
    wdv = D["w_down"].rearrange("(f p) n -> p f n", p=128)
    for dh in range(2):
        for f0 in range(0, NFF, 11):
            cv(D["WD"][dh][:, f0:f0 + 11, :], wdv[:, f0:f0 + 11, dh * 512:(dh + 1) * 512])


def phase_I(P, nc, D, dq, gq):
    TT = 1024
    with ExitStack() as es:
        ident, d_ident = load_ident(P, nc, es, D, dq[0])
        grep, d_grep = make_grep(P, nc, es, D["g_mix"], "gm", dq[1])
        grep2, d_grep2 = make_grep(P, nc, es, D["g_ffn"], "gf", dq[2])
        gfin = T(es, nc, "I_gfin", [128, 1024], F32)
        d_gfin = Dep()
        P.dma("sync", dq[3], gfin[:], D["g_final"], w=[d_gfin])
        xt = T(es, nc, "I_xt", [128, 8, 1024], F32)
        d_xt = [Dep() for _ in range(8)]
        hT = T(es, nc, "I_hT", [128, 8, TT], BF16)
        d_hT = Dep()
        aT = T(es, nc, "I_aT", [128, NFF, TT], BF16)
        d_aT = Dep()
        mT = aT[:, 0:8, :]
        yaT = aT[:, 8:12, :]
        yhT = aT[:, 12:16, :]
        d_yy = Dep()
        d_mT = Dep()
        wout = T(es, nc, "I_wout", [128, 8, 1024], BF16)
        d_wout = Dep()
        wdn = T(es, nc, "I_wdn", [128, NFF, 512], BF16)
        d_wdn = Dep()
        wc = [T(es, nc, f"I_wc{i}", [128, 8, 128], BF16) for i in range(6)]
        d_wc = [Dep() for _ in range(6)]
        wb = [T(es, nc, f"I_wb{i}", [128, 4, 128], BF16) for i in range(4)]
        d_wb = [Dep() for _ in range(4)]
        sg = [T(es, nc, f"I_sg{i}", [128, 512], F32) for i in range(4)]
        d_sg = [Dep() for _ in range(4)]
        xn = [T(es, nc, f"I_xn{i}", [128, 1024], BF16) for i in range(3)]
        d_xn = [Dep() for _ in range(3)]
        junk = T(es, nc, "I_junk", [128, 1024], BF16)
        d_junk = Dep()
        stt = [T(es, nc, f"I_st{i}", [128, 4], F32) for i in range(3)]
        d_st = [Dep() for _ in range(3)]
        yo = [T(es, nc, f"I_yo{i}", [128, 1024], F32) for i in range(2)]
        d_yo = [Dep() for _ in range(2)]
        ps = [PS(es, nc, f"I_ps{i}", [128, 512], F32) for i in range(7)]
        d_ps = [Dep() for _ in range(7)]
        pt = PS(es, nc, "I_pt", [128, 8, 128], BF16)
        d_pt = Dep()
        for i in range(3):
            P.op("gpsimd", lambda e: e.memset(stt[i][:], EPS), w=[d_st[i]])
        P.dma("sync", dq[54], wout[:], D["WO"], w=[d_wout])
        cnt = {"wi": 0, "bi": 0, "pi": 0, "ni": 0}

        def nA(st, gr, d_gr):
            b = st % 3
            P.op("scalar", lambda e: e.activation(out=junk[:], in_=xt[:, st, :], func=AF.Square,
                                                  accum_out=stt[b][:, 0:1]),
                 r=[d_xt[st]], w=[d_junk, d_st[b]])
            P.op("scalar", lambda e: e.activation(out=stt[b][:, 1:2], in_=stt[b][:, 0:1], func=AF.Sqrt,
                                                  scale=1.0 / 1024, bias=stt[b][:, 3:4]),
                 r=[d_st[b]], w=[d_st[b]])

        def nB(st, gr, d_gr):
            b = st % 3
            P.op("vector", lambda e: e.reciprocal(out=stt[b][:, 2:3], in_=stt[b][:, 1:2]), r=[d_st[b]], w=[d_st[b]])
            if st % 2:
                P.op("scalar", lambda e: e.activation(out=xn[b][:], in_=xt[:, st, :], func=AF.Copy,
                                                      scale=stt[b][:, 2:3]),
                     r=[d_st[b], d_xt[st]], w=[d_xn[b]])
            else:
                P.op("vector", lambda e: e.tensor_scalar(out=xn[b][:], in0=xt[:, st, :], scalar1=stt[b][:, 2:3],
                                                         scalar2=None, op0=ALU.mult),
                     r=[d_st[b], d_xt[st]], w=[d_xn[b]])

        def nC(st, gr, d_gr):
            b = st % 3
            for kc in range(8):
                P.op("tensor", lambda e: e.transpose(pt[:, kc, :], xn[b][:, kc * 128:(kc + 1) * 128], ident[:]),
                     r=[d_xn[b], d_ident], w=[d_pt])
            P.op("vector", lambda e: e.tensor_tensor(out=hT[:, :, st * 128:(st + 1) * 128], in0=pt[:], in1=gr[:],
                                                     op=ALU.mult),
                 r=[d_pt, d_gr], w=[d_hT])

        def wload(buf, dbuf, sem, src):
            P.dma("sync", sem, buf[:], src, w=[dbuf])

        for tile in range(4096 // TT):
            tok0 = tile * TT
            P.dma("sync", dq[5], yaT, D["YAT"].rearrange("j p t -> p j t")[:, :, tok0:tok0 + TT], w=[d_yy, d_aT])
            P.dma("sync", dq[6], yhT, D["YHT"].rearrange("j p t -> p j t")[:, :, tok0:tok0 + TT], w=[d_yy, d_aT])

            def L0(st, tok0=tok0):
                P.dma("sync", dq[20 + st], xt[:, st, :],
                      D["xext"][256 + tok0 + st * 128:256 + tok0 + (st + 1) * 128, :], w=[d_xt[st]])
                nA(st, grep, d_grep)

            pipeline(8, [L0, lambda st: nB(st, grep, d_grep), lambda st: nC(st, grep, d_grep)])
            for m in range(8):
                k = cnt["wi"] % 6; w_ga, dga = wc[k], d_wc[k]; wload(w_ga, dga, dq[44 + k], D["WG"][m]); cnt["wi"] += 1
                k = cnt["wi"] % 6; w_gh, dgh = wc[k], d_wc[k]; wload(w_gh, dgh, dq[44 + k], D["WG"][8 + m]); cnt["wi"] += 1
                k = cnt["bi"] % 4; w_ba, dba = wb[k], d_wb[k]; wload(w_ba, dba, dq[50 + k], D["WBA"][m]); cnt["bi"] += 1
                k = cnt["bi"] % 4; w_bh, dbh = wb[k], d_wb[k]; wload(w_bh, dbh, dq[50 + k], D["WBH"][m]); cnt["bi"] += 1
                for th in range(TT // 512):
                    tsl = slice(th * 512, (th + 1) * 512)
                    pi = cnt["pi"]
                    pg = [pi % 7, (pi + 1) % 7, (pi + 2) % 7, (pi + 3) % 7]
                    cnt["pi"] += 4
                    for kc in range(8):
                        P.op("tensor", lambda e: e.matmul(ps[pg[0]][:], w_ga[:, kc, :], hT[:, kc, tsl],
                                                          start=(kc == 0), stop=(kc == 7)),
                             r=[dga, d_hT], w=[d_ps[pg[0]]])
                    for kc in range(8):
                        P.op("tensor", lambda e: e.matmul(ps[pg[1]][:], w_gh[:, kc, :], hT[:, kc, tsl],
                                                          start=(kc == 0), stop=(kc == 7)),
                             r=[dgh, d_hT], w=[d_ps[pg[1]]])
                    for kc in range(4):
                        P.op("tensor", lambda e: e.matmul(ps[pg[2]][:], w_ba[:, kc, :], yaT[:, kc, tsl],
                                                          start=(kc == 0), stop=(kc == 3)),
                             r=[dba, d_yy], w=[d_ps[pg[2]]])
                    for kc in range(4):
                        P.op("tensor", lambda e: e.matmul(ps[pg[3]][:], w_bh[:, kc, :], yhT[:, kc, tsl],
                                                          start=(kc == 0), stop=(kc == 3)),
                             r=[dbh, d_yy], w=[d_ps[pg[3]]])
                    sa = (2 * (m * 2 + th)) % 4
                    P.op("scalar", lambda e: e.activation(out=sg[sa][:], in_=ps[pg[0]][:], func=AF.Sigmoid),
                         r=[d_ps[pg[0]]], w=[d_sg[sa]])
                    P.op("scalar", lambda e: e.activation(out=sg[sa + 1][:], in_=ps[pg[1]][:], func=AF.Sigmoid),
                         r=[d_ps[pg[1]]], w=[d_sg[sa + 1]])
                    P.op("vector", lambda e: e.tensor_tensor(out=sg[sa][:], in0=ps[pg[2]][:], in1=sg[sa][:], op=ALU.mult),
                         r=[d_ps[pg[2]]], w=[d_sg[sa]])
                    P.op("vector", lambda e: e.tensor_tensor(out=sg[sa + 1][:], in0=ps[pg[3]][:], in1=sg[sa + 1][:],
                                                             op=ALU.mult),
                         r=[d_ps[pg[3]]], w=[d_sg[sa + 1]])
                    P.op("vector", lambda e: e.tensor_tensor(out=mT[:, m, tsl], in0=sg[sa][:], in1=sg[sa + 1][:],
                                                             op=ALU.add),
                         r=[d_sg[sa], d_sg[sa + 1]], w=[d_mT])

            def O1(st):
                for dh in range(2):
                    p = cnt["pi"] % 7
                    cnt["pi"] += 1
                    for kc in range(8):
                        P.op("tensor", lambda e: e.matmul(ps[p][:], mT[:, kc, st * 128:(st + 1) * 128],
                                                          wout[:, kc, dh * 512:(dh + 1) * 512],
                                                          start=(kc == 0), stop=(kc == 7)),
                             r=[d_mT, d_wout], w=[d_ps[p]])
                    P.op("vector", lambda e: e.tensor_tensor(out=xt[:, st, dh * 512:(dh + 1) * 512], in0=ps[p][:],
                                                             in1=xt[:, st, dh * 512:(dh + 1) * 512], op=ALU.add),
                         r=[d_ps[p]], w=[d_xt[st]])
                nA(st, grep2, d_grep2)

            pipeline(8, [O1, lambda st: nB(st, grep2, d_grep2), lambda st: nC(st, grep2, d_grep2)])
            for f in range(NFF):
                k = cnt["wi"] % 6; w_g, dg = wc[k], d_wc[k]; wload(w_g, dg, dq[44 + k], D["WFG"][f]); cnt["wi"] += 1
                k = cnt["wi"] % 6; w_u, dup = wc[k], d_wc[k]; wload(w_u, dup, dq[44 + k], D["WFU"][f]); cnt["wi"] += 1
                if f == 2:
                    P.dma("sync", dq[55], wdn[:], D["WD"][0], w=[d_wdn])
                for th in range(TT // 512):
                    tsl = slice(th * 512, (th + 1) * 512)
                    pi = cnt["pi"]
                    pg = [pi % 7, (pi + 1) % 7]
                    cnt["pi"] += 2
                    for kc in range(8):
                        P.op("tensor", lambda e: e.matmul(ps[pg[0]][:], w_g[:, kc, :], hT[:, kc, tsl],
                                                          start=(kc == 0), stop=(kc == 7)),
                             r=[dg, d_hT], w=[d_ps[pg[0]]])
                    for kc in range(8):
                        P.op("tensor", lambda e: e.matmul(ps[pg[1]][:], w_u[:, kc, :], hT[:, kc, tsl],
                                                          start=(kc == 0), stop=(kc == 7)),
                             r=[dup, d_hT], w=[d_ps[pg[1]]])
                    sa = (f * 2 + th) % 4
                    P.op("scalar", lambda e: e.activation(out=sg[sa][:], in_=ps[pg[0]][:], func=AF.Silu),
                         r=[d_ps[pg[0]]], w=[d_sg[sa]])
                    P.op("vector", lambda e: e.tensor_tensor(out=aT[:, f, tsl], in0=ps[pg[1]][:], in1=sg[sa][:],
                                                             op=ALU.mult),
                         r=[d_ps[pg[1]], d_sg[sa]], w=[d_aT, d_mT, d_yy])
            for dh in range(2):
                if dh == 1:
                    P.dma("sync", dq[55], wdn[:], D["WD"][1], w=[d_wdn])
                for st in range(8):
                    p = cnt["pi"] % 7
                    cnt["pi"] += 1
                    for f in range(NFF):
                        P.op("tensor", lambda e: e.matmul(ps[p][:], aT[:, f, st * 128:(st + 1) * 128], wdn[:, f, :],
                                                          start=(f == 0), stop=(f == NFF - 1)),
                             r=[d_aT, d_mT, d_yy, d_wdn], w=[d_ps[p]])
                    P.op("vector", lambda e: e.tensor_tensor(out=xt[:, st, dh * 512:(dh + 1) * 512], in0=ps[p][:],
                                                             in1=xt[:, st, dh * 512:(dh + 1) * 512], op=ALU.add),
                         r=[d_ps[p]], w=[d_xt[st]])
                    if dh == 1:
                        b = st % 3
                        yb = st % 2
                        P.op("scalar", lambda e: e.activation(out=junk[:], in_=xt[:, st, :], func=AF.Square,
                                                              accum_out=stt[b][:, 0:1]),
                             r=[d_xt[st]], w=[d_junk, d_st[b]])
                        P.op("scalar", lambda e: e.activation(out=stt[b][:, 1:2], in_=stt[b][:, 0:1], func=AF.Sqrt,
                                                              scale=1.0 / 1024, bias=stt[b][:, 3:4]),
                             r=[d_st[b]], w=[d_st[b]])
                        P.op("vector", lambda e: e.reciprocal(out=stt[b][:, 2:3], in_=stt[b][:, 1:2]),
                             r=[d_st[b]], w=[d_st[b]])
                        P.op("vector", lambda e: e.scalar_tensor_tensor(out=yo[yb][:], in0=xt[:, st, :],
                                                                        scalar=stt[b][:, 2:3], in1=gfin[:],
                                                                        op0=ALU.mult, op1=ALU.mult),
                             r=[d_st[b], d_xt[st], d_gfin], w=[d_yo[yb]])
                        P.dma("sync", dq[16 + yb], D["y"][tok0 + st * 128:tok0 + (st + 1) * 128, :], yo[yb][:],
                              r=[d_yo[yb]])


def _const_tables():
    N = 8192
    t1 = np.arange(128)[:, None, None]
    t2 = np.arange(64)[None, :, None]
    k1 = np.arange(128)[None, None, :]
    th = 2 * np.pi * (((64 * t1 + t2) * k1) % N) / N
    E = np.stack([np.cos(th), -np.sin(th)], axis=2)
    a = np.arange(64)
    th2 = 2 * np.pi * np.outer(a, a) / 64
    F2 = np.zeros((128, 3, 128))
    for kp in range(2):
        sl = slice(kp * 64, (kp + 1) * 64)
        F2[sl, 0, sl] = np.cos(th2)
        F2[sl, 1, sl] = -np.sin(th2)
        F2[sl, 2, sl] = np.sin(th2)
    IT = np.zeros((128, 33, 3, 128))
    k2 = np.arange(64)[:, None]
    tt = np.arange(64)[None, :]
    for kh in range(33):
        for kp in range(2):
            k1v = kh + 33 * kp
            ph = 2 * np.pi * (tt * k2 / 64.0 + tt * k1v / 8192.0)
            sl = slice(kp * 64, (kp + 1) * 64)
            IT[sl, kh, 0, sl] = np.cos(ph)
            IT[sl, kh, 1, sl] = np.sin(ph)
            IT[sl, kh, 2, sl] = -np.sin(ph)
    kk = np.arange(66)[:, None]
    t1v = np.arange(64)[None, :]
    th3 = 2 * np.pi * kk * t1v / 128.0
    wk = np.full((66, 1), 2.0)
    wk[0] = 1.0
    wk[64] = 1.0
    wk[65] = 0.0
    IC = np.stack([wk * np.cos(th3) / N, -wk * np.sin(th3) / N], axis=1)
    return (E.astype(BF), F2.astype(BF), IT.astype(BF), IC.astype(BF))


def _filter_tables(L, ksegs):
    f32 = np.float32
    tlin = np.linspace(0.0, 1.0, L, dtype=f32)
    omega = (2.0 * math.pi * np.arange(L, dtype=f32) / L).astype(f32)
    fbv = np.linspace(1e-4, 15, 16, dtype=f32)
    n = np.arange(8192)
    d = np.where(n < 4096, n, n - 8192)
    zs = np.zeros((4, 128, 4096), f32)
    maskT = np.zeros((4, 128, 8192), f32)
    tpos = np.zeros((4, 8192), f32)
    vbias = np.full((4, 8192), NEG, f32)
    for s, k in enumerate(ksegs):
        if k is None:
            idx = np.zeros(8192, np.int64)
            valid = np.zeros(8192, bool)
            Dl = np.zeros(8192, np.int64)
        else:
            Dl = k * 4096 + d
            valid = (np.abs(Dl) <= L - 1) & (n != 4096)
            idx = np.where(valid, np.abs(Dl), 0)
        ang = (omega[idx][:, None] * fbv[None, :]).astype(f32)
        z = np.concatenate([tlin[idx][:, None], np.cos(ang), -np.sin(ang)], axis=1)
        zs[s, 0:33, :] = z[0:4096].T
        zs[s, 64:97, :] = z[4096:8192].T
        tpos[s] = tlin[idx]
        vbias[s] = np.where(valid, 0.0, NEG)
        maskT[s, 0:64, :] = (valid & (Dl >= 0)).astype(f32)[None, :]
        maskT[s, 64:128, :] = (valid & (Dl <= 0)).astype(f32)[None, :]
    def lay(a):
        a = a.reshape(4, 128, 64)
        return np.ascontiguousarray(np.transpose(a, (1, 0, 2)))
    return zs.astype(BF), maskT.astype(BF), lay(tpos), lay(vbias)


def _attn_bias(rpb, R0, rows):
    out = np.full((3, 8, 128, 6, 256), NEG, np.float32)
    qc = np.arange(64)
    kc = np.arange(64)
    cs = np.clip(qc - 8, 0, 48)
    colin = (kc[None, :] >= cs[:, None]) & (kc[None, :] < cs[:, None] + 16)
    dc = np.clip(kc[None, :] - qc[:, None], -15, 15) + 15
    for v, g in enumerate((0, 1, 15)):
        for rl in range(4):
            r = 4 * g + rl
            rg = R0 + r
            ws = int(np.clip(rg - 4, 0, rows - 8)) - R0 + 4
            for pr in range(6):
                for er in range(2):
                    e = 4 * g + 2 * pr + er
                    if not (0 <= e - ws < 8):
                        continue
                    drr = e - 4 - r + 7
                    blk = rpb[:, drr, :][:, dc]
                    blk = np.where(colin[None], blk, NEG)
                    out[v, :, er * 64:(er + 1) * 64, pr, rl * 64:(rl + 1) * 64] = np.transpose(blk, (0, 2, 1))
    return out.astype(BF)


def _col128(v, n):
    return np.ascontiguousarray(np.asarray(v, np.float32).reshape(n, 128).T)


def prepare_inputs(inputs, cores=range(8)):
    f32 = np.float32
    I = {k: np.asarray(v) for k, v in inputs.items()}
    E, F2, IT, IC = _const_tables()
    max_decay = math.log(1e-2) / 0.3
    min_decay = math.log(1e-2) / 1.5
    deltas = np.abs(np.linspace(min_decay, max_decay, 512, dtype=f32))
    common = {
        "w_in": I["w_in"][0], "w_br_attn": I["w_br_attn"][0], "w_br_hyena": I["w_br_hyena"][0],
        "w_out": I["w_out"][0], "w_gate": I["w_gate"][0], "w_up": I["w_up"][0], "w_down": I["w_down"][0],
        "g_mix": _col128(I["norm_mix"][0], 8), "g_ffn": _col128(I["norm_ffn"][0], 8),
        "g_final": np.ascontiguousarray(np.broadcast_to(I["norm_final"][None, :], (128, 1024))).astype(f32),
        "conv_w": np.ascontiguousarray(np.transpose(I["conv_w"][0].reshape(3, 12, 128), (2, 1, 0))).astype(f32),
        "conv_b": _col128(I["conv_b"][0], 12),
        "fw1": I["filt_w1"][0], "fw2": I["filt_w2"][0], "fw3": I["filt_w3"][0], "fw4": I["filt_w4"][0],
        "fb": np.ascontiguousarray(np.tile(np.stack([I["filt_b1"][0], I["filt_b2"][0], I["filt_b3"][0]], axis=1),
                                           (2, 1))).astype(f32),
        "ffr": np.ascontiguousarray(np.tile(I["filt_freq"][0][:, None], (2, 1))).astype(f32),
        "e0": np.eye(128, 1, dtype=f32),
        "hyd": np.ascontiguousarray(np.broadcast_to(I["hyena_d"][0][None, :], (128, 512))).astype(f32),
        "ident": np.eye(128, dtype=f32).astype(BF),
        "Etab": E, "F2tab": F2, "ITtab": IT, "ICtab": IC,
        "negdelta": np.ascontiguousarray(np.broadcast_to(-deltas[None, :], (128, 512))).astype(f32),
    }
    rpb = I["rpb"][0].astype(f32)
    xp = I["x_prompt"]
    xs = I["x_sample"][0]
    tab_prompt = _filter_tables(4096, [0, None, None, None])
    ab_prompt = _attn_bias(rpb, 0, 64)
    maps = []
    for c in cores:
        m = dict(common)
        xext = np.zeros((4608, 1024), f32)
        xctx = np.zeros((4, 4224, 1024), f32)
        if c < 4:
            xext[256:4352] = xp[c]
            xctx[0, :4096] = xp[c]
            zs, maskT, tpos, vbias = tab_prompt
            ab = ab_prompt
        else:
            j = c - 4
            L = 16384
            lo, hi = j * 4096 - 256, (j + 1) * 4096 + 256
            a, b = max(lo, 0), min(hi, L)
            xext[a - lo:b - lo] = xs[a:b]
            blks = [j] + [bb for bb in range(4) if bb != j]
            for i, bb in enumerate(blks):
                xctx[i, :4096] = xs[bb * 4096:(bb + 1) * 4096]
                if bb * 4096 - 1 >= 0:
                    xctx[i, 4096] = xs[bb * 4096 - 1]
                if (bb + 1) * 4096 < L:
                    xctx[i, 4097] = xs[(bb + 1) * 4096]
            zs, maskT, tpos, vbias = _filter_tables(L, [j - bb for bb in blks])
            ab = _attn_bias(rpb, j * 64, 256)
        m.update({"xext": xext, "xctx": xctx, "zs": zs, "maskT": maskT, "tpos": tpos, "vbias": vbias, "abias": ab})
        maps.append(m)
    return maps


_NC_CACHE = {}


def kernel(**inputs):
    if "nc" not in _NC_CACHE:
        _NC_CACHE["nc"] = build_program()
    nc = _NC_CACHE["nc"]
    maps = prepare_inputs(inputs)
    res = run_bass_kernel_spmd(nc, maps, core_ids=list(range(8)))
    ys = [np.asarray(res.results[i]["y"], dtype=np.float32) for i in range(8)]
    y_prompt = np.stack(ys[0:4], axis=0)
    y_sample = np.concatenate(ys[4:8], axis=0)[None]
    return (y_prompt, y_sample)
```

```python
import math
import numpy as np
import ml_dtypes
from contextlib import ExitStack
import concourse.bass as bass
import concourse.mybir as mybir
from concourse.bass_utils import run_bass_kernel_spmd

F32 = mybir.dt.float32
BF16 = mybir.dt.bfloat16
AF = mybir.ActivationFunctionType
ALU = mybir.AluOpType
BF = ml_dtypes.bfloat16

D_MODEL = 1024
D_IN = 5120
D_FF = 2816
NFF = 22
EPS = 1e-6
NEG = -30000.0
PI = float(np.pi)


class Dep:
    __slots__ = ("w", "r")

    def __init__(self):
        self.w = {}
        self.r = {}


class DSem:
    def __init__(self, sem):
        self.sem = sem
        self.n = 0
        self.nobar = False


class Prog:
    ENGS = ["sync", "scalar", "vector", "gpsimd", "tensor"]

    def __init__(self, nc, es):
        self.nc = nc
        self.h = {"sync": nc.sync, "scalar": nc.scalar, "vector": nc.vector,
                  "gpsimd": nc.gpsimd, "tensor": nc.tensor}
        self.sem = {e: es.enter_context(nc.semaphore("s_" + e)) for e in self.ENGS}
        self.cnt = {e: 0 for e in self.ENGS}
        self.known = {e: {} for e in self.ENGS}
        self.dsems = []
        self.es = es
        self.rr = 0

    def dsem(self, name):
        s = DSem(self.es.enter_context(self.nc.semaphore(name)))
        self.dsems.append(s)
        return s

    def _need(self, eng, items):
        k = self.known[eng]
        for it in items:
            if it is None:
                continue
            sem, val, e2 = it
            if e2 == eng and eng == "tensor":
                continue
            if k.get(id(sem), 0) >= val:
                continue
            k[id(sem)] = val
            self.h[eng].wait_ge(sem, val)

    def _items(self, reads, writes):
        items = []
        for d in reads:
            items.extend(d.w.values())
        for d in writes:
            items.extend(d.w.values())
            items.extend(d.r.values())
        return items

    def _upd(self, tok, reads, writes):
        key = id(tok[0])
        for d in reads:
            d.r[key] = tok
        for d in writes:
            d.w[key] = tok
            d.r = {}

    def op(self, eng, fn, r=(), w=()):
        self._need(eng, self._items(r, w))
        ins = fn(self.h[eng])
        self.cnt[eng] += 1
        ins.then_inc(self.sem[eng], 1)
        tok = (self.sem[eng], self.cnt[eng], eng)
        self._upd(tok, r, w)
        return tok

    def dma(self, eng, ds, out, in_, r=(), w=()):
        self._need(eng, self._items(r, w))
        ins = self.h[eng].dma_start(out=out, in_=in_)
        ds.n += 16
        ins.then_inc(ds.sem, 16)
        tok = (ds.sem, ds.n, "dma")
        self._upd(tok, r, w)
        return tok

    def barrier(self):
        toks = [(self.sem[e], self.cnt[e], e) for e in self.ENGS if self.cnt[e] > 0]
        toks += [(d.sem, d.n, "dma") for d in self.dsems if d.n > 0 and not d.nobar]
        for e in self.ENGS:
            k = self.known[e]
            for sem, val, e2 in toks:
                if e2 == e:
                    continue
                if k.get(id(sem), 0) >= val:
                    continue
                k[id(sem)] = val
                self.h[e].wait_ge(sem, val)

    def alt(self, a="vector", b="gpsimd"):
        self.rr += 1
        return a if self.rr % 2 else b


_UID = [0]


def pipeline(n_iter, stages):
    ns = len(stages)
    for t in range(n_iter + ns - 1):
        for k, f in enumerate(stages):
            i = t - k
            if 0 <= i < n_iter:
                f(i)


def T(es, nc, name, shape, dt):
    _UID[0] += 1
    return es.enter_context(nc.sbuf_tensor(f"sb{_UID[0]}_{name}", shape, dt))


def PS(es, nc, name, shape, dt):
    _UID[0] += 1
    return es.enter_context(nc.psum_tensor(f"ps{_UID[0]}_{name}", shape, dt))


def rms_transpose(P, nc, es, xrows, nsub, hT, d_hT, grep, d_grep, ident, d_ident, pfx, dq):
    NB = 3
    xs = [T(es, nc, f"{pfx}xs{i}", [128, 1024], F32) for i in range(NB)]
    d_xs = [Dep() for _ in range(NB)]
    xn = [T(es, nc, f"{pfx}xn{i}", [128, 1024], BF16) for i in range(NB)]
    d_xn = [Dep() for _ in range(NB)]
    junk = T(es, nc, f"{pfx}junk", [128, 1024], BF16)
    d_junk = Dep()
    stt = [T(es, nc, f"{pfx}st{i}", [128, 4], F32) for i in range(NB)]
    d_st = [Dep() for _ in range(NB)]
    pt = [PS(es, nc, f"{pfx}pt{i}", [128, 8, 128], BF16) for i in range(2)]
    d_pt = [Dep() for _ in range(2)]
    for i in range(NB):
        P.op("gpsimd", lambda e: e.memset(stt[i][:], EPS), w=[d_st[i]])

    def S1(st):
        b = st % NB
        P.dma("sync", dq[b], xs[b][:], xrows[st * 128:(st + 1) * 128, :], w=[d_xs[b]])
        P.op("scalar", lambda e: e.activation(out=junk[:], in_=xs[b][:], func=AF.Square,
                                              accum_out=stt[b][:, 0:1]),
             r=[d_xs[b]], w=[d_junk, d_st[b]])
        P.op("scalar", lambda e: e.activation(out=stt[b][:, 1:2], in_=stt[b][:, 0:1], func=AF.Sqrt,
                                              scale=1.0 / 1024, bias=stt[b][:, 3:4]),
             r=[d_st[b]], w=[d_st[b]])

    def S2(st):
        b = st % NB
        P.op("vector", lambda e: e.reciprocal(out=stt[b][:, 2:3], in_=stt[b][:, 1:2]),
             r=[d_st[b]], w=[d_st[b]])
        if st % 2:
            P.op("scalar", lambda e: e.activation(out=xn[b][:], in_=xs[b][:], func=AF.Copy, scale=stt[b][:, 2:3]),
                 r=[d_st[b], d_xs[b]], w=[d_xn[b]])
        else:
            P.op("vector", lambda e: e.tensor_scalar(out=xn[b][:], in0=xs[b][:], scalar1=stt[b][:, 2:3],
                                                     scalar2=None, op0=ALU.mult),
                 r=[d_st[b], d_xs[b]], w=[d_xn[b]])

    def S3(st):
        b = st % NB
        p = st % 2
        for kc in range(8):
            P.op("tensor", lambda e: e.transpose(pt[p][:, kc, :], xn[b][:, kc * 128:(kc + 1) * 128], ident[:]),
                 r=[d_xn[b], d_ident], w=[d_pt[p]])
        P.op("vector", lambda e: e.tensor_tensor(out=hT[:, :, st * 128:(st + 1) * 128], in0=pt[p][:],
                                                 in1=grep[:], op=ALU.mult),
             r=[d_pt[p], d_grep], w=[d_hT])

    pipeline(nsub, [S1, S2, S3])


def make_grep(P, nc, es, gcol, name, dq):
    g = T(es, nc, name + "g", [128, 8], F32)
    d_g = Dep()
    grep = T(es, nc, name + "rep", [128, 8, 128], F32)
    d_grep = Dep()
    P.dma("sync", dq, g[:], gcol, w=[d_g])
    P.op("gpsimd", lambda e: e.memset(grep[:], 1.0), w=[d_grep])
    for kc in range(8):
        P.op("vector", lambda e: e.tensor_scalar(out=grep[:, kc, :], in0=grep[:, kc, :],
                                                 scalar1=g[:, kc:kc + 1], scalar2=None, op0=ALU.mult),
             r=[d_g], w=[d_grep])
    return grep, d_grep


def load_ident(P, nc, es, D, dq):
    ident = T(es, nc, "ident", [128, 128], BF16)
    d_ident = Dep()
    P.dma("sync", dq, ident[:], D["ident"], w=[d_ident])
    return ident, d_ident


def eps_init(P, stt_list):
    pass


def build_program(dev=False, phases="ABCDEFGHI"):
    nc = bass.Bass("TRN2", target_bir_lowering=False)
    D = {}

    def inp(name, shape, dt=F32):
        D[name] = nc.dram_tensor(name, list(shape), dt, kind="ExternalInput").ap()

    def scr(name, shape, dt=BF16):
        if dev:
            D[name] = nc.dram_tensor(name, list(shape), dt, kind="ExternalOutput").ap()
        else:
            D[name] = nc.dram_tensor(name, list(shape), dt).ap()

    inp("xext", [4608, 1024])
    inp("xctx", [4, 4224, 1024])
    inp("w_in", [1024, D_IN])
    inp("w_br_attn", [512, 1024])
    inp("w_br_hyena", [512, 1024])
    inp("w_out", [1024, 1024])
    inp("w_gate", [1024, D_FF])
    inp("w_up", [1024, D_FF])
    inp("w_down", [D_FF, 1024])
    inp("g_mix", [128, 8])
    inp("g_ffn", [128, 8])
    inp("g_final", [128, 1024])
    inp("conv_w", [128, 12, 3])
    inp("conv_b", [128, 12])
    inp("fw1", [33, 64])
    inp("fw2", [64, 64])
    inp("fw3", [64, 64])
    inp("fw4", [64, 1024])
    inp("fb", [128, 3])
    inp("ffr", [128, 1])
    inp("hyd", [128, 512])
    inp("ident", [128, 128], BF16)
    inp("Etab", [128, 64, 2, 128], BF16)
    inp("F2tab", [128, 3, 128], BF16)
    inp("ITtab", [128, 33, 3, 128], BF16)
    inp("ICtab", [66, 2, 64], BF16)
    inp("zs", [4, 128, 4096], BF16)
    inp("maskT", [4, 128, 8192], BF16)
    inp("tpos", [128, 4, 64])
    inp("vbias", [128, 4, 64])
    inp("e0", [128, 1])
    inp("negdelta", [128, 512])
    inp("abias", [3, 8, 128, 6, 256], BF16)

    scr("QT", [4, 128, 4096])
    scr("KT", [4, 128, 4608])
    scr("VA", [36, 128, 8 * 65])
    scr("X0T", [4, 128, 4096])
    scr("DU", [4, 64, 4, 64, 128])
    scr("AU", [4, 64, 66, 2, 512])
    scr("AG", [4, 64, 66, 2, 512])
    scr("CS", [66, 64, 2, 512])
    scr("YHT", [4, 128, 4096])
    scr("YAT", [4, 128, 4096])
    for nm, shp in (("WG", [16, 128, 8, 128]), ("WBA", [8, 128, 4, 128]), ("WBH", [8, 128, 4, 128]),
                    ("WO", [128, 8, 1024]), ("WFG", [22, 128, 8, 128]), ("WFU", [22, 128, 8, 128]),
                    ("WD", [2, 128, 22, 512])):
        D[nm] = nc.dram_tensor(nm, shp, BF16).ap()
    D["y"] = nc.dram_tensor("y", [4096, 1024], F32, kind="ExternalOutput").ap()

    with ExitStack() as es0:
        P = Prog(nc, es0)
        dq = [P.dsem(f"dq{i}") for i in range(64)]
        gq = [P.dsem(f"gq{i}") for i in range(16)]
        wsem = [P.dsem(f"wq{i}") for i in range(4)]
        if "A" in phases:
            phase_A(P, nc, D, dq, gq)
            P.barrier()
        if "B" in phases:
            phase_B(P, nc, D, dq, gq)
            P.barrier()
        if "C" in phases:
            phase_C(P, nc, D, dq, gq, (lambda: phase_W(P, nc, D, wsem)) if "I" in phases else None)
        elif "I" in phases:
            phase_W(P, nc, D, wsem)
            P.barrier()
        if "D" in phases:
            phase_D(P, nc, D, dq, gq)
            P.barrier()
        if "F" in phases:
            phase_F(P, nc, D, dq, gq)
            P.barrier()
        if "G" in phases:
            phase_G(P, nc, D, dq, gq)
            P.barrier()
        if "H" in phases:
            phase_H(P, nc, D, dq, gq)
            P.barrier()
        if "I" in phases:
            for w in wsem:
                w.nobar = False
            P.barrier()
            phase_I(P, nc, D, dq, gq)
            P.barrier()
    return nc


def wchunk_src(w, col0, ncols, nk=8):
    return w.rearrange("(kc p) n -> p kc n", p=128)[:, :, col0:col0 + ncols]


def phase_A(P, nc, D, dq, gq):
    with ExitStack() as es:
        ident, d_ident = load_ident(P, nc, es, D, dq[0])
        grep, d_grep = make_grep(P, nc, es, D["g_mix"], "gm", dq[1])
        hT = T(es, nc, "A_hT", [128, 8, 4608], BF16)
        d_hT = Dep()
        with ExitStack() as es1:
            rms_transpose(P, nc, es1, D["xext"], 36, hT, d_hT, grep, d_grep, ident, d_ident, "A_", dq[40:43])
        P.barrier()
        wq = [T(es, nc, f"A_wq{i}", [128, 8, 128], BF16) for i in range(2)]
        d_wq = [Dep() for _ in range(2)]
        stg = [T(es, nc, f"A_stg{i}", [128, 4608], BF16) for i in range(2)]
        d_stg = [Dep() for _ in range(2)]
        ps = [PS(es, nc, f"A_ps{i}", [128, 512], F32) for i in range(4)]
        d_ps = [Dep() for _ in range(4)]
        pi = 0
        for ci in range(8):
            b = ci % 2
            isq = ci < 4
            P.dma("gpsimd", gq[4 + b], wq[b][:], wchunk_src(D["w_in"], ci * 128, 128), w=[d_wq[b]])
            t0, ntile = (256, 8) if isq else (0, 9)
            for tt in range(ntile):
                p = pi % 4
                pi += 1
                for kc in range(8):
                    P.op("tensor", lambda e: e.matmul(ps[p][:], wq[b][:, kc, :],
                                                      hT[:, kc, t0 + tt * 512:t0 + (tt + 1) * 512],
                                                      start=(kc == 0), stop=(kc == 7)),
                         r=[d_wq[b], d_hT], w=[d_ps[p]])
                if tt % 2 == 0:
                    P.op("scalar", lambda e: e.activation(out=stg[b][:, tt * 512:(tt + 1) * 512], in_=ps[p][:],
                                                          func=AF.Copy, scale=(0.125 if isq else 1.0)),
                         r=[d_ps[p]], w=[d_stg[b]])
                else:
                    P.op("vector", lambda e: e.tensor_scalar(out=stg[b][:, tt * 512:(tt + 1) * 512], in0=ps[p][:],
                                                             scalar1=(0.125 if isq else 1.0), scalar2=None,
                                                             op0=ALU.mult),
                         r=[d_ps[p]], w=[d_stg[b]])
            if isq:
                P.dma("sync", dq[6 + b], D["QT"][ci], stg[b][:, 0:4096], r=[d_stg[b]])
            else:
                P.dma("sync", dq[6 + b], D["KT"][ci - 4], stg[b][:, 0:4608], r=[d_stg[b]])
        wv = T(es, nc, "A_wv", [128, 8, 512], BF16)
        d_wv = Dep()
        P.dma("gpsimd", gq[8], wv[:], wchunk_src(D["w_in"], 1024, 512), w=[d_wv])
        vst = [T(es, nc, f"A_vst{i}", [128, 8, 65], BF16) for i in range(2)]
        d_vst = [Dep() for _ in range(2)]
        for i in range(2):
            P.op("gpsimd", lambda e: e.memset(vst[i][:], 1.0), w=[d_vst[i]])
        for st in range(36):
            p = pi % 4
            pi += 1
            b = st % 2
            for kc in range(8):
                P.op("tensor", lambda e: e.matmul(ps[p][:], hT[:, kc, st * 128:(st + 1) * 128], wv[:, kc, :],
                                                  start=(kc == 0), stop=(kc == 7)),
                     r=[d_wv, d_hT], w=[d_ps[p]])
            P.op("scalar" if st % 2 else "vector",
                 (lambda e: e.activation(out=vst[b][:, :, 0:64], in_=ps[p][:].rearrange("p (h d) -> p h d", d=64),
                                         func=AF.Copy)) if st % 2 else
                 (lambda e: e.tensor_copy(out=vst[b][:, :, 0:64], in_=ps[p][:].rearrange("p (h d) -> p h d", d=64))),
                 r=[d_ps[p]], w=[d_vst[b]])
            P.dma("sync", dq[9 + b], D["VA"][st], vst[b][:].rearrange("p h d -> p (h d)"), r=[d_vst[b]])


def phase_B(P, nc, D, dq, gq):
    with ExitStack() as es:
        ident, d_ident = load_ident(P, nc, es, D, dq[0])
        grep, d_grep = make_grep(P, nc, es, D["g_mix"], "gm", dq[1])
        cw = T(es, nc, "B_cw", [128, 12, 3], F32)
        cb = T(es, nc, "B_cb", [128, 12], F32)
        d_c = Dep()
        P.dma("sync", dq[2], cw[:], D["conv_w"], w=[d_c])
        P.dma("sync", dq[3], cb[:], D["conv_b"], w=[d_c])
        hT = T(es, nc, "B_hT", [128, 8, 4224], BF16)
        d_hT = Dep()
        hy = [T(es, nc, f"B_hy{i}", [128, 4224], F32) for i in range(2)]
        d_hy = [Dep() for _ in range(2)]
        cx = [T(es, nc, f"B_cx{i}", [128, 4096], BF16) for i in range(3)]
        d_cx = [Dep() for _ in range(3)]
        uT = [T(es, nc, f"B_uT{i}", [128, 4096], BF16) for i in range(2)]
        d_uT = [Dep() for _ in range(2)]
        du = T(es, nc, "B_du", [64, 64, 128], BF16)
        d_du = Dep()
        wq = [T(es, nc, f"B_wq{i}", [128, 8, 128], BF16) for i in range(2)]
        d_wq = [Dep() for _ in range(2)]
        ps = [PS(es, nc, f"B_ps{i}", [128, 512], F32) for i in range(4)]
        d_ps = [Dep() for _ in range(4)]
        ptr = [PS(es, nc, f"B_ptr{i}", [64, 8, 128], BF16) for i in range(2)]
        d_ptr = [Dep() for _ in range(2)]
        st8 = {"pi": 0, "wi": 0}
        for blk in range(4):
            with ExitStack() as es1:
                P.barrier()
                rms_transpose(P, nc, es1, D["xctx"][blk], 33, hT, d_hT, grep, d_grep, ident, d_ident,
                              f"B{blk}_", dq[40:43])
            P.barrier()
            items = []
            for j in range(4):
                items.append((1, 2048 + j * 128, 4 + j, j))
                items.append((2, 2560 + j * 128, 8 + j, j))
                if blk == 0:
                    items.append((0, 1536 + j * 128, j, j))
            base = st8["wi"]

            def S1(c, items=items, base=base):
                role, col0, cidx, j = items[c]
                b = (base + c) % 2
                k3 = (base + c) % 3
                P.dma("gpsimd", gq[b], wq[b][:], wchunk_src(D["w_in"], col0, 128), w=[d_wq[b]])
                for tt in range(9):
                    p = st8["pi"] % 4
                    st8["pi"] += 1
                    n = 512 if tt < 8 else 128
                    for kc in range(8):
                        P.op("tensor", lambda e: e.matmul(ps[p][:, 0:n], wq[b][:, kc, :],
                                                          hT[:, kc, tt * 512:tt * 512 + n],
                                                          start=(kc == 0), stop=(kc == 7)),
                             r=[d_wq[b], d_hT], w=[d_ps[p]])
                    P.op("scalar", lambda e: e.activation(out=hy[b][:, tt * 512:tt * 512 + n], in_=ps[p][:, 0:n],
                                                          func=AF.Copy),
                         r=[d_ps[p]], w=[d_hy[b]])
                    if tt < 8:
                        P.op("scalar", lambda e: e.activation(out=cx[k3][:, tt * 512:(tt + 1) * 512], in_=ps[p][:],
                                                              func=AF.Identity, scale=cw[:, cidx, 1:2],
                                                              bias=cb[:, cidx:cidx + 1]),
                             r=[d_ps[p], d_c], w=[d_cx[k3]])

            def S2(c, items=items, base=base):
                role, col0, cidx, j = items[c]
                b = (base + c) % 2
                k3 = (base + c) % 3
                cc = cx[k3]
                dc = d_cx[k3]
                h = hy[b]
                P.op("vector", lambda e: e.scalar_tensor_tensor(out=cc[:, 1:4096], in0=h[:, 0:4095],
                                                                scalar=cw[:, cidx, 0:1], in1=cc[:, 1:4096],
                                                                op0=ALU.mult, op1=ALU.add),
                     r=[d_hy[b], d_c], w=[dc])
                P.op("vector", lambda e: e.scalar_tensor_tensor(out=cc[:, 0:4095], in0=h[:, 1:4096],
                                                                scalar=cw[:, cidx, 2:3], in1=cc[:, 0:4095],
                                                                op0=ALU.mult, op1=ALU.add),
                     r=[d_hy[b], d_c], w=[dc])
                P.op("vector", lambda e: e.scalar_tensor_tensor(out=cc[:, 0:1], in0=h[:, 4096:4097],
                                                                scalar=cw[:, cidx, 0:1], in1=cc[:, 0:1],
                                                                op0=ALU.mult, op1=ALU.add),
                     r=[d_hy[b], d_c], w=[dc])
                P.op("vector", lambda e: e.scalar_tensor_tensor(out=cc[:, 4095:4096], in0=h[:, 4097:4098],
                                                                scalar=cw[:, cidx, 2:3], in1=cc[:, 4095:4096],
                                                                op0=ALU.mult, op1=ALU.add),
                     r=[d_hy[b], d_c], w=[dc])
                if role == 2:
                    kx = (base + c - 1) % 3
                    P.op("vector", lambda e: e.tensor_tensor(out=uT[j % 2][:], in0=cx[kx][:], in1=cc[:], op=ALU.mult),
                         r=[d_cx[kx], dc], w=[d_uT[j % 2]])

            def S3(c, items=items, base=base, blk=blk):
                role, col0, cidx, j = items[c]
                k3 = (base + c) % 3
                if role == 2:
                    uv = uT[j % 2][:].rearrange("p (a b) -> p a b", b=64)
                    for g8 in range(8):
                        q = g8 % 2
                        for tl in range(8):
                            t2 = g8 * 8 + tl
                            P.op("tensor", lambda e: e.transpose(ptr[q][:, tl, :], uv[:, :, t2], ident[:]),
                                 r=[d_uT[j % 2], d_ident], w=[d_ptr[q]])
                        P.op("scalar" if g8 % 2 else "vector",
                             (lambda e: e.activation(out=du[:, g8 * 8:(g8 + 1) * 8, :], in_=ptr[q][:], func=AF.Copy))
                             if g8 % 2 else
                             (lambda e: e.tensor_copy(out=du[:, g8 * 8:(g8 + 1) * 8, :], in_=ptr[q][:])),
                             r=[d_ptr[q]], w=[d_du])
                    P.dma("sync", dq[8], D["DU"][blk, :, j], du[:], r=[d_du])
                if role == 0:
                    P.dma("sync", dq[9], D["X0T"][j], cx[k3][:], r=[d_cx[k3]])

            pipeline(len(items), [S1, S2, S3])
            st8["wi"] += len(items)


def fft_stage1(P, nc, es, K, rhs_of, d_src, Et, d_Et, Ascr, dq, pfx, nps=4):
    ps = [PS(es, nc, f"{pfx}s1p{i}", [128, 512], F32) for i in range(nps)]
    d_ps = [Dep() for _ in range(nps)]
    ast = [T(es, nc, f"{pfx}ast{i}", [66, 2, 512], BF16) for i in range(3)]
    d_ast = [Dep() for _ in range(3)]
    for t2 in range(64):
        a = t2 % 3
        for ri in range(2):
            p = (t2 * 2 + ri) % nps
            P.op("tensor", lambda e: e.matmul(ps[p][0:66, :], Et[0:K, t2, ri, 0:66], rhs_of(t2), start=True, stop=True),
                 r=[d_Et, d_src], w=[d_ps[p]])
            if ri == 0:
                P.op("scalar", lambda e: e.activation(out=ast[a][:, ri, :], in_=ps[p][0:66, :], func=AF.Copy),
                     r=[d_ps[p]], w=[d_ast[a]])
            else:
                P.op("vector", lambda e: e.tensor_copy(out=ast[a][:, ri, :], in_=ps[p][0:66, :]),
                     r=[d_ps[p]], w=[d_ast[a]])
        P.dma("sync", dq[a], Ascr[t2], ast[a][:], r=[d_ast[a]])


def sin_group(P, pm, d_pm, tmp, d_tmp, t1, d_t1, fr, frb, d_f, out, d_out):
    P.op("scalar", lambda e: e.activation(out=tmp[:], in_=pm[:], func=AF.Identity, scale=fr, bias=frb),
         r=[d_pm, d_f], w=[d_tmp])
    P.op("vector", lambda e: e.tensor_scalar(out=t1[:], in0=tmp[:], scalar1=-1.0, scalar2=PI,
                                             op0=ALU.mult, op1=ALU.add),
         r=[d_tmp], w=[d_t1])
    P.op("vector", lambda e: e.tensor_tensor(out=tmp[:], in0=tmp[:], in1=t1[:], op=ALU.min),
         r=[d_t1], w=[d_tmp])
    P.op("vector", lambda e: e.scalar_tensor_tensor(out=tmp[:], in0=t1[:], scalar=-2 * PI, in1=tmp[:],
                                                    op0=ALU.add, op1=ALU.max),
         r=[d_t1], w=[d_tmp])
    P.op("scalar", lambda e: e.activation(out=out, in_=tmp[:], func=AF.Sin), r=[d_tmp], w=[d_out])


def phase_C(P, nc, D, dq, gq, after_w=None):
    with ExitStack() as es:
        Et = T(es, nc, "C_Et", [128, 64, 2, 128], BF16)
        d_Et = Dep()
        P.dma("sync", dq[0], Et[:], D["Etab"], w=[d_Et])
        W1 = T(es, nc, "C_W1", [128, 128], BF16)
        W2 = T(es, nc, "C_W2", [128, 128], BF16)
        W3 = T(es, nc, "C_W3", [128, 128], BF16)
        W4 = T(es, nc, "C_W4", [128, 512], BF16)
        d_w = Dep()
        P.op("gpsimd", lambda e: e.memset(W1[:], 0.0), w=[d_w])
        P.op("gpsimd", lambda e: e.memset(W2[:], 0.0), w=[d_w])
        for hf in range(2):
            P.dma("gpsimd", gq[0], W1[hf * 64:hf * 64 + 33, hf * 64:(hf + 1) * 64], D["fw1"], w=[d_w])
            P.dma("gpsimd", gq[1], W2[hf * 64:(hf + 1) * 64, hf * 64:(hf + 1) * 64], D["fw2"], w=[d_w])
            for h2 in range(2):
                P.dma("gpsimd", gq[2], W3[hf * 64:(hf + 1) * 64, h2 * 64:(h2 + 1) * 64], D["fw3"], w=[d_w])
            P.dma("gpsimd", gq[3], W4[hf * 64:(hf + 1) * 64, :], D["fw4"][:, hf * 512:(hf + 1) * 512], w=[d_w])
        if after_w is not None:
            after_w()
        fb = T(es, nc, "C_fb", [128, 3], F32)
        fr = T(es, nc, "C_fr", [128, 1], F32)
        frb = T(es, nc, "C_frb", [128, 3], F32)
        d_f = Dep()
        P.dma("sync", dq[5], fb[:], D["fb"], w=[d_f])
        P.dma("sync", dq[6], fr[:], D["ffr"], w=[d_f])
        P.op("vector", lambda e: e.tensor_scalar(out=frb[:], in0=fb[:], scalar1=fr[:, 0:1], scalar2=None,
                                                 op0=ALU.mult), r=[d_f], w=[d_f])
        tpos = T(es, nc, "C_tpos", [128, 4, 64], F32)
        vbias = T(es, nc, "C_vbias", [128, 4, 64], F32)
        e0 = T(es, nc, "C_e0", [128, 1], F32)
        negd = T(es, nc, "C_negd", [128, 512], F32)
        hyd = T(es, nc, "C_hyd", [128, 512], F32)
        d_t = Dep()
        P.dma("sync", dq[7], tpos[:], D["tpos"], w=[d_t])
        P.dma("sync", dq[8], vbias[:], D["vbias"], w=[d_t])
        P.dma("sync", dq[9], e0[:], D["e0"], w=[d_t])
        P.dma("sync", dq[10], negd[:], D["negdelta"], w=[d_t])
        P.dma("sync", dq[11], hyd[:], D["hyd"], w=[d_t])
        zs = T(es, nc, "C_zs", [128, 4096], BF16)
        d_z = Dep()
        mk = T(es, nc, "C_mk", [128, 8192], BF16)
        d_mk = Dep()
        hA = T(es, nc, "C_hA", [128, 4096], BF16)
        hB = T(es, nc, "C_hB", [128, 4096], BF16)
        d_hAg = [Dep() for _ in range(4)]
        d_hBg = [Dep() for _ in range(4)]
        h3 = T(es, nc, "C_h3", [128, 8192], BF16)
        d_h3g = [Dep() for _ in range(8)]
        tmp = [T(es, nc, f"C_tmp{i}", [128, 1024], F32) for i in range(2)]
        t1b = [T(es, nc, f"C_t1{i}", [128, 1024], F32) for i in range(2)]
        d_tmp = [Dep() for _ in range(2)]
        d_t1b = [Dep() for _ in range(2)]
        gi = 0
        gf = [T(es, nc, f"C_gf{i}", [128, 512], BF16) for i in range(3)]
        d_gf = [Dep() for _ in range(3)]
        ast = [T(es, nc, f"C_ast{i}", [66, 2, 512], BF16) for i in range(3)]
        d_ast = [Dep() for _ in range(3)]
        ps1 = [PS(es, nc, f"C_ps1{i}", [128, 512], F32) for i in range(2)]
        d_ps1 = [Dep() for _ in range(2)]
        dec = [T(es, nc, f"C_dec{i}", [128, 512], F32) for i in range(2)]
        d_dec = [Dep() for _ in range(2)]
        gt = T(es, nc, "C_gt", [128, 512], F32)
        d_gt = Dep()
        pm = [PS(es, nc, f"C_pm{i}", [128, 1024], F32) for i in range(2)]
        d_pm = [Dep() for _ in range(2)]
        p4 = [PS(es, nc, f"C_p4{i}", [128, 512], F32) for i in range(2)]
        d_p4 = [Dep() for _ in range(2)]
        for s in range(4):
            P.dma("sync", dq[12], zs[:], D["zs"][s], w=[d_z])
            P.dma("sync", dq[13], mk[:], D["maskT"][s], w=[d_mk])
            groups = []
            for g in range(4):
                groups.append((W1[:], zs[:, g * 1024:(g + 1) * 1024], [d_z], hA[:, g * 1024:(g + 1) * 1024],
                               d_hAg[g], 0, None))
            for g in range(4):
                groups.append((W2[:], hA[:, g * 1024:(g + 1) * 1024], [d_hAg[g]], hB[:, g * 1024:(g + 1) * 1024],
                               d_hBg[g], 1, None))
            for g in range(8):
                hf = g // 4
                groups.append((W3[hf * 64:(hf + 1) * 64, :],
                               hB[hf * 64:(hf + 1) * 64, (g % 4) * 1024:(g % 4 + 1) * 1024], [d_hBg[g % 4]],
                               h3[:, g * 1024:(g + 1) * 1024], d_h3g[g], 2, mk[:, g * 1024:(g + 1) * 1024]))

            def M1(i):
                Wm, src, dsrc, dst, ddst, li, mask = groups[i]
                b = i % 2
                for k in range(2):
                    P.op("tensor", lambda e: e.matmul(pm[b][:, k * 512:(k + 1) * 512], Wm, src[:, k * 512:(k + 1) * 512],
                                                      start=True, stop=True),
                         r=[d_w] + dsrc, w=[d_pm[b]])
                P.op("scalar", lambda e: e.activation(out=tmp[b][:], in_=pm[b][:], func=AF.Identity,
                                                      scale=fr[:, 0:1], bias=frb[:, li:li + 1]),
                     r=[d_pm[b], d_f], w=[d_tmp[b]])

            def M2(i):
                b = i % 2
                P.op("vector", lambda e: e.tensor_scalar(out=t1b[b][:], in0=tmp[b][:], scalar1=-1.0, scalar2=PI,
                                                         op0=ALU.mult, op1=ALU.add),
                     r=[d_tmp[b]], w=[d_t1b[b]])
                P.op("vector", lambda e: e.tensor_tensor(out=tmp[b][:], in0=tmp[b][:], in1=t1b[b][:], op=ALU.min),
                     r=[d_t1b[b]], w=[d_tmp[b]])
                P.op("vector", lambda e: e.scalar_tensor_tensor(out=t1b[b][:], in0=t1b[b][:], scalar=-2 * PI,
                                                                in1=tmp[b][:], op0=ALU.add, op1=ALU.max),
                     r=[d_tmp[b]], w=[d_t1b[b]])

            def M3(i):
                Wm, src, dsrc, dst, ddst, li, mask = groups[i]
                b = i % 2
                P.op("scalar", lambda e: e.activation(out=dst, in_=t1b[b][:], func=AF.Sin), r=[d_t1b[b]], w=[ddst])
                if mask is not None:
                    P.op("vector", lambda e: e.tensor_tensor(out=dst, in0=dst, in1=mask, op=ALU.mult),
                         r=[d_mk], w=[ddst])

            pipeline(16, [M1, M2, M3])
            h3v = h3[:].rearrange("p (a b) -> p a b", b=64)

            def L1(t2, s=s, h3v=h3v):
                b = t2 % 2
                P.op("tensor", lambda e: e.matmul(p4[b][:], h3v[:, :, t2], W4[:], start=True, stop=True),
                     r=d_h3g + [d_w], w=[d_p4[b]])
                P.op("scalar", lambda e: e.activation(out=dec[b][:], in_=negd[:], func=AF.Exp,
                                                      scale=tpos[:, s, t2:t2 + 1], bias=vbias[:, s, t2:t2 + 1]),
                     r=[d_t], w=[d_dec[b]])
                k = t2 % 3
                if s == 0 and t2 == 0:
                    P.op("vector", lambda e: e.tensor_tensor(out=gt[:], in0=p4[b][:], in1=dec[b][:], op=ALU.mult),
                         r=[d_p4[b], d_dec[b]], w=[d_gt])
                    P.op("vector", lambda e: e.scalar_tensor_tensor(out=gf[k][:], in0=hyd[:], scalar=e0[:, 0:1],
                                                                    in1=gt[:], op0=ALU.mult, op1=ALU.add),
                         r=[d_t, d_gt], w=[d_gf[k]])
                else:
                    P.op("vector", lambda e: e.tensor_tensor(out=gf[k][:], in0=p4[b][:], in1=dec[b][:], op=ALU.mult),
                         r=[d_p4[b], d_dec[b]], w=[d_gf[k]])

            def L2(t2, s=s):
                k = t2 % 3
                a = t2 % 3
                for ri in range(2):
                    p = (t2 * 2 + ri) % 2
                    P.op("tensor", lambda e: e.matmul(ps1[p][0:66, :], Et[:, t2, ri, 0:66], gf[k][:], start=True, stop=True),
                         r=[d_Et, d_gf[k]], w=[d_ps1[p]])
                    if ri == 0:
                        P.op("scalar", lambda e: e.activation(out=ast[a][:, ri, :], in_=ps1[p][0:66, :], func=AF.Copy),
                             r=[d_ps1[p]], w=[d_ast[a]])
                    else:
                        P.op("vector", lambda e: e.tensor_copy(out=ast[a][:, ri, :], in_=ps1[p][0:66, :]),
                             r=[d_ps1[p]], w=[d_ast[a]])
                P.dma("sync", dq[14 + a], D["AG"][s][t2], ast[a][:], r=[d_ast[a]])

            pipeline(64, [L1, L2])


def phase_D(P, nc, D, dq, gq):
    with ExitStack() as es:
        Et = T(es, nc, "D_Et", [64, 64, 2, 128], BF16)
        d_Et = Dep()
        P.dma("sync", dq[0], Et[:], D["Etab"][0:64], w=[d_Et])
        Du = [T(es, nc, f"D_Du{i}", [64, 4, 64, 128], BF16) for i in range(2)]
        d_Du = [Dep() for _ in range(2)]
        ps = [PS(es, nc, f"D_s1p{i}", [128, 512], F32) for i in range(6)]
        d_ps = [Dep() for _ in range(6)]
        ast = [T(es, nc, f"D_ast{i}", [66, 2, 512], BF16) for i in range(4)]
        d_ast = [Dep() for _ in range(4)]
        P.dma("sync", dq[1], Du[0][:], D["DU"][0], w=[d_Du[0]])
        it = 0
        for blk in range(4):
            b = blk % 2
            if blk + 1 < 4:
                P.dma("sync", dq[1 + (blk + 1) % 2], Du[(blk + 1) % 2][:], D["DU"][blk + 1], w=[d_Du[(blk + 1) % 2]])
            for t2 in range(64):
                a = it % 4
                for ri in range(2):
                    p = (it * 2 + ri) % 6
                    P.op("tensor", lambda e: e.matmul(ps[p][0:66, :], Et[:, t2, ri, 0:66], Du[b][:, :, t2, :],
                                                      start=True, stop=True),
                         r=[d_Et, d_Du[b]], w=[d_ps[p]])
                    if ri == 0:
                        P.op("scalar", lambda e: e.activation(out=ast[a][:, ri, :], in_=ps[p][0:66, :], func=AF.Copy),
                             r=[d_ps[p]], w=[d_ast[a]])
                    else:
                        P.op("vector", lambda e: e.tensor_copy(out=ast[a][:, ri, :], in_=ps[p][0:66, :]),
                             r=[d_ps[p]], w=[d_ast[a]])
                P.dma("sync", dq[3 + a], D["AU"][blk][t2], ast[a][:], r=[d_ast[a]])
                it += 1


KG = 3
NQ = 11


def stage2_mm(P, F2, d_F2, Bt, d_B, kk, out, d_out):
    P.op("tensor", lambda e: e.matmul(out[:, 0, :], F2[:, 0, :], Bt[:, kk, 0, :], start=True, stop=False),
         r=[d_F2, d_B], w=[d_out])
    P.op("tensor", lambda e: e.matmul(out[:, 0, :], F2[:, 2, :], Bt[:, kk, 1, :], start=False, stop=True),
         r=[d_F2, d_B], w=[d_out])
    P.op("tensor", lambda e: e.matmul(out[:, 1, :], F2[:, 1, :], Bt[:, kk, 0, :], start=True, stop=False),
         r=[d_F2, d_B], w=[d_out])
    P.op("tensor", lambda e: e.matmul(out[:, 1, :], F2[:, 0, :], Bt[:, kk, 1, :], start=False, stop=True),
         r=[d_F2, d_B], w=[d_out])


def phase_F(P, nc, D, dq, gq):
    with ExitStack() as es:
        ident, d_ident = load_ident(P, nc, es, D, dq[0])
        F2 = T(es, nc, "F_F2", [128, 3, 128], BF16)
        d_F2 = Dep()
        P.dma("sync", dq[1], F2[:], D["F2tab"], w=[d_F2])
        IT = [T(es, nc, f"F_IT{i}", [128, KG, 3, 128], BF16) for i in range(2)]
        d_IT = [Dep() for _ in range(2)]
        Bg = [[T(es, nc, f"F_Bg{i}_{s}", [128, KG, 2, 512], BF16) for s in range(4)] for i in range(2)]
        Bu = [[T(es, nc, f"F_Bu{i}_{s}", [128, KG, 2, 512], BF16) for s in range(4)] for i in range(2)]
        d_Bg = [[Dep() for s in range(4)] for i in range(2)]
        d_Bu = [[Dep() for s in range(4)] for i in range(2)]
        Gs = [T(es, nc, f"F_Gs{i}", [128, 3, 512], BF16) for i in range(2)]
        d_Gs = [Dep() for _ in range(2)]
        TA = [T(es, nc, f"F_TA{i}", [128, 2, 512], BF16) for i in range(2)]
        d_TA = [Dep() for _ in range(2)]
        TB = [T(es, nc, f"F_TB{i}", [128, 2, 512], BF16) for i in range(2)]
        d_TB = [Dep() for _ in range(2)]
        Yb = [T(es, nc, f"F_Yb{i}", [128, 2, 512], BF16) for i in range(2)]
        d_Yb = [Dep() for _ in range(2)]
        Cst = [T(es, nc, f"F_C{i}", [128, 2, 512], BF16) for i in range(2)]
        d_C = [Dep() for _ in range(2)]
        Gp = PS(es, nc, "F_Gp", [128, 2, 512], F32)
        d_Gp = Dep()
        Up = [PS(es, nc, f"F_Up{i}", [128, 2, 512], F32) for i in range(2)]
        d_Up = [Dep() for _ in range(2)]
        Yp = PS(es, nc, "F_Yp", [128, 2, 512], F32)
        d_Yp = Dep()
        def loads(q):
            qb = q % 2
            P.dma("sync", dq[2 + qb], IT[qb][:], D["ITtab"][:, q * KG:(q + 1) * KG], w=[d_IT[qb]])
            for s in range(4):
                for kp in range(2):
                    P.dma("sync", dq[24 + qb * 16 + s * 2 + kp], Bg[qb][s][kp * 64:(kp + 1) * 64],
                          D["AG"][s][:, kp * 33 + q * KG:kp * 33 + (q + 1) * KG], w=[d_Bg[qb][s]])
                    P.dma("sync", dq[32 + qb * 16 + s * 2 + kp], Bu[qb][s][kp * 64:(kp + 1) * 64],
                          D["AU"][s][:, kp * 33 + q * KG:kp * 33 + (q + 1) * KG], w=[d_Bu[qb][s]])

        def idx(i):
            q, r = divmod(i, KG * 4)
            kk, s = divmod(r, 4)
            return q, kk, s

        def S1(i):
            q, kk, s = idx(i)
            qb = q % 2
            p = i % 2
            if i == 0:
                loads(0)
            if kk == 1 and s == 0 and q + 1 < NQ:
                loads(q + 1)
            stage2_mm(P, F2, d_F2, Bg[qb][s], d_Bg[qb][s], kk, Gp, d_Gp)
            P.op("scalar", lambda e: e.activation(out=Gs[p][:, 0:2, :], in_=Gp[:], func=AF.Copy),
                 r=[d_Gp], w=[d_Gs[p]])
            P.op("scalar", lambda e: e.activation(out=Gs[p][:, 2, :], in_=Gp[:, 1, :], func=AF.Copy, scale=-1.0),
                 r=[d_Gp], w=[d_Gs[p]])
            stage2_mm(P, F2, d_F2, Bu[qb][s], d_Bu[qb][s], kk, Up[p], d_Up[p])

        def S2(i):
            p = i % 2
            P.op("vector", lambda e: e.tensor_tensor(out=TA[p][:], in0=Up[p][:],
                                                     in1=Gs[p][:, 0:1, :].to_broadcast([128, 2, 512]), op=ALU.mult),
                 r=[d_Up[p], d_Gs[p]], w=[d_TA[p]])
            P.op("vector", lambda e: e.tensor_tensor(out=TB[p][:, 0, :], in0=Up[p][:, 1, :], in1=Gs[p][:, 2, :],
                                                     op=ALU.mult),
                 r=[d_Up[p], d_Gs[p]], w=[d_TB[p]])
            P.op("vector", lambda e: e.tensor_tensor(out=TB[p][:, 1, :], in0=Up[p][:, 0, :], in1=Gs[p][:, 1, :],
                                                     op=ALU.mult),
                 r=[d_Up[p], d_Gs[p]], w=[d_TB[p]])

        def S3(i):
            q, kk, s = idx(i)
            qb = q % 2
            p = i % 2
            for ri in range(2):
                P.op("tensor", lambda e: e.matmul(Yp[:, ri, :], ident[:], TA[p][:, ri, :], start=(s == 0), stop=False),
                     r=[d_ident, d_TA[p]], w=[d_Yp])
                P.op("tensor", lambda e: e.matmul(Yp[:, ri, :], ident[:], TB[p][:, ri, :], start=False, stop=(s == 3)),
                     r=[d_ident, d_TB[p]], w=[d_Yp])
            if s != 3:
                return
            c = (q * KG + kk) % 2
            P.op("scalar", lambda e: e.activation(out=Yb[c][:], in_=Yp[:], func=AF.Copy), r=[d_Yp], w=[d_Yb[c]])
            ITq = IT[qb]
            P.op("tensor", lambda e: e.matmul(Gp[:, 0, :], ITq[:, kk, 0, :], Yb[c][:, 0, :], start=True, stop=False),
                 r=[d_IT[qb], d_Yb[c]], w=[d_Gp])
            P.op("tensor", lambda e: e.matmul(Gp[:, 0, :], ITq[:, kk, 2, :], Yb[c][:, 1, :], start=False, stop=True),
                 r=[d_IT[qb], d_Yb[c]], w=[d_Gp])
            P.op("tensor", lambda e: e.matmul(Gp[:, 1, :], ITq[:, kk, 1, :], Yb[c][:, 0, :], start=True, stop=False),
                 r=[d_IT[qb], d_Yb[c]], w=[d_Gp])
            P.op("tensor", lambda e: e.matmul(Gp[:, 1, :], ITq[:, kk, 0, :], Yb[c][:, 1, :], start=False, stop=True),
                 r=[d_IT[qb], d_Yb[c]], w=[d_Gp])
            P.op("vector", lambda e: e.tensor_copy(out=Cst[c][:], in_=Gp[:]), r=[d_Gp], w=[d_C[c]])
            kh = q * KG + kk
            for kp in range(2):
                P.dma("sync", dq[20 + 2 * c + kp], D["CS"][kp * 33 + kh], Cst[c][kp * 64:(kp + 1) * 64], r=[d_C[c]])

        pipeline(NQ * KG * 4, [S1, S2, S3])


def phase_G(P, nc, D, dq, gq):
    with ExitStack() as es:
        IC = T(es, nc, "G_IC", [66, 2, 64], BF16)
        d_IC = Dep()
        P.dma("sync", dq[0], IC[:], D["ICtab"], w=[d_IC])
        x0 = T(es, nc, "G_x0", [128, 4, 4096], BF16)
        d_x0 = Dep()
        P.dma("sync", dq[1], x0[:], D["X0T"].rearrange("j p t -> p j t"), w=[d_x0])
        yh = T(es, nc, "G_yh", [128, 4, 4096], BF16)
        d_yh = Dep()
        Cg = [T(es, nc, f"G_C{i}", [66, 8, 2, 512], BF16) for i in range(2)]
        d_Cg = [Dep() for _ in range(2)]
        ps = [PS(es, nc, f"G_ps{i}", [128, 8, 64], F32) for i in range(4)]
        d_ps = [Dep() for _ in range(4)]
        pi = 0
        for tg in range(8):
            b = tg % 2
            P.dma("sync", dq[2 + b], Cg[b][:], D["CS"][:, tg * 8:(tg + 1) * 8], w=[d_Cg[b]])
            for j in range(4):
                p = pi % 4
                pi += 1
                for tl in range(8):
                    P.op("tensor", lambda e: e.matmul(ps[p][:, tl, :], Cg[b][:, tl, 0, j * 128:(j + 1) * 128],
                                                      IC[:, 0, :], start=True, stop=False),
                         r=[d_Cg[b], d_IC], w=[d_ps[p]])
                    P.op("tensor", lambda e: e.matmul(ps[p][:, tl, :], Cg[b][:, tl, 1, j * 128:(j + 1) * 128],
                                                      IC[:, 1, :], start=False, stop=True),
                         r=[d_Cg[b], d_IC], w=[d_ps[p]])
                ov = yh[:, j, :].rearrange("p (a b) -> p b a", b=64)[:, tg * 8:(tg + 1) * 8, :]
                xv = x0[:, j, :].rearrange("p (a b) -> p b a", b=64)[:, tg * 8:(tg + 1) * 8, :]
                P.op("vector", lambda e: e.tensor_tensor(out=ov, in0=ps[p][:], in1=xv, op=ALU.mult),
                     r=[d_ps[p], d_x0], w=[d_yh])
        P.dma("sync", dq[4], D["YHT"].rearrange("j p t -> p j t"), yh[:], r=[d_yh])


def phase_H(P, nc, D, dq, gq):
    with ExitStack() as es:
        QT = T(es, nc, "H_QT", [128, 4, 4096], BF16)
        KT = T(es, nc, "H_KT", [128, 4, 4608], BF16)
        VA = T(es, nc, "H_VA", [128, 36, 8, 65], BF16)
        d_in = Dep()
        P.dma("sync", dq[1], QT[:], D["QT"].rearrange("j p t -> p j t"), w=[d_in])
        P.dma("sync", dq[2], KT[:], D["KT"].rearrange("j p t -> p j t"), w=[d_in])
        P.dma("sync", dq[3], VA[:].rearrange("p s h d -> p s (h d)"), D["VA"].rearrange("s p x -> p s x"), w=[d_in])
        bt = [T(es, nc, f"H_bt{i}", [128, 8, 6, 256], BF16) for i in range(2)]
        d_bt = [Dep() for _ in range(2)]
        EX = [T(es, nc, f"H_EX{i}", [128, 6, 256], BF16) for i in range(2)]
        d_EX = [Dep() for _ in range(2)]
        PT = [T(es, nc, f"H_PT{i}", [128, 6, 256], BF16) for i in range(2)]
        d_PT = [Dep() for _ in range(2)]
        onesr = T(es, nc, "H_ones", [128, 64], BF16)
        d_ones = Dep()
        P.op("gpsimd", lambda e: e.memset(onesr[:], 1.0), w=[d_ones])
        rdb = [T(es, nc, f"H_rdb{i}", [128, 256], BF16) for i in range(2)]
        d_rdb = [Dep() for _ in range(2)]
        rdf = [T(es, nc, f"H_rdf{i}", [128, 256], F32) for i in range(2)]
        d_rdf = [Dep() for _ in range(2)]
        Bs = [T(es, nc, f"H_Bs{i}", [64, 256], F32) for i in range(2)]
        d_Bs = [Dep() for _ in range(2)]
        yTs = [T(es, nc, f"H_yT{i}", [128, 4, 256], BF16) for i in range(2)]
        d_yTs = [Dep() for _ in range(2)]
        YATv = D["YAT"].rearrange("j p t -> p j t")
        psS = [PS(es, nc, f"H_pS{i}", [128, 512], F32) for i in range(4)]
        d_pS = [Dep() for _ in range(4)]
        psO = [PS(es, nc, f"H_pO{i}", [128, 256], F32) for i in range(2)]
        d_pO = [Dep() for _ in range(2)]
        psB = [PS(es, nc, f"H_pB{i}", [64, 256], F32) for i in range(2)]
        d_pB = [Dep() for _ in range(2)]
        EX3 = EX + [T(es, nc, "H_EX2", [128, 6, 256], BF16)]
        PT3 = PT + [T(es, nc, "H_PT2", [128, 6, 256], BF16)]
        d_EX3 = d_EX + [Dep()]
        d_PT3 = d_PT + [Dep()]
        state = {"cur_v": None, "cur_bb": 0, "si": 0}
        vb_of_g = {}

        def S1(i):
            g, h = divmod(i, 8)
            v = 0 if g == 0 else (2 if g == 15 else 1)
            if h == 0 and v != state["cur_v"]:
                bb = g % 2
                P.dma("sync", dq[4 + bb], bt[bb][:], D["abias"][v].rearrange("h p a q -> p h a q"), w=[d_bt[bb]])
                for hh in range(8):
                    P.op("scalar", lambda e: e.activation(out=bt[bb][:, hh], in_=bt[bb][:, hh], func=AF.Exp),
                         w=[d_bt[bb]])
                state["cur_v"] = v
                state["cur_bb"] = bb
            cbb = state["cur_bb"]
            ch, po = h // 2, (h % 2) * 64
            pb = i % 3
            for pp in range(3):
                sp = state["si"] % 4
                state["si"] += 1
                for e2 in range(2):
                    pr = 2 * pp + e2
                    k0 = (4 * g + 2 * pr) * 64
                    P.op("tensor", lambda e: e.matmul(psS[sp][:, e2 * 256:(e2 + 1) * 256],
                                                      KT[po:po + 64, ch, k0:k0 + 128],
                                                      QT[po:po + 64, ch, g * 256:(g + 1) * 256],
                                                      start=True, stop=True),
                         r=[d_in], w=[d_pS[sp]])
                P.op("scalar", lambda e: e.activation(out=EX3[pb][:, 2 * pp:2 * pp + 2, :],
                                                      in_=psS[sp][:].rearrange("p (a q) -> p a q", q=256),
                                                      func=AF.Exp),
                     r=[d_pS[sp]], w=[d_EX3[pb]])
                P.op("vector", lambda e: e.tensor_tensor(out=PT3[pb][:, 2 * pp:2 * pp + 2, :],
                                                         in0=EX3[pb][:, 2 * pp:2 * pp + 2, :],
                                                         in1=bt[cbb][:, h, 2 * pp:2 * pp + 2, :], op=ALU.mult),
                     r=[d_EX3[pb], d_bt[cbb]], w=[d_PT3[pb]])

        def S2(i):
            g, h = divmod(i, 8)
            pb = i % 3
            o = i % 2
            for pr in range(6):
                st = 2 * g + pr
                P.op("tensor", lambda e: e.matmul(psO[o][0:65, :], VA[:, st, h, :], PT3[pb][:, pr, :],
                                                  start=(pr == 0), stop=(pr == 5)),
                     r=[d_PT3[pb], d_in], w=[d_pO[o]])
            P.op("vector", lambda e: e.reciprocal(out=rdf[o][64:65, :], in_=psO[o][64:65, :]),
                 r=[d_pO[o]], w=[d_rdf[o]])
            P.op("vector", lambda e: e.tensor_copy(out=rdb[o][64:65, :], in_=rdf[o][64:65, :]),
                 r=[d_rdf[o]], w=[d_rdb[o]])

        def S3(i):
            g, h = divmod(i, 8)
            ch, po = h // 2, (h % 2) * 64
            o = i % 2
            yb = g % 2
            P.op("tensor", lambda e: e.matmul(psB[o][:], onesr[64:65, :], rdb[o][64:65, :], start=True, stop=True),
                 r=[d_ones, d_rdb[o]], w=[d_pB[o]])
            P.op("scalar", lambda e: e.activation(out=Bs[o][:], in_=psB[o][:], func=AF.Copy),
                 r=[d_pB[o]], w=[d_Bs[o]])
            P.op("vector", lambda e: e.tensor_tensor(out=yTs[yb][po:po + 64, ch, :], in0=psO[o][0:64, :],
                                                     in1=Bs[o][:], op=ALU.mult),
                 r=[d_pO[o], d_Bs[o]], w=[d_yTs[yb]])
            if h == 7:
                P.dma("sync", dq[6 + yb], YATv[:, :, g * 256:(g + 1) * 256], yTs[yb][:], r=[d_yTs[yb]])

        pipeline(128, [S1, S2, S3])


def phase_W(P, nc, D, wsem):
    for w in wsem:
        w.nobar = True
    k = [0]

    def cv(dst, src):
        P.dma("gpsimd", wsem[k[0] % 4], dst, src)
        k[0] += 1

    for m in range(16):
        cv(D["WG"][m], wchunk_src(D["w_in"], 3072 + m * 128, 128))
    for m in range(8):
        cv(D["WBA"][m], wchunk_src(D["w_br_attn"], m * 128, 128))
        cv(D["WBH"][m], wchunk_src(D["w_br_hyena"], m * 128, 128))
    for kc in range(8):
        cv(D["WO"][:, kc, :], D["w_out"][kc * 128:(kc + 1) * 128, :])
    for f in range(NFF):
        cv(D["WFG"][f], wchunk_src(D["w_gate"], f * 128, 128))
        cv(D["WFU"][f], wchunk_src(D["w_up"], f * 128, 128))
    wdv = D["w_down"].rearrange("(f p) n -> p f n", p=128)
    for dh in range(2):
        for f0 in range(0, NFF, 11):
            cv(D["WD"][dh][:, f0:f0 + 11, :], wdv[:, f0:f0 + 11, dh * 512:(dh + 1) * 512])


def phase_I(P, nc, D, dq, gq):
    TT = 1024
    with ExitStack() as es:
        ident, d_ident = load_ident(P, nc, es, D, dq[0])
        grep, d_grep = make_grep(P, nc, es, D["g_mix"], "gm", dq[1])
        grep2, d_grep2 = make_grep(P, nc, es, D["g_ffn"], "gf", dq[2])
        gfin = T(es, nc, "I_gfin", [128, 1024], F32)
        d_gfin = Dep()
        P.dma("sync", dq[3], gfin[:], D["g_final"], w=[d_gfin])
        xt = T(es, nc, "I_xt", [128, 8, 1024], F32)
        d_xt = [Dep() for _ in range(8)]
        hT = T(es, nc, "I_hT", [128, 8, TT], BF16)
        d_hT = Dep()
        aT = T(es, nc, "I_aT", [128, NFF, TT], BF16)
        d_aT = Dep()
        mT = aT[:, 0:8, :]
        yaT = aT[:, 8:12, :]
        yhT = aT[:, 12:16, :]
        d_yy = Dep()
        d_mT = Dep()
        wout = T(es, nc, "I_wout", [128, 8, 1024], BF16)
        d_wout = Dep()
        wdn = T(es, nc, "I_wdn", [128, NFF, 512], BF16)
        d_wdn = Dep()
        wc = [T(es, nc, f"I_wc{i}", [128, 8, 128], BF16) for i in range(6)]
        d_wc = [Dep() for _ in range(6)]
        wb = [T(es, nc, f"I_wb{i}", [128, 4, 128], BF16) for i in range(4)]
        d_wb = [Dep() for _ in range(4)]
        sg = [T(es, nc, f"I_sg{i}", [128, 512], F32) for i in range(4)]
        d_sg = [Dep() for _ in range(4)]
        xn = [T(es, nc, f"I_xn{i}", [128, 1024], BF16) for i in range(3)]
        d_xn = [Dep() for _ in range(3)]
        junk = T(es, nc, "I_junk", [128, 1024], BF16)
        d_junk = Dep()
        stt = [T(es, nc, f"I_st{i}", [128, 4], F32) for i in range(3)]
        d_st = [Dep() for _ in range(3)]
        yo = [T(es, nc, f"I_yo{i}", [128, 1024], F32) for i in range(2)]
        d_yo = [Dep() for _ in range(2)]
        ps = [PS(es, nc, f"I_ps{i}", [128, 512], F32) for i in range(7)]
        d_ps = [Dep() for _ in range(7)]
        pt = PS(es, nc, "I_pt", [128, 8, 128], BF16)
        d_pt = Dep()
        for i in range(3):
            P.op("gpsimd", lambda e: e.memset(stt[i][:], EPS), w=[d_st[i]])
        P.dma("sync", dq[54], wout[:], D["WO"], w=[d_wout])
        cnt = {"wi": 0, "bi": 0, "pi": 0, "ni": 0}

        def nA(st, gr, d_gr):
            b = st % 3
            P.op("scalar", lambda e: e.activation(out=junk[:], in_=xt[:, st, :], func=AF.Square,
                                                  accum_out=stt[b][:, 0:1]),
                 r=[d_xt[st]], w=[d_junk, d_st[b]])
            P.op("scalar", lambda e: e.activation(out=stt[b][:, 1:2], in_=stt[b][:, 0:1], func=AF.Sqrt,
                                                  scale=1.0 / 1024, bias=stt[b][:, 3:4]),
                 r=[d_st[b]], w=[d_st[b]])

        def nB(st, gr, d_gr):
            b = st % 3
            P.op("vector", lambda e: e.reciprocal(out=stt[b][:, 2:3], in_=stt[b][:, 1:2]), r=[d_st[b]], w=[d_st[b]])
            if st % 2:
                P.op("scalar", lambda e: e.activation(out=xn[b][:], in_=xt[:, st, :], func=AF.Copy,
                                                      scale=stt[b][:, 2:3]),
                     r=[d_st[b], d_xt[st]], w=[d_xn[b]])
            else:
                P.op("vector", lambda e: e.tensor_scalar(out=xn[b][:], in0=xt[:, st, :], scalar1=stt[b][:, 2:3],
                                                         scalar2=None, op0=ALU.mult),
                     r=[d_st[b], d_xt[st]], w=[d_xn[b]])

        def nC(st, gr, d_gr):
            b = st % 3
            for kc in range(8):
                P.op("tensor", lambda e: e.transpose(pt[:, kc, :], xn[b][:, kc * 128:(kc + 1) * 128], ident[:]),
                     r=[d_xn[b], d_ident], w=[d_pt])
            P.op("vector", lambda e: e.tensor_tensor(out=hT[:, :, st * 128:(st + 1) * 128], in0=pt[:], in1=gr[:],
                                                     op=ALU.mult),
                 r=[d_pt, d_gr], w=[d_hT])

        def wload(buf, dbuf, sem, src):
            P.dma("sync", sem, buf[:], src, w=[dbuf])

        for tile in range(4096 // TT):
            tok0 = tile * TT
            P.dma("sync", dq[5], yaT, D["YAT"].rearrange("j p t -> p j t")[:, :, tok0:tok0 + TT], w=[d_yy, d_aT])
            P.dma("sync", dq[6], yhT, D["YHT"].rearrange("j p t -> p j t")[:, :, tok0:tok0 + TT], w=[d_yy, d_aT])

            def L0(st, tok0=tok0):
                P.dma("sync", dq[20 + st], xt[:, st, :],
                      D["xext"][256 + tok0 + st * 128:256 + tok0 + (st + 1) * 128, :], w=[d_xt[st]])
                nA(st, grep, d_grep)

            pipeline(8, [L0, lambda st: nB(st, grep, d_grep), lambda st: nC(st, grep, d_grep)])
            for m in range(8):
                k = cnt["wi"] % 6; w_ga, dga = wc[k], d_wc[k]; wload(w_ga, dga, dq[44 + k], D["WG"][m]); cnt["wi"] += 1
                k = cnt["wi"] % 6; w_gh, dgh = wc[k], d_wc[k]; wload(w_gh, dgh, dq[44 + k], D["WG"][8 + m]); cnt["wi"] += 1
                k = cnt["bi"] % 4; w_ba, dba = wb[k], d_wb[k]; wload(w_ba, dba, dq[50 + k], D["WBA"][m]); cnt["bi"] += 1
                k = cnt["bi"] % 4; w_bh, dbh = wb[k], d_wb[k]; wload(w_bh, dbh, dq[50 + k], D["WBH"][m]); cnt["bi"] += 1
                for th in range(TT // 512):
                    tsl = slice(th * 512, (th + 1) * 512)
                    pi = cnt["pi"]
                    pg = [pi % 7, (pi + 1) % 7, (pi + 2) % 7, (pi + 3) % 7]
                    cnt["pi"] += 4
                    for kc in range(8):
                        P.op("tensor", lambda e: e.matmul(ps[pg[0]][:], w_ga[:, kc, :], hT[:, kc, tsl],
                                                          start=(kc == 0), stop=(kc == 7)),
                             r=[dga, d_hT], w=[d_ps[pg[0]]])
                    for kc in range(8):
                        P.op("tensor", lambda e: e.matmul(ps[pg[1]][:], w_gh[:, kc, :], hT[:, kc, tsl],
                                                          start=(kc == 0), stop=(kc == 7)),
                             r=[dgh, d_hT], w=[d_ps[pg[1]]])
                    for kc in range(4):
                        P.op("tensor", lambda e: e.matmul(ps[pg[2]][:], w_ba[:, kc, :], yaT[:, kc, tsl],
                                                          start=(kc == 0), stop=(kc == 3)),
                             r=[dba, d_yy], w=[d_ps[pg[2]]])
                    for kc in range(4):
                        P.op("tensor", lambda e: e.matmul(ps[pg[3]][:], w_bh[:, kc, :], yhT[:, kc, tsl],
                                                          start=(kc == 0), stop=(kc == 3)),
                             r=[dbh, d_yy], w=[d_ps[pg[3]]])
                    sa = (2 * (m * 2 + th)) % 4
                    P.op("scalar", lambda e: e.activation(out=sg[sa][:], in_=ps[pg[0]][:], func=AF.Sigmoid),
                         r=[d_ps[pg[0]]], w=[d_sg[sa]])
                    P.op("scalar", lambda e: e.activation(out=sg[sa + 1][:], in_=ps[pg[1]][:], func=AF.Sigmoid),
                         r=[d_ps[pg[1]]], w=[d_sg[sa + 1]])
                    P.op("vector", lambda e: e.tensor_tensor(out=sg[sa][:], in0=ps[pg[2]][:], in1=sg[sa][:], op=ALU.mult),
                         r=[d_ps[pg[2]]], w=[d_sg[sa]])
                    P.op("vector", lambda e: e.tensor_tensor(out=sg[sa + 1][:], in0=ps[pg[3]][:], in1=sg[sa + 1][:],
                                                             op=ALU.mult),
                         r=[d_ps[pg[3]]], w=[d_sg[sa + 1]])
                    P.op("vector", lambda e: e.tensor_tensor(out=mT[:, m, tsl], in0=sg[sa][:], in1=sg[sa + 1][:],
                                                             op=ALU.add),
                         r=[d_sg[sa], d_sg[sa + 1]], w=[d_mT])

            def O1(st):
                for dh in range(2):
                    p = cnt["pi"] % 7
                    cnt["pi"] += 1
                    for kc in range(8):
                        P.op("tensor", lambda e: e.matmul(ps[p][:], mT[:, kc, st * 128:(st + 1) * 128],
                                                          wout[:, kc, dh * 512:(dh + 1) * 512],
                                                          start=(kc == 0), stop=(kc == 7)),
                             r=[d_mT, d_wout], w=[d_ps[p]])
                    P.op("vector", lambda e: e.tensor_tensor(out=xt[:, st, dh * 512:(dh + 1) * 512], in0=ps[p][:],
                                                             in1=xt[:, st, dh * 512:(dh + 1) * 512], op=ALU.add),
                         r=[d_ps[p]], w=[d_xt[st]])
                nA(st, grep2, d_grep2)

            pipeline(8, [O1, lambda st: nB(st, grep2, d_grep2), lambda st: nC(st, grep2, d_grep2)])
            for f in range(NFF):
                k = cnt["wi"] % 6; w_g, dg = wc[k], d_wc[k]; wload(w_g, dg, dq[44 + k], D["WFG"][f]); cnt["wi"] += 1
                k = cnt["wi"] % 6; w_u, dup = wc[k], d_wc[k]; wload(w_u, dup, dq[44 + k], D["WFU"][f]); cnt["wi"] += 1
                if f == 2:
                    P.dma("sync", dq[55], wdn[:], D["WD"][0], w=[d_wdn])
                for th in range(TT // 512):
                    tsl = slice(th * 512, (th + 1) * 512)
                    pi = cnt["pi"]
                    pg = [pi % 7, (pi + 1) % 7]
                    cnt["pi"] += 2
                    for kc in range(8):
                        P.op("tensor", lambda e: e.matmul(ps[pg[0]][:], w_g[:, kc, :], hT[:, kc, tsl],
                                                          start=(kc == 0), stop=(kc == 7)),
                             r=[dg, d_hT], w=[d_ps[pg[0]]])
                    for kc in range(8):
                        P.op("tensor", lambda e: e.matmul(ps[pg[1]][:], w_u[:, kc, :], hT[:, kc, tsl],
                                                          start=(kc == 0), stop=(kc == 7)),
                             r=[dup, d_hT], w=[d_ps[pg[1]]])
                    sa = (f * 2 + th) % 4
                    P.op("scalar", lambda e: e.activation(out=sg[sa][:], in_=ps[pg[0]][:], func=AF.Silu),
                         r=[d_ps[pg[0]]], w=[d_sg[sa]])
                    P.op("vector", lambda e: e.tensor_tensor(out=aT[:, f, tsl], in0=ps[pg[1]][:], in1=sg[sa][:],
                                                             op=ALU.mult),
                         r=[d_ps[pg[1]], d_sg[sa]], w=[d_aT, d_mT, d_yy])
            for dh in range(2):
                if dh == 1:
                    P.dma("sync", dq[55], wdn[:], D["WD"][1], w=[d_wdn])
                for st in range(8):
                    p = cnt["pi"] % 7
                    cnt["pi"] += 1
                    for f in range(NFF):
                        P.op("tensor", lambda e: e.matmul(ps[p][:], aT[:, f, st * 128:(st + 1) * 128], wdn[:, f, :],
                                                          start=(f == 0), stop=(f == NFF - 1)),
                             r=[d_aT, d_mT, d_yy, d_wdn], w=[d_ps[p]])
                    P.op("vector", lambda e: e.tensor_tensor(out=xt[:, st, dh * 512:(dh + 1) * 512], in0=ps[p][:],
                                                             in1=xt[:, st, dh * 512:(dh + 1) * 512], op=ALU.add),
                         r=[d_ps[p]], w=[d_xt[st]])
                    if dh == 1:
                        b = st % 3
                        yb = st % 2
                        P.op("scalar", lambda e: e.activation(out=junk[:], in_=xt[:, st, :], func=AF.Square,
                                                              accum_out=stt[b][:, 0:1]),
                             r=[d_xt[st]], w=[d_junk, d_st[b]])
                        P.op("scalar", lambda e: e.activation(out=stt[b][:, 1:2], in_=stt[b][:, 0:1], func=AF.Sqrt,
                                                              scale=1.0 / 1024, bias=stt[b][:, 3:4]),
                             r=[d_st[b]], w=[d_st[b]])
                        P.op("vector", lambda e: e.reciprocal(out=stt[b][:, 2:3], in_=stt[b][:, 1:2]),
                             r=[d_st[b]], w=[d_st[b]])
                        P.op("vector", lambda e: e.scalar_tensor_tensor(out=yo[yb][:], in0=xt[:, st, :],
                                                                        scalar=stt[b][:, 2:3], in1=gfin[:],
                                                                        op0=ALU.mult, op1=ALU.mult),
                             r=[d_st[b], d_xt[st], d_gfin], w=[d_yo[yb]])
                        P.dma("sync", dq[16 + yb], D["y"][tok0 + st * 128:tok0 + (st + 1) * 128, :], yo[yb][:],
                              r=[d_yo[yb]])


def _const_tables():
    N = 8192
    t1 = np.arange(128)[:, None, None]
    t2 = np.arange(64)[None, :, None]
    k1 = np.arange(128)[None, None, :]
    th = 2 * np.pi * (((64 * t1 + t2) * k1) % N) / N
    E = np.stack([np.cos(th), -np.sin(th)], axis=2)
    a = np.arange(64)
    th2 = 2 * np.pi * np.outer(a, a) / 64
    F2 = np.zeros((128, 3, 128))
    for kp in range(2):
        sl = slice(kp * 64, (kp + 1) * 64)
        F2[sl, 0, sl] = np.cos(th2)
        F2[sl, 1, sl] = -np.sin(th2)
        F2[sl, 2, sl] = np.sin(th2)
    IT = np.zeros((128, 33, 3, 128))
    k2 = np.arange(64)[:, None]
    tt = np.arange(64)[None, :]
    for kh in range(33):
        for kp in range(2):
            k1v = kh + 33 * kp
            ph = 2 * np.pi * (tt * k2 / 64.0 + tt * k1v / 8192.0)
            sl = slice(kp * 64, (kp + 1) * 64)
            IT[sl, kh, 0, sl] = np.cos(ph)
            IT[sl, kh, 1, sl] = np.sin(ph)
            IT[sl, kh, 2, sl] = -np.sin(ph)
    kk = np.arange(66)[:, None]
    t1v = np.arange(64)[None, :]
    th3 = 2 * np.pi * kk * t1v / 128.0
    wk = np.full((66, 1), 2.0)
    wk[0] = 1.0
    wk[64] = 1.0
    wk[65] = 0.0
    IC = np.stack([wk * np.cos(th3) / N, -wk * np.sin(th3) / N], axis=1)
    return (E.astype(BF), F2.astype(BF), IT.astype(BF), IC.astype(BF))


def _filter_tables(L, ksegs):
    f32 = np.float32
    tlin = np.linspace(0.0, 1.0, L, dtype=f32)
    omega = (2.0 * math.pi * np.arange(L, dtype=f32) / L).astype(f32)
    fbv = np.linspace(1e-4, 15, 16, dtype=f32)
    n = np.arange(8192)
    d = np.where(n < 4096, n, n - 8192)
    zs = np.zeros((4, 128, 4096), f32)
    maskT = np.zeros((4, 128, 8192), f32)
    tpos = np.zeros((4, 8192), f32)
    vbias = np.full((4, 8192), NEG, f32)
    for s, k in enumerate(ksegs):
        if k is None:
            idx = np.zeros(8192, np.int64)
            valid = np.zeros(8192, bool)
            Dl = np.zeros(8192, np.int64)
        else:
            Dl = k * 4096 + d
            valid = (np.abs(Dl) <= L - 1) & (n != 4096)
            idx = np.where(valid, np.abs(Dl), 0)
        ang = (omega[idx][:, None] * fbv[None, :]).astype(f32)
        z = np.concatenate([tlin[idx][:, None], np.cos(ang), -np.sin(ang)], axis=1)
        zs[s, 0:33, :] = z[0:4096].T
        zs[s, 64:97, :] = z[4096:8192].T
        tpos[s] = tlin[idx]
        vbias[s] = np.where(valid, 0.0, NEG)
        maskT[s, 0:64, :] = (valid & (Dl >= 0)).astype(f32)[None, :]
        maskT[s, 64:128, :] = (valid & (Dl <= 0)).astype(f32)[None, :]
    def lay(a):
        a = a.reshape(4, 128, 64)
        return np.ascontiguousarray(np.transpose(a, (1, 0, 2)))
    return zs.astype(BF), maskT.astype(BF), lay(tpos), lay(vbias)


def _attn_bias(rpb, R0, rows):
    out = np.full((3, 8, 128, 6, 256), NEG, np.float32)
    qc = np.arange(64)
    kc = np.arange(64)
    cs = np.clip(qc - 8, 0, 48)
    colin = (kc[None, :] >= cs[:, None]) & (kc[None, :] < cs[:, None] + 16)
    dc = np.clip(kc[None, :] - qc[:, None], -15, 15) + 15
    for v, g in enumerate((0, 1, 15)):
        for rl in range(4):
            r = 4 * g + rl
            rg = R0 + r
            ws = int(np.clip(rg - 4, 0, rows - 8)) - R0 + 4
            for pr in range(6):
                for er in range(2):
                    e = 4 * g + 2 * pr + er
                    if not (0 <= e - ws < 8):
                        continue
                    drr = e - 4 - r + 7
                    blk = rpb[:, drr, :][:, dc]
                    blk = np.where(colin[None], blk, NEG)
                    out[v, :, er * 64:(er + 1) * 64, pr, rl * 64:(rl + 1) * 64] = np.transpose(blk, (0, 2, 1))
    return out.astype(BF)


def _col128(v, n):
    return np.ascontiguousarray(np.asarray(v, np.float32).reshape(n, 128).T)


def prepare_inputs(inputs, cores=range(8)):
    f32 = np.float32
    I = {k: np.asarray(v) for k, v in inputs.items()}
    E, F2, IT, IC = _const_tables()
    max_decay = math.log(1e-2) / 0.3
    min_decay = math.log(1e-2) / 1.5
    deltas = np.abs(np.linspace(min_decay, max_decay, 512, dtype=f32))
    common = {
        "w_in": I["w_in"][0], "w_br_attn": I["w_br_attn"][0], "w_br_hyena": I["w_br_hyena"][0],
        "w_out": I["w_out"][0], "w_gate": I["w_gate"][0], "w_up": I["w_up"][0], "w_down": I["w_down"][0],
        "g_mix": _col128(I["norm_mix"][0], 8), "g_ffn": _col128(I["norm_ffn"][0], 8),
        "g_final": np.ascontiguousarray(np.broadcast_to(I["norm_final"][None, :], (128, 1024))).astype(f32),
        "conv_w": np.ascontiguousarray(np.transpose(I["conv_w"][0].reshape(3, 12, 128), (2, 1, 0))).astype(f32),
        "conv_b": _col128(I["conv_b"][0], 12),
        "fw1": I["filt_w1"][0], "fw2": I["filt_w2"][0], "fw3": I["filt_w3"][0], "fw4": I["filt_w4"][0],
        "fb": np.ascontiguousarray(np.tile(np.stack([I["filt_b1"][0], I["filt_b2"][0], I["filt_b3"][0]], axis=1),
                                           (2, 1))).astype(f32),
        "ffr": np.ascontiguousarray(np.tile(I["filt_freq"][0][:, None], (2, 1))).astype(f32),
        "e0": np.eye(128, 1, dtype=f32),
        "hyd": np.ascontiguousarray(np.broadcast_to(I["hyena_d"][0][None, :], (128, 512))).astype(f32),
        "ident": np.eye(128, dtype=f32).astype(BF),
        "Etab": E, "F2tab": F2, "ITtab": IT, "ICtab": IC,
        "negdelta": np.ascontiguousarray(np.broadcast_to(-deltas[None, :], (128, 512))).astype(f32),
    }
    rpb = I["rpb"][0].astype(f32)
    xp = I["x_prompt"]
    xs = I["x_sample"][0]
    tab_prompt = _filter_tables(4096, [0, None, None, None])
    ab_prompt = _attn_bias(rpb, 0, 64)
    maps = []
    for c in cores:
        m = dict(common)
        xext = np.zeros((4608, 1024), f32)
        xctx = np.zeros((4, 4224, 1024), f32)
        if c < 4:
            xext[256:4352] = xp[c]
            xctx[0, :4096] = xp[c]
            zs, maskT, tpos, vbias = tab_prompt
            ab = ab_prompt
        else:
            j = c - 4
            L = 16384
            lo, hi = j * 4096 - 256, (j + 1) * 4096 + 256
            a, b = max(lo, 0), min(hi, L)
            xext[a - lo:b - lo] = xs[a:b]
            blks = [j] + [bb for bb in range(4) if bb != j]
            for i, bb in enumerate(blks):
                xctx[i, :4096] = xs[bb * 4096:(bb + 1) * 4096]
                if bb * 4096 - 1 >= 0:
                    xctx[i, 4096] = xs[bb * 4096 - 1]
                if (bb + 1) * 4096 < L:
                    xctx[i, 4097] = xs[(bb + 1) * 4096]
            zs, maskT, tpos, vbias = _filter_tables(L, [j - bb for bb in blks])
            ab = _attn_bias(rpb, j * 64, 256)
        m.update({"xext": xext, "xctx": xctx, "zs": zs, "maskT": maskT, "tpos": tpos, "vbias": vbias, "abias": ab})
        maps.append(m)
    return maps


_NC_CACHE = {}


def kernel(**inputs):
    if "nc" not in _NC_CACHE:
        _NC_CACHE["nc"] = build_program()
    nc = _NC_CACHE["nc"]
    maps = prepare_inputs(inputs)
    res = run_bass_kernel_spmd(nc, maps, core_ids=list(range(8)))
    ys = [np.asarray(res.results[i]["y"], dtype=np.float32) for i in range(8)]
    y_prompt = np.stack(ys[0:4], axis=0)
    y_sample = np.concatenate(ys[4:8], axis=0)[None]
    return (y_prompt, y_sample)
```

```python
import math
import numpy as np
import ml_dtypes
from contextlib import ExitStack
import concourse.bass as bass
import concourse.mybir as mybir
from concourse.bass_utils import run_bass_kernel_spmd

F32 = mybir.dt.float32
BF16 = mybir.dt.bfloat16
AF = mybir.ActivationFunctionType
ALU = mybir.AluOpType
BF = ml_dtypes.bfloat16

D_MODEL = 1024
D_IN = 5120
D_FF = 2816
NFF = 22
EPS = 1e-6
NEG = -30000.0
PI = float(np.pi)


class Dep:
    __slots__ = ("w", "r")

    def __init__(self):
        self.w = {}
        self.r = {}


class DSem:
    def __init__(self, sem):
        self.sem = sem
        self.n = 0
        self.nobar = False


class Prog:
    ENGS = ["sync", "scalar", "vector", "gpsimd", "tensor"]

    def __init__(self, nc, es):
        self.nc = nc
        self.h = {"sync": nc.sync, "scalar": nc.scalar, "vector": nc.vector,
                  "gpsimd": nc.gpsimd, "tensor": nc.tensor}
        self.sem = {e: es.enter_context(nc.semaphore("s_" + e)) for e in self.ENGS}
        self.cnt = {e: 0 for e in self.ENGS}
        self.known = {e: {} for e in self.ENGS}
        self.dsems = []
        self.es = es
        self.rr = 0

    def dsem(self, name):
        s = DSem(self.es.enter_context(self.nc.semaphore(name)))
        self.dsems.append(s)
        return s

    def _need(self, eng, items):
        k = self.known[eng]
        for it in items:
            if it is None:
                continue
            sem, val, e2 = it
            if e2 == eng and eng == "tensor":
                continue
            if k.get(id(sem), 0) >= val:
                continue
            k[id(sem)] = val
            self.h[eng].wait_ge(sem, val)

    def _items(self, reads, writes):
        items = []
        for d in reads:
            items.extend(d.w.values())
        for d in writes:
            items.extend(d.w.values())
            items.extend(d.r.values())
        return items

    def _upd(self, tok, reads, writes):
        key = id(tok[0])
        for d in reads:
            d.r[key] = tok
        for d in writes:
            d.w[key] = tok
            d.r = {}

    def op(self, eng, fn, r=(), w=()):
        self._need(eng, self._items(r, w))
        ins = fn(self.h[eng])
        self.cnt[eng] += 1
        ins.then_inc(self.sem[eng], 1)
        tok = (self.sem[eng], self.cnt[eng], eng)
        self._upd(tok, r, w)
        return tok

    def dma(self, eng, ds, out, in_, r=(), w=()):
        self._need(eng, self._items(r, w))
        ins = self.h[eng].dma_start(out=out, in_=in_)
        ds.n += 16
        ins.then_inc(ds.sem, 16)
        tok = (ds.sem, ds.n, "dma")
        self._upd(tok, r, w)
        return tok

    def barrier(self):
        toks = [(self.sem[e], self.cnt[e], e) for e in self.ENGS if self.cnt[e] > 0]
        toks += [(d.sem, d.n, "dma") for d in self.dsems if d.n > 0 and not d.nobar]
        for e in self.ENGS:
            k = self.known[e]
            for sem, val, e2 in toks:
                if e2 == e:
                    continue
                if k.get(id(sem), 0) >= val:
                    continue
                k[id(sem)] = val
                self.h[e].wait_ge(sem, val)

    def alt(self, a="vector", b="gpsimd"):
        self.rr += 1
        return a if self.rr % 2 else b


_UID = [0]


def pipeline(n_iter, stages):
    ns = len(stages)
    for t in range(n_iter + ns - 1):
        for k, f in enumerate(stages):
            i = t - k
            if 0 <= i < n_iter:
                f(i)


def T(es, nc, name, shape, dt):
    _UID[0] += 1
    return es.enter_context(nc.sbuf_tensor(f"sb{_UID[0]}_{name}", shape, dt))


def PS(es, nc, name, shape, dt):
    _UID[0] += 1
    return es.enter_context(nc.psum_tensor(f"ps{_UID[0]}_{name}", shape, dt))


def rms_transpose(P, nc, es, xrows, nsub, hT, d_hT, grep, d_grep, ident, d_ident, pfx, dq):
    NB = 3
    xs = [T(es, nc, f"{pfx}xs{i}", [128, 1024], F32) for i in range(NB)]
    d_xs = [Dep() for _ in range(NB)]
    xn = [T(es, nc, f"{pfx}xn{i}", [128, 1024], BF16) for i in range(NB)]
    d_xn = [Dep() for _ in range(NB)]
    junk = T(es, nc, f"{pfx}junk", [128, 1024], BF16)
    d_junk = Dep()
    stt = [T(es, nc, f"{pfx}st{i}", [128, 4], F32) for i in range(NB)]
    d_st = [Dep() for _ in range(NB)]
    pt = [PS(es, nc, f"{pfx}pt{i}", [128, 8, 128], BF16) for i in range(2)]
    d_pt = [Dep() for _ in range(2)]
    for i in range(NB):
        P.op("gpsimd", lambda e: e.memset(stt[i][:], EPS), w=[d_st[i]])

    def S1(st):
        b = st % NB
        P.dma("sync", dq[b], xs[b][:], xrows[st * 128:(st + 1) * 128, :], w=[d_xs[b]])
        P.op("scalar", lambda e: e.activation(out=junk[:], in_=xs[b][:], func=AF.Square,
                                              accum_out=stt[b][:, 0:1]),
             r=[d_xs[b]], w=[d_junk, d_st[b]])
        P.op("scalar", lambda e: e.activation(out=stt[b][:, 1:2], in_=stt[b][:, 0:1], func=AF.Sqrt,
                                              scale=1.0 / 1024, bias=stt[b][:, 3:4]),
             r=[d_st[b]], w=[d_st[b]])

    def S2(st):
        b = st % NB
        P.op("vector", lambda e: e.reciprocal(out=stt[b][:, 2:3], in_=stt[b][:, 1:2]),
             r=[d_st[b]], w=[d_st[b]])
        if st % 2:
            P.op("scalar", lambda e: e.activation(out=xn[b][:], in_=xs[b][:], func=AF.Copy, scale=stt[b][:, 2:3]),
                 r=[d_st[b], d_xs[b]], w=[d_xn[b]])
        else:
            P.op("vector", lambda e: e.tensor_scalar(out=xn[b][:], in0=xs[b][:], scalar1=stt[b][:, 2:3],
                                                     scalar2=None, op0=ALU.mult),
                 r=[d_st[b], d_xs[b]], w=[d_xn[b]])

    def S3(st):
        b = st % NB
        p = st % 2
        for kc in range(8):
            P.op("tensor", lambda e: e.transpose(pt[p][:, kc, :], xn[b][:, kc * 128:(kc + 1) * 128], ident[:]),
                 r=[d_xn[b], d_ident], w=[d_pt[p]])
        P.op("vector", lambda e: e.tensor_tensor(out=hT[:, :, st * 128:(st + 1) * 128], in0=pt[p][:],
                                                 in1=grep[:], op=ALU.mult),
             r=[d_pt[p], d_grep], w=[d_hT])

    pipeline(nsub, [S1, S2, S3])


def make_grep(P, nc, es, gcol, name, dq):
    g = T(es, nc, name + "g", [128, 8], F32)
    d_g = Dep()
    grep = T(es, nc, name + "rep", [128, 8, 128], F32)
    d_grep = Dep()
    P.dma("sync", dq, g[:], gcol, w=[d_g])
    P.op("gpsimd", lambda e: e.memset(grep[:], 1.0), w=[d_grep])
    for kc in range(8):
        P.op("vector", lambda e: e.tensor_scalar(out=grep[:, kc, :], in0=grep[:, kc, :],
                                                 scalar1=g[:, kc:kc + 1], scalar2=None, op0=ALU.mult),
             r=[d_g], w=[d_grep])
    return grep, d_grep


def load_ident(P, nc, es, D, dq):
    ident = T(es, nc, "ident", [128, 128], BF16)
    d_ident = Dep()
    P.dma("sync", dq, ident[:], D["ident"], w=[d_ident])
    return ident, d_ident


def eps_init(P, stt_list):
    pass


def build_program(dev=False, phases="ABCDEFGHI"):
    nc = bass.Bass("TRN2", target_bir_lowering=False)
    D = {}

    def inp(name, shape, dt=F32):
        D[name] = nc.dram_tensor(name, list(shape), dt, kind="ExternalInput").ap()

    def scr(name, shape, dt=BF16):
        if dev:
            D[name] = nc.dram_tensor(name, list(shape), dt, kind="ExternalOutput").ap()
        else:
            D[name] = nc.dram_tensor(name, list(shape), dt).ap()

    inp("xext", [4608, 1024])
    inp("xctx", [4, 4224, 1024])
    inp("w_in", [1024, D_IN])
    inp("w_br_attn", [512, 1024])
    inp("w_br_hyena", [512, 1024])
    inp("w_out", [1024, 1024])
    inp("w_gate", [1024, D_FF])
    inp("w_up", [1024, D_FF])
    inp("w_down", [D_FF, 1024])
    inp("g_mix", [128, 8])
    inp("g_ffn", [128, 8])
    inp("g_final", [128, 1024])
    inp("conv_w", [128, 12, 3])
    inp("conv_b", [128, 12])
    inp("fw1", [33, 64])
    inp("fw2", [64, 64])
    inp("fw3", [64, 64])
    inp("fw4", [64, 1024])
    inp("fb", [128, 3])
    inp("ffr", [128, 1])
    inp("hyd", [128, 512])
    inp("ident", [128, 128], BF16)
    inp("Etab", [128, 64, 2, 128], BF16)
    inp("F2tab", [128, 3, 128], BF16)
    inp("ITtab", [128, 33, 3, 128], BF16)
    inp("ICtab", [66, 2, 64], BF16)
    inp("zs", [4, 128, 4096], BF16)
    inp("maskT", [4, 128, 8192], BF16)
    inp("tpos", [128, 4, 64])
    inp("vbias", [128, 4, 64])
    inp("e0", [128, 1])
    inp("negdelta", [128, 512])
    inp("abias", [3, 8, 128, 6, 256], BF16)

    scr("QT", [4, 128, 4096])
    scr("KT", [4, 128, 4608])
    scr("VA", [36, 128, 8 * 65])
    scr("X0T", [4, 128, 4096])
    scr("DU", [4, 64, 4, 64, 128])
    scr("AU", [4, 64, 66, 2, 512])
    scr("AG", [4, 64, 66, 2, 512])
    scr("CS", [66, 64, 2, 512])
    scr("YHT", [4, 128, 4096])
    scr("YAT", [4, 128, 4096])
    for nm, shp in (("WG", [16, 128, 8, 128]), ("WBA", [8, 128, 4, 128]), ("WBH", [8, 128, 4, 128]),
                    ("WO", [128, 8, 1024]), ("WFG", [22, 128, 8, 128]), ("WFU", [22, 128, 8, 128]),
                    ("WD", [2, 128, 22, 512])):
        D[nm] = nc.dram_tensor(nm, shp, BF16).ap()
    D["y"] = nc.dram_tensor("y", [4096, 1024], F32, kind="ExternalOutput").ap()

    with ExitStack() as es0:
        P = Prog(nc, es0)
        dq = [P.dsem(f"dq{i}") for i in range(64)]
        gq = [P.dsem(f"gq{i}") for i in range(16)]
        wsem = [P.dsem(f"wq{i}") for i in range(4)]
        if "A" in phases:
            phase_A(P, nc, D, dq, gq)
            P.barrier()
        if "B" in phases:
            phase_B(P, nc, D, dq, gq)
            P.barrier()
        if "C" in phases:
            phase_C(P, nc, D, dq, gq, (lambda: phase_W(P, nc, D, wsem)) if "I" in phases else None)
        elif "I" in phases:
            phase_W(P, nc, D, wsem)
            P.barrier()
        if "D" in phases:
            phase_D(P, nc, D, dq, gq)
            P.barrier()
        if "F" in phases:
            phase_F(P, nc, D, dq, gq)
            P.barrier()
        if "G" in phases:
            phase_G(P, nc, D, dq, gq)
            P.barrier()
        if "H" in phases:
            phase_H(P, nc, D, dq, gq)
            P.barrier()
        if "I" in phases:
            for w in wsem:
                w.nobar = False
            P.barrier()
            phase_I(P, nc, D, dq, gq)
            P.barrier()
    return nc


def wchunk_src(w, col0, ncols, nk=8):
    return w.rearrange("(kc p) n -> p kc n", p=128)[:, :, col0:col0 + ncols]


def phase_A(P, nc, D, dq, gq):
    with ExitStack() as es:
        ident, d_ident = load_ident(P, nc, es, D, dq[0])
        grep, d_grep = make_grep(P, nc, es, D["g_mix"], "gm", dq[1])
        hT = T(es, nc, "A_hT", [128, 8, 4608], BF16)
        d_hT = Dep()
        with ExitStack() as es1:
            rms_transpose(P, nc, es1, D["xext"], 36, hT, d_hT, grep, d_grep, ident, d_ident, "A_", dq[40:43])
        P.barrier()
        wq = [T(es, nc, f"A_wq{i}", [128, 8, 128], BF16) for i in range(2)]
        d_wq = [Dep() for _ in range(2)]
        stg = [T(es, nc, f"A_stg{i}", [128, 4608], BF16) for i in range(2)]
        d_stg = [Dep() for _ in range(2)]
        ps = [PS(es, nc, f"A_ps{i}", [128, 512], F32) for i in range(4)]
        d_ps = [Dep() for _ in range(4)]
        pi = 0
        for ci in range(8):
            b = ci % 2
            isq = ci < 4
            P.dma("gpsimd", gq[4 + b], wq[b][:], wchunk_src(D["w_in"], ci * 128, 128), w=[d_wq[b]])
            t0, ntile = (256, 8) if isq else (0, 9)
            for tt in range(ntile):
                p = pi % 4
                pi += 1
                for kc in range(8):
                    P.op("tensor", lambda e: e.matmul(ps[p][:], wq[b][:, kc, :],
                                                      hT[:, kc, t0 + tt * 512:t0 + (tt + 1) * 512],
                                                      start=(kc == 0), stop=(kc == 7)),
                         r=[d_wq[b], d_hT], w=[d_ps[p]])
                if tt % 2 == 0:
                    P.op("scalar", lambda e: e.activation(out=stg[b][:, tt * 512:(tt + 1) * 512], in_=ps[p][:],
                                                          func=AF.Copy, scale=(0.125 if isq else 1.0)),
                         r=[d_ps[p]], w=[d_stg[b]])
                else:
                    P.op("vector", lambda e: e.tensor_scalar(out=stg[b][:, tt * 512:(tt + 1) * 512], in0=ps[p][:],
                                                             scalar1=(0.125 if isq else 1.0), scalar2=None,
                                                             op0=ALU.mult),
                         r=[d_ps[p]], w=[d_stg[b]])
            if isq:
                P.dma("sync", dq[6 + b], D["QT"][ci], stg[b][:, 0:4096], r=[d_stg[b]])
            else:
                P.dma("sync", dq[6 + b], D["KT"][ci - 4], stg[b][:, 0:4608], r=[d_stg[b]])
        wv = T(es, nc, "A_wv", [128, 8, 512], BF16)
        d_wv = Dep()
        P.dma("gpsimd", gq[8], wv[:], wchunk_src(D["w_in"], 1024, 512), w=[d_wv])
        vst = [T(es, nc, f"A_vst{i}", [128, 8, 65], BF16) for i in range(2)]
        d_vst = [Dep() for _ in range(2)]
        for i in range(2):
            P.op("gpsimd", lambda e: e.memset(vst[i][:], 1.0), w=[d_vst[i]])
        for st in range(36):
            p = pi % 4
            pi += 1
            b = st % 2
            for kc in range(8):
                P.op("tensor", lambda e: e.matmul(ps[p][:], hT[:, kc, st * 128:(st + 1) * 128], wv[:, kc, :],
                                                  start=(kc == 0), stop=(kc == 7)),
                     r=[d_wv, d_hT], w=[d_ps[p]])
            P.op("scalar" if st % 2 else "vector",
                 (lambda e: e.activation(out=vst[b][:, :, 0:64], in_=ps[p][:].rearrange("p (h d) -> p h d", d=64),
                                         func=AF.Copy)) if st % 2 else
                 (lambda e: e.tensor_copy(out=vst[b][:, :, 0:64], in_=ps[p][:].rearrange("p (h d) -> p h d", d=64))),
                 r=[d_ps[p]], w=[d_vst[b]])
            P.dma("sync", dq[9 + b], D["VA"][st], vst[b][:].rearrange("p h d -> p (h d)"), r=[d_vst[b]])


def phase_B(P, nc, D, dq, gq):
    with ExitStack() as es:
        ident, d_ident = load_ident(P, nc, es, D, dq[0])
        grep, d_grep = make_grep(P, nc, es, D["g_mix"], "gm", dq[1])
        cw = T(es, nc, "B_cw", [128, 12, 3], F32)
        cb = T(es, nc, "B_cb", [128, 12], F32)
        d_c = Dep()
        P.dma("sync", dq[2], cw[:], D["conv_w"], w=[d_c])
        P.dma("sync", dq[3], cb[:], D["conv_b"], w=[d_c])
        hT = T(es, nc, "B_hT", [128, 8, 4224], BF16)
        d_hT = Dep()
        hy = [T(es, nc, f"B_hy{i}", [128, 4224], F32) for i in range(2)]
        d_hy = [Dep() for _ in range(2)]
        cx = [T(es, nc, f"B_cx{i}", [128, 4096], BF16) for i in range(3)]
        d_cx = [Dep() for _ in range(3)]
        uT = [T(es, nc, f"B_uT{i}", [128, 4096], BF16) for i in range(2)]
        d_uT = [Dep() for _ in range(2)]
        du = T(es, nc, "B_du", [64, 64, 128], BF16)
        d_du = Dep()
        wq = [T(es, nc, f"B_wq{i}", [128, 8, 128], BF16) for i in range(2)]
        d_wq = [Dep() for _ in range(2)]
        ps = [PS(es, nc, f"B_ps{i}", [128, 512], F32) for i in range(4)]
        d_ps = [Dep() for _ in range(4)]
        ptr = [PS(es, nc, f"B_ptr{i}", [64, 8, 128], BF16) for i in range(2)]
        d_ptr = [Dep() for _ in range(2)]
        st8 = {"pi": 0, "wi": 0}
        for blk in range(4):
            with ExitStack() as es1:
                P.barrier()
                rms_transpose(P, nc, es1, D["xctx"][blk], 33, hT, d_hT, grep, d_grep, ident, d_ident,
                              f"B{blk}_", dq[40:43])
            P.barrier()
            items = []
            for j in range(4):
                items.append((1, 2048 + j * 128, 4 + j, j))
                items.append((2, 2560 + j * 128, 8 + j, j))
                if blk == 0:
                    items.append((0, 1536 + j * 128, j, j))
            base = st8["wi"]

            def S1(c, items=items, base=base):
                role, col0, cidx, j = items[c]
                b = (base + c) % 2
                k3 = (base + c) % 3
                P.dma("gpsimd", gq[b], wq[b][:], wchunk_src(D["w_in"], col0, 128), w=[d_wq[b]])
                for tt in range(9):
                    p = st8["pi"] % 4
                    st8["pi"] += 1
                    n = 512 if tt < 8 else 128
                    for kc in range(8):
                        P.op("tensor", lambda e: e.matmul(ps[p][:, 0:n], wq[b][:, kc, :],
                                                          hT[:, kc, tt * 512:tt * 512 + n],
                                                          start=(kc == 0), stop=(kc == 7)),
                             r=[d_wq[b], d_hT], w=[d_ps[p]])
                    P.op("scalar", lambda e: e.activation(out=hy[b][:, tt * 512:tt * 512 + n], in_=ps[p][:, 0:n],
                                                          func=AF.Copy),
                         r=[d_ps[p]], w=[d_hy[b]])
                    if tt < 8:
                        P.op("scalar", lambda e: e.activation(out=cx[k3][:, tt * 512:(tt + 1) * 512], in_=ps[p][:],
                                                              func=AF.Identity, scale=cw[:, cidx, 1:2],
                                                              bias=cb[:, cidx:cidx + 1]),
                             r=[d_ps[p], d_c], w=[d_cx[k3]])

            def S2(c, items=items, base=base):
                role, col0, cidx, j = items[c]
                b = (base + c) % 2
                k3 = (base + c) % 3
                cc = cx[k3]
                dc = d_cx[k3]
                h = hy[b]
                P.op("vector", lambda e: e.scalar_tensor_tensor(out=cc[:, 1:4096], in0=h[:, 0:4095],
                                                                scalar=cw[:, cidx, 0:1], in1=cc[:, 1:4096],
                                                                op0=ALU.mult, op1=ALU.add),
                     r=[d_hy[b], d_c], w=[dc])
                P.op("vector", lambda e: e.scalar_tensor_tensor(out=cc[:, 0:4095], in0=h[:, 1:4096],
                                                                scalar=cw[:, cidx, 2:3], in1=cc[:, 0:4095],
                                                                op0=ALU.mult, op1=ALU.add),
                     r=[d_hy[b], d_c], w=[dc])
                P.op("vector", lambda e: e.scalar_tensor_tensor(out=cc[:, 0:1], in0=h[:, 4096:4097],
                                                                scalar=cw[:, cidx, 0:1], in1=cc[:, 0:1],
                                                                op0=ALU.mult, op1=ALU.add),
                     r=[d_hy[b], d_c], w=[dc])
                P.op("vector", lambda e: e.scalar_tensor_tensor(out=cc[:, 4095:4096], in0=h[:, 4097:4098],
                                                                scalar=cw[:, cidx, 2:3], in1=cc[:, 4095:4096],
                                                                op0=ALU.mult, op1=ALU.add),
                     r=[d_hy[b], d_c], w=[dc])
                if role == 2:
                    kx = (base + c - 1) % 3
                    P.op("vector", lambda e: e.tensor_tensor(out=uT[j % 2][:], in0=cx[kx][:], in1=cc[:], op=ALU.mult),
                         r=[d_cx[kx], dc], w=[d_uT[j % 2]])

            def S3(c, items=items, base=base, blk=blk):
                role, col0, cidx, j = items[c]
                k3 = (base + c) % 3
                if role == 2:
                    uv = uT[j % 2][:].rearrange("p (a b) -> p a b", b=64)
                    for g8 in range(8):
                        q = g8 % 2
                        for tl in range(8):
                            t2 = g8 * 8 + tl
                            P.op("tensor", lambda e: e.transpose(ptr[q][:, tl, :], uv[:, :, t2], ident[:]),
                                 r=[d_uT[j % 2], d_ident], w=[d_ptr[q]])
                        P.op("scalar" if g8 % 2 else "vector",
                             (lambda e: e.activation(out=du[:, g8 * 8:(g8 + 1) * 8, :], in_=ptr[q][:], func=AF.Copy))
                             if g8 % 2 else
                             (lambda e: e.tensor_copy(out=du[:, g8 * 8:(g8 + 1) * 8, :], in_=ptr[q][:])),
                             r=[d_ptr[q]], w=[d_du])
                    P.dma("sync", dq[8], D["DU"][blk, :, j], du[:], r=[d_du])
                if role == 0:
                    P.dma("sync", dq[9], D["X0T"][j], cx[k3][:], r=[d_cx[k3]])

            pipeline(len(items), [S1, S2, S3])
            st8["wi"] += len(items)


def fft_stage1(P, nc, es, K, rhs_of, d_src, Et, d_Et, Ascr, dq, pfx, nps=4):
    ps = [PS(es, nc, f"{pfx}s1p{i}", [128, 512], F32) for i in range(nps)]
    d_ps = [Dep() for _ in range(nps)]
    ast = [T(es, nc, f"{pfx}ast{i}", [66, 2, 512], BF16) for i in range(3)]
    d_ast = [Dep() for _ in range(3)]
    for t2 in range(64):
        a = t2 % 3
        for ri in range(2):
            p = (t2 * 2 + ri) % nps
            P.op("tensor", lambda e: e.matmul(ps[p][0:66, :], Et[0:K, t2, ri, 0:66], rhs_of(t2), start=True, stop=True),
                 r=[d_Et, d_src], w=[d_ps[p]])
            if ri == 0:
                P.op("scalar", lambda e: e.activation(out=ast[a][:, ri, :], in_=ps[p][0:66, :], func=AF.Copy),
                     r=[d_ps[p]], w=[d_ast[a]])
            else:
                P.op("vector", lambda e: e.tensor_copy(out=ast[a][:, ri, :], in_=ps[p][0:66, :]),
                     r=[d_ps[p]], w=[d_ast[a]])
        P.dma("sync", dq[a], Ascr[t2], ast[a][:], r=[d_ast[a]])


def sin_group(P, pm, d_pm, tmp, d_tmp, t1, d_t1, fr, frb, d_f, out, d_out):
    P.op("scalar", lambda e: e.activation(out=tmp[:], in_=pm[:], func=AF.Identity, scale=fr, bias=frb),
         r=[d_pm, d_f], w=[d_tmp])
    P.op("vector", lambda e: e.tensor_scalar(out=t1[:], in0=tmp[:], scalar1=-1.0, scalar2=PI,
                                             op0=ALU.mult, op1=ALU.add),
         r=[d_tmp], w=[d_t1])
    P.op("vector", lambda e: e.tensor_tensor(out=tmp[:], in0=tmp[:], in1=t1[:], op=ALU.min),
         r=[d_t1], w=[d_tmp])
    P.op("vector", lambda e: e.scalar_tensor_tensor(out=tmp[:], in0=t1[:], scalar=-2 * PI, in1=tmp[:],
                                                    op0=ALU.add, op1=ALU.max),
         r=[d_t1], w=[d_tmp])
    P.op("scalar", lambda e: e.activation(out=out, in_=tmp[:], func=AF.Sin), r=[d_tmp], w=[d_out])


def phase_C(P, nc, D, dq, gq, after_w=None):
    with ExitStack() as es:
        Et = T(es, nc, "C_Et", [128, 64, 2, 128], BF16)
        d_Et = Dep()
        P.dma("sync", dq[0], Et[:], D["Etab"], w=[d_Et])
        W1 = T(es, nc, "C_W1", [128, 128], BF16)
        W2 = T(es, nc, "C_W2", [128, 128], BF16)
        W3 = T(es, nc, "C_W3", [128, 128], BF16)
        W4 = T(es, nc, "C_W4", [128, 512], BF16)
        d_w = Dep()
        P.op("gpsimd", lambda e: e.memset(W1[:], 0.0), w=[d_w])
        P.op("gpsimd", lambda e: e.memset(W2[:], 0.0), w=[d_w])
        for hf in range(2):
            P.dma("gpsimd", gq[0], W1[hf * 64:hf * 64 + 33, hf * 64:(hf + 1) * 64], D["fw1"], w=[d_w])
            P.dma("gpsimd", gq[1], W2[hf * 64:(hf + 1) * 64, hf * 64:(hf + 1) * 64], D["fw2"], w=[d_w])
            for h2 in range(2):
                P.dma("gpsimd", gq[2], W3[hf * 64:(hf + 1) * 64, h2 * 64:(h2 + 1) * 64], D["fw3"], w=[d_w])
            P.dma("gpsimd", gq[3], W4[hf * 64:(hf + 1) * 64, :], D["fw4"][:, hf * 512:(hf + 1) * 512], w=[d_w])
        if after_w is not None:
            after_w()
        fb = T(es, nc, "C_fb", [128, 3], F32)
        fr = T(es, nc, "C_fr", [128, 1], F32)
        frb = T(es, nc, "C_frb", [128, 3], F32)
        d_f = Dep()
        P.dma("sync", dq[5], fb[:], D["fb"], w=[d_f])
        P.dma("sync", dq[6], fr[:], D["ffr"], w=[d_f])
        P.op("vector", lambda e: e.tensor_scalar(out=frb[:], in0=fb[:], scalar1=fr[:, 0:1], scalar2=None,
                                                 op0=ALU.mult), r=[d_f], w=[d_f])
        tpos = T(es, nc, "C_tpos", [128, 4, 64], F32)
        vbias = T(es, nc, "C_vbias", [128, 4, 64], F32)
        e0 = T(es, nc, "C_e0", [128, 1], F32)
        negd = T(es, nc, "C_negd", [128, 512], F32)
        hyd = T(es, nc, "C_hyd", [128, 512], F32)
        d_t = Dep()
        P.dma("sync", dq[7], tpos[:], D["tpos"], w=[d_t])
        P.dma("sync", dq[8], vbias[:], D["vbias"], w=[d_t])
        P.dma("sync", dq[9], e0[:], D["e0"], w=[d_t])
        P.dma("sync", dq[10], negd[:], D["negdelta"], w=[d_t])
        P.dma("sync", dq[11], hyd[:], D["hyd"], w=[d_t])
        zs = T(es, nc, "C_zs", [128, 4096], BF16)
        d_z = Dep()
        mk = T(es, nc, "C_mk", [128, 8192], BF16)
        d_mk = Dep()
        hA = T(es, nc, "C_hA", [128, 4096], BF16)
        hB = T(es, nc, "C_hB", [128, 4096], BF16)
        d_hAg = [Dep() for _ in range(4)]
        d_hBg = [Dep() for _ in range(4)]
        h3 = T(es, nc, "C_h3", [128, 8192], BF16)
        d_h3g = [Dep() for _ in range(8)]
        tmp = [T(es, nc, f"C_tmp{i}", [128, 1024], F32) for i in range(2)]
        t1b = [T(es, nc, f"C_t1{i}", [128, 1024], F32) for i in range(2)]
        d_tmp = [Dep() for _ in range(2)]
        d_t1b = [Dep() for _ in range(2)]
        gi = 0
        gf = [T(es, nc, f"C_gf{i}", [128, 512], BF16) for i in range(3)]
        d_gf = [Dep() for _ in range(3)]
        ast = [T(es, nc, f"C_ast{i}", [66, 2, 512], BF16) for i in range(3)]
        d_ast = [Dep() for _ in range(3)]
        ps1 = [PS(es, nc, f"C_ps1{i}", [128, 512], F32) for i in range(2)]
        d_ps1 = [Dep() for _ in range(2)]
        dec = [T(es, nc, f"C_dec{i}", [128, 512], F32) for i in range(2)]
        d_dec = [Dep() for _ in range(2)]
        gt = T(es, nc, "C_gt", [128, 512], F32)
        d_gt = Dep()
        pm = [PS(es, nc, f"C_pm{i}", [128, 1024], F32) for i in range(2)]
        d_pm = [Dep() for _ in range(2)]
        p4 = [PS(es, nc, f"C_p4{i}", [128, 512], F32) for i in range(2)]
        d_p4 = [Dep() for _ in range(2)]
        for s in range(4):
            P.dma("sync", dq[12], zs[:], D["zs"][s], w=[d_z])
            P.dma("sync", dq[13], mk[:], D["maskT"][s], w=[d_mk])
            groups = []
            for g in range(4):
                groups.append((W1[:], zs[:, g * 1024:(g + 1) * 1024], [d_z], hA[:, g * 1024:(g + 1) * 1024],
                               d_hAg[g], 0, None))
            for g in range(4):
                groups.append((W2[:], hA[:, g * 1024:(g + 1) * 1024], [d_hAg[g]], hB[:, g * 1024:(g + 1) * 1024],
                               d_hBg[g], 1, None))
            for g in range(8):
                hf = g // 4
                groups.append((W3[hf * 64:(hf + 1) * 64, :],
                               hB[hf * 64:(hf + 1) * 64, (g % 4) * 1024:(g % 4 + 1) * 1024], [d_hBg[g % 4]],
                               h3[:, g * 1024:(g + 1) * 1024], d_h3g[g], 2, mk[:, g * 1024:(g + 1) * 1024]))

            def M1(i):
                Wm, src, dsrc, dst, ddst, li, mask = groups[i]
                b = i % 2
                for k in range(2):
                    P.op("tensor", lambda e: e.matmul(pm[b][:, k * 512:(k + 1) * 512], Wm, src[:, k * 512:(k + 1) * 512],
                                                      start=True, stop=True),
                         r=[d_w] + dsrc, w=[d_pm[b]])
                P.op("scalar", lambda e: e.activation(out=tmp[b][:], in_=pm[b][:], func=AF.Identity,
                                                      scale=fr[:, 0:1], bias=frb[:, li:li + 1]),
                     r=[d_pm[b], d_f], w=[d_tmp[b]])

            def M2(i):
                b = i % 2
                P.op("vector", lambda e: e.tensor_scalar(out=t1b[b][:], in0=tmp[b][:], scalar1=-1.0, scalar2=PI,
                                                         op0=ALU.mult, op1=ALU.add),
                     r=[d_tmp[b]], w=[d_t1b[b]])
                P.op("vector", lambda e: e.tensor_tensor(out=tmp[b][:], in0=tmp[b][:], in1=t1b[b][:], op=ALU.min),
                     r=[d_t1b[b]], w=[d_tmp[b]])
                P.op("vector", lambda e: e.scalar_tensor_tensor(out=t1b[b][:], in0=t1b[b][:], scalar=-2 * PI,
                                                                in1=tmp[b][:], op0=ALU.add, op1=ALU.max),
                     r=[d_tmp[b]], w=[d_t1b[b]])

            def M3(i):
                Wm, src, dsrc, dst, ddst, li, mask = groups[i]
                b = i % 2
                P.op("scalar", lambda e: e.activation(out=dst, in_=t1b[b][:], func=AF.Sin), r=[d_t1b[b]], w=[ddst])
                if mask is not None:
                    P.op("vector", lambda e: e.tensor_tensor(out=dst, in0=dst, in1=mask, op=ALU.mult),
                         r=[d_mk], w=[ddst])

            pipeline(16, [M1, M2, M3])
            h3v = h3[:].rearrange("p (a b) -> p a b", b=64)

            def L1(t2, s=s, h3v=h3v):
                b = t2 % 2
                P.op("tensor", lambda e: e.matmul(p4[b][:], h3v[:, :, t2], W4[:], start=True, stop=True),
                     r=d_h3g + [d_w], w=[d_p4[b]])
                P.op("scalar", lambda e: e.activation(out=dec[b][:], in_=negd[:], func=AF.Exp,
                                                      scale=tpos[:, s, t2:t2 + 1], bias=vbias[:, s, t2:t2 + 1]),
                     r=[d_t], w=[d_dec[b]])
                k = t2 % 3
                if s == 0 and t2 == 0:
                    P.op("vector", lambda e: e.tensor_tensor(out=gt[:], in0=p4[b][:], in1=dec[b][:], op=ALU.mult),
                         r=[d_p4[b], d_dec[b]], w=[d_gt])
                    P.op("vector", lambda e: e.scalar_tensor_tensor(out=gf[k][:], in0=hyd[:], scalar=e0[:, 0:1],
                                                                    in1=gt[:], op0=ALU.mult, op1=ALU.add),
                         r=[d_t, d_gt], w=[d_gf[k]])
                else:
                    P.op("vector", lambda e: e.tensor_tensor(out=gf[k][:], in0=p4[b][:], in1=dec[b][:], op=ALU.mult),
                         r=[d_p4[b], d_dec[b]], w=[d_gf[k]])

            def L2(t2, s=s):
                k = t2 % 3
                a = t2 % 3
                for ri in range(2):
                    p = (t2 * 2 + ri) % 2
                    P.op("tensor", lambda e: e.matmul(ps1[p][0:66, :], Et[:, t2, ri, 0:66], gf[k][:], start=True, stop=True),
                         r=[d_Et, d_gf[k]], w=[d_ps1[p]])
                    if ri == 0:
                        P.op("scalar", lambda e: e.activation(out=ast[a][:, ri, :], in_=ps1[p][0:66, :], func=AF.Copy),
                             r=[d_ps1[p]], w=[d_ast[a]])
                    else:
                        P.op("vector", lambda e: e.tensor_copy(out=ast[a][:, ri, :], in_=ps1[p][0:66, :]),
                             r=[d_ps1[p]], w=[d_ast[a]])
                P.dma("sync", dq[14 + a], D["AG"][s][t2], ast[a][:], r=[d_ast[a]])

            pipeline(64, [L1, L2])


def phase_D(P, nc, D, dq, gq):
    with ExitStack() as es:
        Et = T(es, nc, "D_Et", [64, 64, 2, 128], BF16)
        d_Et = Dep()
        P.dma("sync", dq[0], Et[:], D["Etab"][0:64], w=[d_Et])
        Du = [T(es, nc, f"D_Du{i}", [64, 4, 64, 128], BF16) for i in range(2)]
        d_Du = [Dep() for _ in range(2)]
        ps = [PS(es, nc, f"D_s1p{i}", [128, 512], F32) for i in range(6)]
        d_ps = [Dep() for _ in range(6)]
        ast = [T(es, nc, f"D_ast{i}", [66, 2, 512], BF16) for i in range(4)]
        d_ast = [Dep() for _ in range(4)]
        P.dma("sync", dq[1], Du[0][:], D["DU"][0], w=[d_Du[0]])
        it = 0
        for blk in range(4):
            b = blk % 2
            if blk + 1 < 4:
                P.dma("sync", dq[1 + (blk + 1) % 2], Du[(blk + 1) % 2][:], D["DU"][blk + 1], w=[d_Du[(blk + 1) % 2]])
            for t2 in range(64):
                a = it % 4
                for ri in range(2):
                    p = (it * 2 + ri) % 6
                    P.op("tensor", lambda e: e.matmul(ps[p][0:66, :], Et[:, t2, ri, 0:66], Du[b][:, :, t2, :],
                                                      start=True, stop=True),
                         r=[d_Et, d_Du[b]], w=[d_ps[p]])
                    if ri == 0:
                        P.op("scalar", lambda e: e.activation(out=ast[a][:, ri, :], in_=ps[p][0:66, :], func=AF.Copy),
                             r=[d_ps[p]], w=[d_ast[a]])
                    else:
                        P.op("vector", lambda e: e.tensor_copy(out=ast[a][:, ri, :], in_=ps[p][0:66, :]),
                             r=[d_ps[p]], w=[d_ast[a]])
                P.dma("sync", dq[3 + a], D["AU"][blk][t2], ast[a][:], r=[d_ast[a]])
                it += 1


KG = 3
NQ = 11


def stage2_mm(P, F2, d_F2, Bt, d_B, kk, out, d_out):
    P.op("tensor", lambda e: e.matmul(out[:, 0, :], F2[:, 0, :], Bt[:, kk, 0, :], start=True, stop=False),
         r=[d_F2, d_B], w=[d_out])
    P.op("tensor", lambda e: e.matmul(out[:, 0, :], F2[:, 2, :], Bt[:, kk, 1, :], start=False, stop=True),
         r=[d_F2, d_B], w=[d_out])
    P.op("tensor", lambda e: e.matmul(out[:, 1, :], F2[:, 1, :], Bt[:, kk, 0, :], start=True, stop=False),
         r=[d_F2, d_B], w=[d_out])
    P.op("tensor", lambda e: e.matmul(out[:, 1, :], F2[:, 0, :], Bt[:, kk, 1, :], start=False, stop=True),
         r=[d_F2, d_B], w=[d_out])


def phase_F(P, nc, D, dq, gq):
    with ExitStack() as es:
        ident, d_ident = load_ident(P, nc, es, D, dq[0])
        F2 = T(es, nc, "F_F2", [128, 3, 128], BF16)
        d_F2 = Dep()
        P.dma("sync", dq[1], F2[:], D["F2tab"], w=[d_F2])
        IT = [T(es, nc, f"F_IT{i}", [128, KG, 3, 128], BF16) for i in range(2)]
        d_IT = [Dep() for _ in range(2)]
        Bg = [[T(es, nc, f"F_Bg{i}_{s}", [128, KG, 2, 512], BF16) for s in range(4)] for i in range(2)]
        Bu = [[T(es, nc, f"F_Bu{i}_{s}", [128, KG, 2, 512], BF16) for s in range(4)] for i in range(2)]
        d_Bg = [[Dep() for s in range(4)] for i in range(2)]
        d_Bu = [[Dep() for s in range(4)] for i in range(2)]
        Gs = [T(es, nc, f"F_Gs{i}", [128, 3, 512], BF16) for i in range(2)]
        d_Gs = [Dep() for _ in range(2)]
        TA = [T(es, nc, f"F_TA{i}", [128, 2, 512], BF16) for i in range(2)]
        d_TA = [Dep() for _ in range(2)]
        TB = [T(es, nc, f"F_TB{i}", [128, 2, 512], BF16) for i in range(2)]
        d_TB = [Dep() for _ in range(2)]
        Yb = [T(es, nc, f"F_Yb{i}", [128, 2, 512], BF16) for i in range(2)]
        d_Yb = [Dep() for _ in range(2)]
        Cst = [T(es, nc, f"F_C{i}", [128, 2, 512], BF16) for i in range(2)]
        d_C = [Dep() for _ in range(2)]
        Gp = PS(es, nc, "F_Gp", [128, 2, 512], F32)
        d_Gp = Dep()
        Up = [PS(es, nc, f"F_Up{i}", [128, 2, 512], F32) for i in range(2)]
        d_Up = [Dep() for _ in range(2)]
        Yp = PS(es, nc, "F_Yp", [128, 2, 512], F32)
        d_Yp = Dep()
        def loads(q):
            qb = q % 2
            P.dma("sync", dq[2 + qb], IT[qb][:], D["ITtab"][:, q * KG:(q + 1) * KG], w=[d_IT[qb]])
            for s in range(4):
                for kp in range(2):
                    P.dma("sync", dq[24 + qb * 16 + s * 2 + kp], Bg[qb][s][kp * 64:(kp + 1) * 64],
                          D["AG"][s][:, kp * 33 + q * KG:kp * 33 + (q + 1) * KG], w=[d_Bg[qb][s]])
                    P.dma("sync", dq[32 + qb * 16 + s * 2 + kp], Bu[qb][s][kp * 64:(kp + 1) * 64],
                          D["AU"][s][:, kp * 33 + q * KG:kp * 33 + (q + 1) * KG], w=[d_Bu[qb][s]])

        def idx(i):
            q, r = divmod(i, KG * 4)
            kk, s = divmod(r, 4)
            return q, kk, s

        def S1(i):
            q, kk, s = idx(i)
            qb = q % 2
            p = i % 2
            if i == 0:
                loads(0)
            if kk == 1 and s == 0 and q + 1 < NQ:
                loads(q + 1)
            stage2_mm(P, F2, d_F2, Bg[qb][s], d_Bg[qb][s], kk, Gp, d_Gp)
            P.op("scalar", lambda e: e.activation(out=Gs[p][:, 0:2, :], in_=Gp[:], func=AF.Copy),
                 r=[d_Gp], w=[d_Gs[p]])
            P.op("scalar", lambda e: e.activation(out=Gs[p][:, 2, :], in_=Gp[:, 1, :], func=AF.Copy, scale=-1.0),
                 r=[d_Gp], w=[d_Gs[p]])
            stage2_mm(P, F2, d_F2, Bu[qb][s], d_Bu[qb][s], kk, Up[p], d_Up[p])

        def S2(i):
            p = i % 2
            P.op("vector", lambda e: e.tensor_tensor(out=TA[p][:], in0=Up[p][:],
                                                     in1=Gs[p][:, 0:1, :].to_broadcast([128, 2, 512]), op=ALU.mult),
                 r=[d_Up[p], d_Gs[p]], w=[d_TA[p]])
            P.op("vector", lambda e: e.tensor_tensor(out=TB[p][:, 0, :], in0=Up[p][:, 1, :], in1=Gs[p][:, 2, :],
                                                     op=ALU.mult),
                 r=[d_Up[p], d_Gs[p]], w=[d_TB[p]])
            P.op("vector", lambda e: e.tensor_tensor(out=TB[p][:, 1, :], in0=Up[p][:, 0, :], in1=Gs[p][:, 1, :],
                                                     op=ALU.mult),
                 r=[d_Up[p], d_Gs[p]], w=[d_TB[p]])

        def S3(i):
            q, kk, s = idx(i)
            qb = q % 2
            p = i % 2
            for ri in range(2):
                P.op("tensor", lambda e: e.matmul(Yp[:, ri, :], ident[:], TA[p][:, ri, :], start=(s == 0), stop=False),
                     r=[d_ident, d_TA[p]], w=[d_Yp])
                P.op("tensor", lambda e: e.matmul(Yp[:, ri, :], ident[:], TB[p][:, ri, :], start=False, stop=(s == 3)),
                     r=[d_ident, d_TB[p]], w=[d_Yp])
            if s != 3:
                return
            c = (q * KG + kk) % 2
            P.op("scalar", lambda e: e.activation(out=Yb[c][:], in_=Yp[:], func=AF.Copy), r=[d_Yp], w=[d_Yb[c]])
            ITq = IT[qb]
            P.op("tensor", lambda e: e.matmul(Gp[:, 0, :], ITq[:, kk, 0, :], Yb[c][:, 0, :], start=True, stop=False),
                 r=[d_IT[qb], d_Yb[c]], w=[d_Gp])
            P.op("tensor", lambda e: e.matmul(Gp[:, 0, :], ITq[:, kk, 2, :], Yb[c][:, 1, :], start=False, stop=True),
                 r=[d_IT[qb], d_Yb[c]], w=[d_Gp])
            P.op("tensor", lambda e: e.matmul(Gp[:, 1, :], ITq[:, kk, 1, :], Yb[c][:, 0, :], start=True, stop=False),
                 r=[d_IT[qb], d_Yb[c]], w=[d_Gp])
            P.op("tensor", lambda e: e.matmul(Gp[:, 1, :], ITq[:, kk, 0, :], Yb[c][:, 1, :], start=False, stop=True),
                 r=[d_IT[qb], d_Yb[c]], w=[d_Gp])
            P.op("vector", lambda e: e.tensor_copy(out=Cst[c][:], in_=Gp[:]), r=[d_Gp], w=[d_C[c]])
            kh = q * KG + kk
            for kp in range(2):
                P.dma("sync", dq[20 + 2 * c + kp], D["CS"][kp * 33 + kh], Cst[c][kp * 64:(kp + 1) * 64], r=[d_C[c]])

        pipeline(NQ * KG * 4, [S1, S2, S3])


def phase_G(P, nc, D, dq, gq):
    with ExitStack() as es:
        IC = T(es, nc, "G_IC", [66, 2, 64], BF16)
        d_IC = Dep()
        P.dma("sync", dq[0], IC[:], D["ICtab"], w=[d_IC])
        x0 = T(es, nc, "G_x0", [128, 4, 4096], BF16)
        d_x0 = Dep()
        P.dma("sync", dq[1], x0[:], D["X0T"].rearrange("j p t -> p j t"), w=[d_x0])
        yh = T(es, nc, "G_yh", [128, 4, 4096], BF16)
        d_yh = Dep()
        Cg = [T(es, nc, f"G_C{i}", [66, 8, 2, 512], BF16) for i in range(2)]
        d_Cg = [Dep() for _ in range(2)]
        ps = [PS(es, nc, f"G_ps{i}", [128, 8, 64], F32) for i in range(4)]
        d_ps = [Dep() for _ in range(4)]
        pi = 0
        for tg in range(8):
            b = tg % 2
            P.dma("sync", dq[2 + b], Cg[b][:], D["CS"][:, tg * 8:(tg + 1) * 8], w=[d_Cg[b]])
            for j in range(4):
                p = pi % 4
                pi += 1
                for tl in range(8):
                    P.op("tensor", lambda e: e.matmul(ps[p][:, tl, :], Cg[b][:, tl, 0, j * 128:(j + 1) * 128],
                                                      IC[:, 0, :], start=True, stop=False),
                         r=[d_Cg[b], d_IC], w=[d_ps[p]])
                    P.op("tensor", lambda e: e.matmul(ps[p][:, tl, :], Cg[b][:, tl, 1, j * 128:(j + 1) * 128],
                                                      IC[:, 1, :], start=False, stop=True),
                         r=[d_Cg[b], d_IC], w=[d_ps[p]])
                ov = yh[:, j, :].rearrange("p (a b) -> p b a", b=64)[:, tg * 8:(tg + 1) * 8, :]
                xv = x0[:, j, :].rearrange("p (a b) -> p b a", b=64)[:, tg * 8:(tg + 1) * 8, :]
                P.op("vector", lambda e: e.tensor_tensor(out=ov, in0=ps[p][:], in1=xv, op=ALU.mult),
                     r=[d_ps[p], d_x0], w=[d_yh])
        P.dma("sync", dq[4], D["YHT"].rearrange("j p t -> p j t"), yh[:], r=[d_yh])


def phase_H(P, nc, D, dq, gq):
    with ExitStack() as es:
        QT = T(es, nc, "H_QT", [128, 4, 4096], BF16)
        KT = T(es, nc, "H_KT", [128, 4, 4608], BF16)
        VA = T(es, nc, "H_VA", [128, 36, 8, 65], BF16)
        d_in = Dep()
        P.dma("sync", dq[1], QT[:], D["QT"].rearrange("j p t -> p j t"), w=[d_in])
        P.dma("sync", dq[2], KT[:], D["KT"].rearrange("j p t -> p j t"), w=[d_in])
        P.dma("sync", dq[3], VA[:].rearrange("p s h d -> p s (h d)"), D["VA"].rearrange("s p x -> p s x"), w=[d_in])
        bt = [T(es, nc, f"H_bt{i}", [128, 8, 6, 256], BF16) for i in range(2)]
        d_bt = [Dep() for _ in range(2)]
        EX = [T(es, nc, f"H_EX{i}", [128, 6, 256], BF16) for i in range(2)]
        d_EX = [Dep() for _ in range(2)]
        PT = [T(es, nc, f"H_PT{i}", [128, 6, 256], BF16) for i in range(2)]
        d_PT = [Dep() for _ in range(2)]
        onesr = T(es, nc, "H_ones", [128, 64], BF16)
        d_ones = Dep()
        P.op("gpsimd", lambda e: e.memset(onesr[:], 1.0), w=[d_ones])
        rdb = [T(es, nc, f"H_rdb{i}", [128, 256], BF16) for i in range(2)]
        d_rdb = [Dep() for _ in range(2)]
        rdf = [T(es, nc, f"H_rdf{i}", [128, 256], F32) for i in range(2)]
        d_rdf = [Dep() for _ in range(2)]
        Bs = [T(es, nc, f"H_Bs{i}", [64, 256], F32) for i in range(2)]
        d_Bs = [Dep() for _ in range(2)]
        yTs = [T(es, nc, f"H_yT{i}", [128, 4, 256], BF16) for i in range(2)]
        d_yTs = [Dep() for _ in range(2)]
        YATv = D["YAT"].rearrange("j p t -> p j t")
        psS = [PS(es, nc, f"H_pS{i}", [128, 512], F32) for i in range(4)]
        d_pS = [Dep() for _ in range(4)]
        psO = [PS(es, nc, f"H_pO{i}", [128, 256], F32) for i in range(2)]
        d_pO = [Dep() for _ in range(2)]
        psB = [PS(es, nc, f"H_pB{i}", [64, 256], F32) for i in range(2)]
        d_pB = [Dep() for _ in range(2)]
        EX3 = EX + [T(es, nc, "H_EX2", [128, 6, 256], BF16)]
        PT3 = PT + [T(es, nc, "H_PT2", [128, 6, 256], BF16)]
        d_EX3 = d_EX + [Dep()]
        d_PT3 = d_PT + [Dep()]
        state = {"cur_v": None, "cur_bb": 0, "si": 0}
        cbb_of = {}

        def S1(i):
            g, h = divmod(i, 8)
            v = 0 if g == 0 else (2 if g == 15 else 1)
            if h == 0 and v != state["cur_v"]:
                bb = v % 2
                P.dma("sync", dq[4 + bb], bt[bb][:], D["abias"][v].rearrange("h p a q -> p h a q"), w=[d_bt[bb]])
                for hh in range(8):
                    P.op("scalar", lambda e: e.activation(out=bt[bb][:, hh], in_=bt[bb][:, hh], func=AF.Exp),
                         w=[d_bt[bb]])
                state["cur_v"] = v
                state["cur_bb"] = bb
            cbb = state["cur_bb"]
            ch, po = h // 2, (h % 2) * 64
            pb = i % 3
            for pp in range(3):
                sp = state["si"] % 4
                state["si"] += 1
                for e2 in range(2):
                    pr = 2 * pp + e2
                    k0 = (4 * g + 2 * pr) * 64
                    P.op("tensor", lambda e: e.matmul(psS[sp][:, e2 * 256:(e2 + 1) * 256],
                                                      KT[po:po + 64, ch, k0:k0 + 128],
                                                      QT[po:po + 64, ch, g * 256:(g + 1) * 256],
                                                      start=True, stop=True),
                         r=[d_in], w=[d_pS[sp]])
                P.op("scalar", lambda e: e.activation(out=EX3[pb][:, 2 * pp:2 * pp + 2, :],
                                                      in_=psS[sp][:].rearrange("p (a q) -> p a q", q=256),
                                                      func=AF.Exp),
                     r=[d_pS[sp]], w=[d_EX3[pb]])
            cbb_of[i] = cbb

        def S1b(i):
            g, h = divmod(i, 8)
            pb = i % 3
            cbb = cbb_of[i]
            for pp in range(3):
                P.op("vector", lambda e: e.tensor_tensor(out=PT3[pb][:, 2 * pp:2 * pp + 2, :],
                                                         in0=EX3[pb][:, 2 * pp:2 * pp + 2, :],
                                                         in1=bt[cbb][:, h, 2 * pp:2 * pp + 2, :], op=ALU.mult),
                     r=[d_EX3[pb], d_bt[cbb]], w=[d_PT3[pb]])

        def S2(i):
            g, h = divmod(i, 8)
            pb = i % 3
            o = i % 2
            for pr in range(6):
                st = 2 * g + pr
                P.op("tensor", lambda e: e.matmul(psO[o][0:65, :], VA[:, st, h, :], PT3[pb][:, pr, :],
                                                  start=(pr == 0), stop=(pr == 5)),
                     r=[d_PT3[pb], d_in], w=[d_pO[o]])
            P.op("vector", lambda e: e.reciprocal(out=rdf[o][64:65, :], in_=psO[o][64:65, :]),
                 r=[d_pO[o]], w=[d_rdf[o]])
            P.op("vector", lambda e: e.tensor_copy(out=rdb[o][64:65, :], in_=rdf[o][64:65, :]),
                 r=[d_rdf[o]], w=[d_rdb[o]])

        def S3(i):
            g, h = divmod(i, 8)
            ch, po = h // 2, (h % 2) * 64
            o = i % 2
            yb = g % 2
            P.op("tensor", lambda e: e.matmul(psB[o][:], onesr[64:65, :], rdb[o][64:65, :], start=True, stop=True),
                 r=[d_ones, d_rdb[o]], w=[d_pB[o]])
            P.op("scalar", lambda e: e.activation(out=Bs[o][:], in_=psB[o][:], func=AF.Copy),
                 r=[d_pB[o]], w=[d_Bs[o]])
            P.op("vector", lambda e: e.tensor_tensor(out=yTs[yb][po:po + 64, ch, :], in0=psO[o][0:64, :],
                                                     in1=Bs[o][:], op=ALU.mult),
                 r=[d_pO[o], d_Bs[o]], w=[d_yTs[yb]])
            if h == 7:
                P.dma("sync", dq[6 + yb], YATv[:, :, g * 256:(g + 1) * 256], yTs[yb][:], r=[d_yTs[yb]])

        pipeline(128, [S1, S1b, S2, S3])


def phase_W(P, nc, D, wsem):
    for w in wsem:
        w.nobar = True
    k = [0]

    def cv(dst, src):
        P.dma("gpsimd", wsem[k[0] % 4], dst, src)
        k[0] += 1

    for m in range(16):
        cv(D["WG"][m], wchunk_src(D["w_in"], 3072 + m * 128, 128))
    for m in range(8):
        cv(D["WBA"][m], wchunk_src(D["w_br_attn"], m * 128, 128))
        cv(D["WBH"][m], wchunk_src(D["w_br_hyena"], m * 128, 128))
    for kc in range(8):
        cv(D["WO"][:, kc, :], D["w_out"][kc * 128:(kc + 1) * 128, :])
    for f in range(NFF):
        cv(D["WFG"][f], wchunk_src(D["w_gate"], f * 128, 128))
        cv(D["WFU"][f], wchunk_src(D["w_up"], f * 128, 128))
    wdv = D["w_down"].rearrange("(f p) n -> p f n", p=128)
    for dh in range(2):
        for f0 in range(0, NFF, 11):
            cv(D["WD"][dh][:, f0:f0 + 11, :], wdv[:, f0:f0 + 11, dh * 512:(dh + 1) * 512])


def phase_I(P, nc, D, dq, gq):
    TT = 1024
    with ExitStack() as es:
        ident, d_ident = load_ident(P, nc, es, D, dq[0])
        grep, d_grep = make_grep(P, nc, es, D["g_mix"], "gm", dq[1])
        grep2, d_grep2 = make_grep(P, nc, es, D["g_ffn"], "gf", dq[2])
        gfin = T(es, nc, "I_gfin", [128, 1024], F32)
        d_gfin = Dep()
        P.dma("sync", dq[3], gfin[:], D["g_final"], w=[d_gfin])
        xt = T(es, nc, "I_xt", [128, 8, 1024], F32)
        d_xt = [Dep() for _ in range(8)]
        hT = T(es, nc, "I_hT", [128, 8, TT], BF16)
        d_hT = Dep()
        aT = T(es, nc, "I_aT", [128, NFF, TT], BF16)
        d_aT = Dep()
        mT = aT[:, 0:8, :]
        yaT = aT[:, 8:12, :]
        yhT = aT[:, 12:16, :]
        d_yy = Dep()
        d_mT = Dep()
        wout = T(es, nc, "I_wout", [128, 8, 1024], BF16)
        d_wout = Dep()
        wdn = T(es, nc, "I_wdn", [128, NFF, 512], BF16)
        d_wdn = Dep()
        wc = [T(es, nc, f"I_wc{i}", [128, 8, 128], BF16) for i in range(6)]
        d_wc = [Dep() for _ in range(6)]
        wb = [T(es, nc, f"I_wb{i}", [128, 4, 128], BF16) for i in range(4)]
        d_wb = [Dep() for _ in range(4)]
        sg = [T(es, nc, f"I_sg{i}", [128, 512], F32) for i in range(4)]
        d_sg = [Dep() for _ in range(4)]
        xn = [T(es, nc, f"I_xn{i}", [128, 1024], BF16) for i in range(3)]
        d_xn = [Dep() for _ in range(3)]
        junk = T(es, nc, "I_junk", [128, 1024], BF16)
        d_junk = Dep()
        stt = [T(es, nc, f"I_st{i}", [128, 4], F32) for i in range(3)]
        d_st = [Dep() for _ in range(3)]
        yo = [T(es, nc, f"I_yo{i}", [128, 1024], F32) for i in range(2)]
        d_yo = [Dep() for _ in range(2)]
        ps = [PS(es, nc, f"I_ps{i}", [128, 512], F32) for i in range(7)]
        d_ps = [Dep() for _ in range(7)]
        pt = PS(es, nc, "I_pt", [128, 8, 128], BF16)
        d_pt = Dep()
        for i in range(3):
            P.op("gpsimd", lambda e: e.memset(stt[i][:], EPS), w=[d_st[i]])
        P.dma("sync", dq[54], wout[:], D["WO"], w=[d_wout])
        cnt = {"wi": 0, "bi": 0, "pi": 0, "ni": 0}

        def nA(st, gr, d_gr):
            b = st % 3
            P.op("scalar", lambda e: e.activation(out=junk[:], in_=xt[:, st, :], func=AF.Square,
                                                  accum_out=stt[b][:, 0:1]),
                 r=[d_xt[st]], w=[d_junk, d_st[b]])
            P.op("scalar", lambda e: e.activation(out=stt[b][:, 1:2], in_=stt[b][:, 0:1], func=AF.Sqrt,
                                                  scale=1.0 / 1024, bias=stt[b][:, 3:4]),
                 r=[d_st[b]], w=[d_st[b]])

        def nB(st, gr, d_gr):
            b = st % 3
            P.op("vector", lambda e: e.reciprocal(out=stt[b][:, 2:3], in_=stt[b][:, 1:2]), r=[d_st[b]], w=[d_st[b]])
            if st % 2:
                P.op("scalar", lambda e: e.activation(out=xn[b][:], in_=xt[:, st, :], func=AF.Copy,
                                                      scale=stt[b][:, 2:3]),
                     r=[d_st[b], d_xt[st]], w=[d_xn[b]])
            else:
                P.op("vector", lambda e: e.tensor_scalar(out=xn[b][:], in0=xt[:, st, :], scalar1=stt[b][:, 2:3],
                                                         scalar2=None, op0=ALU.mult),
                     r=[d_st[b], d_xt[st]], w=[d_xn[b]])

        def nC(st, gr, d_gr):
            b = st % 3
            for kc in range(8):
                P.op("tensor", lambda e: e.transpose(pt[:, kc, :], xn[b][:, kc * 128:(kc + 1) * 128], ident[:]),
                     r=[d_xn[b], d_ident], w=[d_pt])
            P.op("vector", lambda e: e.tensor_tensor(out=hT[:, :, st * 128:(st + 1) * 128], in0=pt[:], in1=gr[:],
                                                     op=ALU.mult),
                 r=[d_pt, d_gr], w=[d_hT])

        def wload(buf, dbuf, sem, src):
            P.dma("sync", sem, buf[:], src, w=[dbuf])

        for tile in range(4096 // TT):
            tok0 = tile * TT
            P.dma("sync", dq[5], yaT, D["YAT"].rearrange("j p t -> p j t")[:, :, tok0:tok0 + TT], w=[d_yy, d_aT])
            P.dma("sync", dq[6], yhT, D["YHT"].rearrange("j p t -> p j t")[:, :, tok0:tok0 + TT], w=[d_yy, d_aT])

            def L0(st, tok0=tok0):
                P.dma("sync", dq[20 + st], xt[:, st, :],
                      D["xext"][256 + tok0 + st * 128:256 + tok0 + (st + 1) * 128, :], w=[d_xt[st]])
                nA(st, grep, d_grep)

            pipeline(8, [L0, lambda st: nB(st, grep, d_grep), lambda st: nC(st, grep, d_grep)])
            for m in range(8):
                k = cnt["wi"] % 6; w_ga, dga = wc[k], d_wc[k]; wload(w_ga, dga, dq[44 + k], D["WG"][m]); cnt["wi"] += 1
                k = cnt["wi"] % 6; w_gh, dgh = wc[k], d_wc[k]; wload(w_gh, dgh, dq[44 + k], D["WG"][8 + m]); cnt["wi"] += 1
                k = cnt["bi"] % 4; w_ba, dba = wb[k], d_wb[k]; wload(w_ba, dba, dq[50 + k], D["WBA"][m]); cnt["bi"] += 1
                k = cnt["bi"] % 4; w_bh, dbh = wb[k], d_wb[k]; wload(w_bh, dbh, dq[50 + k], D["WBH"][m]); cnt["bi"] += 1
                for th in range(TT // 512):
                    tsl = slice(th * 512, (th + 1) * 512)
                    pi = cnt["pi"]
                    pg = [pi % 7, (pi + 1) % 7, (pi + 2) % 7, (pi + 3) % 7]
                    cnt["pi"] += 4
                    for kc in range(8):
                        P.op("tensor", lambda e: e.matmul(ps[pg[0]][:], w_ga[:, kc, :], hT[:, kc, tsl],
                                                          start=(kc == 0), stop=(kc == 7)),
                             r=[dga, d_hT], w=[d_ps[pg[0]]])
                    for kc in range(8):
                        P.op("tensor", lambda e: e.matmul(ps[pg[1]][:], w_gh[:, kc, :], hT[:, kc, tsl],
                                                          start=(kc == 0), stop=(kc == 7)),
                             r=[dgh, d_hT], w=[d_ps[pg[1]]])
                    for kc in range(4):
                        P.op("tensor", lambda e: e.matmul(ps[pg[2]][:], w_ba[:, kc, :], yaT[:, kc, tsl],
                                                          start=(kc == 0), stop=(kc == 3)),
                             r=[dba, d_yy], w=[d_ps[pg[2]]])
                    for kc in range(4):
                        P.op("tensor", lambda e: e.matmul(ps[pg[3]][:], w_bh[:, kc, :], yhT[:, kc, tsl],
                                                          start=(kc == 0), stop=(kc == 3)),
                             r=[dbh, d_yy], w=[d_ps[pg[3]]])
                    sa = (2 * (m * 2 + th)) % 4
                    P.op("scalar", lambda e: e.activation(out=sg[sa][:], in_=ps[pg[0]][:], func=AF.Sigmoid),
                         r=[d_ps[pg[0]]], w=[d_sg[sa]])
                    P.op("scalar", lambda e: e.activation(out=sg[sa + 1][:], in_=ps[pg[1]][:], func=AF.Sigmoid),
                         r=[d_ps[pg[1]]], w=[d_sg[sa + 1]])
                    P.op("vector", lambda e: e.tensor_tensor(out=sg[sa][:], in0=ps[pg[2]][:], in1=sg[sa][:], op=ALU.mult),
                         r=[d_ps[pg[2]]], w=[d_sg[sa]])
                    P.op("vector", lambda e: e.tensor_tensor(out=sg[sa + 1][:], in0=ps[pg[3]][:], in1=sg[sa + 1][:],
                                                             op=ALU.mult),
                         r=[d_ps[pg[3]]], w=[d_sg[sa + 1]])
                    P.op("vector", lambda e: e.tensor_tensor(out=mT[:, m, tsl], in0=sg[sa][:], in1=sg[sa + 1][:],
                                                             op=ALU.add),
                         r=[d_sg[sa], d_sg[sa + 1]], w=[d_mT])

            def O1(st):
                for dh in range(2):
                    p = cnt["pi"] % 7
                    cnt["pi"] += 1
                    for kc in range(8):
                        P.op("tensor", lambda e: e.matmul(ps[p][:], mT[:, kc, st * 128:(st + 1) * 128],
                                                          wout[:, kc, dh * 512:(dh + 1) * 512],
                                                          start=(kc == 0), stop=(kc == 7)),
                             r=[d_mT, d_wout], w=[d_ps[p]])
                    P.op("vector", lambda e: e.tensor_tensor(out=xt[:, st, dh * 512:(dh + 1) * 512], in0=ps[p][:],
                                                             in1=xt[:, st, dh * 512:(dh + 1) * 512], op=ALU.add),
                         r=[d_ps[p]], w=[d_xt[st]])
                nA(st, grep2, d_grep2)

            pipeline(8, [O1, lambda st: nB(st, grep2, d_grep2), lambda st: nC(st, grep2, d_grep2)])
            for f in range(NFF):
                k = cnt["wi"] % 6; w_g, dg = wc[k], d_wc[k]; wload(w_g, dg, dq[44 + k], D["WFG"][f]); cnt["wi"] += 1
                k = cnt["wi"] % 6; w_u, dup = wc[k], d_wc[k]; wload(w_u, dup, dq[44 + k], D["WFU"][f]); cnt["wi"] += 1
                if f == 2:
                    P.dma("sync", dq[55], wdn[:], D["WD"][0], w=[d_wdn])
                for th in range(TT // 512):
                    tsl = slice(th * 512, (th + 1) * 512)
                    pi = cnt["pi"]
                    pg = [pi % 7, (pi + 1) % 7]
                    cnt["pi"] += 2
                    for kc in range(8):
                        P.op("tensor", lambda e: e.matmul(ps[pg[0]][:], w_g[:, kc, :], hT[:, kc, tsl],
                                                          start=(kc == 0), stop=(kc == 7)),
                             r=[dg, d_hT], w=[d_ps[pg[0]]])
                    for kc in range(8):
                        P.op("tensor", lambda e: e.matmul(ps[pg[1]][:], w_u[:, kc, :], hT[:, kc, tsl],
                                                          start=(kc == 0), stop=(kc == 7)),
                             r=[dup, d_hT], w=[d_ps[pg[1]]])
                    sa = (f * 2 + th) % 4
                    P.op("scalar", lambda e: e.activation(out=sg[sa][:], in_=ps[pg[0]][:], func=AF.Silu),
                         r=[d_ps[pg[0]]], w=[d_sg[sa]])
                    P.op("vector", lambda e: e.tensor_tensor(out=aT[:, f, tsl], in0=ps[pg[1]][:], in1=sg[sa][:],
                                                             op=ALU.mult),
                         r=[d_ps[pg[1]], d_sg[sa]], w=[d_aT, d_mT, d_yy])
            for dh in range(2):
                if dh == 1:
                    P.dma("sync", dq[55], wdn[:], D["WD"][1], w=[d_wdn])
                for st in range(8):
                    p = cnt["pi"] % 7
                    cnt["pi"] += 1
                    for f in range(NFF):
                        P.op("tensor", lambda e: e.matmul(ps[p][:], aT[:, f, st * 128:(st + 1) * 128], wdn[:, f, :],
                                                          start=(f == 0), stop=(f == NFF - 1)),
                             r=[d_aT, d_mT, d_yy, d_wdn], w=[d_ps[p]])
                    P.op("vector", lambda e: e.tensor_tensor(out=xt[:, st, dh * 512:(dh + 1) * 512], in0=ps[p][:],
                                                             in1=xt[:, st, dh * 512:(dh + 1) * 512], op=ALU.add),
                         r=[d_ps[p]], w=[d_xt[st]])
                    if dh == 1:
                        b = st % 3
                        yb = st % 2
                        P.op("scalar", lambda e: e.activation(out=junk[:], in_=xt[:, st, :], func=AF.Square,
                                                              accum_out=stt[b][:, 0:1]),
                             r=[d_xt[st]], w=[d_junk, d_st[b]])
                        P.op("scalar", lambda e: e.activation(out=stt[b][:, 1:2], in_=stt[b][:, 0:1], func=AF.Sqrt,
                                                              scale=1.0 / 1024, bias=stt[b][:, 3:4]),
                             r=[d_st[b]], w=[d_st[b]])
                        P.op("vector", lambda e: e.reciprocal(out=stt[b][:, 2:3], in_=stt[b][:, 1:2]),
                             r=[d_st[b]], w=[d_st[b]])
                        P.op("vector", lambda e: e.scalar_tensor_tensor(out=yo[yb][:], in0=xt[:, st, :],
                                                                        scalar=stt[b][:, 2:3], in1=gfin[:],
                                                                        op0=ALU.mult, op1=ALU.mult),
                             r=[d_st[b], d_xt[st], d_gfin], w=[d_yo[yb]])
                        P.dma("sync", dq[16 + yb], D["y"][tok0 + st * 128:tok0 + (st + 1) * 128, :], yo[yb][:],
                              r=[d_yo[yb]])


def _const_tables():
    N = 8192
    t1 = np.arange(128)[:, None, None]
    t2 = np.arange(64)[None, :, None]
    k1 = np.arange(128)[None, None, :]
    th = 2 * np.pi * (((64 * t1 + t2) * k1) % N) / N
    E = np.stack([np.cos(th), -np.sin(th)], axis=2)
    a = np.arange(64)
    th2 = 2 * np.pi * np.outer(a, a) / 64
    F2 = np.zeros((128, 3, 128))
    for kp in range(2):
        sl = slice(kp * 64, (kp + 1) * 64)
        F2[sl, 0, sl] = np.cos(th2)
        F2[sl, 1, sl] = -np.sin(th2)
        F2[sl, 2, sl] = np.sin(th2)
    IT = np.zeros((128, 33, 3, 128))
    k2 = np.arange(64)[:, None]
    tt = np.arange(64)[None, :]
    for kh in range(33):
        for kp in range(2):
            k1v = kh + 33 * kp
            ph = 2 * np.pi * (tt * k2 / 64.0 + tt * k1v / 8192.0)
            sl = slice(kp * 64, (kp + 1) * 64)
            IT[sl, kh, 0, sl] = np.cos(ph)
            IT[sl, kh, 1, sl] = np.sin(ph)
            IT[sl, kh, 2, sl] = -np.sin(ph)
    kk = np.arange(66)[:, None]
    t1v = np.arange(64)[None, :]
    th3 = 2 * np.pi * kk * t1v / 128.0
    wk = np.full((66, 1), 2.0)
    wk[0] = 1.0
    wk[64] = 1.0
    wk[65] = 0.0
    IC = np.stack([wk * np.cos(th3) / N, -wk * np.sin(th3) / N], axis=1)
    return (E.astype(BF), F2.astype(BF), IT.astype(BF), IC.astype(BF))


def _filter_tables(L, ksegs):
    f32 = np.float32
    tlin = np.linspace(0.0, 1.0, L, dtype=f32)
    omega = (2.0 * math.pi * np.arange(L, dtype=f32) / L).astype(f32)
    fbv = np.linspace(1e-4, 15, 16, dtype=f32)
    n = np.arange(8192)
    d = np.where(n < 4096, n, n - 8192)
    zs = np.zeros((4, 128, 4096), f32)
    maskT = np.zeros((4, 128, 8192), f32)
    tpos = np.zeros((4, 8192), f32)
    vbias = np.full((4, 8192), NEG, f32)
    for s, k in enumerate(ksegs):
        if k is None:
            idx = np.zeros(8192, np.int64)
            valid = np.zeros(8192, bool)
            Dl = np.zeros(8192, np.int64)
        else:
            Dl = k * 4096 + d
            valid = (np.abs(Dl) <= L - 1) & (n != 4096)
            idx = np.where(valid, np.abs(Dl), 0)
        ang = (omega[idx][:, None] * fbv[None, :]).astype(f32)
        z = np.concatenate([tlin[idx][:, None], np.cos(ang), -np.sin(ang)], axis=1)
        zs[s, 0:33, :] = z[0:4096].T
        zs[s, 64:97, :] = z[4096:8192].T
        tpos[s] = tlin[idx]
        vbias[s] = np.where(valid, 0.0, NEG)
        maskT[s, 0:64, :] = (valid & (Dl >= 0)).astype(f32)[None, :]
        maskT[s, 64:128, :] = (valid & (Dl <= 0)).astype(f32)[None, :]
    def lay(a):
        a = a.reshape(4, 128, 64)
        return np.ascontiguousarray(np.transpose(a, (1, 0, 2)))
    return zs.astype(BF), maskT.astype(BF), lay(tpos), lay(vbias)


def _attn_bias(rpb, R0, rows):
    out = np.full((3, 8, 128, 6, 256), NEG, np.float32)
    qc = np.arange(64)
    kc = np.arange(64)
    cs = np.clip(qc - 8, 0, 48)
    colin = (kc[None, :] >= cs[:, None]) & (kc[None, :] < cs[:, None] + 16)
    dc = np.clip(kc[None, :] - qc[:, None], -15, 15) + 15
    for v, g in enumerate((0, 1, 15)):
        for rl in range(4):
            r = 4 * g + rl
            rg = R0 + r
            ws = int(np.clip(rg - 4, 0, rows - 8)) - R0 + 4
            for pr in range(6):
                for er in range(2):
                    e = 4 * g + 2 * pr + er
                    if not (0 <= e - ws < 8):
                        continue
                    drr = e - 4 - r + 7
                    blk = rpb[:, drr, :][:, dc]
                    blk = np.where(colin[None], blk, NEG)
                    out[v, :, er * 64:(er + 1) * 64, pr, rl * 64:(rl + 1) * 64] = np.transpose(blk, (0, 2, 1))
    return out.astype(BF)


def _col128(v, n):
    return np.ascontiguousarray(np.asarray(v, np.float32).reshape(n, 128).T)


def prepare_inputs(inputs, cores=range(8)):
    f32 = np.float32
    I = {k: np.asarray(v) for k, v in inputs.items()}
    E, F2, IT, IC = _const_tables()
    max_decay = math.log(1e-2) / 0.3
    min_decay = math.log(1e-2) / 1.5
    deltas = np.abs(np.linspace(min_decay, max_decay, 512, dtype=f32))
    common = {
        "w_in": I["w_in"][0], "w_br_attn": I["w_br_attn"][0], "w_br_hyena": I["w_br_hyena"][0],
        "w_out": I["w_out"][0], "w_gate": I["w_gate"][0], "w_up": I["w_up"][0], "w_down": I["w_down"][0],
        "g_mix": _col128(I["norm_mix"][0], 8), "g_ffn": _col128(I["norm_ffn"][0], 8),
        "g_final": np.ascontiguousarray(np.broadcast_to(I["norm_final"][None, :], (128, 1024))).astype(f32),
        "conv_w": np.ascontiguousarray(np.transpose(I["conv_w"][0].reshape(3, 12, 128), (2, 1, 0))).astype(f32),
        "conv_b": _col128(I["conv_b"][0], 12),
        "fw1": I["filt_w1"][0], "fw2": I["filt_w2"][0], "fw3": I["filt_w3"][0], "fw4": I["filt_w4"][0],
        "fb": np.ascontiguousarray(np.tile(np.stack([I["filt_b1"][0], I["filt_b2"][0], I["filt_b3"][0]], axis=1),
                                           (2, 1))).astype(f32),
        "ffr": np.ascontiguousarray(np.tile(I["filt_freq"][0][:, None], (2, 1))).astype(f32),
        "e0": np.eye(128, 1, dtype=f32),
        "hyd": np.ascontiguousarray(np.broadcast_to(I["hyena_d"][0][None, :], (128, 512))).astype(f32),
        "ident": np.eye(128, dtype=f32).astype(BF),
        "Etab": E, "F2tab": F2, "ITtab": IT, "ICtab": IC,
        "negdelta": np.ascontiguousarray(np.broadcast_to(-deltas[None, :], (128, 512))).astype(f32),
    }
    rpb = I["rpb"][0].astype(f32)
    xp = I["x_prompt"]
    xs = I["x_sample"][0]
    tab_prompt = _filter_tables(4096, [0, None, None, None])
    ab_prompt = _attn_bias(rpb, 0, 64)
    maps = []
    for c in cores:
        m = dict(common)
        xext = np.zeros((4608, 1024), f32)
        xctx = np.zeros((4, 4224, 1024), f32)
        if c < 4:
            xext[256:4352] = xp[c]
            xctx[0, :4096] = xp[c]
            zs, maskT, tpos, vbias = tab_prompt
            ab = ab_prompt
        else:
            j = c - 4
            L = 16384
            lo, hi = j * 4096 - 256, (j + 1) * 4096 + 256
            a, b = max(lo, 0), min(hi, L)
            xext[a - lo:b - lo] = xs[a:b]
            blks = [j] + [bb for bb in range(4) if bb != j]
            for i, bb in enumerate(blks):
                xctx[i, :4096] = xs[bb * 4096:(bb + 1) * 4096]
                if bb * 4096 - 1 >= 0:
                    xctx[i, 4096] = xs[bb * 4096 - 1]
                if (bb + 1) * 4096 < L:
                    xctx[i, 4097] = xs[(bb + 1) * 4096]
            zs, maskT, tpos, vbias = _filter_tables(L, [j - bb for bb in blks])
            ab = _attn_bias(rpb, j * 64, 256)
        m.update({"xext": xext, "xctx": xctx, "zs": zs, "maskT": maskT, "tpos": tpos, "vbias": vbias, "abias": ab})
        maps.append(m)
    return maps


_NC_CACHE = {}


def kernel(**inputs):
    if "nc" not in _NC_CACHE:
        _NC_CACHE["nc"] = build_program()
    nc = _NC_CACHE["nc"]
    maps = prepare_inputs(inputs)
    res = run_bass_kernel_spmd(nc, maps, core_ids=list(range(8)))
    ys = [np.asarray(res.results[i]["y"], dtype=np.float32) for i in range(8)]
    y_prompt = np.stack(ys[0:4], axis=0)
    y_sample = np.concatenate(ys[4:8], axis=0)[None]
    return (y_prompt, y_sample)
```

```python
import math
import numpy as np
import ml_dtypes
from contextlib import ExitStack
import concourse.bass as bass
import concourse.mybir as mybir
from concourse.bass_utils import run_bass_kernel_spmd

F32 = mybir.dt.float32
BF16 = mybir.dt.bfloat16
AF = mybir.ActivationFunctionType
ALU = mybir.AluOpType
BF = ml_dtypes.bfloat16

D_MODEL = 1024
D_IN = 5120
D_FF = 2816
NFF = 22
EPS = 1e-6
NEG = -30000.0
PI = float(np.pi)


class Dep:
    __slots__ = ("w", "r")

    def __init__(self):
        self.w = {}
        self.r = {}


class DSem:
    def __init__(self, sem):
        self.sem = sem
        self.n = 0
        self.nobar = False


class Prog:
    ENGS = ["sync", "scalar", "vector", "gpsimd", "tensor"]

    def __init__(self, nc, es):
        self.nc = nc
        self.h = {"sync": nc.sync, "scalar": nc.scalar, "vector": nc.vector,
                  "gpsimd": nc.gpsimd, "tensor": nc.tensor}
        self.sem = {e: es.enter_context(nc.semaphore("s_" + e)) for e in self.ENGS}
        self.cnt = {e: 0 for e in self.ENGS}
        self.known = {e: {} for e in self.ENGS}
        self.dsems = []
        self.es = es
        self.rr = 0

    def dsem(self, name):
        s = DSem(self.es.enter_context(self.nc.semaphore(name)))
        self.dsems.append(s)
        return s

    def _need(self, eng, items):
        k = self.known[eng]
        for it in items:
            if it is None:
                continue
            sem, val, e2 = it
            if e2 == eng and eng == "tensor":
                continue
            if k.get(id(sem), 0) >= val:
                continue
            k[id(sem)] = val
            self.h[eng].wait_ge(sem, val)

    def _items(self, reads, writes):
        items = []
        for d in reads:
            items.extend(d.w.values())
        for d in writes:
            items.extend(d.w.values())
            items.extend(d.r.values())
        return items

    def _upd(self, tok, reads, writes):
        key = id(tok[0])
        for d in reads:
            d.r[key] = tok
        for d in writes:
            d.w[key] = tok
            d.r = {}

    def op(self, eng, fn, r=(), w=()):
        self._need(eng, self._items(r, w))
        ins = fn(self.h[eng])
        self.cnt[eng] += 1
        ins.then_inc(self.sem[eng], 1)
        tok = (self.sem[eng], self.cnt[eng], eng)
        self._upd(tok, r, w)
        return tok

    def dma(self, eng, ds, out, in_, r=(), w=()):
        self._need(eng, self._items(r, w))
        ins = self.h[eng].dma_start(out=out, in_=in_)
        ds.n += 16
        ins.then_inc(ds.sem, 16)
        tok = (ds.sem, ds.n, "dma")
        self._upd(tok, r, w)
        return tok

    def barrier(self):
        toks = [(self.sem[e], self.cnt[e], e) for e in self.ENGS if self.cnt[e] > 0]
        toks += [(d.sem, d.n, "dma") for d in self.dsems if d.n > 0 and not d.nobar]
        for e in self.ENGS:
            k = self.known[e]
            for sem, val, e2 in toks:
                if e2 == e:
                    continue
                if k.get(id(sem), 0) >= val:
                    continue
                k[id(sem)] = val
                self.h[e].wait_ge(sem, val)

    def alt(self, a="vector", b="gpsimd"):
        self.rr += 1
        return a if self.rr % 2 else b


_UID = [0]


def pipeline(n_iter, stages):
    ns = len(stages)
    for t in range(n_iter + ns - 1):
        for k, f in enumerate(stages):
            i = t - k
            if 0 <= i < n_iter:
                f(i)


def T(es, nc, name, shape, dt):
    _UID[0] += 1
    return es.enter_context(nc.sbuf_tensor(f"sb{_UID[0]}_{name}", shape, dt))


def PS(es, nc, name, shape, dt):
    _UID[0] += 1
    return es.enter_context(nc.psum_tensor(f"ps{_UID[0]}_{name}", shape, dt))


def rms_transpose(P, nc, es, xrows, nsub, hT, d_hT, grep, d_grep, ident, d_ident, pfx, dq):
    NB = 3
    xs = [T(es, nc, f"{pfx}xs{i}", [128, 1024], F32) for i in range(NB)]
    d_xs = [Dep() for _ in range(NB)]
    xn = [T(es, nc, f"{pfx}xn{i}", [128, 1024], BF16) for i in range(NB)]
    d_xn = [Dep() for _ in range(NB)]
    junk = T(es, nc, f"{pfx}junk", [128, 1024], BF16)
    d_junk = Dep()
    stt = [T(es, nc, f"{pfx}st{i}", [128, 4], F32) for i in range(NB)]
    d_st = [Dep() for _ in range(NB)]
    pt = [PS(es, nc, f"{pfx}pt{i}", [128, 8, 128], BF16) for i in range(2)]
    d_pt = [Dep() for _ in range(2)]
    for i in range(NB):
        P.op("gpsimd", lambda e: e.memset(stt[i][:], EPS), w=[d_st[i]])

    def S1(st):
        b = st % NB
        P.dma("sync", dq[b], xs[b][:], xrows[st * 128:(st + 1) * 128, :], w=[d_xs[b]])
        P.op("scalar", lambda e: e.activation(out=junk[:], in_=xs[b][:], func=AF.Square,
                                              accum_out=stt[b][:, 0:1]),
             r=[d_xs[b]], w=[d_junk, d_st[b]])
        P.op("scalar", lambda e: e.activation(out=stt[b][:, 1:2], in_=stt[b][:, 0:1], func=AF.Sqrt,
                                              scale=1.0 / 1024, bias=stt[b][:, 3:4]),
             r=[d_st[b]], w=[d_st[b]])

    def S2(st):
        b = st % NB
        P.op("vector", lambda e: e.reciprocal(out=stt[b][:, 2:3], in_=stt[b][:, 1:2]),
             r=[d_st[b]], w=[d_st[b]])
        if st % 2:
            P.op("scalar", lambda e: e.activation(out=xn[b][:], in_=xs[b][:], func=AF.Copy, scale=stt[b][:, 2:3]),
                 r=[d_st[b], d_xs[b]], w=[d_xn[b]])
        else:
            P.op("vector", lambda e: e.tensor_scalar(out=xn[b][:], in0=xs[b][:], scalar1=stt[b][:, 2:3],
                                                     scalar2=None, op0=ALU.mult),
                 r=[d_st[b], d_xs[b]], w=[d_xn[b]])

    def S3(st):
        b = st % NB
        p = st % 2
        for kc in range(8):
            P.op("tensor", lambda e: e.transpose(pt[p][:, kc, :], xn[b][:, kc * 128:(kc + 1) * 128], ident[:]),
                 r=[d_xn[b], d_ident], w=[d_pt[p]])
        P.op("vector", lambda e: e.tensor_tensor(out=hT[:, :, st * 128:(st + 1) * 128], in0=pt[p][:],
                                                 in1=grep[:], op=ALU.mult),
             r=[d_pt[p], d_grep], w=[d_hT])

    pipeline(nsub, [S1, S2, S3])


def make_grep(P, nc, es, gcol, name, dq):
    g = T(es, nc, name + "g", [128, 8], F32)
    d_g = Dep()
    grep = T(es, nc, name + "rep", [128, 8, 128], F32)
    d_grep = Dep()
    P.dma("sync", dq, g[:], gcol, w=[d_g])
    P.op("gpsimd", lambda e: e.memset(grep[:], 1.0), w=[d_grep])
    for kc in range(8):
        P.op("vector", lambda e: e.tensor_scalar(out=grep[:, kc, :], in0=grep[:, kc, :],
                                                 scalar1=g[:, kc:kc + 1], scalar2=None, op0=ALU.mult),
             r=[d_g], w=[d_grep])
    return grep, d_grep


def load_ident(P, nc, es, D, dq):
    ident = T(es, nc, "ident", [128, 128], BF16)
    d_ident = Dep()
    P.dma("sync", dq, ident[:], D["ident"], w=[d_ident])
    return ident, d_ident


def eps_init(P, stt_list):
    pass


def build_program(dev=False, phases="ABCDEFGHI"):
    nc = bass.Bass("TRN2", target_bir_lowering=False)
    D = {}

    def inp(name, shape, dt=F32):
        D[name] = nc.dram_tensor(name, list(shape), dt, kind="ExternalInput").ap()

    def scr(name, shape, dt=BF16):
        if dev:
            D[name] = nc.dram_tensor(name, list(shape), dt, kind="ExternalOutput").ap()
        else:
            D[name] = nc.dram_tensor(name, list(shape), dt).ap()

    inp("xext", [4608, 1024])
    inp("xctx", [4, 4224, 1024])
    inp("w_in", [1024, D_IN])
    inp("w_br_attn", [512, 1024])
    inp("w_br_hyena", [512, 1024])
    inp("w_out", [1024, 1024])
    inp("w_gate", [1024, D_FF])
    inp("w_up", [1024, D_FF])
    inp("w_down", [D_FF, 1024])
    inp("g_mix", [128, 8])
    inp("g_ffn", [128, 8])
    inp("g_final", [128, 1024])
    inp("conv_w", [128, 12, 3])
    inp("conv_b", [128, 12])
    inp("fw1", [33, 64])
    inp("fw2", [64, 64])
    inp("fw3", [64, 64])
    inp("fw4", [64, 1024])
    inp("fb", [128, 3])
    inp("ffr", [128, 1])
    inp("hyd", [128, 512])
    inp("ident", [128, 128], BF16)
    inp("Etab", [128, 64, 2, 128], BF16)
    inp("F2tab", [128, 3, 128], BF16)
    inp("ITtab", [128, 33, 3, 128], BF16)
    inp("ICtab", [66, 2, 64], BF16)
    inp("zs", [4, 128, 4096], BF16)
    inp("maskT", [4, 128, 8192], BF16)
    inp("tpos", [128, 4, 64])
    inp("vbias", [128, 4, 64])
    inp("e0", [128, 1])
    inp("negdelta", [128, 512])
    inp("abias", [3, 8, 128, 6, 256], BF16)

    scr("QT", [4, 128, 4096])
    scr("KT", [4, 128, 4608])
    scr("VA", [36, 128, 8 * 65])
    scr("X0T", [4, 128, 4096])
    scr("DU", [4, 64, 4, 64, 128])
    scr("AU", [4, 64, 66, 2, 512])
    scr("AG", [4, 64, 66, 2, 512])
    scr("CS", [66, 64, 2, 512])
    scr("YHT", [4, 128, 4096])
    scr("YAT", [4, 128, 4096])
    for nm, shp in (("WG", [16, 128, 8, 128]), ("WBA", [8, 128, 4, 128]), ("WBH", [8, 128, 4, 128]),
                    ("WO", [128, 8, 1024]), ("WFG", [22, 128, 8, 128]), ("WFU", [22, 128, 8, 128]),
                    ("WD", [2, 128, 22, 512])):
        D[nm] = nc.dram_tensor(nm, shp, BF16).ap()
    D["y"] = nc.dram_tensor("y", [4096, 1024], F32, kind="ExternalOutput").ap()

    with ExitStack() as es0:
        P = Prog(nc, es0)
        dq = [P.dsem(f"dq{i}") for i in range(64)]
        gq = [P.dsem(f"gq{i}") for i in range(16)]
        wsem = [P.dsem(f"wq{i}") for i in range(4)]
        if "A" in phases:
            phase_A(P, nc, D, dq, gq)
            P.barrier()
        if "B" in phases:
            phase_B(P, nc, D, dq, gq)
            P.barrier()
        if "C" in phases:
            phase_C(P, nc, D, dq, gq, (lambda: phase_W(P, nc, D, wsem)) if "I" in phases else None)
        elif "I" in phases:
            phase_W(P, nc, D, wsem)
            P.barrier()
        if "D" in phases:
            phase_D(P, nc, D, dq, gq)
            P.barrier()
        if "F" in phases:
            phase_F(P, nc, D, dq, gq)
            P.barrier()
        if "G" in phases:
            phase_G(P, nc, D, dq, gq)
            P.barrier()
        if "H" in phases:
            phase_H(P, nc, D, dq, gq)
            P.barrier()
        if "I" in phases:
            for w in wsem:
                w.nobar = False
            P.barrier()
            phase_I(P, nc, D, dq, gq)
            P.barrier()
    return nc


def wchunk_src(w, col0, ncols, nk=8):
    return w.rearrange("(kc p) n -> p kc n", p=128)[:, :, col0:col0 + ncols]


def phase_A(P, nc, D, dq, gq):
    with ExitStack() as es:
        ident, d_ident = load_ident(P, nc, es, D, dq[0])
        grep, d_grep = make_grep(P, nc, es, D["g_mix"], "gm", dq[1])
        hT = T(es, nc, "A_hT", [128, 8, 4608], BF16)
        d_hT = Dep()
        with ExitStack() as es1:
            rms_transpose(P, nc, es1, D["xext"], 36, hT, d_hT, grep, d_grep, ident, d_ident, "A_", dq[40:43])
        P.barrier()
        wq = [T(es, nc, f"A_wq{i}", [128, 8, 128], BF16) for i in range(2)]
        d_wq = [Dep() for _ in range(2)]
        stg = [T(es, nc, f"A_stg{i}", [128, 4608], BF16) for i in range(2)]
        d_stg = [Dep() for _ in range(2)]
        ps = [PS(es, nc, f"A_ps{i}", [128, 512], F32) for i in range(4)]
        d_ps = [Dep() for _ in range(4)]
        pi = 0
        for ci in range(8):
            b = ci % 2
            isq = ci < 4
            P.dma("gpsimd", gq[4 + b], wq[b][:], wchunk_src(D["w_in"], ci * 128, 128), w=[d_wq[b]])
            t0, ntile = (256, 8) if isq else (0, 9)
            for tt in range(ntile):
                p = pi % 4
                pi += 1
                for kc in range(8):
                    P.op("tensor", lambda e: e.matmul(ps[p][:], wq[b][:, kc, :],
                                                      hT[:, kc, t0 + tt * 512:t0 + (tt + 1) * 512],
                                                      start=(kc == 0), stop=(kc == 7)),
                         r=[d_wq[b], d_hT], w=[d_ps[p]])
                if tt % 2 == 0:
                    P.op("scalar", lambda e: e.activation(out=stg[b][:, tt * 512:(tt + 1) * 512], in_=ps[p][:],
                                                          func=AF.Copy, scale=(0.125 if isq else 1.0)),
                         r=[d_ps[p]], w=[d_stg[b]])
                else:
                    P.op("vector", lambda e: e.tensor_scalar(out=stg[b][:, tt * 512:(tt + 1) * 512], in0=ps[p][:],
                                                             scalar1=(0.125 if isq else 1.0), scalar2=None,
                                                             op0=ALU.mult),
                         r=[d_ps[p]], w=[d_stg[b]])
            if isq:
                P.dma("sync", dq[6 + b], D["QT"][ci], stg[b][:, 0:4096], r=[d_stg[b]])
            else:
                P.dma("sync", dq[6 + b], D["KT"][ci - 4], stg[b][:, 0:4608], r=[d_stg[b]])
        wv = T(es, nc, "A_wv", [128, 8, 512], BF16)
        d_wv = Dep()
        P.dma("gpsimd", gq[8], wv[:], wchunk_src(D["w_in"], 1024, 512), w=[d_wv])
        vst = [T(es, nc, f"A_vst{i}", [128, 8, 65], BF16) for i in range(2)]
        d_vst = [Dep() for _ in range(2)]
        for i in range(2):
            P.op("gpsimd", lambda e: e.memset(vst[i][:], 1.0), w=[d_vst[i]])
        for st in range(36):
            p = pi % 4
            pi += 1
            b = st % 2
            for kc in range(8):
                P.op("tensor", lambda e: e.matmul(ps[p][:], hT[:, kc, st * 128:(st + 1) * 128], wv[:, kc, :],
                                                  start=(kc == 0), stop=(kc == 7)),
                     r=[d_wv, d_hT], w=[d_ps[p]])
            P.op("scalar" if st % 2 else "vector",
                 (lambda e: e.activation(out=vst[b][:, :, 0:64], in_=ps[p][:].rearrange("p (h d) -> p h d", d=64),
                                         func=AF.Copy)) if st % 2 else
                 (lambda e: e.tensor_copy(out=vst[b][:, :, 0:64], in_=ps[p][:].rearrange("p (h d) -> p h d", d=64))),
                 r=[d_ps[p]], w=[d_vst[b]])
            P.dma("sync", dq[9 + b], D["VA"][st], vst[b][:].rearrange("p h d -> p (h d)"), r=[d_vst[b]])


def phase_B(P, nc, D, dq, gq):
    with ExitStack() as es:
        ident, d_ident = load_ident(P, nc, es, D, dq[0])
        grep, d_grep = make_grep(P, nc, es, D["g_mix"], "gm", dq[1])
        cw = T(es, nc, "B_cw", [128, 12, 3], F32)
        cb = T(es, nc, "B_cb", [128, 12], F32)
        d_c = Dep()
        P.dma("sync", dq[2], cw[:], D["conv_w"], w=[d_c])
        P.dma("sync", dq[3], cb[:], D["conv_b"], w=[d_c])
        hT = T(es, nc, "B_hT", [128, 8, 4224], BF16)
        d_hT = Dep()
        hy = [T(es, nc, f"B_hy{i}", [128, 4224], F32) for i in range(2)]
        d_hy = [Dep() for _ in range(2)]
        cx = [T(es, nc, f"B_cx{i}", [128, 4096], BF16) for i in range(3)]
        d_cx = [Dep() for _ in range(3)]
        uT = [T(es, nc, f"B_uT{i}", [128, 4096], BF16) for i in range(2)]
        d_uT = [Dep() for _ in range(2)]
        du = T(es, nc, "B_du", [64, 64, 128], BF16)
        d_du = Dep()
        wq = [T(es, nc, f"B_wq{i}", [128, 8, 128], BF16) for i in range(2)]
        d_wq = [Dep() for _ in range(2)]
        ps = [PS(es, nc, f"B_ps{i}", [128, 512], F32) for i in range(4)]
        d_ps = [Dep() for _ in range(4)]
        ptr = [PS(es, nc, f"B_ptr{i}", [64, 8, 128], BF16) for i in range(2)]
        d_ptr = [Dep() for _ in range(2)]
        st8 = {"pi": 0, "wi": 0}
        for blk in range(4):
            with ExitStack() as es1:
                P.barrier()
                rms_transpose(P, nc, es1, D["xctx"][blk], 33, hT, d_hT, grep, d_grep, ident, d_ident,
                              f"B{blk}_", dq[40:43])
            P.barrier()
            items = []
            for j in range(4):
                items.append((1, 2048 + j * 128, 4 + j, j))
                items.append((2, 2560 + j * 128, 8 + j, j))
                if blk == 0:
                    items.append((0, 1536 + j * 128, j, j))
            base = st8["wi"]

            def S1(c, items=items, base=base):
                role, col0, cidx, j = items[c]
                b = (base + c) % 2
                k3 = (base + c) % 3
                P.dma("gpsimd", gq[b], wq[b][:], wchunk_src(D["w_in"], col0, 128), w=[d_wq[b]])
                for tt in range(9):
                    p = st8["pi"] % 4
                    st8["pi"] += 1
                    n = 512 if tt < 8 else 128
                    for kc in range(8):
                        P.op("tensor", lambda e: e.matmul(ps[p][:, 0:n], wq[b][:, kc, :],
                                                          hT[:, kc, tt * 512:tt * 512 + n],
                                                          start=(kc == 0), stop=(kc == 7)),
                             r=[d_wq[b], d_hT], w=[d_ps[p]])
                    P.op("scalar", lambda e: e.activation(out=hy[b][:, tt * 512:tt * 512 + n], in_=ps[p][:, 0:n],
                                                          func=AF.Copy),
                         r=[d_ps[p]], w=[d_hy[b]])
                    if tt < 8:
                        P.op("scalar", lambda e: e.activation(out=cx[k3][:, tt * 512:(tt + 1) * 512], in_=ps[p][:],
                                                              func=AF.Identity, scale=cw[:, cidx, 1:2],
                                                              bias=cb[:, cidx:cidx + 1]),
                             r=[d_ps[p], d_c], w=[d_cx[k3]])

            def S2(c, items=items, base=base):
                role, col0, cidx, j = items[c]
                b = (base + c) % 2
                k3 = (base + c) % 3
                cc = cx[k3]
                dc = d_cx[k3]
                h = hy[b]
                P.op("vector", lambda e: e.scalar_tensor_tensor(out=cc[:, 1:4096], in0=h[:, 0:4095],
                                                                scalar=cw[:, cidx, 0:1], in1=cc[:, 1:4096],
                                                                op0=ALU.mult, op1=ALU.add),
                     r=[d_hy[b], d_c], w=[dc])
                P.op("vector", lambda e: e.scalar_tensor_tensor(out=cc[:, 0:4095], in0=h[:, 1:4096],
                                                                scalar=cw[:, cidx, 2:3], in1=cc[:, 0:4095],
                                                                op0=ALU.mult, op1=ALU.add),
                     r=[d_hy[b], d_c], w=[dc])
                P.op("vector", lambda e: e.scalar_tensor_tensor(out=cc[:, 0:1], in0=h[:, 4096:4097],
                                                                scalar=cw[:, cidx, 0:1], in1=cc[:, 0:1],
                                                                op0=ALU.mult, op1=ALU.add),
                     r=[d_hy[b], d_c], w=[dc])
                P.op("vector", lambda e: e.scalar_tensor_tensor(out=cc[:, 4095:4096], in0=h[:, 4097:4098],
                                                                scalar=cw[:, cidx, 2:3], in1=cc[:, 4095:4096],
                                                                op0=ALU.mult, op1=ALU.add),
                     r=[d_hy[b], d_c], w=[dc])
                if role == 2:
                    kx = (base + c - 1) % 3
                    P.op("vector", lambda e: e.tensor_tensor(out=uT[j % 2][:], in0=cx[kx][:], in1=cc[:], op=ALU.mult),
                         r=[d_cx[kx], dc], w=[d_uT[j % 2]])

            def S3(c, items=items, base=base, blk=blk):
                role, col0, cidx, j = items[c]
                k3 = (base + c) % 3
                if role == 2:
                    uv = uT[j % 2][:].rearrange("p (a b) -> p a b", b=64)
                    for g8 in range(8):
                        q = g8 % 2
                        for tl in range(8):
                            t2 = g8 * 8 + tl
                            P.op("tensor", lambda e: e.transpose(ptr[q][:, tl, :], uv[:, :, t2], ident[:]),
                                 r=[d_uT[j % 2], d_ident], w=[d_ptr[q]])
                        P.op("scalar" if g8 % 2 else "vector",
                             (lambda e: e.activation(out=du[:, g8 * 8:(g8 + 1) * 8, :], in_=ptr[q][:], func=AF.Copy))
                             if g8 % 2 else
                             (lambda e: e.tensor_copy(out=du[:, g8 * 8:(g8 + 1) * 8, :], in_=ptr[q][:])),
                             r=[d_ptr[q]], w=[d_du])
                    P.dma("sync", dq[8], D["DU"][blk, :, j], du[:], r=[d_du])
                if role == 0:
                    P.dma("sync", dq[9], D["X0T"][j], cx[k3][:], r=[d_cx[k3]])

            pipeline(len(items), [S1, S2, S3])
            st8["wi"] += len(items)


def fft_stage1(P, nc, es, K, rhs_of, d_src, Et, d_Et, Ascr, dq, pfx, nps=4):
    ps = [PS(es, nc, f"{pfx}s1p{i}", [128, 512], F32) for i in range(nps)]
    d_ps = [Dep() for _ in range(nps)]
    ast = [T(es, nc, f"{pfx}ast{i}", [66, 2, 512], BF16) for i in range(3)]
    d_ast = [Dep() for _ in range(3)]
    for t2 in range(64):
        a = t2 % 3
        for ri in range(2):
            p = (t2 * 2 + ri) % nps
            P.op("tensor", lambda e: e.matmul(ps[p][0:66, :], Et[0:K, t2, ri, 0:66], rhs_of(t2), start=True, stop=True),
                 r=[d_Et, d_src], w=[d_ps[p]])
            if ri == 0:
                P.op("scalar", lambda e: e.activation(out=ast[a][:, ri, :], in_=ps[p][0:66, :], func=AF.Copy),
                     r=[d_ps[p]], w=[d_ast[a]])
            else:
                P.op("vector", lambda e: e.tensor_copy(out=ast[a][:, ri, :], in_=ps[p][0:66, :]),
                     r=[d_ps[p]], w=[d_ast[a]])
        P.dma("sync", dq[a], Ascr[t2], ast[a][:], r=[d_ast[a]])


def sin_group(P, pm, d_pm, tmp, d_tmp, t1, d_t1, fr, frb, d_f, out, d_out):
    P.op("scalar", lambda e: e.activation(out=tmp[:], in_=pm[:], func=AF.Identity, scale=fr, bias=frb),
         r=[d_pm, d_f], w=[d_tmp])
    P.op("vector", lambda e: e.tensor_scalar(out=t1[:], in0=tmp[:], scalar1=-1.0, scalar2=PI,
                                             op0=ALU.mult, op1=ALU.add),
         r=[d_tmp], w=[d_t1])
    P.op("vector", lambda e: e.tensor_tensor(out=tmp[:], in0=tmp[:], in1=t1[:], op=ALU.min),
         r=[d_t1], w=[d_tmp])
    P.op("vector", lambda e: e.scalar_tensor_tensor(out=tmp[:], in0=t1[:], scalar=-2 * PI, in1=tmp[:],
                                                    op0=ALU.add, op1=ALU.max),
         r=[d_t1], w=[d_tmp])
    P.op("scalar", lambda e: e.activation(out=out, in_=tmp[:], func=AF.Sin), r=[d_tmp], w=[d_out])


def phase_C(P, nc, D, dq, gq, after_w=None):
    with ExitStack() as es:
        Et = T(es, nc, "C_Et", [128, 64, 2, 128], BF16)
        d_Et = Dep()
        P.dma("sync", dq[0], Et[:], D["Etab"], w=[d_Et])
        W1 = T(es, nc, "C_W1", [128, 128], BF16)
        W2 = T(es, nc, "C_W2", [128, 128], BF16)
        W3 = T(es, nc, "C_W3", [128, 128], BF16)
        W4 = T(es, nc, "C_W4", [128, 512], BF16)
        d_w = Dep()
        P.op("gpsimd", lambda e: e.memset(W1[:], 0.0), w=[d_w])
        P.op("gpsimd", lambda e: e.memset(W2[:], 0.0), w=[d_w])
        for hf in range(2):
            P.dma("gpsimd", gq[0], W1[hf * 64:hf * 64 + 33, hf * 64:(hf + 1) * 64], D["fw1"], w=[d_w])
            P.dma("gpsimd", gq[1], W2[hf * 64:(hf + 1) * 64, hf * 64:(hf + 1) * 64], D["fw2"], w=[d_w])
            for h2 in range(2):
                P.dma("gpsimd", gq[2], W3[hf * 64:(hf + 1) * 64, h2 * 64:(h2 + 1) * 64], D["fw3"], w=[d_w])
            P.dma("gpsimd", gq[3], W4[hf * 64:(hf + 1) * 64, :], D["fw4"][:, hf * 512:(hf + 1) * 512], w=[d_w])
        if after_w is not None:
            after_w()
        fb = T(es, nc, "C_fb", [128, 3], F32)
        fr = T(es, nc, "C_fr", [128, 1], F32)
        frb = T(es, nc, "C_frb", [128, 3], F32)
        d_f = Dep()
        P.dma("sync", dq[5], fb[:], D["fb"], w=[d_f])
        P.dma("sync", dq[6], fr[:], D["ffr"], w=[d_f])
        P.op("vector", lambda e: e.tensor_scalar(out=frb[:], in0=fb[:], scalar1=fr[:, 0:1], scalar2=None,
                                                 op0=ALU.mult), r=[d_f], w=[d_f])
        tpos = T(es, nc, "C_tpos", [128, 4, 64], F32)
        vbias = T(es, nc, "C_vbias", [128, 4, 64], F32)
        e0 = T(es, nc, "C_e0", [128, 1], F32)
        negd = T(es, nc, "C_negd", [128, 512], F32)
        hyd = T(es, nc, "C_hyd", [128, 512], F32)
        d_t = Dep()
        P.dma("sync", dq[7], tpos[:], D["tpos"], w=[d_t])
        P.dma("sync", dq[8], vbias[:], D["vbias"], w=[d_t])
        P.dma("sync", dq[9], e0[:], D["e0"], w=[d_t])
        P.dma("sync", dq[10], negd[:], D["negdelta"], w=[d_t])
        P.dma("sync", dq[11], hyd[:], D["hyd"], w=[d_t])
        zs = T(es, nc, "C_zs", [128, 4096], BF16)
        d_z = Dep()
        mk = T(es, nc, "C_mk", [128, 8192], BF16)
        d_mk = Dep()
        hA = T(es, nc, "C_hA", [128, 4096], BF16)
        hB = T(es, nc, "C_hB", [128, 4096], BF16)
        d_hAg = [Dep() for _ in range(4)]
        d_hBg = [Dep() for _ in range(4)]
        h3 = T(es, nc, "C_h3", [128, 8192], BF16)
        d_h3g = [Dep() for _ in range(8)]
        tmp = [T(es, nc, f"C_tmp{i}", [128, 1024], F32) for i in range(2)]
        t1b = [T(es, nc, f"C_t1{i}", [128, 1024], F32) for i in range(2)]
        d_tmp = [Dep() for _ in range(2)]
        d_t1b = [Dep() for _ in range(2)]
        gi = 0
        gf = [T(es, nc, f"C_gf{i}", [128, 512], BF16) for i in range(3)]
        d_gf = [Dep() for _ in range(3)]
        ast = [T(es, nc, f"C_ast{i}", [66, 2, 512], BF16) for i in range(3)]
        d_ast = [Dep() for _ in range(3)]
        ps1 = [PS(es, nc, f"C_ps1{i}", [128, 512], F32) for i in range(2)]
        d_ps1 = [Dep() for _ in range(2)]
        dec = [T(es, nc, f"C_dec{i}", [128, 512], F32) for i in range(2)]
        d_dec = [Dep() for _ in range(2)]
        gt = T(es, nc, "C_gt", [128, 512], F32)
        d_gt = Dep()
        pm = [PS(es, nc, f"C_pm{i}", [128, 1024], F32) for i in range(2)]
        d_pm = [Dep() for _ in range(2)]
        p4 = [PS(es, nc, f"C_p4{i}", [128, 512], F32) for i in range(2)]
        d_p4 = [Dep() for _ in range(2)]
        for s in range(4):
            P.dma("sync", dq[12], zs[:], D["zs"][s], w=[d_z])
            P.dma("sync", dq[13], mk[:], D["maskT"][s], w=[d_mk])
            groups = []
            for g in range(4):
                groups.append((W1[:], zs[:, g * 1024:(g + 1) * 1024], [d_z], hA[:, g * 1024:(g + 1) * 1024],
                               d_hAg[g], 0, None))
            for g in range(4):
                groups.append((W2[:], hA[:, g * 1024:(g + 1) * 1024], [d_hAg[g]], hB[:, g * 1024:(g + 1) * 1024],
                               d_hBg[g], 1, None))
            for g in range(8):
                hf = g // 4
                groups.append((W3[hf * 64:(hf + 1) * 64, :],
                               hB[hf * 64:(hf + 1) * 64, (g % 4) * 1024:(g % 4 + 1) * 1024], [d_hBg[g % 4]],
                               h3[:, g * 1024:(g + 1) * 1024], d_h3g[g], 2, mk[:, g * 1024:(g + 1) * 1024]))

            def M1(i):
                Wm, src, dsrc, dst, ddst, li, mask = groups[i]
                b = i % 2
                for k in range(2):
                    P.op("tensor", lambda e: e.matmul(pm[b][:, k * 512:(k + 1) * 512], Wm, src[:, k * 512:(k + 1) * 512],
                                                      start=True, stop=True),
                         r=[d_w] + dsrc, w=[d_pm[b]])
                P.op("scalar", lambda e: e.activation(out=tmp[b][:], in_=pm[b][:], func=AF.Identity,
                                                      scale=fr[:, 0:1], bias=frb[:, li:li + 1]),
                     r=[d_pm[b], d_f], w=[d_tmp[b]])

            def M2(i):
                b = i % 2
                P.op("vector", lambda e: e.tensor_scalar(out=t1b[b][:], in0=tmp[b][:], scalar1=-1.0, scalar2=PI,
                                                         op0=ALU.mult, op1=ALU.add),
                     r=[d_tmp[b]], w=[d_t1b[b]])
                P.op("vector", lambda e: e.tensor_tensor(out=tmp[b][:], in0=tmp[b][:], in1=t1b[b][:], op=ALU.min),
                     r=[d_t1b[b]], w=[d_tmp[b]])
                P.op("vector", lambda e: e.scalar_tensor_tensor(out=t1b[b][:], in0=t1b[b][:], scalar=-2 * PI,
                                                                in1=tmp[b][:], op0=ALU.add, op1=ALU.max),
                     r=[d_tmp[b]], w=[d_t1b[b]])

            def M3(i):
                Wm, src, dsrc, dst, ddst, li, mask = groups[i]
                b = i % 2
                P.op("scalar", lambda e: e.activation(out=dst, in_=t1b[b][:], func=AF.Sin), r=[d_t1b[b]], w=[ddst])
                if mask is not None:
                    P.op("vector", lambda e: e.tensor_tensor(out=dst, in0=dst, in1=mask, op=ALU.mult),
                         r=[d_mk], w=[ddst])

            pipeline(16, [M1, M2, M3])
            h3v = h3[:].rearrange("p (a b) -> p a b", b=64)

            def L1(t2, s=s, h3v=h3v):
                b = t2 % 2
                P.op("tensor", lambda e: e.matmul(p4[b][:], h3v[:, :, t2], W4[:], start=True, stop=True),
                     r=d_h3g + [d_w], w=[d_p4[b]])
                P.op("scalar", lambda e: e.activation(out=dec[b][:], in_=negd[:], func=AF.Exp,
                                                      scale=tpos[:, s, t2:t2 + 1], bias=vbias[:, s, t2:t2 + 1]),
                     r=[d_t], w=[d_dec[b]])
                k = t2 % 3
                if s == 0 and t2 == 0:
                    P.op("vector", lambda e: e.tensor_tensor(out=gt[:], in0=p4[b][:], in1=dec[b][:], op=ALU.mult),
                         r=[d_p4[b], d_dec[b]], w=[d_gt])
                    P.op("vector", lambda e: e.scalar_tensor_tensor(out=gf[k][:], in0=hyd[:], scalar=e0[:, 0:1],
                                                                    in1=gt[:], op0=ALU.mult, op1=ALU.add),
                         r=[d_t, d_gt], w=[d_gf[k]])
                else:
                    P.op("vector", lambda e: e.tensor_tensor(out=gf[k][:], in0=p4[b][:], in1=dec[b][:], op=ALU.mult),
                         r=[d_p4[b], d_dec[b]], w=[d_gf[k]])

            def L2(t2, s=s):
                k = t2 % 3
                a = t2 % 3
                for ri in range(2):
                    p = (t2 * 2 + ri) % 2
                    P.op("tensor", lambda e: e.matmul(ps1[p][0:66, :], Et[:, t2, ri, 0:66], gf[k][:], start=True, stop=True),
                         r=[d_Et, d_gf[k]], w=[d_ps1[p]])
                    if ri == 0:
                        P.op("scalar", lambda e: e.activation(out=ast[a][:, ri, :], in_=ps1[p][0:66, :], func=AF.Copy),
                             r=[d_ps1[p]], w=[d_ast[a]])
                    else:
                        P.op("vector", lambda e: e.tensor_copy(out=ast[a][:, ri, :], in_=ps1[p][0:66, :]),
                             r=[d_ps1[p]], w=[d_ast[a]])
                P.dma("sync", dq[14 + a], D["AG"][s][t2], ast[a][:], r=[d_ast[a]])

            pipeline(64, [L1, L2])


def phase_D(P, nc, D, dq, gq):
    with ExitStack() as es:
        Et = T(es, nc, "D_Et", [64, 64, 2, 128], BF16)
        d_Et = Dep()
        P.dma("sync", dq[0], Et[:], D["Etab"][0:64], w=[d_Et])
        Du = [T(es, nc, f"D_Du{i}", [64, 4, 64, 128], BF16) for i in range(2)]
        d_Du = [Dep() for _ in range(2)]
        ps = [PS(es, nc, f"D_s1p{i}", [128, 512], F32) for i in range(6)]
        d_ps = [Dep() for _ in range(6)]
        ast = [T(es, nc, f"D_ast{i}", [66, 2, 512], BF16) for i in range(4)]
        d_ast = [Dep() for _ in range(4)]
        P.dma("sync", dq[1], Du[0][:], D["DU"][0], w=[d_Du[0]])
        it = 0
        for blk in range(4):
            b = blk % 2
            if blk + 1 < 4:
                P.dma("sync", dq[1 + (blk + 1) % 2], Du[(blk + 1) % 2][:], D["DU"][blk + 1], w=[d_Du[(blk + 1) % 2]])
            for t2 in range(64):
                a = it % 4
                for ri in range(2):
                    p = (it * 2 + ri) % 6
                    P.op("tensor", lambda e: e.matmul(ps[p][0:66, :], Et[:, t2, ri, 0:66], Du[b][:, :, t2, :],
                                                      start=True, stop=True),
                         r=[d_Et, d_Du[b]], w=[d_ps[p]])
                    if ri == 0:
                        P.op("scalar", lambda e: e.activation(out=ast[a][:, ri, :], in_=ps[p][0:66, :], func=AF.Copy),
                             r=[d_ps[p]], w=[d_ast[a]])
                    else:
                        P.op("vector", lambda e: e.tensor_copy(out=ast[a][:, ri, :], in_=ps[p][0:66, :]),
                             r=[d_ps[p]], w=[d_ast[a]])
                P.dma("sync", dq[3 + a], D["AU"][blk][t2], ast[a][:], r=[d_ast[a]])
                it += 1


KG = 3
NQ = 11


def stage2_mm(P, F2, d_F2, Bt, d_B, kk, out, d_out):
    P.op("tensor", lambda e: e.matmul(out[:, 0, :], F2[:, 0, :], Bt[:, kk, 0, :], start=True, stop=False),
         r=[d_F2, d_B], w=[d_out])
    P.op("tensor", lambda e: e.matmul(out[:, 0, :], F2[:, 2, :], Bt[:, kk, 1, :], start=False, stop=True),
         r=[d_F2, d_B], w=[d_out])
    P.op("tensor", lambda e: e.matmul(out[:, 1, :], F2[:, 1, :], Bt[:, kk, 0, :], start=True, stop=False),
         r=[d_F2, d_B], w=[d_out])
    P.op("tensor", lambda e: e.matmul(out[:, 1, :], F2[:, 0, :], Bt[:, kk, 1, :], start=False, stop=True),
         r=[d_F2, d_B], w=[d_out])


def phase_F(P, nc, D, dq, gq):
    with ExitStack() as es:
        ident, d_ident = load_ident(P, nc, es, D, dq[0])
        F2 = T(es, nc, "F_F2", [128, 3, 128], BF16)
        d_F2 = Dep()
        P.dma("sync", dq[1], F2[:], D["F2tab"], w=[d_F2])
        IT = [T(es, nc, f"F_IT{i}", [128, KG, 3, 128], BF16) for i in range(2)]
        d_IT = [Dep() for _ in range(2)]
        Bg = [[T(es, nc, f"F_Bg{i}_{s}", [128, KG, 2, 512], BF16) for s in range(4)] for i in range(2)]
        Bu = [[T(es, nc, f"F_Bu{i}_{s}", [128, KG, 2, 512], BF16) for s in range(4)] for i in range(2)]
        d_Bg = [[Dep() for s in range(4)] for i in range(2)]
        d_Bu = [[Dep() for s in range(4)] for i in range(2)]
        Gs = [T(es, nc, f"F_Gs{i}", [128, 3, 512], BF16) for i in range(2)]
        d_Gs = [Dep() for _ in range(2)]
        TA = [T(es, nc, f"F_TA{i}", [128, 2, 512], BF16) for i in range(2)]
        d_TA = [Dep() for _ in range(2)]
        TB = [T(es, nc, f"F_TB{i}", [128, 2, 512], BF16) for i in range(2)]
        d_TB = [Dep() for _ in range(2)]
        Yb = [T(es, nc, f"F_Yb{i}", [128, 2, 512], BF16) for i in range(2)]
        d_Yb = [Dep() for _ in range(2)]
        Cst = [T(es, nc, f"F_C{i}", [128, 2, 512], BF16) for i in range(2)]
        d_C = [Dep() for _ in range(2)]
        Gp = PS(es, nc, "F_Gp", [128, 2, 512], F32)
        d_Gp = Dep()
        Up = [PS(es, nc, f"F_Up{i}", [128, 2, 512], F32) for i in range(2)]
        d_Up = [Dep() for _ in range(2)]
        Yp = PS(es, nc, "F_Yp", [128, 2, 512], F32)
        d_Yp = Dep()
        def loads(q):
            qb = q % 2
            P.dma("sync", dq[2 + qb], IT[qb][:], D["ITtab"][:, q * KG:(q + 1) * KG], w=[d_IT[qb]])
            for s in range(4):
                for kp in range(2):
                    P.dma("sync", dq[24 + qb * 16 + s * 2 + kp], Bg[qb][s][kp * 64:(kp + 1) * 64],
                          D["AG"][s][:, kp * 33 + q * KG:kp * 33 + (q + 1) * KG], w=[d_Bg[qb][s]])
                    P.dma("sync", dq[32 + qb * 16 + s * 2 + kp], Bu[qb][s][kp * 64:(kp + 1) * 64],
                          D["AU"][s][:, kp * 33 + q * KG:kp * 33 + (q + 1) * KG], w=[d_Bu[qb][s]])

        def idx(i):
            q, r = divmod(i, KG * 4)
            kk, s = divmod(r, 4)
            return q, kk, s

        def S1(i):
            q, kk, s = idx(i)
            qb = q % 2
            p = i % 2
            if i == 0:
                loads(0)
            if kk == 1 and s == 0 and q + 1 < NQ:
                loads(q + 1)
            stage2_mm(P, F2, d_F2, Bg[qb][s], d_Bg[qb][s], kk, Gp, d_Gp)
            P.op("scalar", lambda e: e.activation(out=Gs[p][:, 0:2, :], in_=Gp[:], func=AF.Copy),
                 r=[d_Gp], w=[d_Gs[p]])
            P.op("scalar", lambda e: e.activation(out=Gs[p][:, 2, :], in_=Gp[:, 1, :], func=AF.Copy, scale=-1.0),
                 r=[d_Gp], w=[d_Gs[p]])
            stage2_mm(P, F2, d_F2, Bu[qb][s], d_Bu[qb][s], kk, Up[p], d_Up[p])

        def S2(i):
            p = i % 2
            P.op("vector", lambda e: e.tensor_tensor(out=TA[p][:], in0=Up[p][:],
                                                     in1=Gs[p][:, 0:1, :].to_broadcast([128, 2, 512]), op=ALU.mult),
                 r=[d_Up[p], d_Gs[p]], w=[d_TA[p]])
            P.op("vector", lambda e: e.tensor_tensor(out=TB[p][:, 0, :], in0=Up[p][:, 1, :], in1=Gs[p][:, 2, :],
                                                     op=ALU.mult),
                 r=[d_Up[p], d_Gs[p]], w=[d_TB[p]])
            P.op("vector", lambda e: e.tensor_tensor(out=TB[p][:, 1, :], in0=Up[p][:, 0, :], in1=Gs[p][:, 1, :],
                                                     op=ALU.mult),
                 r=[d_Up[p], d_Gs[p]], w=[d_TB[p]])

        def S3(i):
            q, kk, s = idx(i)
            qb = q % 2
            p = i % 2
            for ri in range(2):
                P.op("tensor", lambda e: e.matmul(Yp[:, ri, :], ident[:], TA[p][:, ri, :], start=(s == 0), stop=False),
                     r=[d_ident, d_TA[p]], w=[d_Yp])
                P.op("tensor", lambda e: e.matmul(Yp[:, ri, :], ident[:], TB[p][:, ri, :], start=False, stop=(s == 3)),
                     r=[d_ident, d_TB[p]], w=[d_Yp])
            if s != 3:
                return
            c = (q * KG + kk) % 2
            P.op("scalar", lambda e: e.activation(out=Yb[c][:], in_=Yp[:], func=AF.Copy), r=[d_Yp], w=[d_Yb[c]])
            ITq = IT[qb]
            P.op("tensor", lambda e: e.matmul(Gp[:, 0, :], ITq[:, kk, 0, :], Yb[c][:, 0, :], start=True, stop=False),
                 r=[d_IT[qb], d_Yb[c]], w=[d_Gp])
            P.op("tensor", lambda e: e.matmul(Gp[:, 0, :], ITq[:, kk, 2, :], Yb[c][:, 1, :], start=False, stop=True),
                 r=[d_IT[qb], d_Yb[c]], w=[d_Gp])
            P.op("tensor", lambda e: e.matmul(Gp[:, 1, :], ITq[:, kk, 1, :], Yb[c][:, 0, :], start=True, stop=False),
                 r=[d_IT[qb], d_Yb[c]], w=[d_Gp])
            P.op("tensor", lambda e: e.matmul(Gp[:, 1, :], ITq[:, kk, 0, :], Yb[c][:, 1, :], start=False, stop=True),
                 r=[d_IT[qb], d_Yb[c]], w=[d_Gp])
            P.op("vector", lambda e: e.tensor_copy(out=Cst[c][:], in_=Gp[:]), r=[d_Gp], w=[d_C[c]])
            kh = q * KG + kk
            for kp in range(2):
                P.dma("sync", dq[20 + 2 * c + kp], D["CS"][kp * 33 + kh], Cst[c][kp * 64:(kp + 1) * 64], r=[d_C[c]])

        pipeline(NQ * KG * 4, [S1, S2, S3])


def phase_G(P, nc, D, dq, gq):
    with ExitStack() as es:
        IC = T(es, nc, "G_IC", [66, 2, 64], BF16)
        d_IC = Dep()
        P.dma("sync", dq[0], IC[:], D["ICtab"], w=[d_IC])
        x0 = T(es, nc, "G_x0", [128, 4, 4096], BF16)
        d_x0 = Dep()
        P.dma("sync", dq[1], x0[:], D["X0T"].rearrange("j p t -> p j t"), w=[d_x0])
        yh = T(es, nc, "G_yh", [128, 4, 4096], BF16)
        d_yh = Dep()
        Cg = [T(es, nc, f"G_C{i}", [66, 8, 2, 512], BF16) for i in range(2)]
        d_Cg = [Dep() for _ in range(2)]
        ps = [PS(es, nc, f"G_ps{i}", [128, 8, 64], F32) for i in range(4)]
        d_ps = [Dep() for _ in range(4)]
        pi = 0
        for tg in range(8):
            b = tg % 2
            P.dma("sync", dq[2 + b], Cg[b][:], D["CS"][:, tg * 8:(tg + 1) * 8], w=[d_Cg[b]])
            for j in range(4):
                p = pi % 4
                pi += 1
                for tl in range(8):
                    P.op("tensor", lambda e: e.matmul(ps[p][:, tl, :], Cg[b][:, tl, 0, j * 128:(j + 1) * 128],
                                                      IC[:, 0, :], start=True, stop=False),
                         r=[d_Cg[b], d_IC], w=[d_ps[p]])
                    P.op("tensor", lambda e: e.matmul(ps[p][:, tl, :], Cg[b][:, tl, 1, j * 128:(j + 1) * 128],
                                                      IC[:, 1, :], start=False, stop=True),
                         r=[d_Cg[b], d_IC], w=[d_ps[p]])
                ov = yh[:, j, :].rearrange("p (a b) -> p b a", b=64)[:, tg * 8:(tg + 1) * 8, :]
                xv = x0[:, j, :].rearrange("p (a b) -> p b a", b=64)[:, tg * 8:(tg + 1) * 8, :]
                P.op("vector", lambda e: e.tensor_tensor(out=ov, in0=ps[p][:], in1=xv, op=ALU.mult),
                     r=[d_ps[p], d_x0], w=[d_yh])
        P.dma("sync", dq[4], D["YHT"].rearrange("j p t -> p j t"), yh[:], r=[d_yh])


def phase_H(P, nc, D, dq, gq):
    with ExitStack() as es:
        QT = T(es, nc, "H_QT", [128, 4, 4096], BF16)
        KT = T(es, nc, "H_KT", [128, 4, 4608], BF16)
        VA = T(es, nc, "H_VA", [128, 36, 8, 65], BF16)
        d_in = Dep()
        P.dma("sync", dq[1], QT[:], D["QT"].rearrange("j p t -> p j t"), w=[d_in])
        P.dma("sync", dq[2], KT[:], D["KT"].rearrange("j p t -> p j t"), w=[d_in])
        P.dma("sync", dq[3], VA[:].rearrange("p s h d -> p s (h d)"), D["VA"].rearrange("s p x -> p s x"), w=[d_in])
        bt = [T(es, nc, f"H_bt{i}", [128, 8, 6, 256], BF16) for i in range(2)]
        d_bt = [Dep() for _ in range(2)]
        EX = [T(es, nc, f"H_EX{i}", [128, 6, 256], BF16) for i in range(2)]
        d_EX = [Dep() for _ in range(2)]
        PT = [T(es, nc, f"H_PT{i}", [128, 6, 256], BF16) for i in range(2)]
        d_PT = [Dep() for _ in range(2)]
        onesr = T(es, nc, "H_ones", [128, 64], BF16)
        d_ones = Dep()
        P.op("gpsimd", lambda e: e.memset(onesr[:], 1.0), w=[d_ones])
        rdb = [T(es, nc, f"H_rdb{i}", [128, 256], BF16) for i in range(2)]
        d_rdb = [Dep() for _ in range(2)]
        rdf = [T(es, nc, f"H_rdf{i}", [128, 256], F32) for i in range(2)]
        d_rdf = [Dep() for _ in range(2)]
        Bs = [T(es, nc, f"H_Bs{i}", [64, 256], F32) for i in range(2)]
        d_Bs = [Dep() for _ in range(2)]
        yTs = [T(es, nc, f"H_yT{i}", [128, 4, 256], BF16) for i in range(2)]
        d_yTs = [Dep() for _ in range(2)]
        YATv = D["YAT"].rearrange("j p t -> p j t")
        psS = [PS(es, nc, f"H_pS{i}", [128, 512], F32) for i in range(4)]
        d_pS = [Dep() for _ in range(4)]
        psO = [PS(es, nc, f"H_pO{i}", [128, 256], F32) for i in range(2)]
        d_pO = [Dep() for _ in range(2)]
        psB = [PS(es, nc, f"H_pB{i}", [64, 256], F32) for i in range(2)]
        d_pB = [Dep() for _ in range(2)]
        EX3 = EX + [T(es, nc, "H_EX2", [128, 6, 256], BF16)]
        PT3 = PT + [T(es, nc, "H_PT2", [128, 6, 256], BF16)]
        d_EX3 = d_EX + [Dep()]
        d_PT3 = d_PT + [Dep()]
        state = {"cur_v": None, "cur_bb": 0, "si": 0}
        cbb_of = {}

        def S1(i):
            g, h = divmod(i, 8)
            v = 0 if g == 0 else (2 if g == 15 else 1)
            if h == 0 and v != state["cur_v"]:
                bb = v % 2
                P.dma("sync", dq[4 + bb], bt[bb][:], D["abias"][v].rearrange("h p a q -> p h a q"), w=[d_bt[bb]])
                for hh in range(8):
                    P.op("scalar", lambda e: e.activation(out=bt[bb][:, hh], in_=bt[bb][:, hh], func=AF.Exp),
                         w=[d_bt[bb]])
                state["cur_v"] = v
                state["cur_bb"] = bb
            cbb = state["cur_bb"]
            ch, po = h // 2, (h % 2) * 64
            pb = i % 3
            for pp in range(3):
                sp = state["si"] % 4
                state["si"] += 1
                for e2 in range(2):
                    pr = 2 * pp + e2
                    k0 = (4 * g + 2 * pr) * 64
                    P.op("tensor", lambda e: e.matmul(psS[sp][:, e2 * 256:(e2 + 1) * 256],
                                                      KT[po:po + 64, ch, k0:k0 + 128],
                                                      QT[po:po + 64, ch, g * 256:(g + 1) * 256],
                                                      start=True, stop=True),
                         r=[d_in], w=[d_pS[sp]])
                P.op("scalar", lambda e: e.activation(out=EX3[pb][:, 2 * pp:2 * pp + 2, :],
                                                      in_=psS[sp][:].rearrange("p (a q) -> p a q", q=256),
                                                      func=AF.Exp),
                     r=[d_pS[sp]], w=[d_EX3[pb]])
            cbb_of[i] = cbb

        def S1b(i):
            g, h = divmod(i, 8)
            pb = i % 3
            cbb = cbb_of[i]
            P.op("vector", lambda e: e.tensor_tensor(out=PT3[pb][:], in0=EX3[pb][:], in1=bt[cbb][:, h], op=ALU.mult),
                 r=[d_EX3[pb], d_bt[cbb]], w=[d_PT3[pb]])

        def S2(i):
            g, h = divmod(i, 8)
            pb = i % 3
            o = i % 2
            for pr in range(6):
                st = 2 * g + pr
                P.op("tensor", lambda e: e.matmul(psO[o][0:65, :], VA[:, st, h, :], PT3[pb][:, pr, :],
                                                  start=(pr == 0), stop=(pr == 5)),
                     r=[d_PT3[pb], d_in], w=[d_pO[o]])
            with nc.allow_low_precision("reciprocal feeds a bf16 matmul operand (K=1 broadcast)"):
                P.op("vector", lambda e: e.reciprocal(out=rdb[o][64:65, :], in_=psO[o][64:65, :]),
                     r=[d_pO[o]], w=[d_rdb[o]])

        def S3(i):
            g, h = divmod(i, 8)
            ch, po = h // 2, (h % 2) * 64
            o = i % 2
            yb = g % 2
            P.op("tensor", lambda e: e.matmul(psB[o][:], onesr[64:65, :], rdb[o][64:65, :], start=True, stop=True),
                 r=[d_ones, d_rdb[o]], w=[d_pB[o]])
            P.op("scalar", lambda e: e.activation(out=Bs[o][:], in_=psB[o][:], func=AF.Copy),
                 r=[d_pB[o]], w=[d_Bs[o]])
            P.op("vector", lambda e: e.tensor_tensor(out=yTs[yb][po:po + 64, ch, :], in0=psO[o][0:64, :],
                                                     in1=Bs[o][:], op=ALU.mult),
                 r=[d_pO[o], d_Bs[o]], w=[d_yTs[yb]])
            if h == 7:
                P.dma("sync", dq[6 + yb], YATv[:, :, g * 256:(g + 1) * 256], yTs[yb][:], r=[d_yTs[yb]])

        pipeline(128, [S1, S1b, S2, S3])


def phase_W(P, nc, D, wsem):
    for w in wsem:
        w.nobar = True
    k = [0]

    def cv(dst, src):
        P.dma("gpsimd", wsem[k[0] % 4], dst, src)
        k[0] += 1

    for m in range(16):
        cv(D["WG"][m], wchunk_src(D["w_in"], 3072 + m * 128, 128))
    for m in range(8):
        cv(D["WBA"][m], wchunk_src(D["w_br_attn"], m * 128, 128))
        cv(D["WBH"][m], wchunk_src(D["w_br_hyena"], m * 128, 128))
    for kc in range(8):
        cv(D["WO"][:, kc, :], D["w_out"][kc * 128:(kc + 1) * 128, :])
    for f in range(NFF):
        cv(D["WFG"][f], wchunk_src(D["w_gate"], f * 128, 128))
        cv(D["WFU"][f], wchunk_src(D["w_up"], f * 128, 128))
    wdv = D["w_down"].rearrange("(f p) n -> p f n", p=128)
    for dh in range(2):
        for f0 in range(0, NFF, 11):
            cv(D["WD"][dh][:, f0:f0 + 11, :], wdv[:, f0:f0 + 11, dh * 512:(dh + 1) * 512])


def phase_I(P, nc, D, dq, gq):
    TT = 1024
    with ExitStack() as es:
        ident, d_ident = load_ident(P, nc, es, D, dq[0])
        grep, d_grep = make_grep(P, nc, es, D["g_mix"], "gm", dq[1])
        grep2, d_grep2 = make_grep(P, nc, es, D["g_ffn"], "gf", dq[2])
        gfin = T(es, nc, "I_gfin", [128, 1024], F32)
        d_gfin = Dep()
        P.dma("sync", dq[3], gfin[:], D["g_final"], w=[d_gfin])
        xt = T(es, nc, "I_xt", [128, 8, 1024], F32)
        d_xt = [Dep() for _ in range(8)]
        hT = T(es, nc, "I_hT", [128, 8, TT], BF16)
        d_hT = Dep()
        aT = T(es, nc, "I_aT", [128, NFF, TT], BF16)
        d_aT = Dep()
        mT = aT[:, 0:8, :]
        yaT = aT[:, 8:12, :]
        yhT = aT[:, 12:16, :]
        d_yy = Dep()
        d_mT = Dep()
        wout = T(es, nc, "I_wout", [128, 8, 1024], BF16)
        d_wout = Dep()
        wdn = T(es, nc, "I_wdn", [128, NFF, 512], BF16)
        d_wdn = Dep()
        wc = [T(es, nc, f"I_wc{i}", [128, 8, 128], BF16) for i in range(6)]
        d_wc = [Dep() for _ in range(6)]
        wb = [T(es, nc, f"I_wb{i}", [128, 4, 128], BF16) for i in range(4)]
        d_wb = [Dep() for _ in range(4)]
        sg = [T(es, nc, f"I_sg{i}", [128, 512], F32) for i in range(4)]
        d_sg = [Dep() for _ in range(4)]
        xn = [T(es, nc, f"I_xn{i}", [128, 1024], BF16) for i in range(3)]
        d_xn = [Dep() for _ in range(3)]
        junk = T(es, nc, "I_junk", [128, 1024], BF16)
        d_junk = Dep()
        stt = [T(es, nc, f"I_st{i}", [128, 4], F32) for i in range(3)]
        d_st = [Dep() for _ in range(3)]
        yo = [T(es, nc, f"I_yo{i}", [128, 1024], F32) for i in range(2)]
        d_yo = [Dep() for _ in range(2)]
        ps = [PS(es, nc, f"I_ps{i}", [128, 512], F32) for i in range(7)]
        d_ps = [Dep() for _ in range(7)]
        pt = PS(es, nc, "I_pt", [128, 8, 128], BF16)
        d_pt = Dep()
        for i in range(3):
            P.op("gpsimd", lambda e: e.memset(stt[i][:], EPS), w=[d_st[i]])
        P.dma("sync", dq[54], wout[:], D["WO"], w=[d_wout])
        cnt = {"wi": 0, "bi": 0, "pi": 0, "ni": 0}

        def nA(st, gr, d_gr):
            b = st % 3
            P.op("scalar", lambda e: e.activation(out=junk[:], in_=xt[:, st, :], func=AF.Square,
                                                  accum_out=stt[b][:, 0:1]),
                 r=[d_xt[st]], w=[d_junk, d_st[b]])
            P.op("scalar", lambda e: e.activation(out=stt[b][:, 1:2], in_=stt[b][:, 0:1], func=AF.Sqrt,
                                                  scale=1.0 / 1024, bias=stt[b][:, 3:4]),
                 r=[d_st[b]], w=[d_st[b]])

        def nB(st, gr, d_gr):
            b = st % 3
            P.op("vector", lambda e: e.reciprocal(out=stt[b][:, 2:3], in_=stt[b][:, 1:2]), r=[d_st[b]], w=[d_st[b]])
            if st % 2:
                P.op("scalar", lambda e: e.activation(out=xn[b][:], in_=xt[:, st, :], func=AF.Copy,
                                                      scale=stt[b][:, 2:3]),
                     r=[d_st[b], d_xt[st]], w=[d_xn[b]])
            else:
                P.op("vector", lambda e: e.tensor_scalar(out=xn[b][:], in0=xt[:, st, :], scalar1=stt[b][:, 2:3],
                                                         scalar2=None, op0=ALU.mult),
                     r=[d_st[b], d_xt[st]], w=[d_xn[b]])

        def nC(st, gr, d_gr):
            b = st % 3
            for kc in range(8):
                P.op("tensor", lambda e: e.transpose(pt[:, kc, :], xn[b][:, kc * 128:(kc + 1) * 128], ident[:]),
                     r=[d_xn[b], d_ident], w=[d_pt])
            P.op("vector", lambda e: e.tensor_tensor(out=hT[:, :, st * 128:(st + 1) * 128], in0=pt[:], in1=gr[:],
                                                     op=ALU.mult),
                 r=[d_pt, d_gr], w=[d_hT])

        def wload(buf, dbuf, sem, src):
            P.dma("sync", sem, buf[:], src, w=[dbuf])

        for tile in range(4096 // TT):
            tok0 = tile * TT
            P.dma("sync", dq[5], yaT, D["YAT"].rearrange("j p t -> p j t")[:, :, tok0:tok0 + TT], w=[d_yy, d_aT])
            P.dma("sync", dq[6], yhT, D["YHT"].rearrange("j p t -> p j t")[:, :, tok0:tok0 + TT], w=[d_yy, d_aT])

            def L0(st, tok0=tok0):
                P.dma("sync", dq[20 + st], xt[:, st, :],
                      D["xext"][256 + tok0 + st * 128:256 + tok0 + (st + 1) * 128, :], w=[d_xt[st]])
                nA(st, grep, d_grep)

            pipeline(8, [L0, lambda st: nB(st, grep, d_grep), lambda st: nC(st, grep, d_grep)])
            for m in range(8):
                k = cnt["wi"] % 6; w_ga, dga = wc[k], d_wc[k]; wload(w_ga, dga, dq[44 + k], D["WG"][m]); cnt["wi"] += 1
                k = cnt["wi"] % 6; w_gh, dgh = wc[k], d_wc[k]; wload(w_gh, dgh, dq[44 + k], D["WG"][8 + m]); cnt["wi"] += 1
                k = cnt["bi"] % 4; w_ba, dba = wb[k], d_wb[k]; wload(w_ba, dba, dq[50 + k], D["WBA"][m]); cnt["bi"] += 1
                k = cnt["bi"] % 4; w_bh, dbh = wb[k], d_wb[k]; wload(w_bh, dbh, dq[50 + k], D["WBH"][m]); cnt["bi"] += 1
                for th in range(TT // 512):
                    tsl = slice(th * 512, (th + 1) * 512)
                    pi = cnt["pi"]
                    pg = [pi % 7, (pi + 1) % 7, (pi + 2) % 7, (pi + 3) % 7]
                    cnt["pi"] += 4
                    for kc in range(8):
                        P.op("tensor", lambda e: e.matmul(ps[pg[0]][:], w_ga[:, kc, :], hT[:, kc, tsl],
                                                          start=(kc == 0), stop=(kc == 7)),
                             r=[dga, d_hT], w=[d_ps[pg[0]]])
                    for kc in range(8):
                        P.op("tensor", lambda e: e.matmul(ps[pg[1]][:], w_gh[:, kc, :], hT[:, kc, tsl],
                                                          start=(kc == 0), stop=(kc == 7)),
                             r=[dgh, d_hT], w=[d_ps[pg[1]]])
                    for kc in range(4):
                        P.op("tensor", lambda e: e.matmul(ps[pg[2]][:], w_ba[:, kc, :], yaT[:, kc, tsl],
                                                          start=(kc == 0), stop=(kc == 3)),
                             r=[dba, d_yy], w=[d_ps[pg[2]]])
                    for kc in range(4):
                        P.op("tensor", lambda e: e.matmul(ps[pg[3]][:], w_bh[:, kc, :], yhT[:, kc, tsl],
                                                          start=(kc == 0), stop=(kc == 3)),
                             r=[dbh, d_yy], w=[d_ps[pg[3]]])
                    sa = (2 * (m * 2 + th)) % 4
                    P.op("scalar", lambda e: e.activation(out=sg[sa][:], in_=ps[pg[0]][:], func=AF.Sigmoid),
                         r=[d_ps[pg[0]]], w=[d_sg[sa]])
                    P.op("scalar", lambda e: e.activation(out=sg[sa + 1][:], in_=ps[pg[1]][:], func=AF.Sigmoid),
                         r=[d_ps[pg[1]]], w=[d_sg[sa + 1]])
                    P.op("vector", lambda e: e.tensor_tensor(out=sg[sa][:], in0=ps[pg[2]][:], in1=sg[sa][:], op=ALU.mult),
                         r=[d_ps[pg[2]]], w=[d_sg[sa]])
                    P.op("vector", lambda e: e.tensor_tensor(out=sg[sa + 1][:], in0=ps[pg[3]][:], in1=sg[sa + 1][:],
                                                             op=ALU.mult),
                         r=[d_ps[pg[3]]], w=[d_sg[sa + 1]])
                    P.op("vector", lambda e: e.tensor_tensor(out=mT[:, m, tsl], in0=sg[sa][:], in1=sg[sa + 1][:],
                                                             op=ALU.add),
                         r=[d_sg[sa], d_sg[sa + 1]], w=[d_mT])

            def O1(st):
                for dh in range(2):
                    p = cnt["pi"] % 7
                    cnt["pi"] += 1
                    for kc in range(8):
                        P.op("tensor", lambda e: e.matmul(ps[p][:], mT[:, kc, st * 128:(st + 1) * 128],
                                                          wout[:, kc, dh * 512:(dh + 1) * 512],
                                                          start=(kc == 0), stop=(kc == 7)),
                             r=[d_mT, d_wout], w=[d_ps[p]])
                    P.op("vector", lambda e: e.tensor_tensor(out=xt[:, st, dh * 512:(dh + 1) * 512], in0=ps[p][:],
                                                             in1=xt[:, st, dh * 512:(dh + 1) * 512], op=ALU.add),
                         r=[d_ps[p]], w=[d_xt[st]])
                nA(st, grep2, d_grep2)

            pipeline(8, [O1, lambda st: nB(st, grep2, d_grep2), lambda st: nC(st, grep2, d_grep2)])
            for f in range(NFF):
                k = cnt["wi"] % 6; w_g, dg = wc[k], d_wc[k]; wload(w_g, dg, dq[44 + k], D["WFG"][f]); cnt["wi"] += 1
                k = cnt["wi"] % 6; w_u, dup = wc[k], d_wc[k]; wload(w_u, dup, dq[44 + k], D["WFU"][f]); cnt["wi"] += 1
                if f == 2:
                    P.dma("sync", dq[55], wdn[:], D["WD"][0], w=[d_wdn])
                for th in range(TT // 512):
                    tsl = slice(th * 512, (th + 1) * 512)
                    pi = cnt["pi"]
                    pg = [pi % 7, (pi + 1) % 7]
                    cnt["pi"] += 2
                    for kc in range(8):
                        P.op("tensor", lambda e: e.matmul(ps[pg[0]][:], w_g[:, kc, :], hT[:, kc, tsl],
                                                          start=(kc == 0), stop=(kc == 7)),
                             r=[dg, d_hT], w=[d_ps[pg[0]]])
                    for kc in range(8):
                        P.op("tensor", lambda e: e.matmul(ps[pg[1]][:], w_u[:, kc, :], hT[:, kc, tsl],
                                                          start=(kc == 0), stop=(kc == 7)),
                             r=[dup, d_hT], w=[d_ps[pg[1]]])
                    sa = (f * 2 + th) % 4
                    P.op("scalar", lambda e: e.activation(out=sg[sa][:], in_=ps[pg[0]][:], func=AF.Silu),
                         r=[d_ps[pg[0]]], w=[d_sg[sa]])
                    P.op("vector", lambda e: e.tensor_tensor(out=aT[:, f, tsl], in0=ps[pg[1]][:], in1=sg[sa][:],
                                                             op=ALU.mult),
                         r=[d_ps[pg[1]], d_sg[sa]], w=[d_aT, d_mT, d_yy])
            for dh in range(2):
                if dh == 1:
                    P.dma("sync", dq[55], wdn[:], D["WD"][1], w=[d_wdn])
                for st in range(8):
                    p = cnt["pi"] % 7
                    cnt["pi"] += 1
                    for f in range(NFF):
                        P.op("tensor", lambda e: e.matmul(ps[p][:], aT[:, f, st * 128:(st + 1) * 128], wdn[:, f, :],
                                                          start=(f == 0), stop=(f == NFF - 1)),
                             r=[d_aT, d_mT, d_yy, d_wdn], w=[d_ps[p]])
                    P.op("vector", lambda e: e.tensor_tensor(out=xt[:, st, dh * 512:(dh + 1) * 512], in0=ps[p][:],
                                                             in1=xt[:, st, dh * 512:(dh + 1) * 512], op=ALU.add),
                         r=[d_ps[p]], w=[d_xt[st]])
                    if dh == 1:
                        b = st % 3
                        yb = st % 2
                        P.op("scalar", lambda e: e.activation(out=junk[:], in_=xt[:, st, :], func=AF.Square,
                                                              accum_out=stt[b][:, 0:1]),
                             r=[d_xt[st]], w=[d_junk, d_st[b]])
                        P.op("scalar", lambda e: e.activation(out=stt[b][:, 1:2], in_=stt[b][:, 0:1], func=AF.Sqrt,
                                                              scale=1.0 / 1024, bias=stt[b][:, 3:4]),
                             r=[d_st[b]], w=[d_st[b]])
                        P.op("vector", lambda e: e.reciprocal(out=stt[b][:, 2:3], in_=stt[b][:, 1:2]),
                             r=[d_st[b]], w=[d_st[b]])
                        P.op("vector", lambda e: e.scalar_tensor_tensor(out=yo[yb][:], in0=xt[:, st, :],
                                                                        scalar=stt[b][:, 2:3], in1=gfin[:],
                                                                        op0=ALU.mult, op1=ALU.mult),
                             r=[d_st[b], d_xt[st], d_gfin], w=[d_yo[yb]])
                        P.dma("sync", dq[16 + yb], D["y"][tok0 + st * 128:tok0 + (st + 1) * 128, :], yo[yb][:],
                              r=[d_yo[yb]])


def _const_tables():
    N = 8192
    t1 = np.arange(128)[:, None, None]
    t2 = np.arange(64)[None, :, None]
    k1 = np.arange(128)[None, None, :]
    th = 2 * np.pi * (((64 * t1 + t2) * k1) % N) / N
    E = np.stack([np.cos(th), -np.sin(th)], axis=2)
    a = np.arange(64)
    th2 = 2 * np.pi * np.outer(a, a) / 64
    F2 = np.zeros((128, 3, 128))
    for kp in range(2):
        sl = slice(kp * 64, (kp + 1) * 64)
        F2[sl, 0, sl] = np.cos(th2)
        F2[sl, 1, sl] = -np.sin(th2)
        F2[sl, 2, sl] = np.sin(th2)
    IT = np.zeros((128, 33, 3, 128))
    k2 = np.arange(64)[:, None]
    tt = np.arange(64)[None, :]
    for kh in range(33):
        for kp in range(2):
            k1v = kh + 33 * kp
            ph = 2 * np.pi * (tt * k2 / 64.0 + tt * k1v / 8192.0)
            sl = slice(kp * 64, (kp + 1) * 64)
            IT[sl, kh, 0, sl] = np.cos(ph)
            IT[sl, kh, 1, sl] = np.sin(ph)
            IT[sl, kh, 2, sl] = -np.sin(ph)
    kk = np.arange(66)[:, None]
    t1v = np.arange(64)[None, :]
    th3 = 2 * np.pi * kk * t1v / 128.0
    wk = np.full((66, 1), 2.0)
    wk[0] = 1.0
    wk[64] = 1.0
    wk[65] = 0.0
    IC = np.stack([wk * np.cos(th3) / N, -wk * np.sin(th3) / N], axis=1)
    return (E.astype(BF), F2.astype(BF), IT.astype(BF), IC.astype(BF))


def _filter_tables(L, ksegs):
    f32 = np.float32
    tlin = np.linspace(0.0, 1.0, L, dtype=f32)
    omega = (2.0 * math.pi * np.arange(L, dtype=f32) / L).astype(f32)
    fbv = np.linspace(1e-4, 15, 16, dtype=f32)
    n = np.arange(8192)
    d = np.where(n < 4096, n, n - 8192)
    zs = np.zeros((4, 128, 4096), f32)
    maskT = np.zeros((4, 128, 8192), f32)
    tpos = np.zeros((4, 8192), f32)
    vbias = np.full((4, 8192), NEG, f32)
    for s, k in enumerate(ksegs):
        if k is None:
            idx = np.zeros(8192, np.int64)
            valid = np.zeros(8192, bool)
            Dl = np.zeros(8192, np.int64)
        else:
            Dl = k * 4096 + d
            valid = (np.abs(Dl) <= L - 1) & (n != 4096)
            idx = np.where(valid, np.abs(Dl), 0)
        ang = (omega[idx][:, None] * fbv[None, :]).astype(f32)
        z = np.concatenate([tlin[idx][:, None], np.cos(ang), -np.sin(ang)], axis=1)
        zs[s, 0:33, :] = z[0:4096].T
        zs[s, 64:97, :] = z[4096:8192].T
        tpos[s] = tlin[idx]
        vbias[s] = np.where(valid, 0.0, NEG)
        maskT[s, 0:64, :] = (valid & (Dl >= 0)).astype(f32)[None, :]
        maskT[s, 64:128, :] = (valid & (Dl <= 0)).astype(f32)[None, :]
    def lay(a):
        a = a.reshape(4, 128, 64)
        return np.ascontiguousarray(np.transpose(a, (1, 0, 2)))
    return zs.astype(BF), maskT.astype(BF), lay(tpos), lay(vbias)


def _attn_bias(rpb, R0, rows):
    out = np.full((3, 8, 128, 6, 256), NEG, np.float32)
    qc = np.arange(64)
    kc = np.arange(64)
    cs = np.clip(qc - 8, 0, 48)
    colin = (kc[None, :] >= cs[:, None]) & (kc[None, :] < cs[:, None] + 16)
    dc = np.clip(kc[None, :] - qc[:, None], -15, 15) + 15
    for v, g in enumerate((0, 1, 15)):
        for rl in range(4):
            r = 4 * g + rl
            rg = R0 + r
            ws = int(np.clip(rg - 4, 0, rows - 8)) - R0 + 4
            for pr in range(6):
                for er in range(2):
                    e = 4 * g + 2 * pr + er
                    if not (0 <= e - ws < 8):
                        continue
                    drr = e - 4 - r + 7
                    blk = rpb[:, drr, :][:, dc]
                    blk = np.where(colin[None], blk, NEG)
                    out[v, :, er * 64:(er + 1) * 64, pr, rl * 64:(rl + 1) * 64] = np.transpose(blk, (0, 2, 1))
    return out.astype(BF)


def _col128(v, n):
    return np.ascontiguousarray(np.asarray(v, np.float32).reshape(n, 128).T)


def prepare_inputs(inputs, cores=range(8)):
    f32 = np.float32
    I = {k: np.asarray(v) for k, v in inputs.items()}
    E, F2, IT, IC = _const_tables()
    max_decay = math.log(1e-2) / 0.3
    min_decay = math.log(1e-2) / 1.5
    deltas = np.abs(np.linspace(min_decay, max_decay, 512, dtype=f32))
    common = {
        "w_in": I["w_in"][0], "w_br_attn": I["w_br_attn"][0], "w_br_hyena": I["w_br_hyena"][0],
        "w_out": I["w_out"][0], "w_gate": I["w_gate"][0], "w_up": I["w_up"][0], "w_down": I["w_down"][0],
        "g_mix": _col128(I["norm_mix"][0], 8), "g_ffn": _col128(I["norm_ffn"][0], 8),
        "g_final": np.ascontiguousarray(np.broadcast_to(I["norm_final"][None, :], (128, 1024))).astype(f32),
        "conv_w": np.ascontiguousarray(np.transpose(I["conv_w"][0].reshape(3, 12, 128), (2, 1, 0))).astype(f32),
        "conv_b": _col128(I["conv_b"][0], 12),
        "fw1": I["filt_w1"][0], "fw2": I["filt_w2"][0], "fw3": I["filt_w3"][0], "fw4": I["filt_w4"][0],
        "fb": np.ascontiguousarray(np.tile(np.stack([I["filt_b1"][0], I["filt_b2"][0], I["filt_b3"][0]], axis=1),
                                           (2, 1))).astype(f32),
        "ffr": np.ascontiguousarray(np.tile(I["filt_freq"][0][:, None], (2, 1))).astype(f32),
        "e0": np.eye(128, 1, dtype=f32),
        "hyd": np.ascontiguousarray(np.broadcast_to(I["hyena_d"][0][None, :], (128, 512))).astype(f32),
        "ident": np.eye(128, dtype=f32).astype(BF),
        "Etab": E, "F2tab": F2, "ITtab": IT, "ICtab": IC,
        "negdelta": np.ascontiguousarray(np.broadcast_to(-deltas[None, :], (128, 512))).astype(f32),
    }
    rpb = I["rpb"][0].astype(f32)
    xp = I["x_prompt"]
    xs = I["x_sample"][0]
    tab_prompt = _filter_tables(4096, [0, None, None, None])
    ab_prompt = _attn_bias(rpb, 0, 64)
    maps = []
    for c in cores:
        m = dict(common)
        xext = np.zeros((4608, 1024), f32)
        xctx = np.zeros((4, 4224, 1024), f32)
        if c < 4:
            xext[256:4352] = xp[c]
            xctx[0, :4096] = xp[c]
            zs, maskT, tpos, vbias = tab_prompt
            ab = ab_prompt
        else:
            j = c - 4
            L = 16384
            lo, hi = j * 4096 - 256, (j + 1) * 4096 + 256
            a, b = max(lo, 0), min(hi, L)
            xext[a - lo:b - lo] = xs[a:b]
            blks = [j] + [bb for bb in range(4) if bb != j]
            for i, bb in enumerate(blks):
                xctx[i, :4096] = xs[bb * 4096:(bb + 1) * 4096]
                if bb * 4096 - 1 >= 0:
                    xctx[i, 4096] = xs[bb * 4096 - 1]
                if (bb + 1) * 4096 < L:
                    xctx[i, 4097] = xs[(bb + 1) * 4096]
            zs, maskT, tpos, vbias = _filter_tables(L, [j - bb for bb in blks])
            ab = _attn_bias(rpb, j * 64, 256)
        m.update({"xext": xext, "xctx": xctx, "zs": zs, "maskT": maskT, "tpos": tpos, "vbias": vbias, "abias": ab})
        maps.append(m)
    return maps


_NC_CACHE = {}


def kernel(**inputs):
    if "nc" not in _NC_CACHE:
        _NC_CACHE["nc"] = build_program()
    nc = _NC_CACHE["nc"]
    maps = prepare_inputs(inputs)
    res = run_bass_kernel_spmd(nc, maps, core_ids=list(range(8)))
    ys = [np.asarray(res.results[i]["y"], dtype=np.float32) for i in range(8)]
    y_prompt = np.stack(ys[0:4], axis=0)
    y_sample = np.concatenate(ys[4:8], axis=0)[None]
    return (y_prompt, y_sample)
```

```python
import math
import numpy as np
import ml_dtypes
from contextlib import ExitStack
import concourse.bass as bass
import concourse.mybir as mybir
from concourse.bass_utils import run_bass_kernel_spmd

F32 = mybir.dt.float32
BF16 = mybir.dt.bfloat16
AF = mybir.ActivationFunctionType
ALU = mybir.AluOpType
BF = ml_dtypes.bfloat16

D_MODEL = 1024
D_IN = 5120
D_FF = 2816
NFF = 22
EPS = 1e-6
NEG = -30000.0
PI = float(np.pi)


class Dep:
    __slots__ = ("w", "r")

    def __init__(self):
        self.w = {}
        self.r = {}


class DSem:
    def __init__(self, sem):
        self.sem = sem
        self.n = 0
        self.nobar = False


class Prog:
    ENGS = ["sync", "scalar", "vector", "gpsimd", "tensor"]

    def __init__(self, nc, es):
        self.nc = nc
        self.h = {"sync": nc.sync, "scalar": nc.scalar, "vector": nc.vector,
                  "gpsimd": nc.gpsimd, "tensor": nc.tensor}
        self.sem = {e: es.enter_context(nc.semaphore("s_" + e)) for e in self.ENGS}
        self.cnt = {e: 0 for e in self.ENGS}
        self.known = {e: {} for e in self.ENGS}
        self.dsems = []
        self.es = es
        self.rr = 0

    def dsem(self, name):
        s = DSem(self.es.enter_context(self.nc.semaphore(name)))
        self.dsems.append(s)
        return s

    def _need(self, eng, items):
        k = self.known[eng]
        for it in items:
            if it is None:
                continue
            sem, val, e2 = it
            if e2 == eng and eng == "tensor":
                continue
            if k.get(id(sem), 0) >= val:
                continue
            k[id(sem)] = val
            self.h[eng].wait_ge(sem, val)

    def _items(self, reads, writes):
        items = []
        for d in reads:
            items.extend(d.w.values())
        for d in writes:
            items.extend(d.w.values())
            items.extend(d.r.values())
        return items

    def _upd(self, tok, reads, writes):
        key = id(tok[0])
        for d in reads:
            d.r[key] = tok
        for d in writes:
            d.w[key] = tok
            d.r = {}

    def op(self, eng, fn, r=(), w=()):
        self._need(eng, self._items(r, w))
        ins = fn(self.h[eng])
        self.cnt[eng] += 1
        ins.then_inc(self.sem[eng], 1)
        tok = (self.sem[eng], self.cnt[eng], eng)
        self._upd(tok, r, w)
        return tok

    def dma(self, eng, ds, out, in_, r=(), w=()):
        self._need(eng, self._items(r, w))
        ins = self.h[eng].dma_start(out=out, in_=in_)
        ds.n += 16
        ins.then_inc(ds.sem, 16)
        tok = (ds.sem, ds.n, "dma")
        self._upd(tok, r, w)
        return tok

    def barrier(self):
        toks = [(self.sem[e], self.cnt[e], e) for e in self.ENGS if self.cnt[e] > 0]
        toks += [(d.sem, d.n, "dma") for d in self.dsems if d.n > 0 and not d.nobar]
        for e in self.ENGS:
            k = self.known[e]
            for sem, val, e2 in toks:
                if e2 == e:
                    continue
                if k.get(id(sem), 0) >= val:
                    continue
                k[id(sem)] = val
                self.h[e].wait_ge(sem, val)

    def alt(self, a="vector", b="gpsimd"):
        self.rr += 1
        return a if self.rr % 2 else b


_UID = [0]


def pipeline(n_iter, stages):
    ns = len(stages)
    for t in range(n_iter + ns - 1):
        for k, f in enumerate(stages):
            i = t - k
            if 0 <= i < n_iter:
                f(i)


def T(es, nc, name, shape, dt):
    _UID[0] += 1
    return es.enter_context(nc.sbuf_tensor(f"sb{_UID[0]}_{name}", shape, dt))


def PS(es, nc, name, shape, dt):
    _UID[0] += 1
    return es.enter_context(nc.psum_tensor(f"ps{_UID[0]}_{name}", shape, dt))


def rms_transpose(P, nc, es, xrows, nsub, hT, d_hT, grep, d_grep, ident, d_ident, pfx, dq):
    NB = 3
    xs = [T(es, nc, f"{pfx}xs{i}", [128, 1024], F32) for i in range(NB)]
    d_xs = [Dep() for _ in range(NB)]
    xn = [T(es, nc, f"{pfx}xn{i}", [128, 1024], BF16) for i in range(NB)]
    d_xn = [Dep() for _ in range(NB)]
    junk = T(es, nc, f"{pfx}junk", [128, 1024], BF16)
    d_junk = Dep()
    stt = [T(es, nc, f"{pfx}st{i}", [128, 4], F32) for i in range(NB)]
    d_st = [Dep() for _ in range(NB)]
    pt = [PS(es, nc, f"{pfx}pt{i}", [128, 8, 128], BF16) for i in range(2)]
    d_pt = [Dep() for _ in range(2)]
    for i in range(NB):
        P.op("gpsimd", lambda e: e.memset(stt[i][:], EPS), w=[d_st[i]])

    def S1(st):
        b = st % NB
        P.dma("sync", dq[b], xs[b][:], xrows[st * 128:(st + 1) * 128, :], w=[d_xs[b]])
        P.op("scalar", lambda e: e.activation(out=junk[:], in_=xs[b][:], func=AF.Square,
                                              accum_out=stt[b][:, 0:1]),
             r=[d_xs[b]], w=[d_junk, d_st[b]])
        P.op("scalar", lambda e: e.activation(out=stt[b][:, 1:2], in_=stt[b][:, 0:1], func=AF.Sqrt,
                                              scale=1.0 / 1024, bias=stt[b][:, 3:4]),
             r=[d_st[b]], w=[d_st[b]])

    def S2(st):
        b = st % NB
        P.op("vector", lambda e: e.reciprocal(out=stt[b][:, 2:3], in_=stt[b][:, 1:2]),
             r=[d_st[b]], w=[d_st[b]])
        if st % 2:
            P.op("scalar", lambda e: e.activation(out=xn[b][:], in_=xs[b][:], func=AF.Copy, scale=stt[b][:, 2:3]),
                 r=[d_st[b], d_xs[b]], w=[d_xn[b]])
        else:
            P.op("vector", lambda e: e.tensor_scalar(out=xn[b][:], in0=xs[b][:], scalar1=stt[b][:, 2:3],
                                                     scalar2=None, op0=ALU.mult),
                 r=[d_st[b], d_xs[b]], w=[d_xn[b]])

    def S3(st):
        b = st % NB
        p = st % 2
        for kc in range(8):
            P.op("tensor", lambda e: e.transpose(pt[p][:, kc, :], xn[b][:, kc * 128:(kc + 1) * 128], ident[:]),
                 r=[d_xn[b], d_ident], w=[d_pt[p]])
        P.op("vector", lambda e: e.tensor_tensor(out=hT[:, :, st * 128:(st + 1) * 128], in0=pt[p][:],
                                                 in1=grep[:], op=ALU.mult),
             r=[d_pt[p], d_grep], w=[d_hT])

    pipeline(nsub, [S1, S2, S3])


def make_grep(P, nc, es, gcol, name, dq):
    g = T(es, nc, name + "g", [128, 8], F32)
    d_g = Dep()
    grep = T(es, nc, name + "rep", [128, 8, 128], F32)
    d_grep = Dep()
    P.dma("sync", dq, g[:], gcol, w=[d_g])
    P.op("gpsimd", lambda e: e.memset(grep[:], 1.0), w=[d_grep])
    for kc in range(8):
        P.op("vector", lambda e: e.tensor_scalar(out=grep[:, kc, :], in0=grep[:, kc, :],
                                                 scalar1=g[:, kc:kc + 1], scalar2=None, op0=ALU.mult),
             r=[d_g], w=[d_grep])
    return grep, d_grep


def load_ident(P, nc, es, D, dq):
    ident = T(es, nc, "ident", [128, 128], BF16)
    d_ident = Dep()
    P.dma("sync", dq, ident[:], D["ident"], w=[d_ident])
    return ident, d_ident


def eps_init(P, stt_list):
    pass


def build_program(dev=False, phases="ABCDEFGHI"):
    nc = bass.Bass("TRN2", target_bir_lowering=False)
    D = {}

    def inp(name, shape, dt=F32):
        D[name] = nc.dram_tensor(name, list(shape), dt, kind="ExternalInput").ap()

    def scr(name, shape, dt=BF16):
        if dev:
            D[name] = nc.dram_tensor(name, list(shape), dt, kind="ExternalOutput").ap()
        else:
            D[name] = nc.dram_tensor(name, list(shape), dt).ap()

    inp("xext", [4608, 1024])
    inp("xctx", [4, 4224, 1024])
    inp("w_in", [1024, D_IN])
    inp("w_br_attn", [512, 1024])
    inp("w_br_hyena", [512, 1024])
    inp("w_out", [1024, 1024])
    inp("w_gate", [1024, D_FF])
    inp("w_up", [1024, D_FF])
    inp("w_down", [D_FF, 1024])
    inp("g_mix", [128, 8])
    inp("g_ffn", [128, 8])
    inp("g_final", [128, 1024])
    inp("conv_w", [128, 12, 3])
    inp("conv_b", [128, 12])
    inp("fw1", [33, 64])
    inp("fw2", [64, 64])
    inp("fw3", [64, 64])
    inp("fw4", [64, 1024])
    inp("fb", [128, 3])
    inp("ffr", [128, 1])
    inp("hyd", [128, 512])
    inp("ident", [128, 128], BF16)
    inp("Etab", [128, 64, 2, 128], BF16)
    inp("F2tab", [128, 3, 128], BF16)
    inp("ITtab", [128, 33, 3, 128], BF16)
    inp("ICtab", [66, 2, 64], BF16)
    inp("zs", [4, 128, 4096], BF16)
    inp("maskT", [4, 128, 8192], BF16)
    inp("tpos", [128, 4, 64])
    inp("vbias", [128, 4, 64])
    inp("e0", [128, 1])
    inp("negdelta", [128, 512])
    inp("abias", [3, 8, 128, 6, 256], BF16)

    scr("QT", [4, 128, 4096])
    scr("KT", [4, 128, 4608])
    scr("VA", [36, 128, 8 * 65])
    scr("X0T", [4, 128, 4096])
    scr("DU", [4, 64, 4, 64, 128])
    scr("AU", [4, 64, 66, 2, 512])
    scr("AG", [4, 64, 66, 2, 512])
    scr("CS", [66, 64, 2, 512])
    scr("YHT", [4, 128, 4096])
    scr("YAT", [4, 128, 4096])
    for nm, shp in (("WG", [16, 128, 8, 128]), ("WBA", [8, 128, 4, 128]), ("WBH", [8, 128, 4, 128]),
                    ("WO", [128, 8, 1024]), ("WFG", [22, 128, 8, 128]), ("WFU", [22, 128, 8, 128]),
                    ("WD", [2, 128, 22, 512])):
        D[nm] = nc.dram_tensor(nm, shp, BF16).ap()
    D["y"] = nc.dram_tensor("y", [4096, 1024], F32, kind="ExternalOutput").ap()

    with ExitStack() as es0:
        P = Prog(nc, es0)
        dq = [P.dsem(f"dq{i}") for i in range(64)]
        gq = [P.dsem(f"gq{i}") for i in range(16)]
        wsem = [P.dsem(f"wq{i}") for i in range(4)]
        if "A" in phases:
            phase_A(P, nc, D, dq, gq)
            P.barrier()
        if "B" in phases:
            phase_B(P, nc, D, dq, gq)
            P.barrier()
        if "C" in phases:
            phase_C(P, nc, D, dq, gq, (lambda: phase_W(P, nc, D, wsem)) if "I" in phases else None)
            P.barrier()
        elif "I" in phases:
            phase_W(P, nc, D, wsem)
        if "D" in phases:
            phase_D(P, nc, D, dq, gq)
            P.barrier()
        if "F" in phases:
            phase_F(P, nc, D, dq, gq)
            P.barrier()
        if "G" in phases:
            phase_G(P, nc, D, dq, gq)
            P.barrier()
        if "H" in phases:
            phase_H(P, nc, D, dq, gq)
            P.barrier()
        if "I" in phases:
            for w in wsem:
                w.nobar = False
            P.barrier()
            phase_I(P, nc, D, dq, gq)
            P.barrier()
    return nc


def wchunk_src(w, col0, ncols, nk=8):
    return w.rearrange("(kc p) n -> p kc n", p=128)[:, :, col0:col0 + ncols]


def phase_A(P, nc, D, dq, gq):
    with ExitStack() as es:
        ident, d_ident = load_ident(P, nc, es, D, dq[0])
        grep, d_grep = make_grep(P, nc, es, D["g_mix"], "gm", dq[1])
        hT = T(es, nc, "A_hT", [128, 8, 4608], BF16)
        d_hT = Dep()
        with ExitStack() as es1:
            rms_transpose(P, nc, es1, D["xext"], 36, hT, d_hT, grep, d_grep, ident, d_ident, "A_", dq[40:43])
        P.barrier()
        wq = [T(es, nc, f"A_wq{i}", [128, 8, 128], BF16) for i in range(2)]
        d_wq = [Dep() for _ in range(2)]
        stg = [T(es, nc, f"A_stg{i}", [128, 4608], BF16) for i in range(2)]
        d_stg = [Dep() for _ in range(2)]
        ps = [PS(es, nc, f"A_ps{i}", [128, 512], F32) for i in range(4)]
        d_ps = [Dep() for _ in range(4)]
        pi = 0
        for ci in range(8):
            b = ci % 2
            isq = ci < 4
            P.dma("gpsimd", gq[4 + b], wq[b][:], wchunk_src(D["w_in"], ci * 128, 128), w=[d_wq[b]])
            t0, ntile = (256, 8) if isq else (0, 9)
            for tt in range(ntile):
                p = pi % 4
                pi += 1
                for kc in range(8):
                    P.op("tensor", lambda e: e.matmul(ps[p][:], wq[b][:, kc, :],
                                                      hT[:, kc, t0 + tt * 512:t0 + (tt + 1) * 512],
                                                      start=(kc == 0), stop=(kc == 7)),
                         r=[d_wq[b], d_hT], w=[d_ps[p]])
                if tt % 2 == 0:
                    P.op("scalar", lambda e: e.activation(out=stg[b][:, tt * 512:(tt + 1) * 512], in_=ps[p][:],
                                                          func=AF.Copy, scale=(0.125 if isq else 1.0)),
                         r=[d_ps[p]], w=[d_stg[b]])
                else:
                    P.op("vector", lambda e: e.tensor_scalar(out=stg[b][:, tt * 512:(tt + 1) * 512], in0=ps[p][:],
                                                             scalar1=(0.125 if isq else 1.0), scalar2=None,
                                                             op0=ALU.mult),
                         r=[d_ps[p]], w=[d_stg[b]])
            if isq:
                P.dma("sync", dq[6 + b], D["QT"][ci], stg[b][:, 0:4096], r=[d_stg[b]])
            else:
                P.dma("sync", dq[6 + b], D["KT"][ci - 4], stg[b][:, 0:4608], r=[d_stg[b]])
        wv = T(es, nc, "A_wv", [128, 8, 512], BF16)
        d_wv = Dep()
        P.dma("gpsimd", gq[8], wv[:], wchunk_src(D["w_in"], 1024, 512), w=[d_wv])
        vst = [T(es, nc, f"A_vst{i}", [128, 8, 65], BF16) for i in range(2)]
        d_vst = [Dep() for _ in range(2)]
        for i in range(2):
            P.op("gpsimd", lambda e: e.memset(vst[i][:], 1.0), w=[d_vst[i]])
        for st in range(36):
            p = pi % 4
            pi += 1
            b = st % 2
            for kc in range(8):
                P.op("tensor", lambda e: e.matmul(ps[p][:], hT[:, kc, st * 128:(st + 1) * 128], wv[:, kc, :],
                                                  start=(kc == 0), stop=(kc == 7)),
                     r=[d_wv, d_hT], w=[d_ps[p]])
            P.op("scalar" if st % 2 else "vector",
                 (lambda e: e.activation(out=vst[b][:, :, 0:64], in_=ps[p][:].rearrange("p (h d) -> p h d", d=64),
                                         func=AF.Copy)) if st % 2 else
                 (lambda e: e.tensor_copy(out=vst[b][:, :, 0:64], in_=ps[p][:].rearrange("p (h d) -> p h d", d=64))),
                 r=[d_ps[p]], w=[d_vst[b]])
            P.dma("sync", dq[9 + b], D["VA"][st], vst[b][:].rearrange("p h d -> p (h d)"), r=[d_vst[b]])


def phase_B(P, nc, D, dq, gq):
    with ExitStack() as es:
        ident, d_ident = load_ident(P, nc, es, D, dq[0])
        grep, d_grep = make_grep(P, nc, es, D["g_mix"], "gm", dq[1])
        cw = T(es, nc, "B_cw", [128, 12, 3], F32)
        cb = T(es, nc, "B_cb", [128, 12], F32)
        d_c = Dep()
        P.dma("sync", dq[2], cw[:], D["conv_w"], w=[d_c])
        P.dma("sync", dq[3], cb[:], D["conv_b"], w=[d_c])
        hT = T(es, nc, "B_hT", [128, 8, 4224], BF16)
        d_hT = Dep()
        hy = [T(es, nc, f"B_hy{i}", [128, 4224], F32) for i in range(2)]
        d_hy = [Dep() for _ in range(2)]
        cx = [T(es, nc, f"B_cx{i}", [128, 4096], BF16) for i in range(3)]
        d_cx = [Dep() for _ in range(3)]
        uT = [T(es, nc, f"B_uT{i}", [128, 4096], BF16) for i in range(2)]
        d_uT = [Dep() for _ in range(2)]
        du = T(es, nc, "B_du", [64, 64, 128], BF16)
        d_du = Dep()
        wq = [T(es, nc, f"B_wq{i}", [128, 8, 128], BF16) for i in range(2)]
        d_wq = [Dep() for _ in range(2)]
        ps = [PS(es, nc, f"B_ps{i}", [128, 512], F32) for i in range(4)]
        d_ps = [Dep() for _ in range(4)]
        ptr = [PS(es, nc, f"B_ptr{i}", [64, 8, 128], BF16) for i in range(2)]
        d_ptr = [Dep() for _ in range(2)]
        st8 = {"pi": 0, "wi": 0}
        for blk in range(4):
            with ExitStack() as es1:
                P.barrier()
                rms_transpose(P, nc, es1, D["xctx"][blk], 33, hT, d_hT, grep, d_grep, ident, d_ident,
                              f"B{blk}_", dq[40:43])
            P.barrier()
            items = []
            for j in range(4):
                items.append((1, 2048 + j * 128, 4 + j, j))
                items.append((2, 2560 + j * 128, 8 + j, j))
                if blk == 0:
                    items.append((0, 1536 + j * 128, j, j))
            base = st8["wi"]

            def S1(c, items=items, base=base):
                role, col0, cidx, j = items[c]
                b = (base + c) % 2
                k3 = (base + c) % 3
                P.dma("gpsimd", gq[b], wq[b][:], wchunk_src(D["w_in"], col0, 128), w=[d_wq[b]])
                for tt in range(9):
                    p = st8["pi"] % 4
                    st8["pi"] += 1
                    n = 512 if tt < 8 else 128
                    for kc in range(8):
                        P.op("tensor", lambda e: e.matmul(ps[p][:, 0:n], wq[b][:, kc, :],
                                                          hT[:, kc, tt * 512:tt * 512 + n],
                                                          start=(kc == 0), stop=(kc == 7)),
                             r=[d_wq[b], d_hT], w=[d_ps[p]])
                    P.op("scalar", lambda e: e.activation(out=hy[b][:, tt * 512:tt * 512 + n], in_=ps[p][:, 0:n],
                                                          func=AF.Copy),
                         r=[d_ps[p]], w=[d_hy[b]])
                    if tt < 8:
                        P.op("scalar", lambda e: e.activation(out=cx[k3][:, tt * 512:(tt + 1) * 512], in_=ps[p][:],
                                                              func=AF.Identity, scale=cw[:, cidx, 1:2],
                                                              bias=cb[:, cidx:cidx + 1]),
                             r=[d_ps[p], d_c], w=[d_cx[k3]])

            def S2(c, items=items, base=base):
                role, col0, cidx, j = items[c]
                b = (base + c) % 2
                k3 = (base + c) % 3
                cc = cx[k3]
                dc = d_cx[k3]
                h = hy[b]
                P.op("vector", lambda e: e.scalar_tensor_tensor(out=cc[:, 1:4096], in0=h[:, 0:4095],
                                                                scalar=cw[:, cidx, 0:1], in1=cc[:, 1:4096],
                                                                op0=ALU.mult, op1=ALU.add),
                     r=[d_hy[b], d_c], w=[dc])
                P.op("vector", lambda e: e.scalar_tensor_tensor(out=cc[:, 0:4095], in0=h[:, 1:4096],
                                                                scalar=cw[:, cidx, 2:3], in1=cc[:, 0:4095],
                                                                op0=ALU.mult, op1=ALU.add),
                     r=[d_hy[b], d_c], w=[dc])
                P.op("vector", lambda e: e.scalar_tensor_tensor(out=cc[:, 0:1], in0=h[:, 4096:4097],
                                                                scalar=cw[:, cidx, 0:1], in1=cc[:, 0:1],
                                                                op0=ALU.mult, op1=ALU.add),
                     r=[d_hy[b], d_c], w=[dc])
                P.op("vector", lambda e: e.scalar_tensor_tensor(out=cc[:, 4095:4096], in0=h[:, 4097:4098],
                                                                scalar=cw[:, cidx, 2:3], in1=cc[:, 4095:4096],
                                                                op0=ALU.mult, op1=ALU.add),
                     r=[d_hy[b], d_c], w=[dc])
                if role == 2:
                    kx = (base + c - 1) % 3
                    P.op("vector", lambda e: e.tensor_tensor(out=uT[j % 2][:], in0=cx[kx][:], in1=cc[:], op=ALU.mult),
                         r=[d_cx[kx], dc], w=[d_uT[j % 2]])

            def S3(c, items=items, base=base, blk=blk):
                role, col0, cidx, j = items[c]
                k3 = (base + c) % 3
                if role == 2:
                    uv = uT[j % 2][:].rearrange("p (a b) -> p a b", b=64)
                    for g8 in range(8):
                        q = g8 % 2
                        for tl in range(8):
                            t2 = g8 * 8 + tl
                            P.op("tensor", lambda e: e.transpose(ptr[q][:, tl, :], uv[:, :, t2], ident[:]),
                                 r=[d_uT[j % 2], d_ident], w=[d_ptr[q]])
                        P.op("scalar" if g8 % 2 else "vector",
                             (lambda e: e.activation(out=du[:, g8 * 8:(g8 + 1) * 8, :], in_=ptr[q][:], func=AF.Copy))
                             if g8 % 2 else
                             (lambda e: e.tensor_copy(out=du[:, g8 * 8:(g8 + 1) * 8, :], in_=ptr[q][:])),
                             r=[d_ptr[q]], w=[d_du])
                    P.dma("sync", dq[8], D["DU"][blk, :, j], du[:], r=[d_du])
                if role == 0:
                    P.dma("sync", dq[9], D["X0T"][j], cx[k3][:], r=[d_cx[k3]])

            pipeline(len(items), [S1, S2, S3])
            st8["wi"] += len(items)


def fft_stage1(P, nc, es, K, rhs_of, d_src, Et, d_Et, Ascr, dq, pfx, nps=4):
    ps = [PS(es, nc, f"{pfx}s1p{i}", [128, 512], F32) for i in range(nps)]
    d_ps = [Dep() for _ in range(nps)]
    ast = [T(es, nc, f"{pfx}ast{i}", [66, 2, 512], BF16) for i in range(3)]
    d_ast = [Dep() for _ in range(3)]
    for t2 in range(64):
        a = t2 % 3
        for ri in range(2):
            p = (t2 * 2 + ri) % nps
            P.op("tensor", lambda e: e.matmul(ps[p][0:66, :], Et[0:K, t2, ri, 0:66], rhs_of(t2), start=True, stop=True),
                 r=[d_Et, d_src], w=[d_ps[p]])
            if ri == 0:
                P.op("scalar", lambda e: e.activation(out=ast[a][:, ri, :], in_=ps[p][0:66, :], func=AF.Copy),
                     r=[d_ps[p]], w=[d_ast[a]])
            else:
                P.op("vector", lambda e: e.tensor_copy(out=ast[a][:, ri, :], in_=ps[p][0:66, :]),
                     r=[d_ps[p]], w=[d_ast[a]])
        P.dma("sync", dq[a], Ascr[t2], ast[a][:], r=[d_ast[a]])


def sin_group(P, pm, d_pm, tmp, d_tmp, t1, d_t1, fr, frb, d_f, out, d_out):
    P.op("scalar", lambda e: e.activation(out=tmp[:], in_=pm[:], func=AF.Identity, scale=fr, bias=frb),
         r=[d_pm, d_f], w=[d_tmp])
    P.op("vector", lambda e: e.tensor_scalar(out=t1[:], in0=tmp[:], scalar1=-1.0, scalar2=PI,
                                             op0=ALU.mult, op1=ALU.add),
         r=[d_tmp], w=[d_t1])
    P.op("vector", lambda e: e.tensor_tensor(out=tmp[:], in0=tmp[:], in1=t1[:], op=ALU.min),
         r=[d_t1], w=[d_tmp])
    P.op("vector", lambda e: e.scalar_tensor_tensor(out=tmp[:], in0=t1[:], scalar=-2 * PI, in1=tmp[:],
                                                    op0=ALU.add, op1=ALU.max),
         r=[d_t1], w=[d_tmp])
    P.op("scalar", lambda e: e.activation(out=out, in_=tmp[:], func=AF.Sin), r=[d_tmp], w=[d_out])


def phase_C(P, nc, D, dq, gq, after_w=None):
    with ExitStack() as es:
        Et = T(es, nc, "C_Et", [128, 64, 2, 128], BF16)
        d_Et = Dep()
        P.dma("sync", dq[0], Et[:], D["Etab"], w=[d_Et])
        W1 = T(es, nc, "C_W1", [128, 128], BF16)
        W2 = T(es, nc, "C_W2", [128, 128], BF16)
        W3 = T(es, nc, "C_W3", [128, 128], BF16)
        W4 = T(es, nc, "C_W4", [128, 512], BF16)
        d_w = Dep()
        P.op("gpsimd", lambda e: e.memset(W1[:], 0.0), w=[d_w])
        P.op("gpsimd", lambda e: e.memset(W2[:], 0.0), w=[d_w])
        for hf in range(2):
            P.dma("gpsimd", gq[0], W1[hf * 64:hf * 64 + 33, hf * 64:(hf + 1) * 64], D["fw1"], w=[d_w])
            P.dma("gpsimd", gq[1], W2[hf * 64:(hf + 1) * 64, hf * 64:(hf + 1) * 64], D["fw2"], w=[d_w])
            for h2 in range(2):
                P.dma("gpsimd", gq[2], W3[hf * 64:(hf + 1) * 64, h2 * 64:(h2 + 1) * 64], D["fw3"], w=[d_w])
            P.dma("gpsimd", gq[3], W4[hf * 64:(hf + 1) * 64, :], D["fw4"][:, hf * 512:(hf + 1) * 512], w=[d_w])
        if after_w is not None:
            after_w()
        fb = T(es, nc, "C_fb", [128, 3], F32)
        fr = T(es, nc, "C_fr", [128, 1], F32)
        frb = T(es, nc, "C_frb", [128, 3], F32)
        d_f = Dep()
        P.dma("sync", dq[5], fb[:], D["fb"], w=[d_f])
        P.dma("sync", dq[6], fr[:], D["ffr"], w=[d_f])
        P.op("vector", lambda e: e.tensor_scalar(out=frb[:], in0=fb[:], scalar1=fr[:, 0:1], scalar2=None,
                                                 op0=ALU.mult), r=[d_f], w=[d_f])
        tpos = T(es, nc, "C_tpos", [128, 4, 64], F32)
        vbias = T(es, nc, "C_vbias", [128, 4, 64], F32)
        e0 = T(es, nc, "C_e0", [128, 1], F32)
        negd = T(es, nc, "C_negd", [128, 512], F32)
        hyd = T(es, nc, "C_hyd", [128, 512], F32)
        d_t = Dep()
        P.dma("sync", dq[7], tpos[:], D["tpos"], w=[d_t])
        P.dma("sync", dq[8], vbias[:], D["vbias"], w=[d_t])
        P.dma("sync", dq[9], e0[:], D["e0"], w=[d_t])
        P.dma("sync", dq[10], negd[:], D["negdelta"], w=[d_t])
        P.dma("sync", dq[11], hyd[:], D["hyd"], w=[d_t])
        zs = T(es, nc, "C_zs", [128, 4096], BF16)
        d_z = Dep()
        mk = T(es, nc, "C_mk", [128, 8192], BF16)
        d_mk = Dep()
        hA = T(es, nc, "C_hA", [128, 4096], BF16)
        hB = T(es, nc, "C_hB", [128, 4096], BF16)
        d_hAg = [Dep() for _ in range(4)]
        d_hBg = [Dep() for _ in range(4)]
        h3 = T(es, nc, "C_h3", [128, 8192], BF16)
        d_h3g = [Dep() for _ in range(8)]
        tmp = [T(es, nc, f"C_tmp{i}", [128, 1024], F32) for i in range(2)]
        t1b = [T(es, nc, f"C_t1{i}", [128, 1024], F32) for i in range(2)]
        d_tmp = [Dep() for _ in range(2)]
        d_t1b = [Dep() for _ in range(2)]
        gi = 0
        gf = [T(es, nc, f"C_gf{i}", [128, 512], BF16) for i in range(3)]
        d_gf = [Dep() for _ in range(3)]
        ast = [T(es, nc, f"C_ast{i}", [66, 2, 512], BF16) for i in range(3)]
        d_ast = [Dep() for _ in range(3)]
        ps1 = [PS(es, nc, f"C_ps1{i}", [128, 512], F32) for i in range(2)]
        d_ps1 = [Dep() for _ in range(2)]
        dec = [T(es, nc, f"C_dec{i}", [128, 512], F32) for i in range(2)]
        d_dec = [Dep() for _ in range(2)]
        gt = T(es, nc, "C_gt", [128, 512], F32)
        d_gt = Dep()
        pm = [PS(es, nc, f"C_pm{i}", [128, 1024], F32) for i in range(2)]
        d_pm = [Dep() for _ in range(2)]
        p4 = [PS(es, nc, f"C_p4{i}", [128, 512], F32) for i in range(2)]
        d_p4 = [Dep() for _ in range(2)]
        for s in range(4):
            P.dma("sync", dq[12], zs[:], D["zs"][s], w=[d_z])
            P.dma("sync", dq[13], mk[:], D["maskT"][s], w=[d_mk])
            groups = []
            for g in range(4):
                groups.append((W1[:], zs[:, g * 1024:(g + 1) * 1024], [d_z], hA[:, g * 1024:(g + 1) * 1024],
                               d_hAg[g], 0, None))
            for g in range(4):
                groups.append((W2[:], hA[:, g * 1024:(g + 1) * 1024], [d_hAg[g]], hB[:, g * 1024:(g + 1) * 1024],
                               d_hBg[g], 1, None))
            for g in range(8):
                hf = g // 4
                groups.append((W3[hf * 64:(hf + 1) * 64, :],
                               hB[hf * 64:(hf + 1) * 64, (g % 4) * 1024:(g % 4 + 1) * 1024], [d_hBg[g % 4]],
                               h3[:, g * 1024:(g + 1) * 1024], d_h3g[g], 2, mk[:, g * 1024:(g + 1) * 1024]))

            def M1(i):
                Wm, src, dsrc, dst, ddst, li, mask = groups[i]
                b = i % 2
                for k in range(2):
                    P.op("tensor", lambda e: e.matmul(pm[b][:, k * 512:(k + 1) * 512], Wm, src[:, k * 512:(k + 1) * 512],
                                                      start=True, stop=True),
                         r=[d_w] + dsrc, w=[d_pm[b]])
                P.op("scalar", lambda e: e.activation(out=tmp[b][:], in_=pm[b][:], func=AF.Identity,
                                                      scale=fr[:, 0:1], bias=frb[:, li:li + 1]),
                     r=[d_pm[b], d_f], w=[d_tmp[b]])

            def M2(i):
                b = i % 2
                P.op("vector", lambda e: e.tensor_scalar(out=t1b[b][:], in0=tmp[b][:], scalar1=-1.0, scalar2=PI,
                                                         op0=ALU.mult, op1=ALU.add),
                     r=[d_tmp[b]], w=[d_t1b[b]])
                P.op("vector", lambda e: e.tensor_tensor(out=tmp[b][:], in0=tmp[b][:], in1=t1b[b][:], op=ALU.min),
                     r=[d_t1b[b]], w=[d_tmp[b]])
                P.op("vector", lambda e: e.scalar_tensor_tensor(out=t1b[b][:], in0=t1b[b][:], scalar=-2 * PI,
                                                                in1=tmp[b][:], op0=ALU.add, op1=ALU.max),
                     r=[d_tmp[b]], w=[d_t1b[b]])

            def M3(i):
                Wm, src, dsrc, dst, ddst, li, mask = groups[i]
                b = i % 2
                P.op("scalar", lambda e: e.activation(out=dst, in_=t1b[b][:], func=AF.Sin), r=[d_t1b[b]], w=[ddst])
                if mask is not None:
                    P.op("vector", lambda e: e.tensor_tensor(out=dst, in0=dst, in1=mask, op=ALU.mult),
                         r=[d_mk], w=[ddst])

            pipeline(16, [M1, M2, M3])
            h3v = h3[:].rearrange("p (a b) -> p a b", b=64)

            def L1(t2, s=s, h3v=h3v):
                b = t2 % 2
                P.op("tensor", lambda e: e.matmul(p4[b][:], h3v[:, :, t2], W4[:], start=True, stop=True),
                     r=d_h3g + [d_w], w=[d_p4[b]])
                P.op("scalar", lambda e: e.activation(out=dec[b][:], in_=negd[:], func=AF.Exp,
                                                      scale=tpos[:, s, t2:t2 + 1], bias=vbias[:, s, t2:t2 + 1]),
                     r=[d_t], w=[d_dec[b]])
                k = t2 % 3
                if s == 0 and t2 == 0:
                    P.op("vector", lambda e: e.tensor_tensor(out=gt[:], in0=p4[b][:], in1=dec[b][:], op=ALU.mult),
                         r=[d_p4[b], d_dec[b]], w=[d_gt])
                    P.op("vector", lambda e: e.scalar_tensor_tensor(out=gf[k][:], in0=hyd[:], scalar=e0[:, 0:1],
                                                                    in1=gt[:], op0=ALU.mult, op1=ALU.add),
                         r=[d_t, d_gt], w=[d_gf[k]])
                else:
                    P.op("vector", lambda e: e.tensor_tensor(out=gf[k][:], in0=p4[b][:], in1=dec[b][:], op=ALU.mult),
                         r=[d_p4[b], d_dec[b]], w=[d_gf[k]])

            def L2(t2, s=s):
                k = t2 % 3
                a = t2 % 3
                for ri in range(2):
                    p = (t2 * 2 + ri) % 2
                    P.op("tensor", lambda e: e.matmul(ps1[p][0:66, :], Et[:, t2, ri, 0:66], gf[k][:], start=True, stop=True),
                         r=[d_Et, d_gf[k]], w=[d_ps1[p]])
                    if ri == 0:
                        P.op("scalar", lambda e: e.activation(out=ast[a][:, ri, :], in_=ps1[p][0:66, :], func=AF.Copy),
                             r=[d_ps1[p]], w=[d_ast[a]])
                    else:
                        P.op("vector", lambda e: e.tensor_copy(out=ast[a][:, ri, :], in_=ps1[p][0:66, :]),
                             r=[d_ps1[p]], w=[d_ast[a]])
                P.dma("sync", dq[14 + a], D["AG"][s][t2], ast[a][:], r=[d_ast[a]])

            pipeline(64, [L1, L2])


def phase_D(P, nc, D, dq, gq):
    with ExitStack() as es:
        Et = T(es, nc, "D_Et", [64, 64, 2, 128], BF16)
        d_Et = Dep()
        P.dma("sync", dq[0], Et[:], D["Etab"][0:64], w=[d_Et])
        Du = [T(es, nc, f"D_Du{i}", [64, 4, 64, 128], BF16) for i in range(2)]
        d_Du = [Dep() for _ in range(2)]
        ps = [PS(es, nc, f"D_s1p{i}", [128, 512], F32) for i in range(6)]
        d_ps = [Dep() for _ in range(6)]
        ast = [T(es, nc, f"D_ast{i}", [66, 2, 512], BF16) for i in range(4)]
        d_ast = [Dep() for _ in range(4)]
        P.dma("sync", dq[1], Du[0][:], D["DU"][0], w=[d_Du[0]])
        it = 0
        for blk in range(4):
            b = blk % 2
            if blk + 1 < 4:
                P.dma("sync", dq[1 + (blk + 1) % 2], Du[(blk + 1) % 2][:], D["DU"][blk + 1], w=[d_Du[(blk + 1) % 2]])
            for t2 in range(64):
                a = it % 4
                for ri in range(2):
                    p = (it * 2 + ri) % 6
                    P.op("tensor", lambda e: e.matmul(ps[p][0:66, :], Et[:, t2, ri, 0:66], Du[b][:, :, t2, :],
                                                      start=True, stop=True),
                         r=[d_Et, d_Du[b]], w=[d_ps[p]])
                    if ri == 0:
                        P.op("scalar", lambda e: e.activation(out=ast[a][:, ri, :], in_=ps[p][0:66, :], func=AF.Copy),
                             r=[d_ps[p]], w=[d_ast[a]])
                    else:
                        P.op("vector", lambda e: e.tensor_copy(out=ast[a][:, ri, :], in_=ps[p][0:66, :]),
                             r=[d_ps[p]], w=[d_ast[a]])
                P.dma("sync", dq[3 + a], D["AU"][blk][t2], ast[a][:], r=[d_ast[a]])
                it += 1


KG = 3
NQ = 11


def stage2_mm(P, F2, d_F2, Bt, d_B, kk, out, d_out):
    P.op("tensor", lambda e: e.matmul(out[:, 0, :], F2[:, 0, :], Bt[:, kk, 0, :], start=True, stop=False),
         r=[d_F2, d_B], w=[d_out])
    P.op("tensor", lambda e: e.matmul(out[:, 0, :], F2[:, 2, :], Bt[:, kk, 1, :], start=False, stop=True),
         r=[d_F2, d_B], w=[d_out])
    P.op("tensor", lambda e: e.matmul(out[:, 1, :], F2[:, 1, :], Bt[:, kk, 0, :], start=True, stop=False),
         r=[d_F2, d_B], w=[d_out])
    P.op("tensor", lambda e: e.matmul(out[:, 1, :], F2[:, 0, :], Bt[:, kk, 1, :], start=False, stop=True),
         r=[d_F2, d_B], w=[d_out])


def phase_F(P, nc, D, dq, gq):
    with ExitStack() as es:
        ident, d_ident = load_ident(P, nc, es, D, dq[0])
        F2 = T(es, nc, "F_F2", [128, 3, 128], BF16)
        d_F2 = Dep()
        P.dma("sync", dq[1], F2[:], D["F2tab"], w=[d_F2])
        IT = [T(es, nc, f"F_IT{i}", [128, KG, 3, 128], BF16) for i in range(2)]
        d_IT = [Dep() for _ in range(2)]
        Bg = [[T(es, nc, f"F_Bg{i}_{s}", [128, KG, 2, 512], BF16) for s in range(4)] for i in range(2)]
        Bu = [[T(es, nc, f"F_Bu{i}_{s}", [128, KG, 2, 512], BF16) for s in range(4)] for i in range(2)]
        d_Bg = [[Dep() for s in range(4)] for i in range(2)]
        d_Bu = [[Dep() for s in range(4)] for i in range(2)]
        Gs = [T(es, nc, f"F_Gs{i}", [128, 3, 512], BF16) for i in range(2)]
        d_Gs = [Dep() for _ in range(2)]
        TA = [T(es, nc, f"F_TA{i}", [128, 2, 512], BF16) for i in range(2)]
        d_TA = [Dep() for _ in range(2)]
        TB = [T(es, nc, f"F_TB{i}", [128, 2, 512], BF16) for i in range(2)]
        d_TB = [Dep() for _ in range(2)]
        Yb = [T(es, nc, f"F_Yb{i}", [128, 2, 512], BF16) for i in range(2)]
        d_Yb = [Dep() for _ in range(2)]
        Cst = [T(es, nc, f"F_C{i}", [128, 2, 512], BF16) for i in range(2)]
        d_C = [Dep() for _ in range(2)]
        Gp = PS(es, nc, "F_Gp", [128, 2, 512], F32)
        d_Gp = Dep()
        Up = [PS(es, nc, f"F_Up{i}", [128, 2, 512], F32) for i in range(2)]
        d_Up = [Dep() for _ in range(2)]
        Yp = PS(es, nc, "F_Yp", [128, 2, 512], F32)
        d_Yp = Dep()
        def loads(q):
            qb = q % 2
            P.dma("sync", dq[2 + qb], IT[qb][:], D["ITtab"][:, q * KG:(q + 1) * KG], w=[d_IT[qb]])
            for s in range(4):
                for kp in range(2):
                    P.dma("sync", dq[24 + qb * 16 + s * 2 + kp], Bg[qb][s][kp * 64:(kp + 1) * 64],
                          D["AG"][s][:, kp * 33 + q * KG:kp * 33 + (q + 1) * KG], w=[d_Bg[qb][s]])
                    P.dma("sync", dq[32 + qb * 16 + s * 2 + kp], Bu[qb][s][kp * 64:(kp + 1) * 64],
                          D["AU"][s][:, kp * 33 + q * KG:kp * 33 + (q + 1) * KG], w=[d_Bu[qb][s]])

        def idx(i):
            q, r = divmod(i, KG * 4)
            kk, s = divmod(r, 4)
            return q, kk, s

        def S1(i):
            q, kk, s = idx(i)
            qb = q % 2
            p = i % 2
            if i == 0:
                loads(0)
            if kk == 1 and s == 0 and q + 1 < NQ:
                loads(q + 1)
            stage2_mm(P, F2, d_F2, Bg[qb][s], d_Bg[qb][s], kk, Gp, d_Gp)
            P.op("scalar", lambda e: e.activation(out=Gs[p][:, 0:2, :], in_=Gp[:], func=AF.Copy),
                 r=[d_Gp], w=[d_Gs[p]])
            P.op("scalar", lambda e: e.activation(out=Gs[p][:, 2, :], in_=Gp[:, 1, :], func=AF.Copy, scale=-1.0),
                 r=[d_Gp], w=[d_Gs[p]])
            stage2_mm(P, F2, d_F2, Bu[qb][s], d_Bu[qb][s], kk, Up[p], d_Up[p])

        def S2(i):
            p = i % 2
            P.op("vector", lambda e: e.tensor_tensor(out=TA[p][:], in0=Up[p][:],
                                                     in1=Gs[p][:, 0:1, :].to_broadcast([128, 2, 512]), op=ALU.mult),
                 r=[d_Up[p], d_Gs[p]], w=[d_TA[p]])
            P.op("vector", lambda e: e.tensor_tensor(out=TB[p][:, 0, :], in0=Up[p][:, 1, :], in1=Gs[p][:, 2, :],
                                                     op=ALU.mult),
                 r=[d_Up[p], d_Gs[p]], w=[d_TB[p]])
            P.op("vector", lambda e: e.tensor_tensor(out=TB[p][:, 1, :], in0=Up[p][:, 0, :], in1=Gs[p][:, 1, :],
                                                     op=ALU.mult),
                 r=[d_Up[p], d_Gs[p]], w=[d_TB[p]])

        def S3(i):
            q, kk, s = idx(i)
            qb = q % 2
            p = i % 2
            for ri in range(2):
                P.op("tensor", lambda e: e.matmul(Yp[:, ri, :], ident[:], TA[p][:, ri, :], start=(s == 0), stop=False),
                     r=[d_ident, d_TA[p]], w=[d_Yp])
                P.op("tensor", lambda e: e.matmul(Yp[:, ri, :], ident[:], TB[p][:, ri, :], start=False, stop=(s == 3)),
                     r=[d_ident, d_TB[p]], w=[d_Yp])
            if s != 3:
                return
            c = (q * KG + kk) % 2
            P.op("scalar", lambda e: e.activation(out=Yb[c][:], in_=Yp[:], func=AF.Copy), r=[d_Yp], w=[d_Yb[c]])
            ITq = IT[qb]
            P.op("tensor", lambda e: e.matmul(Gp[:, 0, :], ITq[:, kk, 0, :], Yb[c][:, 0, :], start=True, stop=False),
                 r=[d_IT[qb], d_Yb[c]], w=[d_Gp])
            P.op("tensor", lambda e: e.matmul(Gp[:, 0, :], ITq[:, kk, 2, :], Yb[c][:, 1, :], start=False, stop=True),
                 r=[d_IT[qb], d_Yb[c]], w=[d_Gp])
            P.op("tensor", lambda e: e.matmul(Gp[:, 1, :], ITq[:, kk, 1, :], Yb[c][:, 0, :], start=True, stop=False),
                 r=[d_IT[qb], d_Yb[c]], w=[d_Gp])
            P.op("tensor", lambda e: e.matmul(Gp[:, 1, :], ITq[:, kk, 0, :], Yb[c][:, 1, :], start=False, stop=True),
                 r=[d_IT[qb], d_Yb[c]], w=[d_Gp])
            P.op("vector", lambda e: e.tensor_copy(out=Cst[c][:], in_=Gp[:]), r=[d_Gp], w=[d_C[c]])
            kh = q * KG + kk
            for kp in range(2):
                P.dma("sync", dq[20 + 2 * c + kp], D["CS"][kp * 33 + kh], Cst[c][kp * 64:(kp + 1) * 64], r=[d_C[c]])

        pipeline(NQ * KG * 4, [S1, S2, S3])


def phase_G(P, nc, D, dq, gq):
    with ExitStack() as es:
        IC = T(es, nc, "G_IC", [66, 2, 64], BF16)
        d_IC = Dep()
        P.dma("sync", dq[0], IC[:], D["ICtab"], w=[d_IC])
        x0 = T(es, nc, "G_x0", [128, 4, 4096], BF16)
        d_x0 = Dep()
        P.dma("sync", dq[1], x0[:], D["X0T"].rearrange("j p t -> p j t"), w=[d_x0])
        yh = T(es, nc, "G_yh", [128, 4, 4096], BF16)
        d_yh = Dep()
        Cg = [T(es, nc, f"G_C{i}", [66, 8, 2, 512], BF16) for i in range(2)]
        d_Cg = [Dep() for _ in range(2)]
        ps = [PS(es, nc, f"G_ps{i}", [128, 8, 64], F32) for i in range(4)]
        d_ps = [Dep() for _ in range(4)]
        pi = 0
        for tg in range(8):
            b = tg % 2
            P.dma("sync", dq[2 + b], Cg[b][:], D["CS"][:, tg * 8:(tg + 1) * 8], w=[d_Cg[b]])
            for j in range(4):
                p = pi % 4
                pi += 1
                for tl in range(8):
                    P.op("tensor", lambda e: e.matmul(ps[p][:, tl, :], Cg[b][:, tl, 0, j * 128:(j + 1) * 128],
                                                      IC[:, 0, :], start=True, stop=False),
                         r=[d_Cg[b], d_IC], w=[d_ps[p]])
                    P.op("tensor", lambda e: e.matmul(ps[p][:, tl, :], Cg[b][:, tl, 1, j * 128:(j + 1) * 128],
                                                      IC[:, 1, :], start=False, stop=True),
                         r=[d_Cg[b], d_IC], w=[d_ps[p]])
                ov = yh[:, j, :].rearrange("p (a b) -> p b a", b=64)[:, tg * 8:(tg + 1) * 8, :]
                xv = x0[:, j, :].rearrange("p (a b) -> p b a", b=64)[:, tg * 8:(tg + 1) * 8, :]
                P.op("vector", lambda e: e.tensor_tensor(out=ov, in0=ps[p][:], in1=xv, op=ALU.mult),
                     r=[d_ps[p], d_x0], w=[d_yh])
        P.dma("sync", dq[4], D["YHT"].rearrange("j p t -> p j t"), yh[:], r=[d_yh])


def phase_H(P, nc, D, dq, gq):
    with ExitStack() as es:
        QT = T(es, nc, "H_QT", [128, 4, 4096], BF16)
        KT = T(es, nc, "H_KT", [128, 4, 4608], BF16)
        VA = T(es, nc, "H_VA", [128, 36, 8, 65], BF16)
        d_in = Dep()
        P.dma("sync", dq[1], QT[:], D["QT"].rearrange("j p t -> p j t"), w=[d_in])
        P.dma("sync", dq[2], KT[:], D["KT"].rearrange("j p t -> p j t"), w=[d_in])
        P.dma("sync", dq[3], VA[:].rearrange("p s h d -> p s (h d)"), D["VA"].rearrange("s p x -> p s x"), w=[d_in])
        bt = [T(es, nc, f"H_bt{i}", [128, 8, 6, 256], BF16) for i in range(2)]
        d_bt = [Dep() for _ in range(2)]
        EX = [T(es, nc, f"H_EX{i}", [128, 6, 256], BF16) for i in range(2)]
        d_EX = [Dep() for _ in range(2)]
        PT = [T(es, nc, f"H_PT{i}", [128, 6, 256], BF16) for i in range(2)]
        d_PT = [Dep() for _ in range(2)]
        onesr = T(es, nc, "H_ones", [128, 64], BF16)
        d_ones = Dep()
        P.op("gpsimd", lambda e: e.memset(onesr[:], 1.0), w=[d_ones])
        rdb = [T(es, nc, f"H_rdb{i}", [128, 256], BF16) for i in range(2)]
        d_rdb = [Dep() for _ in range(2)]
        rdf = [T(es, nc, f"H_rdf{i}", [128, 256], F32) for i in range(2)]
        d_rdf = [Dep() for _ in range(2)]
        Bs = [T(es, nc, f"H_Bs{i}", [64, 256], F32) for i in range(2)]
        d_Bs = [Dep() for _ in range(2)]
        yTs = [T(es, nc, f"H_yT{i}", [128, 4, 256], BF16) for i in range(2)]
        d_yTs = [Dep() for _ in range(2)]
        YATv = D["YAT"].rearrange("j p t -> p j t")
        psS = [PS(es, nc, f"H_pS{i}", [128, 512], F32) for i in range(4)]
        d_pS = [Dep() for _ in range(4)]
        psO = [PS(es, nc, f"H_pO{i}", [128, 256], F32) for i in range(2)]
        d_pO = [Dep() for _ in range(2)]
        psB = [PS(es, nc, f"H_pB{i}", [64, 256], F32) for i in range(2)]
        d_pB = [Dep() for _ in range(2)]
        EX3 = EX + [T(es, nc, "H_EX2", [128, 6, 256], BF16)]
        PT3 = PT + [T(es, nc, "H_PT2", [128, 6, 256], BF16)]
        d_EX3 = d_EX + [Dep()]
        d_PT3 = d_PT + [Dep()]
        state = {"cur_v": None, "cur_bb": 0, "si": 0}
        cbb_of = {}

        def S1(i):
            g, h = divmod(i, 8)
            v = 0 if g == 0 else (2 if g == 15 else 1)
            if h == 0 and v != state["cur_v"]:
                bb = v % 2
                P.dma("sync", dq[4 + bb], bt[bb][:], D["abias"][v].rearrange("h p a q -> p h a q"), w=[d_bt[bb]])
                for hh in range(8):
                    P.op("scalar", lambda e: e.activation(out=bt[bb][:, hh], in_=bt[bb][:, hh], func=AF.Exp),
                         w=[d_bt[bb]])
                state["cur_v"] = v
                state["cur_bb"] = bb
            cbb = state["cur_bb"]
            ch, po = h // 2, (h % 2) * 64
            pb = i % 3
            for pp in range(3):
                sp = state["si"] % 4
                state["si"] += 1
                for e2 in range(2):
                    pr = 2 * pp + e2
                    k0 = (4 * g + 2 * pr) * 64
                    P.op("tensor", lambda e: e.matmul(psS[sp][:, e2 * 256:(e2 + 1) * 256],
                                                      KT[po:po + 64, ch, k0:k0 + 128],
                                                      QT[po:po + 64, ch, g * 256:(g + 1) * 256],
                                                      start=True, stop=True),
                         r=[d_in], w=[d_pS[sp]])
                P.op("scalar", lambda e: e.activation(out=EX3[pb][:, 2 * pp:2 * pp + 2, :],
                                                      in_=psS[sp][:].rearrange("p (a q) -> p a q", q=256),
                                                      func=AF.Exp),
                     r=[d_pS[sp]], w=[d_EX3[pb]])
            cbb_of[i] = cbb

        def S1b(i):
            g, h = divmod(i, 8)
            pb = i % 3
            cbb = cbb_of[i]
            P.op("vector", lambda e: e.tensor_tensor(out=PT3[pb][:], in0=EX3[pb][:], in1=bt[cbb][:, h], op=ALU.mult),
                 r=[d_EX3[pb], d_bt[cbb]], w=[d_PT3[pb]])

        def S2(i):
            g, h = divmod(i, 8)
            pb = i % 3
            o = i % 2
            for pr in range(6):
                st = 2 * g + pr
                P.op("tensor", lambda e: e.matmul(psO[o][0:65, :], VA[:, st, h, :], PT3[pb][:, pr, :],
                                                  start=(pr == 0), stop=(pr == 5)),
                     r=[d_PT3[pb], d_in], w=[d_pO[o]])
            with nc.allow_low_precision("reciprocal feeds a bf16 matmul operand (K=1 broadcast)"):
                P.op("vector", lambda e: e.reciprocal(out=rdb[o][64:65, :], in_=psO[o][64:65, :]),
                     r=[d_pO[o]], w=[d_rdb[o]])

        def S3(i):
            g, h = divmod(i, 8)
            ch, po = h // 2, (h % 2) * 64
            o = i % 2
            yb = g % 2
            P.op("tensor", lambda e: e.matmul(psB[o][:], onesr[64:65, :], rdb[o][64:65, :], start=True, stop=True),
                 r=[d_ones, d_rdb[o]], w=[d_pB[o]])
            P.op("scalar", lambda e: e.activation(out=Bs[o][:], in_=psB[o][:], func=AF.Copy),
                 r=[d_pB[o]], w=[d_Bs[o]])
            P.op("vector", lambda e: e.tensor_tensor(out=yTs[yb][po:po + 64, ch, :], in0=psO[o][0:64, :],
                                                     in1=Bs[o][:], op=ALU.mult),
                 r=[d_pO[o], d_Bs[o]], w=[d_yTs[yb]])
            if h == 7:
                P.dma("sync", dq[6 + yb], YATv[:, :, g * 256:(g + 1) * 256], yTs[yb][:], r=[d_yTs[yb]])

        pipeline(128, [S1, S1b, S2, S3])


def phase_W(P, nc, D, wsem):
    for w in wsem:
        w.nobar = True
    k = [0]

    def cv(dst, src):
        P.dma("gpsimd", wsem[k[0] % 4], dst, src)
        k[0] += 1

    for m in range(16):
        cv(D["WG"][m], wchunk_src(D["w_in"], 3072 + m * 128, 128))
    for m in range(8):
        cv(D["WBA"][m], wchunk_src(D["w_br_attn"], m * 128, 128))
        cv(D["WBH"][m], wchunk_src(D["w_br_hyena"], m * 128, 128))
    for kc in range(8):
        cv(D["WO"][:, kc, :], D["w_out"][kc * 128:(kc + 1) * 128, :])
    for f in range(NFF):
        cv(D["WFG"][f], wchunk_src(D["w_gate"], f * 128, 128))
        cv(D["WFU"][f], wchunk_src(D["w_up"], f * 128, 128))
    wdv = D["w_down"].rearrange("(f p) n -> p f n", p=128)
    for dh in range(2):
        for f0 in range(0, NFF, 11):
            cv(D["WD"][dh][:, f0:f0 + 11, :], wdv[:, f0:f0 + 11, dh * 512:(dh + 1) * 512])


def phase_I(P, nc, D, dq, gq):
    TT = 1024
    with ExitStack() as es:
        ident, d_ident = load_ident(P, nc, es, D, dq[0])
        grep, d_grep = make_grep(P, nc, es, D["g_mix"], "gm", dq[1])
        grep2, d_grep2 = make_grep(P, nc, es, D["g_ffn"], "gf", dq[2])
        gfin = T(es, nc, "I_gfin", [128, 1024], F32)
        d_gfin = Dep()
        P.dma("sync", dq[3], gfin[:], D["g_final"], w=[d_gfin])
        xt = T(es, nc, "I_xt", [128, 8, 1024], F32)
        d_xt = [Dep() for _ in range(8)]
        hT = T(es, nc, "I_hT", [128, 8, TT], BF16)
        d_hT = Dep()
        aT = T(es, nc, "I_aT", [128, NFF, TT], BF16)
        d_aT = Dep()
        mT = aT[:, 0:8, :]
        yaT = aT[:, 8:12, :]
        yhT = aT[:, 12:16, :]
        d_yy = Dep()
        d_mT = Dep()
        wout = T(es, nc, "I_wout", [128, 8, 1024], BF16)
        d_wout = Dep()
        wdn = T(es, nc, "I_wdn", [128, NFF, 512], BF16)
        d_wdn = Dep()
        wc = [T(es, nc, f"I_wc{i}", [128, 8, 128], BF16) for i in range(6)]
        d_wc = [Dep() for _ in range(6)]
        wb = [T(es, nc, f"I_wb{i}", [128, 4, 128], BF16) for i in range(4)]
        d_wb = [Dep() for _ in range(4)]
        sg = [T(es, nc, f"I_sg{i}", [128, 512], F32) for i in range(4)]
        d_sg = [Dep() for _ in range(4)]
        xn = [T(es, nc, f"I_xn{i}", [128, 1024], BF16) for i in range(3)]
        d_xn = [Dep() for _ in range(3)]
        junk = T(es, nc, "I_junk", [128, 1024], BF16)
        d_junk = Dep()
        stt = [T(es, nc, f"I_st{i}", [128, 4], F32) for i in range(3)]
        d_st = [Dep() for _ in range(3)]
        yo = [T(es, nc, f"I_yo{i}", [128, 1024], F32) for i in range(2)]
        d_yo = [Dep() for _ in range(2)]
        ps = [PS(es, nc, f"I_ps{i}", [128, 512], F32) for i in range(7)]
        d_ps = [Dep() for _ in range(7)]
        pt = PS(es, nc, "I_pt", [128, 8, 128], BF16)
        d_pt = Dep()
        for i in range(3):
            P.op("gpsimd", lambda e: e.memset(stt[i][:], EPS), w=[d_st[i]])
        P.dma("sync", dq[54], wout[:], D["WO"], w=[d_wout])
        cnt = {"wi": 0, "bi": 0, "pi": 0, "ni": 0}

        def nA(st, gr, d_gr):
            b = st % 3
            P.op("scalar", lambda e: e.activation(out=junk[:], in_=xt[:, st, :], func=AF.Square,
                                                  accum_out=stt[b][:, 0:1]),
                 r=[d_xt[st]], w=[d_junk, d_st[b]])
            P.op("scalar", lambda e: e.activation(out=stt[b][:, 1:2], in_=stt[b][:, 0:1], func=AF.Sqrt,
                                                  scale=1.0 / 1024, bias=stt[b][:, 3:4]),
                 r=[d_st[b]], w=[d_st[b]])

        def nB(st, gr, d_gr):
            b = st % 3
            P.op("vector", lambda e: e.reciprocal(out=stt[b][:, 2:3], in_=stt[b][:, 1:2]), r=[d_st[b]], w=[d_st[b]])
            if st % 2:
                P.op("scalar", lambda e: e.activation(out=xn[b][:], in_=xt[:, st, :], func=AF.Copy,
                                                      scale=stt[b][:, 2:3]),
                     r=[d_st[b], d_xt[st]], w=[d_xn[b]])
            else:
                P.op("vector", lambda e: e.tensor_scalar(out=xn[b][:], in0=xt[:, st, :], scalar1=stt[b][:, 2:3],
                                                         scalar2=None, op0=ALU.mult),
                     r=[d_st[b], d_xt[st]], w=[d_xn[b]])

        def nC(st, gr, d_gr):
            b = st % 3
            for kc in range(8):
                P.op("tensor", lambda e: e.transpose(pt[:, kc, :], xn[b][:, kc * 128:(kc + 1) * 128], ident[:]),
                     r=[d_xn[b], d_ident], w=[d_pt])
            P.op("vector", lambda e: e.tensor_tensor(out=hT[:, :, st * 128:(st + 1) * 128], in0=pt[:], in1=gr[:],
                                                     op=ALU.mult),
                 r=[d_pt, d_gr], w=[d_hT])

        def wload(buf, dbuf, sem, src):
            P.dma("sync", sem, buf[:], src, w=[dbuf])

        for tile in range(4096 // TT):
            tok0 = tile * TT
            P.dma("sync", dq[5], yaT, D["YAT"].rearrange("j p t -> p j t")[:, :, tok0:tok0 + TT], w=[d_yy, d_aT])
            P.dma("sync", dq[6], yhT, D["YHT"].rearrange("j p t -> p j t")[:, :, tok0:tok0 + TT], w=[d_yy, d_aT])

            def L0(st, tok0=tok0):
                P.dma("sync", dq[20 + st], xt[:, st, :],
                      D["xext"][256 + tok0 + st * 128:256 + tok0 + (st + 1) * 128, :], w=[d_xt[st]])
                nA(st, grep, d_grep)

            pipeline(8, [L0, lambda st: nB(st, grep, d_grep), lambda st: nC(st, grep, d_grep)])
            for m in range(8):
                k = cnt["wi"] % 6; w_ga, dga = wc[k], d_wc[k]; wload(w_ga, dga, dq[44 + k], D["WG"][m]); cnt["wi"] += 1
                k = cnt["wi"] % 6; w_gh, dgh = wc[k], d_wc[k]; wload(w_gh, dgh, dq[44 + k], D["WG"][8 + m]); cnt["wi"] += 1
                k = cnt["bi"] % 4; w_ba, dba = wb[k], d_wb[k]; wload(w_ba, dba, dq[50 + k], D["WBA"][m]); cnt["bi"] += 1
                k = cnt["bi"] % 4; w_bh, dbh = wb[k], d_wb[k]; wload(w_bh, dbh, dq[50 + k], D["WBH"][m]); cnt["bi"] += 1
                for th in range(TT // 512):
                    tsl = slice(th * 512, (th + 1) * 512)
                    pi = cnt["pi"]
                    pg = [pi % 7, (pi + 1) % 7, (pi + 2) % 7, (pi + 3) % 7]
                    cnt["pi"] += 4
                    for kc in range(8):
                        P.op("tensor", lambda e: e.matmul(ps[pg[0]][:], w_ga[:, kc, :], hT[:, kc, tsl],
                                                          start=(kc == 0), stop=(kc == 7)),
                             r=[dga, d_hT], w=[d_ps[pg[0]]])
                    for kc in range(8):
                        P.op("tensor", lambda e: e.matmul(ps[pg[1]][:], w_gh[:, kc, :], hT[:, kc, tsl],
                                                          start=(kc == 0), stop=(kc == 7)),
                             r=[dgh, d_hT], w=[d_ps[pg[1]]])
                    for kc in range(4):
                        P.op("tensor", lambda e: e.matmul(ps[pg[2]][:], w_ba[:, kc, :], yaT[:, kc, tsl],
                                                          start=(kc == 0), stop=(kc == 3)),
                             r=[dba, d_yy], w=[d_ps[pg[2]]])
                    for kc in range(4):
                        P.op("tensor", lambda e: e.matmul(ps[pg[3]][:], w_bh[:, kc, :], yhT[:, kc, tsl],
                                                          start=(kc == 0), stop=(kc == 3)),
                             r=[dbh, d_yy], w=[d_ps[pg[3]]])
                    sa = (2 * (m * 2 + th)) % 4
                    P.op("scalar", lambda e: e.activation(out=sg[sa][:], in_=ps[pg[0]][:], func=AF.Sigmoid),
                         r=[d_ps[pg[0]]], w=[d_sg[sa]])
                    P.op("scalar", lambda e: e.activation(out=sg[sa + 1][:], in_=ps[pg[1]][:], func=AF.Sigmoid),
                         r=[d_ps[pg[1]]], w=[d_sg[sa + 1]])
                    P.op("vector", lambda e: e.tensor_tensor(out=sg[sa][:], in0=ps[pg[2]][:], in1=sg[sa][:], op=ALU.mult),
                         r=[d_ps[pg[2]]], w=[d_sg[sa]])
                    P.op("vector", lambda e: e.tensor_tensor(out=sg[sa + 1][:], in0=ps[pg[3]][:], in1=sg[sa + 1][:],
                                                             op=ALU.mult),
                         r=[d_ps[pg[3]]], w=[d_sg[sa + 1]])
                    P.op("vector", lambda e: e.tensor_tensor(out=mT[:, m, tsl], in0=sg[sa][:], in1=sg[sa + 1][:],
                                                             op=ALU.add),
                         r=[d_sg[sa], d_sg[sa + 1]], w=[d_mT])

            def O1(st):
                for dh in range(2):
                    p = cnt["pi"] % 7
                    cnt["pi"] += 1
                    for kc in range(8):
                        P.op("tensor", lambda e: e.matmul(ps[p][:], mT[:, kc, st * 128:(st + 1) * 128],
                                                          wout[:, kc, dh * 512:(dh + 1) * 512],
                                                          start=(kc == 0), stop=(kc == 7)),
                             r=[d_mT, d_wout], w=[d_ps[p]])
                    P.op("vector", lambda e: e.tensor_tensor(out=xt[:, st, dh * 512:(dh + 1) * 512], in0=ps[p][:],
                                                             in1=xt[:, st, dh * 512:(dh + 1) * 512], op=ALU.add),
                         r=[d_ps[p]], w=[d_xt[st]])
                nA(st, grep2, d_grep2)

            pipeline(8, [O1, lambda st: nB(st, grep2, d_grep2), lambda st: nC(st, grep2, d_grep2)])
            for f in range(NFF):
                k = cnt["wi"] % 6; w_g, dg = wc[k], d_wc[k]; wload(w_g, dg, dq[44 + k], D["WFG"][f]); cnt["wi"] += 1
                k = cnt["wi"] % 6; w_u, dup = wc[k], d_wc[k]; wload(w_u, dup, dq[44 + k], D["WFU"][f]); cnt["wi"] += 1
                if f == 2:
                    P.dma("sync", dq[55], wdn[:], D["WD"][0], w=[d_wdn])
                for th in range(TT // 512):
                    tsl = slice(th * 512, (th + 1) * 512)
                    pi = cnt["pi"]
                    pg = [pi % 7, (pi + 1) % 7]
                    cnt["pi"] += 2
                    for kc in range(8):
                        P.op("tensor", lambda e: e.matmul(ps[pg[0]][:], w_g[:, kc, :], hT[:, kc, tsl],
                                                          start=(kc == 0), stop=(kc == 7)),
                             r=[dg, d_hT], w=[d_ps[pg[0]]])
                    for kc in range(8):
                        P.op("tensor", lambda e: e.matmul(ps[pg[1]][:], w_u[:, kc, :], hT[:, kc, tsl],
                                                          start=(kc == 0), stop=(kc == 7)),
                             r=[dup, d_hT], w=[d_ps[pg[1]]])
                    sa = (f * 2 + th) % 4
                    P.op("scalar", lambda e: e.activation(out=sg[sa][:], in_=ps[pg[0]][:], func=AF.Silu),
                         r=[d_ps[pg[0]]], w=[d_sg[sa]])
                    P.op("vector", lambda e: e.tensor_tensor(out=aT[:, f, tsl], in0=ps[pg[1]][:], in1=sg[sa][:],
                                                             op=ALU.mult),
                         r=[d_ps[pg[1]], d_sg[sa]], w=[d_aT, d_mT, d_yy])
            for dh in range(2):
                if dh == 1:
                    P.dma("sync", dq[55], wdn[:], D["WD"][1], w=[d_wdn])
                for st in range(8):
                    p = cnt["pi"] % 7
                    cnt["pi"] += 1
                    for f in range(NFF):
                        P.op("tensor", lambda e: e.matmul(ps[p][:], aT[:, f, st * 128:(st + 1) * 128], wdn[:, f, :],
                                                          start=(f == 0), stop=(f == NFF - 1)),
                             r=[d_aT, d_mT, d_yy, d_wdn], w=[d_ps[p]])
                    P.op("vector", lambda e: e.tensor_tensor(out=xt[:, st, dh * 512:(dh + 1) * 512], in0=ps[p][:],
                                                             in1=xt[:, st, dh * 512:(dh + 1) * 512], op=ALU.add),
                         r=[d_ps[p]], w=[d_xt[st]])
                    if dh == 1:
                        b = st % 3
                        yb = st % 2
                        P.op("scalar", lambda e: e.activation(out=junk[:], in_=xt[:, st, :], func=AF.Square,
                                                              accum_out=stt[b][:, 0:1]),
                             r=[d_xt[st]], w=[d_junk, d_st[b]])
                        P.op("scalar", lambda e: e.activation(out=stt[b][:, 1:2], in_=stt[b][:, 0:1], func=AF.Sqrt,
                                                              scale=1.0 / 1024, bias=stt[b][:, 3:4]),
                             r=[d_st[b]], w=[d_st[b]])
                        P.op("vector", lambda e: e.reciprocal(out=stt[b][:, 2:3], in_=stt[b][:, 1:2]),
                             r=[d_st[b]], w=[d_st[b]])
                        P.op("vector", lambda e: e.scalar_tensor_tensor(out=yo[yb][:], in0=xt[:, st, :],
                                                                        scalar=stt[b][:, 2:3], in1=gfin[:],
                                                                        op0=ALU.mult, op1=ALU.mult),
                             r=[d_st[b], d_xt[st], d_gfin], w=[d_yo[yb]])
                        P.dma("sync", dq[16 + yb], D["y"][tok0 + st * 128:tok0 + (st + 1) * 128, :], yo[yb][:],
                              r=[d_yo[yb]])


def _const_tables():
    N = 8192
    t1 = np.arange(128)[:, None, None]
    t2 = np.arange(64)[None, :, None]
    k1 = np.arange(128)[None, None, :]
    th = 2 * np.pi * (((64 * t1 + t2) * k1) % N) / N
    E = np.stack([np.cos(th), -np.sin(th)], axis=2)
    a = np.arange(64)
    th2 = 2 * np.pi * np.outer(a, a) / 64
    F2 = np.zeros((128, 3, 128))
    for kp in range(2):
        sl = slice(kp * 64, (kp + 1) * 64)
        F2[sl, 0, sl] = np.cos(th2)
        F2[sl, 1, sl] = -np.sin(th2)
        F2[sl, 2, sl] = np.sin(th2)
    IT = np.zeros((128, 33, 3, 128))
    k2 = np.arange(64)[:, None]
    tt = np.arange(64)[None, :]
    for kh in range(33):
        for kp in range(2):
            k1v = kh + 33 * kp
            ph = 2 * np.pi * (tt * k2 / 64.0 + tt * k1v / 8192.0)
            sl = slice(kp * 64, (kp + 1) * 64)
            IT[sl, kh, 0, sl] = np.cos(ph)
            IT[sl, kh, 1, sl] = np.sin(ph)
            IT[sl, kh, 2, sl] = -np.sin(ph)
    kk = np.arange(66)[:, None]
    t1v = np.arange(64)[None, :]
    th3 = 2 * np.pi * kk * t1v / 128.0
    wk = np.full((66, 1), 2.0)
    wk[0] = 1.0
    wk[64] = 1.0
    wk[65] = 0.0
    IC = np.stack([wk * np.cos(th3) / N, -wk * np.sin(th3) / N], axis=1)
    return (E.astype(BF), F2.astype(BF), IT.astype(BF), IC.astype(BF))


def _filter_tables(L, ksegs):
    f32 = np.float32
    tlin = np.linspace(0.0, 1.0, L, dtype=f32)
    omega = (2.0 * math.pi * np.arange(L, dtype=f32) / L).astype(f32)
    fbv = np.linspace(1e-4, 15, 16, dtype=f32)
    n = np.arange(8192)
    d = np.where(n < 4096, n, n - 8192)
    zs = np.zeros((4, 128, 4096), f32)
    maskT = np.zeros((4, 128, 8192), f32)
    tpos = np.zeros((4, 8192), f32)
    vbias = np.full((4, 8192), NEG, f32)
    for s, k in enumerate(ksegs):
        if k is None:
            idx = np.zeros(8192, np.int64)
            valid = np.zeros(8192, bool)
            Dl = np.zeros(8192, np.int64)
        else:
            Dl = k * 4096 + d
            valid = (np.abs(Dl) <= L - 1) & (n != 4096)
            idx = np.where(valid, np.abs(Dl), 0)
        ang = (omega[idx][:, None] * fbv[None, :]).astype(f32)
        z = np.concatenate([tlin[idx][:, None], np.cos(ang), -np.sin(ang)], axis=1)
        zs[s, 0:33, :] = z[0:4096].T
        zs[s, 64:97, :] = z[4096:8192].T
        tpos[s] = tlin[idx]
        vbias[s] = np.where(valid, 0.0, NEG)
        maskT[s, 0:64, :] = (valid & (Dl >= 0)).astype(f32)[None, :]
        maskT[s, 64:128, :] = (valid & (Dl <= 0)).astype(f32)[None, :]
    def lay(a):
        a = a.reshape(4, 128, 64)
        return np.ascontiguousarray(np.transpose(a, (1, 0, 2)))
    return zs.astype(BF), maskT.astype(BF), lay(tpos), lay(vbias)


def _attn_bias(rpb, R0, rows):
    out = np.full((3, 8, 128, 6, 256), NEG, np.float32)
    qc = np.arange(64)
    kc = np.arange(64)
    cs = np.clip(qc - 8, 0, 48)
    colin = (kc[None, :] >= cs[:, None]) & (kc[None, :] < cs[:, None] + 16)
    dc = np.clip(kc[None, :] - qc[:, None], -15, 15) + 15
    for v, g in enumerate((0, 1, 15)):
        for rl in range(4):
            r = 4 * g + rl
            rg = R0 + r
            ws = int(np.clip(rg - 4, 0, rows - 8)) - R0 + 4
            for pr in range(6):
                for er in range(2):
                    e = 4 * g + 2 * pr + er
                    if not (0 <= e - ws < 8):
                        continue
                    drr = e - 4 - r + 7
                    blk = rpb[:, drr, :][:, dc]
                    blk = np.where(colin[None], blk, NEG)
                    out[v, :, er * 64:(er + 1) * 64, pr, rl * 64:(rl + 1) * 64] = np.transpose(blk, (0, 2, 1))
    return out.astype(BF)


def _col128(v, n):
    return np.ascontiguousarray(np.asarray(v, np.float32).reshape(n, 128).T)


def prepare_inputs(inputs, cores=range(8)):
    f32 = np.float32
    I = {k: np.asarray(v) for k, v in inputs.items()}
    E, F2, IT, IC = _const_tables()
    max_decay = math.log(1e-2) / 0.3
    min_decay = math.log(1e-2) / 1.5
    deltas = np.abs(np.linspace(min_decay, max_decay, 512, dtype=f32))
    common = {
        "w_in": I["w_in"][0], "w_br_attn": I["w_br_attn"][0], "w_br_hyena": I["w_br_hyena"][0],
        "w_out": I["w_out"][0], "w_gate": I["w_gate"][0], "w_up": I["w_up"][0], "w_down": I["w_down"][0],
        "g_mix": _col128(I["norm_mix"][0], 8), "g_ffn": _col128(I["norm_ffn"][0], 8),
        "g_final": np.ascontiguousarray(np.broadcast_to(I["norm_final"][None, :], (128, 1024))).astype(f32),
        "conv_w": np.ascontiguousarray(np.transpose(I["conv_w"][0].reshape(3, 12, 128), (2, 1, 0))).astype(f32),
        "conv_b": _col128(I["conv_b"][0], 12),
        "fw1": I["filt_w1"][0], "fw2": I["filt_w2"][0], "fw3": I["filt_w3"][0], "fw4": I["filt_w4"][0],
        "fb": np.ascontiguousarray(np.tile(np.stack([I["filt_b1"][0], I["filt_b2"][0], I["filt_b3"][0]], axis=1),
                                           (2, 1))).astype(f32),
        "ffr": np.ascontiguousarray(np.tile(I["filt_freq"][0][:, None], (2, 1))).astype(f32),
        "e0": np.eye(128, 1, dtype=f32),
        "hyd": np.ascontiguousarray(np.broadcast_to(I["hyena_d"][0][None, :], (128, 512))).astype(f32),
        "ident": np.eye(128, dtype=f32).astype(BF),
        "Etab": E, "F2tab": F2, "ITtab": IT, "ICtab": IC,
        "negdelta": np.ascontiguousarray(np.broadcast_to(-deltas[None, :], (128, 512))).astype(f32),
    }
    rpb = I["rpb"][0].astype(f32)
    xp = I["x_prompt"]
    xs = I["x_sample"][0]
    tab_prompt = _filter_tables(4096, [0, None, None, None])
    ab_prompt = _attn_bias(rpb, 0, 64)
    maps = []
    for c in cores:
        m = dict(common)
        xext = np.zeros((4608, 1024), f32)
        xctx = np.zeros((4, 4224, 1024), f32)
        if c < 4:
            xext[256:4352] = xp[c]
            xctx[0, :4096] = xp[c]
            zs, maskT, tpos, vbias = tab_prompt
            ab = ab_prompt
        else:
            j = c - 4
            L = 16384
            lo, hi = j * 4096 - 256, (j + 1) * 4096 + 256
            a, b = max(lo, 0), min(hi, L)
            xext[a - lo:b - lo] = xs[a:b]
            blks = [j] + [bb for bb in range(4) if bb != j]
            for i, bb in enumerate(blks):
                xctx[i, :4096] = xs[bb * 4096:(bb + 1) * 4096]
                if bb * 4096 - 1 >= 0:
                    xctx[i, 4096] = xs[bb * 4096 - 1]
                if (bb + 1) * 4096 < L:
                    xctx[i, 4097] = xs[(bb + 1) * 4096]
            zs, maskT, tpos, vbias = _filter_tables(L, [j - bb for bb in blks])
            ab = _attn_bias(rpb, j * 64, 256)
        m.update({"xext": xext, "xctx": xctx, "zs": zs, "maskT": maskT, "tpos": tpos, "vbias": vbias, "abias": ab})
        maps.append(m)
    return maps


_NC_CACHE = {}


def kernel(**inputs):
    if "nc" not in _NC_CACHE:
        _NC_CACHE["nc"] = build_program()
    nc = _NC_CACHE["nc"]
    maps = prepare_inputs(inputs)
    res = run_bass_kernel_spmd(nc, maps, core_ids=list(range(8)))
    ys = [np.asarray(res.results[i]["y"], dtype=np.float32) for i in range(8)]
    y_prompt = np.stack(ys[0:4], axis=0)
    y_sample = np.concatenate(ys[4:8], axis=0)[None]
    return (y_prompt, y_sample)
```
